# Optimizing a Trainium2 kernel written in Bass

```python
import math
import jax, jax.numpy as jnp
from jax import lax
import numpy as np

D_MODEL = 2048
BATCH = 1
SEQ = 8192
DEPTH = 4

N_MIXERS = 2
CONV_WIDTH = 31
SSM_GROUP = 16
SSM_GROUPS = D_MODEL // SSM_GROUP
SSM_STATE = 64
D_FF = 4 * D_MODEL
N_CONV_LAYERS = (DEPTH + 1) // 2
N_SSM_LAYERS = DEPTH // 2
EPS = 1e-6
DT_MIN = 1e-3
DT_MAX = 1e-1

kernel_name = "interleaved_conformer_conv_s5_hybrid"


def rms_norm(x, g):
    xf = x.astype(jnp.float32)
    y = xf * lax.rsqrt(jnp.mean(xf * xf, axis=-1, keepdims=True) + EPS)
    return (y * g.astype(jnp.float32)).astype(x.dtype)


def layer_norm(x, g, b):
    xf = x.astype(jnp.float32)
    mu = jnp.mean(xf, axis=-1, keepdims=True)
    xc = xf - mu
    var = jnp.mean(xc * xc, axis=-1, keepdims=True)
    y = xc * lax.rsqrt(var + EPS) * g.astype(jnp.float32) + b.astype(jnp.float32)
    return y.astype(x.dtype)


def conv_module(h, w_in, b_in, dw, dw_b, ln_g, ln_b, w_out, b_out):
    u = h @ w_in + b_in
    a, gate = jnp.split(u, 2, axis=-1)
    v = a * jax.nn.sigmoid(gate)
    v = lax.conv_general_dilated(
        v, dw[:, None, :], window_strides=(1,),
        padding=[(CONV_WIDTH - 1, 0)],
        dimension_numbers=("NWC", "WIO", "NWC"),
        feature_group_count=D_MODEL) + dw_b
    v = layer_norm(v, ln_g, ln_b)
    v = jax.nn.silu(v)
    return v @ w_out + b_out


def ssm_module(h, lam_re, lam_im, log_dt, b_re, b_im, c_re, c_im, d_skip, w_glu):
    f32 = jnp.float32
    bsz, L, _ = h.shape
    u = h.astype(f32).reshape(bsz, L, SSM_GROUPS, SSM_GROUP)
    dt = jnp.exp(log_dt.astype(f32))[:, None]
    lr = lam_re.astype(f32)
    li = lam_im.astype(f32)
    mag = jnp.exp(lr * dt)
    ab_re = mag * jnp.cos(li * dt)
    ab_im = mag * jnp.sin(li * dt)
    nr = ab_re - 1.0
    ni = ab_im
    den = lr * lr + li * li
    k_re = ((nr * lr + ni * li) / den)[..., None]
    k_im = ((ni * lr - nr * li) / den)[..., None]
    br = b_re.astype(f32)
    bi = b_im.astype(f32)
    bb_re = k_re * br - k_im * bi
    bb_im = k_re * bi + k_im * br
    bu_re = jnp.einsum("blgc,gpc->blgp", u, bb_re)
    bu_im = jnp.einsum("blgc,gpc->blgp", u, bb_im)
    a_re = jnp.broadcast_to(ab_re, bu_re.shape)
    a_im = jnp.broadcast_to(ab_im, bu_im.shape)

    def combine(e1, e2):
        a1r, a1i, b1r, b1i = e1
        a2r, a2i, b2r, b2i = e2
        return (a2r * a1r - a2i * a1i,
                a2r * a1i + a2i * a1r,
                a2r * b1r - a2i * b1i + b2r,
                a2r * b1i + a2i * b1r + b2i)

    _, _, s_re, s_im = lax.associative_scan(combine, (a_re, a_im, bu_re, bu_im), axis=1)
    y = (jnp.einsum("blgp,gcp->blgc", s_re, c_re.astype(f32))
         - jnp.einsum("blgp,gcp->blgc", s_im, c_im.astype(f32)))
    y = y.reshape(bsz, L, D_MODEL) + d_skip.astype(f32) * h.astype(f32)
    y = jax.nn.gelu(y).astype(h.dtype)
    z = y @ w_glu
    val, gate = jnp.split(z, 2, axis=-1)
    return val * jax.nn.sigmoid(gate)


def mlp(h, w_up, w_down):
    a = jax.nn.relu(h @ w_up)
    return (a * a) @ w_down


def setup_inputs(seed: int = 0) -> dict:
    key = jax.random.key(seed)
    ks = jax.random.split(key, 24)
    f32 = jnp.float32
    D, NC, NS, G, P, C = D_MODEL, N_CONV_LAYERS, N_SSM_LAYERS, SSM_GROUPS, SSM_STATE, SSM_GROUP

    def nrm(k, shape, scale):
        return jax.random.normal(k, shape, f32) * scale

    x = jax.random.normal(ks[0], (BATCH, SEQ, D), f32)
    mix_norm = 1.0 + nrm(ks[1], (DEPTH, D), 0.02)
    conv_w_in = nrm(ks[2], (NC, D, 2 * D), D ** -0.5)
    conv_b_in = nrm(ks[3], (NC, 2 * D), 0.02)
    conv_dw = nrm(ks[4], (NC, CONV_WIDTH, D), CONV_WIDTH ** -0.5)
    conv_dw_b = nrm(ks[5], (NC, D), 0.02)
    conv_ln_g = 1.0 + nrm(ks[6], (NC, D), 0.02)
    conv_ln_b = nrm(ks[7], (NC, D), 0.02)
    conv_w_out = nrm(ks[8], (NC, D, D), D ** -0.5)
    conv_b_out = nrm(ks[9], (NC, D), 0.02)
    ssm_lambda_re = -0.5 + nrm(ks[10], (NS, G, P), 0.01)
    ssm_lambda_im = (math.pi * jnp.arange(P, dtype=f32))[None, None, :] + nrm(ks[11], (NS, G, P), 0.01)
    ssm_log_dt = jax.random.uniform(ks[12], (NS, G), f32, math.log(DT_MIN), math.log(DT_MAX))
    ssm_b_re = nrm(ks[13], (NS, G, P, C), (2 * C) ** -0.5)
    ssm_b_im = nrm(ks[14], (NS, G, P, C), (2 * C) ** -0.5)
    ssm_c_re = nrm(ks[15], (NS, G, C, P), P ** -0.5)
    ssm_c_im = nrm(ks[16], (NS, G, C, P), P ** -0.5)
    ssm_d = 1.0 + nrm(ks[17], (NS, D), 0.1)
    ssm_w_glu = nrm(ks[18], (NS, D, 2 * D), D ** -0.5)
    mlp_norm = 1.0 + nrm(ks[19], (DEPTH, D), 0.02)
    mlp_w_up = nrm(ks[20], (DEPTH, D, D_FF), D ** -0.5)
    mlp_w_down = nrm(ks[21], (DEPTH, D_FF, D), D_FF ** -0.5)
    final_norm = 1.0 + nrm(ks[22], (D,), 0.02)
    return {"x": x, "mix_norm": mix_norm,
            "conv_w_in": conv_w_in, "conv_b_in": conv_b_in, "conv_dw": conv_dw,
            "conv_dw_b": conv_dw_b, "conv_ln_g": conv_ln_g, "conv_ln_b": conv_ln_b,
            "conv_w_out": conv_w_out, "conv_b_out": conv_b_out,
            "ssm_lambda_re": ssm_lambda_re, "ssm_lambda_im": ssm_lambda_im,
            "ssm_log_dt": ssm_log_dt, "ssm_b_re": ssm_b_re, "ssm_b_im": ssm_b_im,
            "ssm_c_re": ssm_c_re, "ssm_c_im": ssm_c_im, "ssm_d": ssm_d,
            "ssm_w_glu": ssm_w_glu, "mlp_norm": mlp_norm, "mlp_w_up": mlp_w_up,
            "mlp_w_down": mlp_w_down, "final_norm": final_norm}


def reference(x, mix_norm, conv_w_in, conv_b_in, conv_dw, conv_dw_b, conv_ln_g, conv_ln_b,
              conv_w_out, conv_b_out, ssm_lambda_re, ssm_lambda_im, ssm_log_dt, ssm_b_re,
              ssm_b_im, ssm_c_re, ssm_c_im, ssm_d, ssm_w_glu, mlp_norm, mlp_w_up,
              mlp_w_down, final_norm):
    for i in range(DEPTH):
        h = rms_norm(x, mix_norm[i])
        j = i // N_MIXERS
        if i % N_MIXERS == 0:
            x = x + conv_module(h, conv_w_in[j], conv_b_in[j], conv_dw[j], conv_dw_b[j],
                                conv_ln_g[j], conv_ln_b[j], conv_w_out[j], conv_b_out[j])
        else:
            x = x + ssm_module(h, ssm_lambda_re[j], ssm_lambda_im[j], ssm_log_dt[j],
                               ssm_b_re[j], ssm_b_im[j], ssm_c_re[j], ssm_c_im[j],
                               ssm_d[j], ssm_w_glu[j])
        h = rms_norm(x, mlp_norm[i])
        x = x + mlp(h, mlp_w_up[i], mlp_w_down[i])
    return rms_norm(x, final_norm)
```

```python
import numpy as np
import concourse.bass as bass
import concourse.mybir as mybir
from concourse.bass_utils import run_bass_kernel_spmd

F32 = mybir.dt.float32
BF16 = mybir.dt.bfloat16
AF = mybir.ActivationFunctionType
ALU = mybir.AluOpType

NCORES = 8
D = 2048
SEQ = 8192
TOK = SEQ // NCORES
KT = D // 128
DFF = 4 * D
CW = 31
HALO = 32
EPS = 1e-6


class Prog:
    ENG = ("pe", "act", "dve", "pool", "sp")

    def __init__(self, nc):
        self.nc = nc
        self.ops = []
        self.last_w = {}
        self.readers = {}
        self.dma_sems = {}
        self.epoch = 0

    def _deps(self, reads, writes):
        deps = set()
        for t in reads:
            w = self.last_w.get(t)
            if w is not None:
                deps.add(w)
        for t in writes:
            w = self.last_w.get(t)
            if w is not None:
                deps.add(w)
            for r in self.readers.get(t, ()):
                deps.add(r)
        return deps

    def _commit(self, idx, reads, writes):
        o = self.ops[idx]
        for t in reads:
            lst = self.readers.setdefault(t, [])
            if o["dma"] is None:
                lst[:] = [r for r in lst if not (self.ops[r]["dma"] is None and self.ops[r]["eng"] == o["eng"])]
            lst.append(idx)
        for t in writes:
            self.last_w[t] = idx
            self.readers[t] = []

    def barrier(self):
        nc = self.nc
        scr = self.barrier_scratch
        idx = len(self.ops)
        deps = self._deps([], [("phase",)])
        self.epoch += 1
        self.ops.append(dict(eng="dve", fn=lambda e: e.memset(scr[:, :], 0.0), deps=deps, dma=None, ms=None, inc=16, ep=self.epoch))
        self.last_w[("phase",)] = idx
        self.readers[("phase",)] = []
        return idx

    def op(self, eng, fn, reads=(), writes=()):
        idx = len(self.ops)
        reads = list(reads) + [("phase",)]
        deps = self._deps(reads, writes)
        deps.discard(idx)
        self.ops.append(dict(eng=eng, fn=fn, deps=deps, dma=None, ms=None, inc=16, ep=self.epoch))
        self._commit(idx, reads, writes)
        return idx

    def dma(self, eng, fn, reads, writes, sem, inc=16):
        idx = len(self.ops)
        reads = list(reads) + [("phase",)]
        deps = self._deps(reads, writes)
        cnt = self.dma_sems.setdefault(sem, [0])
        cnt[0] += inc
        self.ops.append(dict(eng=eng, fn=fn, deps=deps, dma=(sem, cnt[0]), ms=None, inc=inc, ep=self.epoch))
        self._commit(idx, reads, writes)
        return idx

    def emit(self, final_wait_ops=()):
        nc = self.nc
        ops = self.ops
        needed = set()
        for o in ops:
            for d in o["deps"]:
                needed.add(d)
        counters = {}
        for i, o in enumerate(ops):
            if o["dma"] is None and i in needed:
                k_ = (o["eng"], o["ep"])
                counters[k_] = counters.get(k_, 0) + 1
                o["ms"] = counters[k_]
        from contextlib import ExitStack
        with ExitStack() as es:
            esem = {k_: es.enter_context(nc.semaphore("e_%s_%d" % k_)) for k_ in counters}
            dsem = {k: es.enter_context(nc.semaphore("d_" + str(k))) for k in self.dma_sems}
            block = es.enter_context(nc.Block())

            dma_hist = {}
            for oi, o_ in enumerate(ops):
                if o_["dma"] is not None:
                    dma_hist.setdefault(o_["dma"][0], []).append((oi, o_["dma"][1]))

            def run_engine(eng_name, eng):
                waited = {}
                for i, o in enumerate(ops):
                    if o["eng"] != eng_name:
                        continue
                    w = {}
                    for d in o["deps"]:
                        po = ops[d]
                        if po["dma"] is not None:
                            key = ("d", po["dma"][0])
                            val = po["dma"][1]
                            for (oi, cv_) in dma_hist[po["dma"][0]]:
                                if oi < i and cv_ > val:
                                    val = cv_
                        else:
                            if po["eng"] == eng_name and False:
                                continue
                            key = ("e", (po["eng"], po["ep"]))
                            val = po["ms"]
                        if val > w.get(key, 0):
                            w[key] = val
                    for key, val in w.items():
                        if waited.get(key, 0) >= val:
                            continue
                        waited[key] = val
                        s = dsem[key[1]] if key[0] == "d" else esem[key[1]]
                        eng.wait_ge(s, val)
                    ins = o["fn"](eng)
                    if o["dma"] is not None:
                        if o["inc"] == 1:
                            ins.then_inc(dsem[o["dma"][0]])
                        else:
                            ins.then_inc(dsem[o["dma"][0]], 16)
                    elif o["ms"] is not None:
                        ins.then_inc(esem[(eng_name, o["ep"])], 1)
                if eng_name == "sp":
                    for i in final_wait_ops:
                        po = ops[i]
                        eng.wait_ge(dsem[po["dma"][0]], po["dma"][1])

            @block.tensor
            def _(e):
                run_engine("pe", e)

            @block.scalar
            def _(e):
                run_engine("act", e)

            @block.vector
            def _(e):
                run_engine("dve", e)

            @block.gpsimd
            def _(e):
                run_engine("pool", e)

            @block.sync
            def _(e):
                run_engine("sp", e)


class Ctx:
    def __init__(self, nc, es):
        self.nc = nc
        self.es = es
        self.P = Prog(nc)
        self.ps_rr = 0

    def sb(self, name, shape, dt):
        return self.es.enter_context(self.nc.sbuf_tensor(name, shape, dt))

    def ps(self, name, shape, dt=F32):
        return self.es.enter_context(self.nc.psum_tensor(name, shape, dt))


def emit_rmsnorm(c, xT, hT, gcol, epsc, ones, sqb, rstd, psb, n_tok=TOK, x_tag="x", h_tag="h"):
    P = c.P
    nh = n_tok // 512
    i = 0
    for kt in range(KT):
        for h in range(nh):
            s = i % 2
            i += 1
            P.op("act", lambda e, kt=kt, s=s, h=h: e.activation(out=sqb[:, s, :], in_=xT[:, kt, h * 512:(h + 1) * 512], func=AF.Square),
                 reads=[(x_tag, kt)], writes=[("tmp", s)])
            P.op("pe", lambda e, kt=kt, s=s, h=h: e.matmul(psb[h][:, :], lhsT=ones[:, :], rhs=sqb[:, s, :],
                                                          start=(kt == 0), stop=(kt == KT - 1)),
                 reads=[("tmp", s), ("ones",)], writes=[("psn", h)])
    for h in range(nh):
        P.op("act", lambda e, h=h: e.activation(out=rstd[:, h * 512:(h + 1) * 512], in_=psb[h][:, :], func=AF.Sqrt, bias=epsc[:, 0:1], scale=1.0 / D),
             reads=[("psn", h), ("epsc",)], writes=[("rstd", h)])
        P.op("dve", lambda e, h=h: e.reciprocal(out=rstd[:, h * 512:(h + 1) * 512], in_=rstd[:, h * 512:(h + 1) * 512]),
             reads=[("rstd", h)], writes=[("rstd", h)])
    for kt in range(KT):
        eng = "dve"
        P.op(eng, lambda e, kt=kt: e.scalar_tensor_tensor(out=hT[:, kt, :n_tok], in0=xT[:, kt, :n_tok], scalar=gcol[:, kt:kt + 1],
                                                          in1=rstd[:, :n_tok], op0=ALU.mult, op1=ALU.mult),
             reads=[(x_tag, kt), ("gcol",)] + [("rstd", h) for h in range(nh)], writes=[(h_tag, kt)])


class WStream:
    def __init__(self, c, nslots, slot_elems, name):
        self.c = c
        self.n = nslots
        self.buf = c.sb(name, [128, nslots, slot_elems], BF16)
        self.i = 0
        self.name = name

    def load(self, src_ap, shape):
        s = self.i % self.n
        self.i += 1
        a, b = shape
        view = self.buf[:, s, :a * b].rearrange("p (a b) -> p a b", a=a)
        tag = (self.name, s)
        self.c.P.dma("pool", lambda e: e.dma_start(out=view, in_=src_ap), reads=[], writes=[tag], sem=(self.name, s))
        return view, tag


def emit_mlp(c, xT, hT, w_up, w_down, ws, hid, tmp, psbanks):
    P = c.P
    wu = w_up.rearrange("(kt p) m -> p kt m", p=128)
    wd = w_down.rearrange("(kt p) m -> p kt m", p=128)
    nb = len(psbanks)
    for half in range(TOK // 512):
        tsl = slice(half * 512, (half + 1) * 512)
        for mc in range(DFF // 256):
            wv, wtag = ws.load(wu[:, :, mc * 256:(mc + 1) * 256], (KT, 256))
            for mi in range(2):
                mt = mc * 2 + mi
                b = c.ps_rr % nb
                c.ps_rr += 1
                for kt in range(KT):
                    P.op("pe", lambda e, kt=kt, mi=mi, b=b, wv=wv, tsl=tsl: e.matmul(psbanks[b][:, :], lhsT=wv[:, kt, mi * 128:(mi + 1) * 128],
                                                                          rhs=hT[:, kt, tsl], start=(kt == 0), stop=(kt == KT - 1)),
                         reads=[wtag, ("h", kt)], writes=[("ps", b)])
                ts_ = mt % 2
                P.op("act", lambda e, b=b, ts_=ts_: e.activation(out=tmp[:, ts_, :], in_=psbanks[b][:, :], func=AF.Relu),
                     reads=[("ps", b)], writes=[("tmp", ts_)])
                eng = "dve" if mt % 2 == 0 else "pool"
                P.op(eng, lambda e, mt=mt, ts_=ts_: e.tensor_tensor(out=hid[:, mt, :], in0=tmp[:, ts_, :], in1=tmp[:, ts_, :], op=ALU.mult),
                     reads=[("tmp", ts_)], writes=[("hid", mt)])
        for dt_ in range(KT):
            b = c.ps_rr % nb
            c.ps_rr += 1
            for q in range(4):
                wv, wtag = ws.load(wd[:, q * 16:(q + 1) * 16, dt_ * 128:(dt_ + 1) * 128], (16, 128))
                for k2 in range(16):
                    kk = q * 16 + k2
                    P.op("pe", lambda e, k2=k2, kk=kk, b=b, wv=wv: e.matmul(psbanks[b][:, :], lhsT=wv[:, k2, :], rhs=hid[:, kk, :],
                                                                          start=(kk == 0), stop=(kk == DFF // 128 - 1)),
                         reads=[wtag, ("hid", kk)], writes=[("ps", b)])
            P.op("dve", lambda e, dt_=dt_, b=b, tsl=tsl: e.tensor_tensor(out=xT[:, dt_, tsl], in0=xT[:, dt_, tsl], in1=psbanks[b][:, :], op=ALU.add),
                 reads=[("ps", b), ("x", dt_)], writes=[("x", dt_)])


GPC = 16
TCH = 256
PI = float(np.pi)


def build_ssm(seq=SEQ):
    from contextlib import ExitStack
    nc = bass.Bass("TRN2", target_bir_lowering=False)
    din = lambda n, s: nc.dram_tensor(n, s, F32, kind="ExternalInput").ap()
    hT_d = din("hT", [256, seq])
    LR1_d, LI1_d, DT1_d = din("LR1", [128, GPC]), din("LI1", [128, GPC]), din("DT1", [128, GPC])
    LR2_d, LI2_d, DT2_d = din("LR2", [128, 128]), din("LI2", [128, 128]), din("DT2", [128, 128])
    BRT_d, BIT_d = din("BRT", [128, 128]), din("BIT", [128, 128])
    C1_d = din("C1", [128, GPC * 16])
    SGN_d, MASK_d, DCOL_d = din("SGN", [128, 1]), din("MASK", [128, 8]), din("DCOL", [128, 2])
    yT_d = nc.dram_tensor("yT", [256, seq], F32, kind="ExternalOutput").ap()
    nchunk = seq // TCH
    with ExitStack() as es:
        c = Ctx(nc, es)
        P = c.P
        sbt = lambda n, s, dt=F32: c.sb(n, s, dt)
        LR1, LI1, DT1 = sbt("LR1s", [128, GPC]), sbt("LI1s", [128, GPC]), sbt("DT1s", [128, GPC])
        LR2, LI2, DT2 = sbt("LR2s", [128, 128]), sbt("LI2s", [128, 128]), sbt("DT2s", [128, 128])
        BRT, BIT = sbt("BRTs", [128, 128]), sbt("BITs", [128, 128])
        C1 = sbt("C1s", [128, GPC * 16])
        SGN, MASK, DCOL = sbt("SGNs", [128, 1]), sbt("MASKs", [128, 8]), sbt("DCOLs", [128, 2])
        npi = sbt("npi", [128, 1])
        for i, (s_, d_) in enumerate([(LR1, LR1_d), (LI1, LI1_d), (DT1, DT1_d), (LR2, LR2_d), (LI2, LI2_d), (DT2, DT2_d),
                                      (BRT, BRT_d), (BIT, BIT_d), (C1, C1_d), (SGN, SGN_d), (MASK, MASK_d), (DCOL, DCOL_d)]):
            P.dma("sp", lambda e, s_=s_, d_=d_: e.dma_start(out=s_[:, :], in_=d_[:, :]), [], [("par",)], sem="par")
        P.op("dve", lambda e: e.memset(npi[:, :], -PI), [], [("par",)])

        def abar(LR, LI, DT, n, nm):
            dt = sbt(nm + "dt", [128, n]); t1 = sbt(nm + "t1", [128, n]); t2 = sbt(nm + "t2", [128, n])
            ar = sbt(nm + "ar", [128, n]); ai = sbt(nm + "ai", [128, n]); mg = sbt(nm + "mg", [128, n])
            T = [(nm,)]
            R = [("par",), (nm,)]
            P.op("act", lambda e: e.activation(out=dt[:, :], in_=DT[:, :], func=AF.Exp), R, T)
            P.op("dve", lambda e: e.tensor_tensor(out=t1[:, :], in0=LR[:, :], in1=dt[:, :], op=ALU.mult), R, T)
            P.op("act", lambda e: e.activation(out=mg[:, :], in_=t1[:, :], func=AF.Exp), R, T)
            P.op("dve", lambda e: e.tensor_tensor(out=t1[:, :], in0=LI[:, :], in1=dt[:, :], op=ALU.mult), R, T)
            ki = sbt(nm + "ki", [128, n], mybir.dt.int32); kf = sbt(nm + "kf", [128, n])

            def sin_of(dst, shift):
                P.op("dve", lambda e: e.tensor_scalar(out=t2[:, :], in0=t1[:, :], scalar1=shift, scalar2=None, op0=ALU.add), R, T)
                P.op("dve", lambda e: e.tensor_scalar(out=kf[:, :], in0=t2[:, :], scalar1=1.0 / (2 * PI), scalar2=0.5, op0=ALU.mult, op1=ALU.add), R, T)
                P.op("dve", lambda e: e.tensor_copy(out=ki[:, :], in_=kf[:, :]), R, T)
                P.op("dve", lambda e: e.tensor_copy(out=kf[:, :], in_=ki[:, :]), R, T)
                P.op("dve", lambda e: e.scalar_tensor_tensor(out=t2[:, :], in0=kf[:, :], scalar=-2 * PI, in1=t2[:, :], op0=ALU.mult, op1=ALU.add), R, T)
                P.op("dve", lambda e: e.tensor_scalar(out=kf[:, :], in0=t2[:, :], scalar1=-PI, scalar2=2 * PI, op0=ALU.is_lt, op1=ALU.mult), R, T)
                P.op("dve", lambda e: e.tensor_tensor(out=t2[:, :], in0=t2[:, :], in1=kf[:, :], op=ALU.add), R, T)
                P.op("dve", lambda e: e.tensor_scalar(out=t2[:, :], in0=t2[:, :], scalar1=-PI, scalar2=PI, op0=ALU.max, op1=ALU.min), R, T)
                P.op("act", lambda e: e.activation(out=t2[:, :], in_=t2[:, :], func=AF.Sin), R, T)
                P.op("dve", lambda e: e.tensor_tensor(out=dst[:, :], in0=mg[:, :], in1=t2[:, :], op=ALU.mult), R, T)
            sin_of(ai, 0.0)
            sin_of(ar, 0.5 * PI)
            return ar, ai

        ar1, ai1 = abar(LR1, LI1, DT1, GPC, "a1")
        ARR = sbt("ARR", [128, 2, GPC]); AIP = sbt("AIP", [128, 2, GPC])
        T1t = [("tab1",)]
        R1 = [("a1",), ("tab1",)]
        P.op("dve", lambda e: e.tensor_copy(out=ARR[:, 0, :], in_=ar1[:, :]), R1, T1t)
        P.op("dve", lambda e: e.tensor_copy(out=ARR[:, 1, :], in_=ar1[:, :]), R1, T1t)
        P.op("dve", lambda e: e.tensor_copy(out=AIP[:, 0, :], in_=ai1[:, :]), R1, T1t)
        P.op("dve", lambda e: e.tensor_scalar(out=AIP[:, 1, :], in0=ai1[:, :], scalar1=-1.0, scalar2=None, op0=ALU.mult), R1, T1t)

        ar2, ai2 = abar(LR2, LI2, DT2, 128, "a2")
        w = [sbt(f"w{i}", [128, 128]) for i in range(6)]
        R2 = [("a2",), ("par",), ("tab2",)]
        T2t = [("tab2",)]
        tt = lambda o, a, b, op: P.op("dve", lambda e: e.tensor_tensor(out=o, in0=a, in1=b, op=op), R2, T2t)
        nr, den, kre, kim, bbr, bbi = w
        P.op("dve", lambda e: e.tensor_scalar(out=nr[:, :], in0=ar2[:, :], scalar1=-1.0, scalar2=None, op0=ALU.add), R2, T2t)
        tt(den[:, :], LR2[:, :], LR2[:, :], ALU.mult)
        tt(kre[:, :], LI2[:, :], LI2[:, :], ALU.mult)
        tt(den[:, :], den[:, :], kre[:, :], ALU.add)
        P.op("dve", lambda e: e.reciprocal(out=den[:, :], in_=den[:, :]), R2, T2t)
        tt(kre[:, :], nr[:, :], LR2[:, :], ALU.mult)
        tt(kim[:, :], ai2[:, :], LI2[:, :], ALU.mult)
        tt(kre[:, :], kre[:, :], kim[:, :], ALU.add)
        tt(kre[:, :], kre[:, :], den[:, :], ALU.mult)
        tt(kim[:, :], ai2[:, :], LR2[:, :], ALU.mult)
        tt(bbr[:, :], nr[:, :], LI2[:, :], ALU.mult)
        tt(kim[:, :], kim[:, :], bbr[:, :], ALU.subtract)
        tt(kim[:, :], kim[:, :], den[:, :], ALU.mult)
        tt(bbr[:, :], kre[:, :], BRT[:, :], ALU.mult)
        tt(bbi[:, :], kim[:, :], BIT[:, :], ALU.mult)
        tt(bbr[:, :], bbr[:, :], bbi[:, :], ALU.subtract)
        tt(bbi[:, :], kre[:, :], BIT[:, :], ALU.mult)
        tt(nr[:, :], kim[:, :], BRT[:, :], ALU.mult)
        tt(bbi[:, :], bbi[:, :], nr[:, :], ALU.add)
        nbbi = den
        P.op("dve", lambda e: e.tensor_scalar(out=nbbi[:, :], in0=bbi[:, :], scalar1=-1.0, scalar2=None, op0=ALU.mult), R2, T2t)
        BzPad = sbt("BzPad", [128, GPC, 2, 128], BF16)
        for g in range(GPC):
            tile, j = g // 8, g % 8
            cs = slice(tile * 64, tile * 64 + 64)
            for zz, (lo, hi) in enumerate([(bbr, bbi), (nbbi, bbr)]):
                P.op("dve", lambda e, g=g, zz=zz, lo=lo, cs=cs, j=j: e.tensor_scalar(out=BzPad[:, g, zz, 0:64], in0=lo[:, cs], scalar1=MASK[:, j:j + 1], scalar2=None, op0=ALU.mult), R2, T2t)
                P.op("dve", lambda e, g=g, zz=zz, hi=hi, cs=cs, j=j: e.tensor_scalar(out=BzPad[:, g, zz, 64:128], in0=hi[:, cs], scalar1=MASK[:, j:j + 1], scalar2=None, op0=ALU.mult), R2, T2t)
        CzPad = sbt("CzPad", [128, GPC, 128], BF16)
        Cz = sbt("Cz", [128, GPC * 16])
        P.op("dve", lambda e: e.memset(CzPad[:, :, :], 0.0), R2, T2t)
        P.op("dve", lambda e: e.tensor_scalar(out=Cz[:, :], in0=C1[:, :], scalar1=SGN[:, 0:1], scalar2=None, op0=ALU.mult), R2, T2t)
        for g in range(GPC):
            j = g % 8
            P.op("dve", lambda e, g=g, j=j: e.tensor_copy(out=CzPad[:, g, 16 * j:16 * j + 16], in_=Cz[:, 16 * g:16 * g + 16]), R2, T2t)

        V = [sbt(f"V{i}", [128, 2, GPC, TCH]) for i in range(2)]
        Sbf = [sbt(f"Sbf{i}", [128, GPC, TCH], BF16) for i in range(2)]
        hc = [sbt(f"hc{i}", [128, 2, TCH]) for i in range(2)]
        hb = [sbt(f"hb{i}", [128, 2, TCH], BF16) for i in range(2)]
        T1 = sbt("T1", [128, 2, GPC]); T2 = sbt("T2", [128, 2, GPC])
        Z0 = sbt("Z0", [128, 2, GPC])
        P.op("dve", lambda e: e.memset(Z0[:, :, :], 0.0), [], [("Z0",)])
        ysb = [sbt(f"ysb{i}", [128, TCH]) for i in range(2)]
        xg = [sbt(f"xg{i}", [128, TCH]) for i in range(2)]
        wg = [sbt(f"wg{i}", [128, TCH]) for i in range(2)]
        og = [sbt(f"og{i}", [128, TCH]) for i in range(4)]
        pse = [c.ps(f"pse{i}", [128, TCH]) for i in range(4)]
        psy = [c.ps(f"psy{i}", [128, TCH]) for i in range(2)]
        hTv = hT_d.rearrange("(tile p) t -> p tile t", p=128)
        yTv = yT_d.rearrange("(tile p) t -> p tile t", p=128)
        outs = []
        er = 0
        yr = 0
        orr = 0
        for k in range(nchunk):
            b = k % 2
            tsl = slice(k * TCH, (k + 1) * TCH)
            P.dma("sp", lambda e, b=b, tsl=tsl: e.dma_start(out=hc[b][:, :, :], in_=hTv[:, :, tsl]), [], [("hc", b)], sem=("hc", b))
            P.op("act", lambda e, b=b: e.activation(out=hb[b][:, :, :], in_=hc[b][:, :, :], func=AF.Copy), [("hc", b)], [("hb", b)])
            for g in range(GPC):
                for zz in range(2):
                    pb = er % 4
                    er += 1
                    P.op("pe", lambda e, g=g, zz=zz, pb=pb, b=b: e.matmul(pse[pb][:, :], lhsT=BzPad[:, g, zz, :], rhs=hb[b][:, g // 8, :], start=True, stop=True),
                         [("hb", b), ("tab2",)], [("pse", pb)])
                    P.op("act", lambda e, g=g, zz=zz, pb=pb, b=b: e.activation(out=V[b][:, zz, g, :], in_=pse[pb][:, :], func=AF.Copy),
                         [("pse", pb)], [("V", b)])
            for t in range(TCH):
                if t == 0:
                    zp = Z0[:, :, :] if k == 0 else V[1 - b][:, :, :, TCH - 1]
                    zr = [("Z0",)] if k == 0 else [("V", 1 - b)]
                else:
                    zp = V[b][:, :, :, t - 1]
                    zr = [("V", b)]
                zp0 = Z0[:, 0, :] if (k == 0 and t == 0) else (V[1 - b][:, 0, :, TCH - 1] if t == 0 else V[b][:, 0, :, t - 1])
                zp1 = Z0[:, 1, :] if (k == 0 and t == 0) else (V[1 - b][:, 1, :, TCH - 1] if t == 0 else V[b][:, 1, :, t - 1])
                P.op("dve", lambda e, zp=zp: e.tensor_tensor(out=T1[:, :, :], in0=zp, in1=ARR[:, :, :], op=ALU.mult), zr + [("tab1",), ("T1",)], [("T1",)])
                P.op("dve", lambda e, zp1=zp1: e.tensor_tensor(out=T2[:, 0, :], in0=zp1, in1=AIP[:, 0, :], op=ALU.mult), zr + [("tab1",), ("T2",)], [("T2",)])
                P.op("dve", lambda e, zp0=zp0: e.tensor_tensor(out=T2[:, 1, :], in0=zp0, in1=AIP[:, 1, :], op=ALU.mult), zr + [("tab1",), ("T2",)], [("T2",)])
                P.op("dve", lambda e: e.tensor_tensor(out=T1[:, :, :], in0=T1[:, :, :], in1=T2[:, :, :], op=ALU.add), [("T1",), ("T2",)], [("T1",)])
                P.op("dve", lambda e, b=b, t=t: e.tensor_tensor(out=V[b][:, :, :, t], in0=V[b][:, :, :, t], in1=T1[:, :, :], op=ALU.add), [("T1",), ("V", b)], [("V", b)])
            P.op("act", lambda e, b=b: e.activation(out=Sbf[b][:, :, :], in_=V[b][:, 0, :, :], func=AF.Copy), [("V", b)], [("Sbf", b)])
            for tile in range(2):
                yb = yr % 2
                yr += 1
                for j in range(8):
                    g = tile * 8 + j
                    P.op("pe", lambda e, g=g, j=j, yb=yb, b=b: e.matmul(psy[yb][:, :], lhsT=CzPad[:, g, :], rhs=Sbf[b][:, g, :], start=(j == 0), stop=(j == 7)),
                         [("Sbf", b), ("tab2",)], [("psy", yb)])
                ob = orr % 4
                orr += 1
                P.op("act", lambda e, yb=yb: e.activation(out=ysb[yb][:, :], in_=psy[yb][:, :], func=AF.Copy), [("psy", yb)], [("ysb", yb)])
                P.op("pool", lambda e, yb=yb, b=b, tile=tile: e.tensor_scalar(out=xg[yb][:, :], in0=hc[b][:, tile, :], scalar1=DCOL[:, tile:tile + 1], scalar2=None, op0=ALU.mult),
                     [("hc", b), ("par",)], [("xg", yb)])
                P.op("pool", lambda e, yb=yb: e.tensor_tensor(out=xg[yb][:, :], in0=xg[yb][:, :], in1=ysb[yb][:, :], op=ALU.add), [("xg", yb), ("ysb", yb)], [("xg", yb)])
                P.op("pool", lambda e, yb=yb: e.tensor_tensor(out=wg[yb][:, :], in0=xg[yb][:, :], in1=xg[yb][:, :], op=ALU.mult), [("xg", yb)], [("wg", yb)])
                P.op("pool", lambda e, yb=yb: e.tensor_scalar(out=wg[yb][:, :], in0=wg[yb][:, :], scalar1=0.044715, scalar2=1.0, op0=ALU.mult, op1=ALU.add), [("wg", yb)], [("wg", yb)])
                P.op("pool", lambda e, yb=yb: e.tensor_tensor(out=wg[yb][:, :], in0=wg[yb][:, :], in1=xg[yb][:, :], op=ALU.mult), [("wg", yb), ("xg", yb)], [("wg", yb)])
                P.op("act", lambda e, yb=yb: e.activation(out=wg[yb][:, :], in_=wg[yb][:, :], func=AF.Sigmoid, scale=1.5957691216), [("wg", yb)], [("wg", yb)])
                P.op("pool", lambda e, yb=yb, ob=ob: e.tensor_tensor(out=og[ob][:, :], in0=wg[yb][:, :], in1=xg[yb][:, :], op=ALU.mult), [("wg", yb), ("xg", yb)], [("og", ob)])
                outs.append(P.dma("sp", lambda e, ob=ob, tile=tile, tsl=tsl: e.dma_start(out=yTv[:, tile, tsl], in_=og[ob][:, :]), [("og", ob)], [], sem=("og", ob)))
        P.emit(final_wait_ops=outs[-4:])
    return nc


def ssm_host_inputs(hT_core, lam_re, lam_im, log_dt, b_re, b_im, c_re, c_im, d, core):
    G0 = core * GPC
    gs = slice(G0, G0 + GPC)
    lr, li, ld = lam_re[gs], lam_im[gs], log_dt[gs]
    LR1 = np.ascontiguousarray(np.concatenate([lr.T, lr.T], 0))
    LI1 = np.ascontiguousarray(np.concatenate([li.T, li.T], 0))
    DT1 = np.ascontiguousarray(np.broadcast_to(ld[None, :], (128, GPC)))

    def l2(a):
        a = a.reshape(2, 8, 1, 64)
        a = np.broadcast_to(a, (2, 8, 16, 64))
        return np.ascontiguousarray(a.transpose(1, 2, 0, 3).reshape(128, 128))
    LR2, LI2 = l2(lr), l2(li)
    DT2 = l2(np.broadcast_to(ld[:, None], (GPC, 64)))

    def bt(bb):
        a = bb.reshape(2, 8, 64, 16).transpose(1, 3, 0, 2)
        return np.ascontiguousarray(a.reshape(128, 128))
    BRT, BIT = bt(b_re[gs]), bt(b_im[gs])
    cr = c_re[gs].transpose(2, 0, 1)
    ci = c_im[gs].transpose(2, 0, 1)
    C1 = np.ascontiguousarray(np.concatenate([cr, ci], 0).reshape(128, GPC * 16))
    SGN = np.concatenate([np.ones((64, 1), np.float32), -np.ones((64, 1), np.float32)], 0)
    MASK = np.zeros((128, 8), np.float32)
    for j in range(8):
        MASK[16 * j:16 * j + 16, j] = 1.0
    DCOL = np.ascontiguousarray(d[core * 256:(core + 1) * 256].reshape(2, 128).T)
    return {"hT": np.ascontiguousarray(hT_core), "LR1": LR1, "LI1": LI1, "DT1": DT1, "LR2": LR2, "LI2": LI2, "DT2": DT2,
            "BRT": BRT, "BIT": BIT, "C1": C1, "SGN": SGN, "MASK": MASK, "DCOL": DCOL}


def build_dense(mode):
    from contextlib import ExitStack
    nc = bass.Bass("TRN2", target_bir_lowering=False)
    din = lambda n, s: nc.dram_tensor(n, s, F32, kind="ExternalInput").ap()
    x_d = din("xT", [D, TOK])
    vec_names = ["g_mlp", "g_next"]
    if mode == "conv":
        xh_d = din("xhT", [D, HALO])
        flag_d = din("flag", [128, 1])
        w1_d = din("w_in", [D, 2 * D])
        w2_d = din("w_out", [D, D])
        dw_d = din("dw", [CW, D])
        vec_names += ["g_mix", "b_in_a", "b_in_g", "dw_b", "ln_g", "ln_b", "b_out"]
    else:
        y_d = din("yT", [D, TOK])
        w1_d = din("w_glu", [D, 2 * D])
    vec_d = {n: din(n, [D]) for n in vec_names}
    wu_d = din("w_up", [D, DFF])
    wd_d = din("w_down", [DFF, D])
    xo_d = nc.dram_tensor("xo", [D, TOK], F32, kind="ExternalOutput").ap()
    ho_d = nc.dram_tensor("ho", [D, TOK], F32, kind="ExternalOutput").ap()
    NT = TOK + HALO if mode == "conv" else TOK
    with ExitStack() as es:
        c = Ctx(nc, es)
        P = c.P
        xT = c.sb("xT_sb", [128, KT, TOK], F32)
        hT = c.sb("hT_sb", [128, KT, NT], BF16)
        big = c.sb("big", [128, KT * TOK], F32)
        bigf = big[:, :].rearrange("p (k t) -> p k t", k=KT)
        hid = big[:, :].bitcast(BF16)[:, :DFF // 128 * 512].rearrange("p (k t) -> p k t", k=DFF // 128) if hasattr(big[:, :], "bitcast") else None
        tmp = c.sb("tmp", [128, 2, 512], F32)
        rstd = c.sb("rstd", [128, NT], F32)
        ones = c.sb("ones", [128, 128], F32)
        epsc = c.sb("epsc", [128, 1], F32)
        vec = {n: c.sb("v_" + n, [128, KT], F32) for n in vec_names}
        ws = WStream(c, 2, KT * 256, "ws")
        banks = [c.ps(f"b{i}", [128, 512]) for i in range(6)]
        psn = [c.ps(f"n{i}", [128, 512]) for i in range(2)]
        for kt in range(KT):
            P.dma("sp", lambda e, kt=kt: e.dma_start(out=xT[:, kt, :], in_=x_d[kt * 128:(kt + 1) * 128, :]), [], [("x", kt)], sem="xin")
        for n in vec_names:
            P.dma("sp", lambda e, n=n: e.dma_start(out=vec[n][:, :], in_=vec_d[n].rearrange("(kt p) -> p kt", p=128), allow_slow_non_contiguous=True),
                  [], [("gcol",)], sem="vin")
        P.op("dve", lambda e: e.memset(ones[:, :], 1.0), [], [("ones",)])
        P.op("dve", lambda e: e.memset(epsc[:, :], EPS), [], [("epsc",)])

        def pair_glu(w_d, ba, bg, n_cols, consume):
            wv_ = w_d.rearrange("(kt p) m -> p kt m", p=128)
            chunks = [(s0, min(s0 + 512, n_cols)) for s0 in range(0, n_cols, 512)]
            for mt in range(KT):
                wa, ta = ws.load(wv_[:, :, mt * 128:(mt + 1) * 128], (KT, 128))
                wg_, tg = ws.load(wv_[:, :, D + mt * 128:D + (mt + 1) * 128], (KT, 128))
                for ci, (s0, s1) in enumerate(chunks):
                    ba_, bb_ = ci, 3 + ci
                    for (wv, wt, b) in ((wa, ta, ba_), (wg_, tg, bb_)):
                        for kt in range(KT):
                            P.op("pe", lambda e, kt=kt, wv=wv, b=b, s0=s0, s1=s1: e.matmul(banks[b][:, :s1 - s0], lhsT=wv[:, kt, :], rhs=hT[:, kt, s0:s1],
                                                                                      start=(kt == 0), stop=(kt == KT - 1)),
                                 reads=[wt, ("h", kt)], writes=[("ps", b)])
                    consume(mt, ci, s0, s1, banks[ba_], banks[bb_], ("ps", ba_), ("ps", bb_))

        if mode == "conv":
            xh = c.sb("xh", [128, KT, HALO], F32)
            flag = c.sb("flag_sb", [128, 1], F32)
            dwc = c.sb("dwc", [128, KT, CW], F32)
            vext = c.sb("vext", [128, 2, TOK + HALO], F32)
            mean = c.sb("mean", [128, TOK], F32)
            P.dma("sp", lambda e: e.dma_start(out=xh[:, :, :], in_=xh_d.rearrange("(kt p) t -> p kt t", p=128)), [], [("xh",)], sem="xh")
            P.dma("sp", lambda e: e.dma_start(out=flag[:, :], in_=flag_d[:, :]), [], [("gcol",)], sem="vin")
            for kt in range(KT):
                P.dma("sp", lambda e, kt=kt: e.dma_start(out=dwc[:, kt, :], in_=dw_d[:, kt * 128:(kt + 1) * 128].rearrange("j p -> p j"), allow_slow_non_contiguous=True),
                      [], [("gcol",)], sem="vin")
            emit_rmsnorm(c, xT, hT, vec["g_mix"], epsc, ones, tmp, rstd, psn)
            for kt in range(KT):
                s = kt % 2
                P.op("act", lambda e, kt=kt, s=s: e.activation(out=tmp[:, s, :HALO], in_=xh[:, kt, :], func=AF.Square), [("xh",)], [("tmp", s)])
                P.op("pe", lambda e, kt=kt, s=s: e.matmul(psn[0][:, :HALO], lhsT=ones[:, :], rhs=tmp[:, s, :HALO], start=(kt == 0), stop=(kt == KT - 1)),
                     [("tmp", s), ("ones",)], [("psn", 0)])
            P.op("act", lambda e: e.activation(out=rstd[:, TOK:NT], in_=psn[0][:, :HALO], func=AF.Sqrt, bias=epsc[:, 0:1], scale=1.0 / D), [("psn", 0), ("epsc",)], [("rstdh",)])
            P.op("dve", lambda e: e.reciprocal(out=rstd[:, TOK:NT], in_=rstd[:, TOK:NT]), [("rstdh",)], [("rstdh",)])
            for kt in range(KT):
                P.op("dve", lambda e, kt=kt: e.scalar_tensor_tensor(out=hT[:, kt, TOK:NT], in0=xh[:, kt, :], scalar=vec["g_mix"][:, kt:kt + 1], in1=rstd[:, TOK:NT],
                                                                   op0=ALU.mult, op1=ALU.mult), [("xh",), ("rstdh",), ("gcol",)], [("h", kt)])

            def consume(mt, ci, s0, s1, pa, pg, ta, tg):
                vs = mt % 2
                n = s1 - s0
                d0 = 0 if s0 == TOK else HALO + s0
                P.op("act", lambda e, n=n, mt=mt: e.activation(out=tmp[:, 0, :n], in_=pg[:, :n], func=AF.Sigmoid, bias=vec["b_in_g"][:, mt:mt + 1], scale=1.0),
                     [tg, ("gcol",)], [("tmp", 0)])
                P.op("dve", lambda e, n=n, mt=mt, vs=vs, d0=d0: e.scalar_tensor_tensor(out=vext[:, vs, d0:d0 + n], in0=pa[:, :n], scalar=vec["b_in_a"][:, mt:mt + 1], in1=tmp[:, 0, :n],
                                                                                    op0=ALU.add, op1=ALU.mult), [ta, ("tmp", 0), ("gcol",)], [("vext", vs)])
                if s0 == TOK:
                    P.op("dve", lambda e, vs=vs: e.tensor_scalar(out=vext[:, vs, 0:HALO], in0=vext[:, vs, 0:HALO], scalar1=flag[:, 0:1], scalar2=None, op0=ALU.mult),
                         [("vext", vs), ("gcol",)], [("vext", vs)])
                    P.op("dve", lambda e, mt=mt, vs=vs: e.tensor_scalar(out=bigf[:, mt, :], in0=vext[:, vs, 2:2 + TOK], scalar1=dwc[:, mt, 0:1], scalar2=vec["dw_b"][:, mt:mt + 1],
                                                                      op0=ALU.mult, op1=ALU.add), [("vext", vs), ("gcol",), ("bigbar",)], [("cv", mt)])
                    for j in range(1, CW):
                        P.op("dve", lambda e, mt=mt, vs=vs, j=j: e.scalar_tensor_tensor(out=bigf[:, mt, :], in0=vext[:, vs, 2 + j:2 + j + TOK], scalar=dwc[:, mt, j:j + 1], in1=bigf[:, mt, :],
                                                                                      op0=ALU.mult, op1=ALU.add), [("vext", vs), ("cv", mt), ("gcol",)], [("cv", mt)])
            pair_glu(w1_d, vec["b_in_a"], vec["b_in_g"], NT, consume)
            for h in range(2):
                tsl = slice(h * 512, (h + 1) * 512)
                for kt in range(KT):
                    s = kt % 2
                    P.op("pe", lambda e, kt=kt, h=h, tsl=tsl: e.matmul(banks[h][:, :], lhsT=ones[:, :], rhs=bigf[:, kt, tsl], start=(kt == 0), stop=(kt == KT - 1)),
                         [("cv", kt), ("ones",)], [("ps", h)])
                    P.op("act", lambda e, kt=kt, s=s, tsl=tsl: e.activation(out=tmp[:, s, :], in_=bigf[:, kt, tsl], func=AF.Square), [("cv", kt)], [("tmp", s)])
                    P.op("pe", lambda e, kt=kt, h=h, s=s: e.matmul(banks[2 + h][:, :], lhsT=ones[:, :], rhs=tmp[:, s, :], start=(kt == 0), stop=(kt == KT - 1)),
                         [("tmp", s), ("ones",)], [("ps", 2 + h)])
                P.op("act", lambda e, h=h, tsl=tsl: e.activation(out=mean[:, tsl], in_=banks[h][:, :], func=AF.Copy, scale=1.0 / D), [("ps", h)], [("mean",)])
                P.op("act", lambda e, h=h, tsl=tsl: e.activation(out=rstd[:, tsl], in_=banks[2 + h][:, :], func=AF.Copy, scale=1.0 / D), [("ps", 2 + h)], [("rstd", h)])
                P.op("dve", lambda e, tsl=tsl: e.tensor_tensor(out=tmp[:, 0, :], in0=mean[:, tsl], in1=mean[:, tsl], op=ALU.mult), [("mean",)], [("tmp", 0)])
                P.op("dve", lambda e, tsl=tsl: e.tensor_tensor(out=rstd[:, tsl], in0=rstd[:, tsl], in1=tmp[:, 0, :], op=ALU.subtract), [("rstd", h), ("tmp", 0)], [("rstd", h)])
                P.op("act", lambda e, tsl=tsl: e.activation(out=rstd[:, tsl], in_=rstd[:, tsl], func=AF.Sqrt, bias=epsc[:, 0:1], scale=1.0), [("rstd", h), ("epsc",)], [("rstd", h)])
                P.op("dve", lambda e, tsl=tsl: e.reciprocal(out=rstd[:, tsl], in_=rstd[:, tsl]), [("rstd", h)], [("rstd", h)])
            for kt in range(KT):
                eng = "dve" if kt % 2 == 0 else "pool"
                P.op(eng, lambda e, kt=kt: e.tensor_tensor(out=bigf[:, kt, :], in0=bigf[:, kt, :], in1=mean[:, :], op=ALU.subtract), [("cv", kt), ("mean",)], [("cv", kt)])
                P.op(eng, lambda e, kt=kt: e.tensor_tensor(out=bigf[:, kt, :], in0=bigf[:, kt, :], in1=rstd[:, :TOK], op=ALU.mult), [("cv", kt), ("rstd", 0), ("rstd", 1)], [("cv", kt)])
                P.op("act", lambda e, kt=kt: e.activation(out=hT[:, kt, :TOK], in_=bigf[:, kt, :], func=AF.Silu, bias=vec["ln_b"][:, kt:kt + 1], scale=vec["ln_g"][:, kt:kt + 1]),
                     [("cv", kt), ("gcol",)], [("h", kt)])
            wo = w2_d.rearrange("(kt p) m -> p kt m", p=128)
            for dt_ in range(KT):
                wv, wt = ws.load(wo[:, :, dt_ * 128:(dt_ + 1) * 128], (KT, 128))
                for h in range(2):
                    tsl = slice(h * 512, (h + 1) * 512)
                    b = c.ps_rr % 6
                    c.ps_rr += 1
                    for kt in range(KT):
                        P.op("pe", lambda e, kt=kt, wv=wv, b=b, tsl=tsl: e.matmul(banks[b][:, :], lhsT=wv[:, kt, :], rhs=hT[:, kt, tsl], start=(kt == 0), stop=(kt == KT - 1)),
                             [wt, ("h", kt)], [("ps", b)])
                    P.op("dve", lambda e, dt_=dt_, b=b, tsl=tsl: e.scalar_tensor_tensor(out=xT[:, dt_, tsl], in0=banks[b][:, :], scalar=vec["b_out"][:, dt_:dt_ + 1], in1=xT[:, dt_, tsl],
                                                                                      op0=ALU.add, op1=ALU.add), [("ps", b), ("x", dt_), ("gcol",)], [("x", dt_)])
            big_readers = [("cv", kt) for kt in range(KT)]
        else:
            for kt in range(KT):
                P.dma("sp", lambda e, kt=kt: e.dma_start(out=bigf[:, kt, :], in_=y_d[kt * 128:(kt + 1) * 128, :]), [], [("cv", kt)], sem="yin")
                P.op("act", lambda e, kt=kt: e.activation(out=hT[:, kt, :], in_=bigf[:, kt, :], func=AF.Copy), [("cv", kt)], [("h", kt)])

            def consume(mt, ci, s0, s1, pa, pg, ta, tg):
                n = s1 - s0
                P.op("act", lambda e, n=n: e.activation(out=tmp[:, 0, :n], in_=pg[:, :n], func=AF.Sigmoid), [tg], [("tmp", 0)])
                P.op("dve", lambda e, n=n: e.tensor_tensor(out=tmp[:, 0, :n], in0=tmp[:, 0, :n], in1=pa[:, :n], op=ALU.mult), [ta, ("tmp", 0)], [("tmp", 0)])
                P.op("dve", lambda e, n=n, mt=mt, s0=s0, s1=s1: e.tensor_tensor(out=xT[:, mt, s0:s1], in0=xT[:, mt, s0:s1], in1=tmp[:, 0, :n], op=ALU.add),
                     [("tmp", 0), ("x", mt)], [("x", mt)])
            pair_glu(w1_d, None, None, TOK, consume)
            big_readers = [("cv", kt) for kt in range(KT)]

        P.op("dve", lambda e: e.memset(epsc[:, :], EPS), big_readers + [("epsc",)], [("epsc",)] + [("hid", m) for m in range(DFF // 128)])
        emit_rmsnorm(c, xT, hT, vec["g_mlp"], epsc, ones, tmp, rstd, psn)
        emit_mlp(c, xT, hT, wu_d, wd_d, ws, hid, tmp, banks)
        outs = []
        for kt in range(KT):
            outs.append(P.dma("sp", lambda e, kt=kt: e.dma_start(out=xo_d[kt * 128:(kt + 1) * 128, :], in_=xT[:, kt, :]), [("x", kt)], [], sem="xout"))
        P.op("dve", lambda e: e.memset(epsc[:, :], EPS), [("hid", m) for m in range(DFF // 128)] + [("epsc",)], [("epsc",)] + [("hn", kt) for kt in range(KT)])
        emit_rmsnorm(c, xT, bigf, vec["g_next"], epsc, ones, tmp, rstd, psn, h_tag="hn")
        for kt in range(KT):
            outs.append(P.dma("sp", lambda e, kt=kt: e.dma_start(out=ho_d[kt * 128:(kt + 1) * 128, :], in_=bigf[:, kt, :]), [("hn", kt)], [], sem="hout"))
        P.emit(final_wait_ops=outs)
    return nc


U32 = mybir.dt.uint32
NK = SEQ // TCH
KPC = TOK // TCH


def dense_phase(c, mode, xT, ones, epsc, io):
    from contextlib import ExitStack
    nc = c.nc
    P = c.P
    vec_names = ["g_mlp", "g_next"] + (["g_mix", "b_in_a", "b_in_g", "dw_b", "ln_g", "ln_b", "b_out"] if mode == "conv" else [])
    NT = TOK + HALO if mode == "conv" else TOK
    with ExitStack() as es:
        sb = lambda n, shp, dt=F32: es.enter_context(nc.sbuf_tensor(f"{n}_{io['uid']}", shp, dt))
        hT = sb("hT_sb", [128, KT, NT], BF16)
        big = sb("big", [128, KT * TOK], F32)
        bigf = big[:, :].rearrange("p (k t) -> p k t", k=KT)
        hid = big[:, :].bitcast(BF16)[:, :DFF // 128 * 512].rearrange("p (k t) -> p k t", k=DFF // 128)
        tmp = sb("tmp", [128, 2, 512], F32)
        rstd = sb("rstd", [128, NT], F32)
        vec = {n: sb("v_" + n, [128, KT], F32) for n in vec_names}
        ws = WStream.__new__(WStream)
        ws.c, ws.n, ws.i, ws.name = c, 2, 0, "ws"
        ws.buf = sb("ws", [128, 2, KT * 256], BF16)
        banks = [es.enter_context(nc.psum_tensor(f"b{i}_{io['uid']}", [128, 512], F32)) for i in range(6)]
        psn = [es.enter_context(nc.psum_tensor(f"n{i}_{io['uid']}", [128, 512], F32)) for i in range(2)]
        for n in vec_names:
            P.dma("sp", lambda e, n=n: e.dma_start(out=vec[n][:, :], in_=io[n].rearrange("(kt p) -> p kt", p=128), allow_slow_non_contiguous=True),
                  [], [("gcol",)], sem="vin")

        def pair_glu(w_d, n_cols, consume):
            wv_ = w_d.rearrange("(kt p) m -> p kt m", p=128)
            chunks = [(s0, min(s0 + 512, n_cols)) for s0 in range(0, n_cols, 512)]
            for mt in range(KT):
                wa, ta = ws.load(wv_[:, :, mt * 128:(mt + 1) * 128], (KT, 128))
                wg_, tg = ws.load(wv_[:, :, D + mt * 128:D + (mt + 1) * 128], (KT, 128))
                for ci, (s0, s1) in enumerate(chunks):
                    ba_, bb_ = ci, 3 + ci
                    for (wv, wt, b) in ((wa, ta, ba_), (wg_, tg, bb_)):
                        for kt in range(KT):
                            P.op("pe", lambda e, kt=kt, wv=wv, b=b, s0=s0, s1=s1: e.matmul(banks[b][:, :s1 - s0], lhsT=wv[:, kt, :], rhs=hT[:, kt, s0:s1],
                                                                                      start=(kt == 0), stop=(kt == KT - 1)),
                                 reads=[wt, ("h", kt)], writes=[("ps", b)])
                    consume(mt, ci, s0, s1, banks[ba_], banks[bb_], ("ps", ba_), ("ps", bb_))

        if mode == "conv":
            xh = sb("xh", [128, KT, HALO], F32)
            flag = sb("flag_sb", [128, 1], F32)
            dwc = sb("dwc", [128, KT, CW], F32)
            vext = sb("vext", [128, 2, TOK + HALO], F32)
            mean = sb("mean", [128, TOK], F32)
            if io["halo"][0] == "dram":
                P.dma("sp", lambda e: e.dma_start(out=xh[:, :, :], in_=io["halo"][1].rearrange("(kt p) t -> p kt t", p=128)), [], [("xh",)], sem="xh")
            else:
                xtg, idx2 = io["halo"][1], io["halo"][2]
                for kt in range(KT):
                    P.dma("pool", lambda e, kt=kt: e.indirect_dma_start(out=xh[:, kt, :], out_offset=None, in_=xtg[:, :],
                                                                        in_offset=bass.IndirectOffsetOnAxis(ap=idx2[:, 0:1], axis=0), element_offset=kt * 128 * HALO),
                          [("xtg",), ("idx",)], [("xh",)], sem="xh")
            P.dma("sp", lambda e: e.dma_start(out=flag[:, :], in_=io["flag"][:, :]), [], [("gcol",)], sem="vin")
            for kt in range(KT):
                P.dma("sp", lambda e, kt=kt: e.dma_start(out=dwc[:, kt, :], in_=io["dw"][:, kt * 128:(kt + 1) * 128].rearrange("j p -> p j"), allow_slow_non_contiguous=True),
                      [], [("gcol",)], sem="vin")
            emit_rmsnorm(c, xT, hT, vec["g_mix"], epsc, ones, tmp, rstd, psn)
            for kt in range(KT):
                s = kt % 2
                P.op("act", lambda e, kt=kt, s=s: e.activation(out=tmp[:, s, :HALO], in_=xh[:, kt, :], func=AF.Square), [("xh",)], [("tmp", s)])
                P.op("pe", lambda e, kt=kt, s=s: e.matmul(psn[0][:, :HALO], lhsT=ones[:, :], rhs=tmp[:, s, :HALO], start=(kt == 0), stop=(kt == KT - 1)),
                     [("tmp", s), ("ones",)], [("psn", 0)])
            P.op("act", lambda e: e.activation(out=rstd[:, TOK:NT], in_=psn[0][:, :HALO], func=AF.Sqrt, bias=epsc[:, 0:1], scale=1.0 / D), [("psn", 0), ("epsc",)], [("rstdh",)])
            P.op("dve", lambda e: e.reciprocal(out=rstd[:, TOK:NT], in_=rstd[:, TOK:NT]), [("rstdh",)], [("rstdh",)])
            for kt in range(KT):
                P.op("dve", lambda e, kt=kt: e.scalar_tensor_tensor(out=hT[:, kt, TOK:NT], in0=xh[:, kt, :], scalar=vec["g_mix"][:, kt:kt + 1], in1=rstd[:, TOK:NT],
                                                                   op0=ALU.mult, op1=ALU.mult), [("xh",), ("rstdh",), ("gcol",)], [("h", kt)])

            def consume(mt, ci, s0, s1, pa, pg, ta, tg):
                vs = mt % 2
                n = s1 - s0
                d0 = 0 if s0 == TOK else HALO + s0
                P.op("act", lambda e, n=n, mt=mt: e.activation(out=tmp[:, 0, :n], in_=pg[:, :n], func=AF.Sigmoid, bias=vec["b_in_g"][:, mt:mt + 1], scale=1.0),
                     [tg, ("gcol",)], [("tmp", 0)])
                P.op("dve", lambda e, n=n, mt=mt, vs=vs, d0=d0: e.scalar_tensor_tensor(out=vext[:, vs, d0:d0 + n], in0=pa[:, :n], scalar=vec["b_in_a"][:, mt:mt + 1], in1=tmp[:, 0, :n],
                                                                                    op0=ALU.add, op1=ALU.mult), [ta, ("tmp", 0), ("gcol",)], [("vext", vs)])
                if s0 == TOK:
                    P.op("dve", lambda e, vs=vs: e.tensor_scalar(out=vext[:, vs, 0:HALO], in0=vext[:, vs, 0:HALO], scalar1=flag[:, 0:1], scalar2=None, op0=ALU.mult),
                         [("vext", vs), ("gcol",)], [("vext", vs)])
                    P.op("dve", lambda e, mt=mt, vs=vs: e.tensor_scalar(out=bigf[:, mt, :], in0=vext[:, vs, 2:2 + TOK], scalar1=dwc[:, mt, 0:1], scalar2=vec["dw_b"][:, mt:mt + 1],
                                                                      op0=ALU.mult, op1=ALU.add), [("vext", vs), ("gcol",)], [("cv", mt)])
                    for j in range(1, CW):
                        P.op("dve", lambda e, mt=mt, vs=vs, j=j: e.scalar_tensor_tensor(out=bigf[:, mt, :], in0=vext[:, vs, 2 + j:2 + j + TOK], scalar=dwc[:, mt, j:j + 1], in1=bigf[:, mt, :],
                                                                                      op0=ALU.mult, op1=ALU.add), [("vext", vs), ("cv", mt), ("gcol",)], [("cv", mt)])
            pair_glu(io["w_in"], NT, consume)
            for h in range(2):
                tsl = slice(h * 512, (h + 1) * 512)
                for kt in range(KT):
                    s = kt % 2
                    P.op("pe", lambda e, kt=kt, h=h, tsl=tsl: e.matmul(banks[h][:, :], lhsT=ones[:, :], rhs=bigf[:, kt, tsl], start=(kt == 0), stop=(kt == KT - 1)),
                         [("cv", kt), ("ones",)], [("ps", h)])
                    P.op("act", lambda e, kt=kt, s=s, tsl=tsl: e.activation(out=tmp[:, s, :], in_=bigf[:, kt, tsl], func=AF.Square), [("cv", kt)], [("tmp", s)])
                    P.op("pe", lambda e, kt=kt, h=h, s=s: e.matmul(banks[2 + h][:, :], lhsT=ones[:, :], rhs=tmp[:, s, :], start=(kt == 0), stop=(kt == KT - 1)),
                         [("tmp", s), ("ones",)], [("ps", 2 + h)])
                P.op("act", lambda e, h=h, tsl=tsl: e.activation(out=mean[:, tsl], in_=banks[h][:, :], func=AF.Copy, scale=1.0 / D), [("ps", h)], [("mean",)])
                P.op("act", lambda e, h=h, tsl=tsl: e.activation(out=rstd[:, tsl], in_=banks[2 + h][:, :], func=AF.Copy, scale=1.0 / D), [("ps", 2 + h)], [("rstd", h)])
                P.op("dve", lambda e, tsl=tsl: e.tensor_tensor(out=tmp[:, 0, :], in0=mean[:, tsl], in1=mean[:, tsl], op=ALU.mult), [("mean",)], [("tmp", 0)])
                P.op("dve", lambda e, tsl=tsl: e.tensor_tensor(out=rstd[:, tsl], in0=rstd[:, tsl], in1=tmp[:, 0, :], op=ALU.subtract), [("rstd", h), ("tmp", 0)], [("rstd", h)])
                P.op("act", lambda e, tsl=tsl: e.activation(out=rstd[:, tsl], in_=rstd[:, tsl], func=AF.Sqrt, bias=epsc[:, 0:1], scale=1.0), [("rstd", h), ("epsc",)], [("rstd", h)])
                P.op("dve", lambda e, tsl=tsl: e.reciprocal(out=rstd[:, tsl], in_=rstd[:, tsl]), [("rstd", h)], [("rstd", h)])
            for kt in range(KT):
                eng = "dve" if kt % 2 == 0 else "pool"
                P.op(eng, lambda e, kt=kt: e.tensor_tensor(out=bigf[:, kt, :], in0=bigf[:, kt, :], in1=mean[:, :], op=ALU.subtract), [("cv", kt), ("mean",)], [("cv", kt)])
                P.op(eng, lambda e, kt=kt: e.tensor_tensor(out=bigf[:, kt, :], in0=bigf[:, kt, :], in1=rstd[:, :TOK], op=ALU.mult), [("cv", kt), ("rstd", 0), ("rstd", 1)], [("cv", kt)])
                P.op("act", lambda e, kt=kt: e.activation(out=hT[:, kt, :TOK], in_=bigf[:, kt, :], func=AF.Silu, bias=vec["ln_b"][:, kt:kt + 1], scale=vec["ln_g"][:, kt:kt + 1]),
                     [("cv", kt), ("gcol",)], [("h", kt)])
            wo = io["w_out"].rearrange("(kt p) m -> p kt m", p=128)
            for dt_ in range(KT):
                wv, wt = ws.load(wo[:, :, dt_ * 128:(dt_ + 1) * 128], (KT, 128))
                for h in range(2):
                    tsl = slice(h * 512, (h + 1) * 512)
                    b = c.ps_rr % 6
                    c.ps_rr += 1
                    for kt in range(KT):
                        P.op("pe", lambda e, kt=kt, wv=wv, b=b, tsl=tsl: e.matmul(banks[b][:, :], lhsT=wv[:, kt, :], rhs=hT[:, kt, tsl], start=(kt == 0), stop=(kt == KT - 1)),
                             [wt, ("h", kt)], [("ps", b)])
                    P.op("dve", lambda e, dt_=dt_, b=b, tsl=tsl: e.scalar_tensor_tensor(out=xT[:, dt_, tsl], in0=banks[b][:, :], scalar=vec["b_out"][:, dt_:dt_ + 1], in1=xT[:, dt_, tsl],
                                                                                      op0=ALU.add, op1=ALU.add), [("ps", b), ("x", dt_), ("gcol",)], [("x", dt_)])
        else:
            ygath, idx1 = io["ygath"], io["idx1"]
            for r in range(NCORES):
                for kk in range(KPC):
                    for tile in range(2):
                        kt = 2 * r + tile
                        eo = ((r * NK + kk) * 256 + tile * 128) * TCH
                        P.dma("pool", lambda e, kt=kt, kk=kk, eo=eo: e.indirect_dma_start(out=hT[:, kt, kk * TCH:(kk + 1) * TCH], out_offset=None, in_=ygath[:, :],
                                                                                          in_offset=bass.IndirectOffsetOnAxis(ap=idx1[:, 0:1], axis=0), element_offset=eo),
                              [("ygath",), ("idx",)], [("h", kt)], sem=("yg", kt % 4))

            def consume(mt, ci, s0, s1, pa, pg, ta, tg):
                n = s1 - s0
                P.op("act", lambda e, n=n: e.activation(out=tmp[:, 0, :n], in_=pg[:, :n], func=AF.Sigmoid), [tg], [("tmp", 0)])
                P.op("dve", lambda e, n=n: e.tensor_tensor(out=tmp[:, 0, :n], in0=tmp[:, 0, :n], in1=pa[:, :n], op=ALU.mult), [ta, ("tmp", 0)], [("tmp", 0)])
                P.op("dve", lambda e, n=n, mt=mt, s0=s0, s1=s1: e.tensor_tensor(out=xT[:, mt, s0:s1], in0=xT[:, mt, s0:s1], in1=tmp[:, 0, :n], op=ALU.add),
                     [("tmp", 0), ("x", mt)], [("x", mt)])
            pair_glu(io["w_glu"], TOK, consume)

        big_readers = [("cv", kt) for kt in range(KT)]
        P.op("dve", lambda e: e.memset(epsc[:, :], EPS), big_readers + [("epsc",)], [("epsc",)] + [("hid", m) for m in range(DFF // 128)])
        emit_rmsnorm(c, xT, hT, vec["g_mlp"], epsc, ones, tmp, rstd, psn)
        emit_mlp(c, xT, hT, io["w_up"], io["w_down"], ws, hid, tmp, banks)
        outs = []
        nxt = io["next"]
        if nxt[0] == "ssm":
            hb_d = nxt[1]
            emit_rmsnorm(c, xT, hT, vec["g_next"], epsc, ones, tmp, rstd, psn)
            hbv = hb_d.rearrange("kk (kt p) t -> p kt kk t", p=128)
            for kt in range(KT):
                P.dma("sp", lambda e, kt=kt: e.dma_start(out=hbv[:, kt, :, :], in_=hT[:, kt, :TOK].rearrange("p (kk t) -> p kk t", kk=KPC)), [("h", kt)], [("hb",)], sem="hbout")
        elif nxt[0] == "conv":
            xtb = nxt[1]
            for kt in range(KT):
                P.dma("sp", lambda e, kt=kt: e.dma_start(out=xtb[kt * 128:(kt + 1) * 128, :], in_=xT[:, kt, TOK - HALO:TOK]), [("x", kt)], [("xtb",)], sem="xtbout")
        else:
            out_d = nxt[1]
            P.op("dve", lambda e: e.memset(epsc[:, :], EPS), [("hid", m) for m in range(DFF // 128)] + [("epsc",)], [("epsc",)] + [("hn", kt) for kt in range(KT)])
            emit_rmsnorm(c, xT, bigf, vec["g_next"], epsc, ones, tmp, rstd, psn, h_tag="hn")
            for kt in range(KT):
                outs.append(P.dma("sp", lambda e, kt=kt: e.dma_start(out=out_d[kt * 128:(kt + 1) * 128, :], in_=bigf[:, kt, :]), [("hn", kt)], [], sem="hout"))
        P.barrier()
    return outs


def ssm_phase(c, io, seq=SEQ):
    from contextlib import ExitStack
    nc = c.nc
    P = c.P
    uid = io["uid"]
    LR1_d, LI1_d, DT1_d, LR2_d, LI2_d, DT2_d = io["LR1"], io["LI1"], io["DT1"], io["LR2"], io["LI2"], io["DT2"]
    BRT_d, BIT_d, C1_d, SGN_d, MASK_d, DCOL_d = io["BRT"], io["BIT"], io["C1"], io["SGN"], io["MASK"], io["DCOL"]
    hgath, idx0, ybd = io["hgath"], io["idx0"], io["yb"]
    nchunk = seq // TCH
    with ExitStack() as es:
        sbt = lambda n, s, dt=F32: es.enter_context(nc.sbuf_tensor(f"{n}_{uid}", s, dt))
        LR1, LI1, DT1 = sbt("LR1s", [128, GPC]), sbt("LI1s", [128, GPC]), sbt("DT1s", [128, GPC])
        LR2, LI2, DT2 = sbt("LR2s", [128, 128]), sbt("LI2s", [128, 128]), sbt("DT2s", [128, 128])
        BRT, BIT = sbt("BRTs", [128, 128]), sbt("BITs", [128, 128])
        C1 = sbt("C1s", [128, GPC * 16])
        SGN, MASK, DCOL = sbt("SGNs", [128, 1]), sbt("MASKs", [128, 8]), sbt("DCOLs", [128, 2])
        npi = sbt("npi", [128, 1])
        for i, (s_, d_) in enumerate([(LR1, LR1_d), (LI1, LI1_d), (DT1, DT1_d), (LR2, LR2_d), (LI2, LI2_d), (DT2, DT2_d),
                                      (BRT, BRT_d), (BIT, BIT_d), (C1, C1_d), (SGN, SGN_d), (MASK, MASK_d), (DCOL, DCOL_d)]):
            P.dma("sp", lambda e, s_=s_, d_=d_: e.dma_start(out=s_[:, :], in_=d_[:, :]), [], [("par",)], sem="par")
        P.op("dve", lambda e: e.memset(npi[:, :], -PI), [], [("par",)])

        def abar(LR, LI, DT, n, nm):
            dt = sbt(nm + "dt", [128, n]); t1 = sbt(nm + "t1", [128, n]); t2 = sbt(nm + "t2", [128, n])
            ar = sbt(nm + "ar", [128, n]); ai = sbt(nm + "ai", [128, n]); mg = sbt(nm + "mg", [128, n])
            T = [(nm,)]
            R = [("par",), (nm,)]
            P.op("act", lambda e: e.activation(out=dt[:, :], in_=DT[:, :], func=AF.Exp), R, T)
            P.op("dve", lambda e: e.tensor_tensor(out=t1[:, :], in0=LR[:, :], in1=dt[:, :], op=ALU.mult), R, T)
            P.op("act", lambda e: e.activation(out=mg[:, :], in_=t1[:, :], func=AF.Exp), R, T)
            P.op("dve", lambda e: e.tensor_tensor(out=t1[:, :], in0=LI[:, :], in1=dt[:, :], op=ALU.mult), R, T)
            ki = sbt(nm + "ki", [128, n], mybir.dt.int32); kf = sbt(nm + "kf", [128, n])

            def sin_of(dst, shift):
                P.op("dve", lambda e: e.tensor_scalar(out=t2[:, :], in0=t1[:, :], scalar1=shift, scalar2=None, op0=ALU.add), R, T)
                P.op("dve", lambda e: e.tensor_scalar(out=kf[:, :], in0=t2[:, :], scalar1=1.0 / (2 * PI), scalar2=0.5, op0=ALU.mult, op1=ALU.add), R, T)
                P.op("dve", lambda e: e.tensor_copy(out=ki[:, :], in_=kf[:, :]), R, T)
                P.op("dve", lambda e: e.tensor_copy(out=kf[:, :], in_=ki[:, :]), R, T)
                P.op("dve", lambda e: e.scalar_tensor_tensor(out=t2[:, :], in0=kf[:, :], scalar=-2 * PI, in1=t2[:, :], op0=ALU.mult, op1=ALU.add), R, T)
                P.op("dve", lambda e: e.tensor_scalar(out=kf[:, :], in0=t2[:, :], scalar1=-PI, scalar2=2 * PI, op0=ALU.is_lt, op1=ALU.mult), R, T)
                P.op("dve", lambda e: e.tensor_tensor(out=t2[:, :], in0=t2[:, :], in1=kf[:, :], op=ALU.add), R, T)
                P.op("dve", lambda e: e.tensor_scalar(out=t2[:, :], in0=t2[:, :], scalar1=-PI, scalar2=PI, op0=ALU.max, op1=ALU.min), R, T)
                P.op("act", lambda e: e.activation(out=t2[:, :], in_=t2[:, :], func=AF.Sin), R, T)
                P.op("dve", lambda e: e.tensor_tensor(out=dst[:, :], in0=mg[:, :], in1=t2[:, :], op=ALU.mult), R, T)
            sin_of(ai, 0.0)
            sin_of(ar, 0.5 * PI)
            return ar, ai

        ar1, ai1 = abar(LR1, LI1, DT1, GPC, "a1")
        ARR = sbt("ARR", [128, 2, GPC]); AIP = sbt("AIP", [128, 2, GPC])
        T1t = [("tab1",)]
        R1 = [("a1",), ("tab1",)]
        P.op("dve", lambda e: e.tensor_copy(out=ARR[:, 0, :], in_=ar1[:, :]), R1, T1t)
        P.op("dve", lambda e: e.tensor_copy(out=ARR[:, 1, :], in_=ar1[:, :]), R1, T1t)
        P.op("dve", lambda e: e.tensor_copy(out=AIP[:, 0, :], in_=ai1[:, :]), R1, T1t)
        P.op("dve", lambda e: e.tensor_scalar(out=AIP[:, 1, :], in0=ai1[:, :], scalar1=-1.0, scalar2=None, op0=ALU.mult), R1, T1t)

        ar2, ai2 = abar(LR2, LI2, DT2, 128, "a2")
        w = [sbt(f"w{i}", [128, 128]) for i in range(6)]
        R2 = [("a2",), ("par",), ("tab2",)]
        T2t = [("tab2",)]
        tt = lambda o, a, b, op: P.op("dve", lambda e: e.tensor_tensor(out=o, in0=a, in1=b, op=op), R2, T2t)
        nr, den, kre, kim, bbr, bbi = w
        P.op("dve", lambda e: e.tensor_scalar(out=nr[:, :], in0=ar2[:, :], scalar1=-1.0, scalar2=None, op0=ALU.add), R2, T2t)
        tt(den[:, :], LR2[:, :], LR2[:, :], ALU.mult)
        tt(kre[:, :], LI2[:, :], LI2[:, :], ALU.mult)
        tt(den[:, :], den[:, :], kre[:, :], ALU.add)
        P.op("dve", lambda e: e.reciprocal(out=den[:, :], in_=den[:, :]), R2, T2t)
        tt(kre[:, :], nr[:, :], LR2[:, :], ALU.mult)
        tt(kim[:, :], ai2[:, :], LI2[:, :], ALU.mult)
        tt(kre[:, :], kre[:, :], kim[:, :], ALU.add)
        tt(kre[:, :], kre[:, :], den[:, :], ALU.mult)
        tt(kim[:, :], ai2[:, :], LR2[:, :], ALU.mult)
        tt(bbr[:, :], nr[:, :], LI2[:, :], ALU.mult)
        tt(kim[:, :], kim[:, :], bbr[:, :], ALU.subtract)
        tt(kim[:, :], kim[:, :], den[:, :], ALU.mult)
        tt(bbr[:, :], kre[:, :], BRT[:, :], ALU.mult)
        tt(bbi[:, :], kim[:, :], BIT[:, :], ALU.mult)
        tt(bbr[:, :], bbr[:, :], bbi[:, :], ALU.subtract)
        tt(bbi[:, :], kre[:, :], BIT[:, :], ALU.mult)
        tt(nr[:, :], kim[:, :], BRT[:, :], ALU.mult)
        tt(bbi[:, :], bbi[:, :], nr[:, :], ALU.add)
        nbbi = den
        P.op("dve", lambda e: e.tensor_scalar(out=nbbi[:, :], in0=bbi[:, :], scalar1=-1.0, scalar2=None, op0=ALU.mult), R2, T2t)
        BzPad = sbt("BzPad", [128, GPC, 2, 128], BF16)
        for g in range(GPC):
            tile, j = g // 8, g % 8
            cs = slice(tile * 64, tile * 64 + 64)
            for zz, (lo, hi) in enumerate([(bbr, bbi), (nbbi, bbr)]):
                P.op("dve", lambda e, g=g, zz=zz, lo=lo, cs=cs, j=j: e.tensor_scalar(out=BzPad[:, g, zz, 0:64], in0=lo[:, cs], scalar1=MASK[:, j:j + 1], scalar2=None, op0=ALU.mult), R2, T2t)
                P.op("dve", lambda e, g=g, zz=zz, hi=hi, cs=cs, j=j: e.tensor_scalar(out=BzPad[:, g, zz, 64:128], in0=hi[:, cs], scalar1=MASK[:, j:j + 1], scalar2=None, op0=ALU.mult), R2, T2t)
        CzPad = sbt("CzPad", [128, GPC, 128], BF16)
        Cz = sbt("Cz", [128, GPC * 16])
        P.op("dve", lambda e: e.memset(CzPad[:, :, :], 0.0), R2, T2t)
        P.op("dve", lambda e: e.tensor_scalar(out=Cz[:, :], in0=C1[:, :], scalar1=SGN[:, 0:1], scalar2=None, op0=ALU.mult), R2, T2t)
        for g in range(GPC):
            j = g % 8
            P.op("dve", lambda e, g=g, j=j: e.tensor_copy(out=CzPad[:, g, 16 * j:16 * j + 16], in_=Cz[:, 16 * g:16 * g + 16]), R2, T2t)

        V = [sbt(f"V{i}", [128, 2, GPC, TCH]) for i in range(2)]
        Sbf = [sbt(f"Sbf{i}", [128, GPC, TCH], BF16) for i in range(2)]
        hb = [sbt(f"hb{i}", [128, 2, TCH], BF16) for i in range(2)]
        T1 = sbt("T1", [128, 2, GPC]); T2 = sbt("T2", [128, 2, GPC])
        Z0 = sbt("Z0", [128, 2, GPC])
        P.op("dve", lambda e: e.memset(Z0[:, :, :], 0.0), [], [("Z0",)])
        ysb = [sbt(f"ysb{i}", [128, TCH]) for i in range(2)]
        xg = [sbt(f"xg{i}", [128, TCH]) for i in range(2)]
        wg = [sbt(f"wg{i}", [128, TCH]) for i in range(2)]
        og = [sbt(f"og{i}", [128, TCH], BF16) for i in range(4)]
        pse = [es.enter_context(nc.psum_tensor(f"pse{i}_{uid}", [128, TCH], F32)) for i in range(4)]
        psy = [es.enter_context(nc.psum_tensor(f"psy{i}_{uid}", [128, TCH], F32)) for i in range(2)]
        outs = []
        er = 0
        yr = 0
        orr = 0
        for k in range(nchunk):
            b = k % 2
            tsl = slice(k * TCH, (k + 1) * TCH)
            for tile in range(2):
                eo = (k * D + tile * 128) * TCH
                P.dma("pool", lambda e, b=b, tile=tile, eo=eo: e.indirect_dma_start(out=hb[b][:, tile, :], out_offset=None, in_=hgath[:, :],
                                                                                   in_offset=bass.IndirectOffsetOnAxis(ap=idx0[:, 0:1], axis=0), element_offset=eo),
                      [("hgath",), ("idx",)], [("hb", b)], sem=("hbl", b, tile))
            for g in range(GPC):
                for zz in range(2):
                    pb = er % 4
                    er += 1
                    P.op("pe", lambda e, g=g, zz=zz, pb=pb, b=b: e.matmul(pse[pb][:, :], lhsT=BzPad[:, g, zz, :], rhs=hb[b][:, g // 8, :], start=True, stop=True),
                         [("hb", b), ("tab2",)], [("pse", pb)])
                    P.op("act", lambda e, g=g, zz=zz, pb=pb, b=b: e.activation(out=V[b][:, zz, g, :], in_=pse[pb][:, :], func=AF.Copy),
                         [("pse", pb)], [("V", b)])
            for t in range(TCH):
                if t == 0:
                    zp = Z0[:, :, :] if k == 0 else V[1 - b][:, :, :, TCH - 1]
                    zr = [("Z0",)] if k == 0 else [("V", 1 - b)]
                else:
                    zp = V[b][:, :, :, t - 1]
                    zr = [("V", b)]
                zp0 = Z0[:, 0, :] if (k == 0 and t == 0) else (V[1 - b][:, 0, :, TCH - 1] if t == 0 else V[b][:, 0, :, t - 1])
                zp1 = Z0[:, 1, :] if (k == 0 and t == 0) else (V[1 - b][:, 1, :, TCH - 1] if t == 0 else V[b][:, 1, :, t - 1])
                P.op("dve", lambda e, zp=zp: e.tensor_tensor(out=T1[:, :, :], in0=zp, in1=ARR[:, :, :], op=ALU.mult), zr + [("tab1",), ("T1",)], [("T1",)])
                P.op("dve", lambda e, zp1=zp1: e.tensor_tensor(out=T2[:, 0, :], in0=zp1, in1=AIP[:, 0, :], op=ALU.mult), zr + [("tab1",), ("T2",)], [("T2",)])
                P.op("dve", lambda e, zp0=zp0: e.tensor_tensor(out=T2[:, 1, :], in0=zp0, in1=AIP[:, 1, :], op=ALU.mult), zr + [("tab1",), ("T2",)], [("T2",)])
                P.op("dve", lambda e: e.tensor_tensor(out=T1[:, :, :], in0=T1[:, :, :], in1=T2[:, :, :], op=ALU.add), [("T1",), ("T2",)], [("T1",)])
                P.op("dve", lambda e, b=b, t=t: e.tensor_tensor(out=V[b][:, :, :, t], in0=V[b][:, :, :, t], in1=T1[:, :, :], op=ALU.add), [("T1",), ("V", b)], [("V", b)])
            P.op("act", lambda e, b=b: e.activation(out=Sbf[b][:, :, :], in_=V[b][:, 0, :, :], func=AF.Copy), [("V", b)], [("Sbf", b)])
            for tile in range(2):
                yb = yr % 2
                yr += 1
                for j in range(8):
                    g = tile * 8 + j
                    P.op("pe", lambda e, g=g, j=j, yb=yb, b=b: e.matmul(psy[yb][:, :], lhsT=CzPad[:, g, :], rhs=Sbf[b][:, g, :], start=(j == 0), stop=(j == 7)),
                         [("Sbf", b), ("tab2",)], [("psy", yb)])
                ob = orr % 4
                orr += 1
                P.op("act", lambda e, yb=yb: e.activation(out=ysb[yb][:, :], in_=psy[yb][:, :], func=AF.Copy), [("psy", yb)], [("ysb", yb)])
                P.op("pool", lambda e, yb=yb, b=b, tile=tile: e.tensor_scalar(out=xg[yb][:, :], in0=hb[b][:, tile, :], scalar1=DCOL[:, tile:tile + 1], scalar2=None, op0=ALU.mult),
                     [("hb", b), ("par",)], [("xg", yb)])
                P.op("pool", lambda e, yb=yb: e.tensor_tensor(out=xg[yb][:, :], in0=xg[yb][:, :], in1=ysb[yb][:, :], op=ALU.add), [("xg", yb), ("ysb", yb)], [("xg", yb)])
                P.op("pool", lambda e, yb=yb: e.tensor_tensor(out=wg[yb][:, :], in0=xg[yb][:, :], in1=xg[yb][:, :], op=ALU.mult), [("xg", yb)], [("wg", yb)])
                P.op("pool", lambda e, yb=yb: e.tensor_scalar(out=wg[yb][:, :], in0=wg[yb][:, :], scalar1=0.044715, scalar2=1.0, op0=ALU.mult, op1=ALU.add), [("wg", yb)], [("wg", yb)])
                P.op("pool", lambda e, yb=yb: e.tensor_tensor(out=wg[yb][:, :], in0=wg[yb][:, :], in1=xg[yb][:, :], op=ALU.mult), [("wg", yb), ("xg", yb)], [("wg", yb)])
                P.op("act", lambda e, yb=yb: e.activation(out=wg[yb][:, :], in_=wg[yb][:, :], func=AF.Sigmoid, scale=1.5957691216), [("wg", yb)], [("wg", yb)])
                P.op("pool", lambda e, yb=yb, ob=ob: e.tensor_tensor(out=og[ob][:, :], in0=wg[yb][:, :], in1=xg[yb][:, :], op=ALU.mult), [("wg", yb), ("xg", yb)], [("og", ob)])
                P.dma("sp", lambda e, ob=ob, tile=tile, k=k: e.dma_start(out=ybd[k * 256 + tile * 128:k * 256 + (tile + 1) * 128, :], in_=og[ob][:, :]), [("og", ob)], [("yb",)], sem=("og", ob))
        P.barrier()


SSM_KEYS = ["LR1", "LI1", "DT1", "LR2", "LI2", "DT2", "BRT", "BIT", "C1", "DCOL"]
SSM_SHAPES = {"LR1": [128, GPC], "LI1": [128, GPC], "DT1": [128, GPC], "LR2": [128, 128], "LI2": [128, 128], "DT2": [128, 128],
              "BRT": [128, 128], "BIT": [128, 128], "C1": [128, GPC * 16], "DCOL": [128, 2]}


def build_fused():
    from contextlib import ExitStack
    nc = bass.Bass("TRN2", target_bir_lowering=False)
    din = lambda n, s, dt=F32: nc.dram_tensor(n, s, dt, kind="ExternalInput").ap()
    x_d = din("xT", [D, TOK])
    xh0_d = din("xh0", [D, HALO])
    flag_d = din("flag", [128, 1])
    idx_d = din("idx", [128, 3], U32)
    mixn = din("mix_norm", [4, D]); mlpn = din("mlp_norm", [4, D]); finn = din("final_norm", [D])
    cwin = din("conv_w_in", [2, D, 2 * D]); cbin = din("conv_b_in", [2, 2 * D]); cdw = din("conv_dw", [2, CW, D])
    cdwb = din("conv_dw_b", [2, D]); clg = din("conv_ln_g", [2, D]); clb = din("conv_ln_b", [2, D])
    cwo = din("conv_w_out", [2, D, D]); cbo = din("conv_b_out", [2, D])
    wglu = din("ssm_w_glu", [2, D, 2 * D]); wup = din("mlp_w_up", [4, D, DFF]); wdn = din("mlp_w_down", [4, DFF, D])
    ssm_d = {k: din("S_" + k, [2] + SSM_SHAPES[k]) for k in SSM_KEYS}
    sgn_d = din("SGN", [128, 1]); mask_d = din("MASK", [128, 8])
    out_d = nc.dram_tensor("out", [D, TOK], F32, kind="ExternalOutput").ap()
    hb = [nc.dram_tensor(f"hb{j}", [KPC * D, TCH], BF16) for j in range(2)]
    hgath = [nc.dram_tensor(f"hgath{j}", [NCORES * KPC * D, TCH], BF16) for j in range(2)]
    yb = [nc.dram_tensor(f"yb{j}", [NK * 256, TCH], BF16) for j in range(2)]
    ygath = [nc.dram_tensor(f"ygath{j}", [NCORES * NK * 256, TCH], BF16) for j in range(2)]
    xtb = nc.dram_tensor("xtb", [D, HALO], F32)
    xtg = nc.dram_tensor("xtg", [NCORES * D, HALO], F32)
    rg = [list(range(NCORES))]
    with ExitStack() as es:
        c = Ctx(nc, es)
        P = c.P
        xT = c.sb("xT_sb", [128, KT, TOK], F32)
        ones = c.sb("ones", [128, 128], F32)
        epsc = c.sb("epsc", [128, 1], F32)
        idx = c.sb("idx_sb", [128, 3], U32)
        P.barrier_scratch = c.sb("bscr", [128, 1], F32)
        for kt in range(KT):
            P.dma("sp", lambda e, kt=kt: e.dma_start(out=xT[:, kt, :], in_=x_d[kt * 128:(kt + 1) * 128, :]), [], [("x", kt)], sem="xin")
        P.dma("sp", lambda e: e.dma_start(out=idx[:, :], in_=idx_d[:, :]), [], [("idx",)], sem="idxin")
        P.op("dve", lambda e: e.memset(ones[:, :], 1.0), [], [("ones",)])
        P.op("dve", lambda e: e.memset(epsc[:, :], EPS), [], [("epsc",)])
        outs = []
        for layer in range(4):
            j = layer // 2
            g_next = mixn[layer + 1] if layer < 3 else finn
            if layer % 2 == 0:
                io = {"uid": layer, "g_mlp": mlpn[layer], "g_next": g_next, "g_mix": mixn[layer], "b_in_a": cbin[j, 0:D], "b_in_g": cbin[j, D:2 * D],
                      "dw_b": cdwb[j], "ln_g": clg[j], "ln_b": clb[j], "b_out": cbo[j], "flag": flag_d, "dw": cdw[j], "w_in": cwin[j], "w_out": cwo[j],
                      "w_up": wup[layer], "w_down": wdn[layer],
                      "halo": ("dram", xh0_d) if layer == 0 else ("gather", xtg.ap(), idx[:, 2:3]),
                      "next": ("ssm", hb[j].ap().rearrange("(kk c) t -> kk c t", kk=KPC))}
                dense_phase(c, "conv", xT, ones, epsc, io)
                P.dma("pool", lambda e, j=j: e.collective_compute("AllGather", ALU.bypass, replica_groups=rg, ins=[hb[j].ap().opt()], outs=[hgath[j].ap().opt()]),
                      [("hb",)], [("hgath",)], sem=("cch", j), inc=1)
            else:
                io = {"uid": layer, "hgath": hgath[j].ap(), "idx0": idx[:, 0:1], "yb": yb[j].ap(), "SGN": sgn_d, "MASK": mask_d}
                for k in SSM_KEYS:
                    io[k] = ssm_d[k][j]
                ssm_phase(c, io)
                P.dma("pool", lambda e, j=j: e.collective_compute("AllGather", ALU.bypass, replica_groups=rg, ins=[yb[j].ap().opt()], outs=[ygath[j].ap().opt()]),
                      [("yb",)], [("ygath",)], sem=("ccy", j), inc=1)
                io = {"uid": 10 + layer, "g_mlp": mlpn[layer], "g_next": g_next, "w_glu": wglu[j], "w_up": wup[layer], "w_down": wdn[layer],
                      "ygath": ygath[j].ap(), "idx1": idx[:, 1:2],
                      "next": ("conv", xtb.ap()) if layer < 3 else ("final", out_d)}
                outs += dense_phase(c, "glu", xT, ones, epsc, io)
                if layer < 3:
                    P.dma("pool", lambda e: e.collective_compute("AllGather", ALU.bypass, replica_groups=rg, ins=[xtb.ap().opt()], outs=[xtg.ap().opt()]),
                          [("xtb",)], [("xtg",)], sem="ccx", inc=1)
        P.emit(final_wait_ops=outs)
    return nc


_NC_CACHE = {}


def kernel(x, mix_norm, conv_w_in, conv_b_in, conv_dw, conv_dw_b, conv_ln_g, conv_ln_b, conv_w_out, conv_b_out,
           ssm_lambda_re, ssm_lambda_im, ssm_log_dt, ssm_b_re, ssm_b_im, ssm_c_re, ssm_c_im, ssm_d, ssm_w_glu,
           mlp_norm, mlp_w_up, mlp_w_down, final_norm):
    f = lambda a: np.ascontiguousarray(np.asarray(a, dtype=np.float32))
    x = f(x)
    cores = list(range(NCORES))
    if "nc" not in _NC_CACHE:
        _NC_CACHE["nc"] = build_fused()
    nc = _NC_CACHE["nc"]
    shared = {"mix_norm": f(mix_norm), "mlp_norm": f(mlp_norm), "final_norm": f(final_norm), "conv_w_in": f(conv_w_in), "conv_b_in": f(conv_b_in),
              "conv_dw": f(conv_dw), "conv_dw_b": f(conv_dw_b), "conv_ln_g": f(conv_ln_g), "conv_ln_b": f(conv_ln_b), "conv_w_out": f(conv_w_out),
              "conv_b_out": f(conv_b_out), "ssm_w_glu": f(ssm_w_glu), "mlp_w_up": f(mlp_w_up), "mlp_w_down": f(mlp_w_down)}
    lre, lim, ldt = f(ssm_lambda_re), f(ssm_lambda_im), f(ssm_log_dt)
    bre, bim, cre, cim, dsk = f(ssm_b_re), f(ssm_b_im), f(ssm_c_re), f(ssm_c_im), f(ssm_d)
    maps = []
    dummy_h = np.zeros((256, 1), np.float32)
    for c in cores:
        m = dict(shared)
        m["xT"] = f(x[0, c * TOK:(c + 1) * TOK].T)
        m["xh0"] = f(x[0, c * TOK - HALO:c * TOK].T) if c > 0 else np.zeros((D, HALO), np.float32)
        m["flag"] = np.full((128, 1), 0.0 if c == 0 else 1.0, np.float32)
        p = np.arange(128, dtype=np.uint32)
        m["idx"] = np.ascontiguousarray(np.stack([256 * c + p, 1024 * c + p, max(c - 1, 0) * D + p], axis=1).astype(np.uint32))
        per = [ssm_host_inputs(dummy_h, lre[j], lim[j], ldt[j], bre[j], bim[j], cre[j], cim[j], dsk[j], c) for j in range(2)]
        for k in SSM_KEYS:
            m["S_" + k] = np.ascontiguousarray(np.stack([per[0][k], per[1][k]], axis=0))
        m["SGN"] = per[0]["SGN"]
        m["MASK"] = per[0]["MASK"]
        maps.append(m)
    res = run_bass_kernel_spmd(nc, maps, core_ids=cores)
    out = np.concatenate([r["out"].T for r in res.results], axis=0)[None]
    return np.ascontiguousarray(out.astype(np.float32))
```

```python
import numpy as np
import concourse.bass as bass
import concourse.mybir as mybir
from concourse.bass_utils import run_bass_kernel_spmd

F32 = mybir.dt.float32
BF16 = mybir.dt.bfloat16
AF = mybir.ActivationFunctionType
ALU = mybir.AluOpType

NCORES = 8
D = 2048
SEQ = 8192
TOK = SEQ // NCORES
KT = D // 128
DFF = 4 * D
CW = 31
HALO = 32
EPS = 1e-6
SAME_ENGINE_SYNC = True
NOSYNC_ENGINES = ("pe",)


class Prog:
    ENG = ("pe", "act", "dve", "pool", "sp")

    def __init__(self, nc):
        self.nc = nc
        self.ops = []
        self.last_w = {}
        self.readers = {}
        self.dma_sems = {}
        self.epoch = 0

    def _deps(self, reads, writes):
        deps = set()
        for t in reads:
            w = self.last_w.get(t)
            if w is not None:
                deps.add(w)
        for t in writes:
            w = self.last_w.get(t)
            if w is not None:
                deps.add(w)
            for r in self.readers.get(t, ()):
                deps.add(r)
        return deps

    def _commit(self, idx, reads, writes):
        o = self.ops[idx]
        for t in reads:
            lst = self.readers.setdefault(t, [])
            if o["dma"] is None:
                lst[:] = [r for r in lst if not (self.ops[r]["dma"] is None and self.ops[r]["eng"] == o["eng"])]
            lst.append(idx)
        for t in writes:
            self.last_w[t] = idx
            self.readers[t] = []

    def barrier(self):
        nc = self.nc
        scr = self.barrier_scratch
        idx = len(self.ops)
        deps = self._deps([], [("phase",)])
        self.epoch += 1
        self.ops.append(dict(eng="dve", fn=lambda e: e.memset(scr[:, :], 0.0), deps=deps, dma=None, ms=None, inc=16, ep=self.epoch))
        self.last_w[("phase",)] = idx
        self.readers[("phase",)] = []
        return idx

    def op(self, eng, fn, reads=(), writes=()):
        idx = len(self.ops)
        reads = list(reads) + [("phase",)]
        deps = self._deps(reads, writes)
        deps.discard(idx)
        self.ops.append(dict(eng=eng, fn=fn, deps=deps, dma=None, ms=None, inc=16, ep=self.epoch))
        self._commit(idx, reads, writes)
        return idx

    def dma(self, eng, fn, reads, writes, sem, inc=16):
        idx = len(self.ops)
        reads = list(reads) + [("phase",)]
        deps = self._deps(reads, writes)
        cnt = self.dma_sems.setdefault(sem, [0])
        cnt[0] += inc
        self.ops.append(dict(eng=eng, fn=fn, deps=deps, dma=(sem, cnt[0]), ms=None, inc=inc, ep=self.epoch))
        self._commit(idx, reads, writes)
        return idx

    def emit(self, final_wait_ops=()):
        nc = self.nc
        ops = self.ops
        needed = set()
        for o in ops:
            for d in o["deps"]:
                po = ops[d]
                if po["dma"] is None and o["dma"] is None and po["eng"] == o["eng"] and ((not SAME_ENGINE_SYNC) or o["eng"] in NOSYNC_ENGINES):
                    continue
                needed.add(d)
        counters = {}
        for i, o in enumerate(ops):
            if o["dma"] is None and i in needed:
                k_ = (o["eng"], o["ep"])
                counters[k_] = counters.get(k_, 0) + 1
                o["ms"] = counters[k_]
        from contextlib import ExitStack
        with ExitStack() as es:
            esem = {k_: es.enter_context(nc.semaphore("e_%s_%d" % k_)) for k_ in counters}
            dsem = {k: es.enter_context(nc.semaphore("d_" + str(k))) for k in self.dma_sems}
            block = es.enter_context(nc.Block())

            dma_hist = {}
            for oi, o_ in enumerate(ops):
                if o_["dma"] is not None:
                    dma_hist.setdefault(o_["dma"][0], []).append((oi, o_["dma"][1]))

            def run_engine(eng_name, eng):
                waited = {}
                for i, o in enumerate(ops):
                    if o["eng"] != eng_name:
                        continue
                    w = {}
                    for d in o["deps"]:
                        po = ops[d]
                        if po["dma"] is not None:
                            key = ("d", po["dma"][0])
                            val = po["dma"][1]
                            for (oi, cv_) in dma_hist[po["dma"][0]]:
                                if oi < i and cv_ > val:
                                    val = cv_
                        else:
                            if po["eng"] == eng_name and o["dma"] is None and ((not SAME_ENGINE_SYNC) or eng_name in NOSYNC_ENGINES):
                                continue
                            key = ("e", (po["eng"], po["ep"]))
                            val = po["ms"]
                        if val > w.get(key, 0):
                            w[key] = val
                    for key, val in w.items():
                        if waited.get(key, 0) >= val:
                            continue
                        waited[key] = val
                        s = dsem[key[1]] if key[0] == "d" else esem[key[1]]
                        eng.wait_ge(s, val)
                    ins = o["fn"](eng)
                    if o["dma"] is not None:
                        if o["inc"] == 1:
                            ins.then_inc(dsem[o["dma"][0]])
                        else:
                            ins.then_inc(dsem[o["dma"][0]], 16)
                    elif o["ms"] is not None:
                        ins.then_inc(esem[(eng_name, o["ep"])], 1)
                if eng_name == "sp":
                    for i in final_wait_ops:
                        po = ops[i]
                        eng.wait_ge(dsem[po["dma"][0]], po["dma"][1])

            @block.tensor
            def _(e):
                run_engine("pe", e)

            @block.scalar
            def _(e):
                run_engine("act", e)

            @block.vector
            def _(e):
                run_engine("dve", e)

            @block.gpsimd
            def _(e):
                run_engine("pool", e)

            @block.sync
            def _(e):
                run_engine("sp", e)


class Ctx:
    def __init__(self, nc, es):
        self.nc = nc
        self.es = es
        self.P = Prog(nc)
        self.ps_rr = 0

    def sb(self, name, shape, dt):
        return self.es.enter_context(self.nc.sbuf_tensor(name, shape, dt))

    def ps(self, name, shape, dt=F32):
        return self.es.enter_context(self.nc.psum_tensor(name, shape, dt))


def emit_rmsnorm(c, xT, hT, gcol, epsc, ones, sqb, rstd, psb, n_tok=TOK, x_tag="x", h_tag="h"):
    P = c.P
    nh = n_tok // 512
    i = 0
    for kt in range(KT):
        for h in range(nh):
            s = i % 2
            i += 1
            P.op("act", lambda e, kt=kt, s=s, h=h: e.activation(out=sqb[:, s, :], in_=xT[:, kt, h * 512:(h + 1) * 512], func=AF.Square),
                 reads=[(x_tag, kt)], writes=[("tmp", s)])
            P.op("pe", lambda e, kt=kt, s=s, h=h: e.matmul(psb[h][:, :], lhsT=ones[:, :], rhs=sqb[:, s, :],
                                                          start=(kt == 0), stop=(kt == KT - 1)),
                 reads=[("tmp", s), ("ones",)], writes=[("psn", h)])
    for h in range(nh):
        P.op("act", lambda e, h=h: e.activation(out=rstd[:, h * 512:(h + 1) * 512], in_=psb[h][:, :], func=AF.Sqrt, bias=epsc[:, 0:1], scale=1.0 / D),
             reads=[("psn", h), ("epsc",)], writes=[("rstd", h)])
        P.op("dve", lambda e, h=h: e.reciprocal(out=rstd[:, h * 512:(h + 1) * 512], in_=rstd[:, h * 512:(h + 1) * 512]),
             reads=[("rstd", h)], writes=[("rstd", h)])
    for kt in range(KT):
        eng = "dve"
        P.op(eng, lambda e, kt=kt: e.scalar_tensor_tensor(out=hT[:, kt, :n_tok], in0=xT[:, kt, :n_tok], scalar=gcol[:, kt:kt + 1],
                                                          in1=rstd[:, :n_tok], op0=ALU.mult, op1=ALU.mult),
             reads=[(x_tag, kt), ("gcol",)] + [("rstd", h) for h in range(nh)], writes=[(h_tag, kt)])


class WStream:
    def __init__(self, c, nslots, slot_elems, name):
        self.c = c
        self.n = nslots
        self.buf = c.sb(name, [128, nslots, slot_elems], BF16)
        self.i = 0
        self.name = name

    def load(self, src_ap, shape):
        s = self.i % self.n
        self.i += 1
        a, b = shape
        view = self.buf[:, s, :a * b].rearrange("p (a b) -> p a b", a=a)
        tag = (self.name, s)
        self.c.P.dma("pool", lambda e: e.dma_start(out=view, in_=src_ap), reads=[], writes=[tag], sem=(self.name, s))
        return view, tag


def emit_mlp(c, xT, hT, w_up, w_down, ws, hid, tmp, psbanks):
    P = c.P
    wu = w_up.rearrange("(kt p) m -> p kt m", p=128)
    wd = w_down.rearrange("(kt p) m -> p kt m", p=128)
    nb = len(psbanks)
    for half in range(TOK // 512):
        tsl = slice(half * 512, (half + 1) * 512)
        for mc in range(DFF // 256):
            wv, wtag = ws.load(wu[:, :, mc * 256:(mc + 1) * 256], (KT, 256))
            for mi in range(2):
                mt = mc * 2 + mi
                b = c.ps_rr % nb
                c.ps_rr += 1
                for kt in range(KT):
                    P.op("pe", lambda e, kt=kt, mi=mi, b=b, wv=wv, tsl=tsl: e.matmul(psbanks[b][:, :], lhsT=wv[:, kt, mi * 128:(mi + 1) * 128],
                                                                          rhs=hT[:, kt, tsl], start=(kt == 0), stop=(kt == KT - 1)),
                         reads=[wtag, ("h", kt)], writes=[("ps", b)])
                ts_ = mt % 2
                P.op("act", lambda e, b=b, ts_=ts_: e.activation(out=tmp[:, ts_, :], in_=psbanks[b][:, :], func=AF.Relu),
                     reads=[("ps", b)], writes=[("tmp", ts_)])
                eng = "dve" if mt % 2 == 0 else "pool"
                P.op(eng, lambda e, mt=mt, ts_=ts_: e.tensor_tensor(out=hid[:, mt, :], in0=tmp[:, ts_, :], in1=tmp[:, ts_, :], op=ALU.mult),
                     reads=[("tmp", ts_)], writes=[("hid", mt)])
        for dt_ in range(KT):
            b = c.ps_rr % nb
            c.ps_rr += 1
            for q in range(4):
                wv, wtag = ws.load(wd[:, q * 16:(q + 1) * 16, dt_ * 128:(dt_ + 1) * 128], (16, 128))
                for k2 in range(16):
                    kk = q * 16 + k2
                    P.op("pe", lambda e, k2=k2, kk=kk, b=b, wv=wv: e.matmul(psbanks[b][:, :], lhsT=wv[:, k2, :], rhs=hid[:, kk, :],
                                                                          start=(kk == 0), stop=(kk == DFF // 128 - 1)),
                         reads=[wtag, ("hid", kk)], writes=[("ps", b)])
            P.op("dve", lambda e, dt_=dt_, b=b, tsl=tsl: e.tensor_tensor(out=xT[:, dt_, tsl], in0=xT[:, dt_, tsl], in1=psbanks[b][:, :], op=ALU.add),
                 reads=[("ps", b), ("x", dt_)], writes=[("x", dt_)])


GPC = 16
TCH = 256
PI = float(np.pi)


def build_ssm(seq=SEQ):
    from contextlib import ExitStack
    nc = bass.Bass("TRN2", target_bir_lowering=False)
    din = lambda n, s: nc.dram_tensor(n, s, F32, kind="ExternalInput").ap()
    hT_d = din("hT", [256, seq])
    LR1_d, LI1_d, DT1_d = din("LR1", [128, GPC]), din("LI1", [128, GPC]), din("DT1", [128, GPC])
    LR2_d, LI2_d, DT2_d = din("LR2", [128, 128]), din("LI2", [128, 128]), din("DT2", [128, 128])
    BRT_d, BIT_d = din("BRT", [128, 128]), din("BIT", [128, 128])
    C1_d = din("C1", [128, GPC * 16])
    SGN_d, MASK_d, DCOL_d = din("SGN", [128, 1]), din("MASK", [128, 8]), din("DCOL", [128, 2])
    yT_d = nc.dram_tensor("yT", [256, seq], F32, kind="ExternalOutput").ap()
    nchunk = seq // TCH
    with ExitStack() as es:
        c = Ctx(nc, es)
        P = c.P
        sbt = lambda n, s, dt=F32: c.sb(n, s, dt)
        LR1, LI1, DT1 = sbt("LR1s", [128, GPC]), sbt("LI1s", [128, GPC]), sbt("DT1s", [128, GPC])
        LR2, LI2, DT2 = sbt("LR2s", [128, 128]), sbt("LI2s", [128, 128]), sbt("DT2s", [128, 128])
        BRT, BIT = sbt("BRTs", [128, 128]), sbt("BITs", [128, 128])
        C1 = sbt("C1s", [128, GPC * 16])
        SGN, MASK, DCOL = sbt("SGNs", [128, 1]), sbt("MASKs", [128, 8]), sbt("DCOLs", [128, 2])
        npi = sbt("npi", [128, 1])
        for i, (s_, d_) in enumerate([(LR1, LR1_d), (LI1, LI1_d), (DT1, DT1_d), (LR2, LR2_d), (LI2, LI2_d), (DT2, DT2_d),
                                      (BRT, BRT_d), (BIT, BIT_d), (C1, C1_d), (SGN, SGN_d), (MASK, MASK_d), (DCOL, DCOL_d)]):
            P.dma("sp", lambda e, s_=s_, d_=d_: e.dma_start(out=s_[:, :], in_=d_[:, :]), [], [("par",)], sem="par")
        P.op("dve", lambda e: e.memset(npi[:, :], -PI), [], [("par",)])

        def abar(LR, LI, DT, n, nm):
            dt = sbt(nm + "dt", [128, n]); t1 = sbt(nm + "t1", [128, n]); t2 = sbt(nm + "t2", [128, n])
            ar = sbt(nm + "ar", [128, n]); ai = sbt(nm + "ai", [128, n]); mg = sbt(nm + "mg", [128, n])
            T = [(nm,)]
            R = [("par",), (nm,)]
            P.op("act", lambda e: e.activation(out=dt[:, :], in_=DT[:, :], func=AF.Exp), R, T)
            P.op("dve", lambda e: e.tensor_tensor(out=t1[:, :], in0=LR[:, :], in1=dt[:, :], op=ALU.mult), R, T)
            P.op("act", lambda e: e.activation(out=mg[:, :], in_=t1[:, :], func=AF.Exp), R, T)
            P.op("dve", lambda e: e.tensor_tensor(out=t1[:, :], in0=LI[:, :], in1=dt[:, :], op=ALU.mult), R, T)
            ki = sbt(nm + "ki", [128, n], mybir.dt.int32); kf = sbt(nm + "kf", [128, n])

            def sin_of(dst, shift):
                P.op("dve", lambda e: e.tensor_scalar(out=t2[:, :], in0=t1[:, :], scalar1=shift, scalar2=None, op0=ALU.add), R, T)
                P.op("dve", lambda e: e.tensor_scalar(out=kf[:, :], in0=t2[:, :], scalar1=1.0 / (2 * PI), scalar2=0.5, op0=ALU.mult, op1=ALU.add), R, T)
                P.op("dve", lambda e: e.tensor_copy(out=ki[:, :], in_=kf[:, :]), R, T)
                P.op("dve", lambda e: e.tensor_copy(out=kf[:, :], in_=ki[:, :]), R, T)
                P.op("dve", lambda e: e.scalar_tensor_tensor(out=t2[:, :], in0=kf[:, :], scalar=-2 * PI, in1=t2[:, :], op0=ALU.mult, op1=ALU.add), R, T)
                P.op("dve", lambda e: e.tensor_scalar(out=kf[:, :], in0=t2[:, :], scalar1=-PI, scalar2=2 * PI, op0=ALU.is_lt, op1=ALU.mult), R, T)
                P.op("dve", lambda e: e.tensor_tensor(out=t2[:, :], in0=t2[:, :], in1=kf[:, :], op=ALU.add), R, T)
                P.op("dve", lambda e: e.tensor_scalar(out=t2[:, :], in0=t2[:, :], scalar1=-PI, scalar2=PI, op0=ALU.max, op1=ALU.min), R, T)
                P.op("act", lambda e: e.activation(out=t2[:, :], in_=t2[:, :], func=AF.Sin), R, T)
                P.op("dve", lambda e: e.tensor_tensor(out=dst[:, :], in0=mg[:, :], in1=t2[:, :], op=ALU.mult), R, T)
            sin_of(ai, 0.0)
            sin_of(ar, 0.5 * PI)
            return ar, ai

        ar1, ai1 = abar(LR1, LI1, DT1, GPC, "a1")
        ARR = sbt("ARR", [128, 2, GPC]); AIP = sbt("AIP", [128, 2, GPC])
        T1t = [("tab1",)]
        R1 = [("a1",), ("tab1",)]
        P.op("dve", lambda e: e.tensor_copy(out=ARR[:, 0, :], in_=ar1[:, :]), R1, T1t)
        P.op("dve", lambda e: e.tensor_copy(out=ARR[:, 1, :], in_=ar1[:, :]), R1, T1t)
        P.op("dve", lambda e: e.tensor_copy(out=AIP[:, 0, :], in_=ai1[:, :]), R1, T1t)
        P.op("dve", lambda e: e.tensor_scalar(out=AIP[:, 1, :], in0=ai1[:, :], scalar1=-1.0, scalar2=None, op0=ALU.mult), R1, T1t)

        ar2, ai2 = abar(LR2, LI2, DT2, 128, "a2")
        w = [sbt(f"w{i}", [128, 128]) for i in range(6)]
        R2 = [("a2",), ("par",), ("tab2",)]
        T2t = [("tab2",)]
        tt = lambda o, a, b, op: P.op("dve", lambda e: e.tensor_tensor(out=o, in0=a, in1=b, op=op), R2, T2t)
        nr, den, kre, kim, bbr, bbi = w
        P.op("dve", lambda e: e.tensor_scalar(out=nr[:, :], in0=ar2[:, :], scalar1=-1.0, scalar2=None, op0=ALU.add), R2, T2t)
        tt(den[:, :], LR2[:, :], LR2[:, :], ALU.mult)
        tt(kre[:, :], LI2[:, :], LI2[:, :], ALU.mult)
        tt(den[:, :], den[:, :], kre[:, :], ALU.add)
        P.op("dve", lambda e: e.reciprocal(out=den[:, :], in_=den[:, :]), R2, T2t)
        tt(kre[:, :], nr[:, :], LR2[:, :], ALU.mult)
        tt(kim[:, :], ai2[:, :], LI2[:, :], ALU.mult)
        tt(kre[:, :], kre[:, :], kim[:, :], ALU.add)
        tt(kre[:, :], kre[:, :], den[:, :], ALU.mult)
        tt(kim[:, :], ai2[:, :], LR2[:, :], ALU.mult)
        tt(bbr[:, :], nr[:, :], LI2[:, :], ALU.mult)
        tt(kim[:, :], kim[:, :], bbr[:, :], ALU.subtract)
        tt(kim[:, :], kim[:, :], den[:, :], ALU.mult)
        tt(bbr[:, :], kre[:, :], BRT[:, :], ALU.mult)
        tt(bbi[:, :], kim[:, :], BIT[:, :], ALU.mult)
        tt(bbr[:, :], bbr[:, :], bbi[:, :], ALU.subtract)
        tt(bbi[:, :], kre[:, :], BIT[:, :], ALU.mult)
        tt(nr[:, :], kim[:, :], BRT[:, :], ALU.mult)
        tt(bbi[:, :], bbi[:, :], nr[:, :], ALU.add)
        nbbi = den
        P.op("dve", lambda e: e.tensor_scalar(out=nbbi[:, :], in0=bbi[:, :], scalar1=-1.0, scalar2=None, op0=ALU.mult), R2, T2t)
        BzPad = sbt("BzPad", [128, GPC, 2, 128], BF16)
        for g in range(GPC):
            tile, j = g // 8, g % 8
            cs = slice(tile * 64, tile * 64 + 64)
            for zz, (lo, hi) in enumerate([(bbr, bbi), (nbbi, bbr)]):
                P.op("dve", lambda e, g=g, zz=zz, lo=lo, cs=cs, j=j: e.tensor_scalar(out=BzPad[:, g, zz, 0:64], in0=lo[:, cs], scalar1=MASK[:, j:j + 1], scalar2=None, op0=ALU.mult), R2, T2t)
                P.op("dve", lambda e, g=g, zz=zz, hi=hi, cs=cs, j=j: e.tensor_scalar(out=BzPad[:, g, zz, 64:128], in0=hi[:, cs], scalar1=MASK[:, j:j + 1], scalar2=None, op0=ALU.mult), R2, T2t)
        CzPad = sbt("CzPad", [128, GPC, 128], BF16)
        Cz = sbt("Cz", [128, GPC * 16])
        P.op("dve", lambda e: e.memset(CzPad[:, :, :], 0.0), R2, T2t)
        P.op("dve", lambda e: e.tensor_scalar(out=Cz[:, :], in0=C1[:, :], scalar1=SGN[:, 0:1], scalar2=None, op0=ALU.mult), R2, T2t)
        for g in range(GPC):
            j = g % 8
            P.op("dve", lambda e, g=g, j=j: e.tensor_copy(out=CzPad[:, g, 16 * j:16 * j + 16], in_=Cz[:, 16 * g:16 * g + 16]), R2, T2t)

        V = [sbt(f"V{i}", [128, 2, GPC, TCH]) for i in range(2)]
        Sbf = [sbt(f"Sbf{i}", [128, GPC, TCH], BF16) for i in range(2)]
        hc = [sbt(f"hc{i}", [128, 2, TCH]) for i in range(2)]
        hb = [sbt(f"hb{i}", [128, 2, TCH], BF16) for i in range(2)]
        T1 = sbt("T1", [128, 2, GPC]); T2 = sbt("T2", [128, 2, GPC])
        Z0 = sbt("Z0", [128, 2, GPC])
        P.op("dve", lambda e: e.memset(Z0[:, :, :], 0.0), [], [("Z0",)])
        ysb = [sbt(f"ysb{i}", [128, TCH]) for i in range(2)]
        xg = [sbt(f"xg{i}", [128, TCH]) for i in range(2)]
        wg = [sbt(f"wg{i}", [128, TCH]) for i in range(2)]
        og = [sbt(f"og{i}", [128, TCH]) for i in range(4)]
        pse = [c.ps(f"pse{i}", [128, TCH]) for i in range(4)]
        psy = [c.ps(f"psy{i}", [128, TCH]) for i in range(2)]
        hTv = hT_d.rearrange("(tile p) t -> p tile t", p=128)
        yTv = yT_d.rearrange("(tile p) t -> p tile t", p=128)
        outs = []
        er = 0
        yr = 0
        orr = 0
        for k in range(nchunk):
            b = k % 2
            tsl = slice(k * TCH, (k + 1) * TCH)
            P.dma("sp", lambda e, b=b, tsl=tsl: e.dma_start(out=hc[b][:, :, :], in_=hTv[:, :, tsl]), [], [("hc", b)], sem=("hc", b))
            P.op("act", lambda e, b=b: e.activation(out=hb[b][:, :, :], in_=hc[b][:, :, :], func=AF.Copy), [("hc", b)], [("hb", b)])
            for g in range(GPC):
                for zz in range(2):
                    pb = er % 4
                    er += 1
                    P.op("pe", lambda e, g=g, zz=zz, pb=pb, b=b: e.matmul(pse[pb][:, :], lhsT=BzPad[:, g, zz, :], rhs=hb[b][:, g // 8, :], start=True, stop=True),
                         [("hb", b), ("tab2",)], [("pse", pb)])
                    P.op("act", lambda e, g=g, zz=zz, pb=pb, b=b: e.activation(out=V[b][:, zz, g, :], in_=pse[pb][:, :], func=AF.Copy),
                         [("pse", pb)], [("V", b)])
            for t in range(TCH):
                if t == 0:
                    zp = Z0[:, :, :] if k == 0 else V[1 - b][:, :, :, TCH - 1]
                    zr = [("Z0",)] if k == 0 else [("V", 1 - b)]
                else:
                    zp = V[b][:, :, :, t - 1]
                    zr = [("V", b)]
                zp0 = Z0[:, 0, :] if (k == 0 and t == 0) else (V[1 - b][:, 0, :, TCH - 1] if t == 0 else V[b][:, 0, :, t - 1])
                zp1 = Z0[:, 1, :] if (k == 0 and t == 0) else (V[1 - b][:, 1, :, TCH - 1] if t == 0 else V[b][:, 1, :, t - 1])
                P.op("dve", lambda e, zp=zp: e.tensor_tensor(out=T1[:, :, :], in0=zp, in1=ARR[:, :, :], op=ALU.mult), zr + [("tab1",), ("T1",)], [("T1",)])
                P.op("dve", lambda e, zp1=zp1: e.tensor_tensor(out=T2[:, 0, :], in0=zp1, in1=AIP[:, 0, :], op=ALU.mult), zr + [("tab1",), ("T2",)], [("T2",)])
                P.op("dve", lambda e, zp0=zp0: e.tensor_tensor(out=T2[:, 1, :], in0=zp0, in1=AIP[:, 1, :], op=ALU.mult), zr + [("tab1",), ("T2",)], [("T2",)])
                P.op("dve", lambda e: e.tensor_tensor(out=T1[:, :, :], in0=T1[:, :, :], in1=T2[:, :, :], op=ALU.add), [("T1",), ("T2",)], [("T1",)])
                P.op("dve", lambda e, b=b, t=t: e.tensor_tensor(out=V[b][:, :, :, t], in0=V[b][:, :, :, t], in1=T1[:, :, :], op=ALU.add), [("T1",), ("V", b)], [("V", b)])
            P.op("act", lambda e, b=b: e.activation(out=Sbf[b][:, :, :], in_=V[b][:, 0, :, :], func=AF.Copy), [("V", b)], [("Sbf", b)])
            for tile in range(2):
                yb = yr % 2
                yr += 1
                for j in range(8):
                    g = tile * 8 + j
                    P.op("pe", lambda e, g=g, j=j, yb=yb, b=b: e.matmul(psy[yb][:, :], lhsT=CzPad[:, g, :], rhs=Sbf[b][:, g, :], start=(j == 0), stop=(j == 7)),
                         [("Sbf", b), ("tab2",)], [("psy", yb)])
                ob = orr % 4
                orr += 1
                P.op("act", lambda e, yb=yb: e.activation(out=ysb[yb][:, :], in_=psy[yb][:, :], func=AF.Copy), [("psy", yb)], [("ysb", yb)])
                P.op("pool", lambda e, yb=yb, b=b, tile=tile: e.tensor_scalar(out=xg[yb][:, :], in0=hc[b][:, tile, :], scalar1=DCOL[:, tile:tile + 1], scalar2=None, op0=ALU.mult),
                     [("hc", b), ("par",)], [("xg", yb)])
                P.op("pool", lambda e, yb=yb: e.tensor_tensor(out=xg[yb][:, :], in0=xg[yb][:, :], in1=ysb[yb][:, :], op=ALU.add), [("xg", yb), ("ysb", yb)], [("xg", yb)])
                P.op("pool", lambda e, yb=yb: e.tensor_tensor(out=wg[yb][:, :], in0=xg[yb][:, :], in1=xg[yb][:, :], op=ALU.mult), [("xg", yb)], [("wg", yb)])
                P.op("pool", lambda e, yb=yb: e.tensor_scalar(out=wg[yb][:, :], in0=wg[yb][:, :], scalar1=0.044715, scalar2=1.0, op0=ALU.mult, op1=ALU.add), [("wg", yb)], [("wg", yb)])
                P.op("pool", lambda e, yb=yb: e.tensor_tensor(out=wg[yb][:, :], in0=wg[yb][:, :], in1=xg[yb][:, :], op=ALU.mult), [("wg", yb), ("xg", yb)], [("wg", yb)])
                P.op("act", lambda e, yb=yb: e.activation(out=wg[yb][:, :], in_=wg[yb][:, :], func=AF.Sigmoid, scale=1.5957691216), [("wg", yb)], [("wg", yb)])
                P.op("pool", lambda e, yb=yb, ob=ob: e.tensor_tensor(out=og[ob][:, :], in0=wg[yb][:, :], in1=xg[yb][:, :], op=ALU.mult), [("wg", yb), ("xg", yb)], [("og", ob)])
                outs.append(P.dma("sp", lambda e, ob=ob, tile=tile, tsl=tsl: e.dma_start(out=yTv[:, tile, tsl], in_=og[ob][:, :]), [("og", ob)], [], sem=("og", ob)))
        P.emit(final_wait_ops=outs[-4:])
    return nc


def ssm_host_inputs(hT_core, lam_re, lam_im, log_dt, b_re, b_im, c_re, c_im, d, core):
    G0 = core * GPC
    gs = slice(G0, G0 + GPC)
    lr, li, ld = lam_re[gs], lam_im[gs], log_dt[gs]
    LR1 = np.ascontiguousarray(np.concatenate([lr.T, lr.T], 0))
    LI1 = np.ascontiguousarray(np.concatenate([li.T, li.T], 0))
    DT1 = np.ascontiguousarray(np.broadcast_to(ld[None, :], (128, GPC)))

    def l2(a):
        a = a.reshape(2, 8, 1, 64)
        a = np.broadcast_to(a, (2, 8, 16, 64))
        return np.ascontiguousarray(a.transpose(1, 2, 0, 3).reshape(128, 128))
    LR2, LI2 = l2(lr), l2(li)
    DT2 = l2(np.broadcast_to(ld[:, None], (GPC, 64)))

    def bt(bb):
        a = bb.reshape(2, 8, 64, 16).transpose(1, 3, 0, 2)
        return np.ascontiguousarray(a.reshape(128, 128))
    BRT, BIT = bt(b_re[gs]), bt(b_im[gs])
    cr = c_re[gs].transpose(2, 0, 1)
    ci = c_im[gs].transpose(2, 0, 1)
    C1 = np.ascontiguousarray(np.concatenate([cr, ci], 0).reshape(128, GPC * 16))
    SGN = np.concatenate([np.ones((64, 1), np.float32), -np.ones((64, 1), np.float32)], 0)
    MASK = np.zeros((128, 8), np.float32)
    for j in range(8):
        MASK[16 * j:16 * j + 16, j] = 1.0
    DCOL = np.ascontiguousarray(d[core * 256:(core + 1) * 256].reshape(2, 128).T)
    return {"hT": np.ascontiguousarray(hT_core), "LR1": LR1, "LI1": LI1, "DT1": DT1, "LR2": LR2, "LI2": LI2, "DT2": DT2,
            "BRT": BRT, "BIT": BIT, "C1": C1, "SGN": SGN, "MASK": MASK, "DCOL": DCOL}


def build_dense(mode):
    from contextlib import ExitStack
    nc = bass.Bass("TRN2", target_bir_lowering=False)
    din = lambda n, s: nc.dram_tensor(n, s, F32, kind="ExternalInput").ap()
    x_d = din("xT", [D, TOK])
    vec_names = ["g_mlp", "g_next"]
    if mode == "conv":
        xh_d = din("xhT", [D, HALO])
        flag_d = din("flag", [128, 1])
        w1_d = din("w_in", [D, 2 * D])
        w2_d = din("w_out", [D, D])
        dw_d = din("dw", [CW, D])
        vec_names += ["g_mix", "b_in_a", "b_in_g", "dw_b", "ln_g", "ln_b", "b_out"]
    else:
        y_d = din("yT", [D, TOK])
        w1_d = din("w_glu", [D, 2 * D])
    vec_d = {n: din(n, [D]) for n in vec_names}
    wu_d = din("w_up", [D, DFF])
    wd_d = din("w_down", [DFF, D])
    xo_d = nc.dram_tensor("xo", [D, TOK], F32, kind="ExternalOutput").ap()
    ho_d = nc.dram_tensor("ho", [D, TOK], F32, kind="ExternalOutput").ap()
    NT = TOK + HALO if mode == "conv" else TOK
    with ExitStack() as es:
        c = Ctx(nc, es)
        P = c.P
        xT = c.sb("xT_sb", [128, KT, TOK], F32)
        hT = c.sb("hT_sb", [128, KT, NT], BF16)
        big = c.sb("big", [128, KT * TOK], F32)
        bigf = big[:, :].rearrange("p (k t) -> p k t", k=KT)
        hid = big[:, :].bitcast(BF16)[:, :DFF // 128 * 512].rearrange("p (k t) -> p k t", k=DFF // 128) if hasattr(big[:, :], "bitcast") else None
        tmp = c.sb("tmp", [128, 2, 512], F32)
        rstd = c.sb("rstd", [128, NT], F32)
        ones = c.sb("ones", [128, 128], F32)
        epsc = c.sb("epsc", [128, 1], F32)
        vec = {n: c.sb("v_" + n, [128, KT], F32) for n in vec_names}
        ws = WStream(c, 2, KT * 256, "ws")
        banks = [c.ps(f"b{i}", [128, 512]) for i in range(6)]
        psn = [c.ps(f"n{i}", [128, 512]) for i in range(2)]
        for kt in range(KT):
            P.dma("sp", lambda e, kt=kt: e.dma_start(out=xT[:, kt, :], in_=x_d[kt * 128:(kt + 1) * 128, :]), [], [("x", kt)], sem="xin")
        for n in vec_names:
            P.dma("sp", lambda e, n=n: e.dma_start(out=vec[n][:, :], in_=vec_d[n].rearrange("(kt p) -> p kt", p=128), allow_slow_non_contiguous=True),
                  [], [("gcol",)], sem="vin")
        P.op("dve", lambda e: e.memset(ones[:, :], 1.0), [], [("ones",)])
        P.op("dve", lambda e: e.memset(epsc[:, :], EPS), [], [("epsc",)])

        def pair_glu(w_d, ba, bg, n_cols, consume):
            wv_ = w_d.rearrange("(kt p) m -> p kt m", p=128)
            chunks = [(s0, min(s0 + 512, n_cols)) for s0 in range(0, n_cols, 512)]
            for mt in range(KT):
                wa, ta = ws.load(wv_[:, :, mt * 128:(mt + 1) * 128], (KT, 128))
                wg_, tg = ws.load(wv_[:, :, D + mt * 128:D + (mt + 1) * 128], (KT, 128))
                for ci, (s0, s1) in enumerate(chunks):
                    ba_, bb_ = ci, 3 + ci
                    for (wv, wt, b) in ((wa, ta, ba_), (wg_, tg, bb_)):
                        for kt in range(KT):
                            P.op("pe", lambda e, kt=kt, wv=wv, b=b, s0=s0, s1=s1: e.matmul(banks[b][:, :s1 - s0], lhsT=wv[:, kt, :], rhs=hT[:, kt, s0:s1],
                                                                                      start=(kt == 0), stop=(kt == KT - 1)),
                                 reads=[wt, ("h", kt)], writes=[("ps", b)])
                    consume(mt, ci, s0, s1, banks[ba_], banks[bb_], ("ps", ba_), ("ps", bb_))

        if mode == "conv":
            xh = c.sb("xh", [128, KT, HALO], F32)
            flag = c.sb("flag_sb", [128, 1], F32)
            dwc = c.sb("dwc", [128, KT, CW], F32)
            vext = c.sb("vext", [128, 2, TOK + HALO], F32)
            mean = c.sb("mean", [128, TOK], F32)
            P.dma("sp", lambda e: e.dma_start(out=xh[:, :, :], in_=xh_d.rearrange("(kt p) t -> p kt t", p=128)), [], [("xh",)], sem="xh")
            P.dma("sp", lambda e: e.dma_start(out=flag[:, :], in_=flag_d[:, :]), [], [("gcol",)], sem="vin")
            for kt in range(KT):
                P.dma("sp", lambda e, kt=kt: e.dma_start(out=dwc[:, kt, :], in_=dw_d[:, kt * 128:(kt + 1) * 128].rearrange("j p -> p j"), allow_slow_non_contiguous=True),
                      [], [("gcol",)], sem="vin")
            emit_rmsnorm(c, xT, hT, vec["g_mix"], epsc, ones, tmp, rstd, psn)
            for kt in range(KT):
                s = kt % 2
                P.op("act", lambda e, kt=kt, s=s: e.activation(out=tmp[:, s, :HALO], in_=xh[:, kt, :], func=AF.Square), [("xh",)], [("tmp", s)])
                P.op("pe", lambda e, kt=kt, s=s: e.matmul(psn[0][:, :HALO], lhsT=ones[:, :], rhs=tmp[:, s, :HALO], start=(kt == 0), stop=(kt == KT - 1)),
                     [("tmp", s), ("ones",)], [("psn", 0)])
            P.op("act", lambda e: e.activation(out=rstd[:, TOK:NT], in_=psn[0][:, :HALO], func=AF.Sqrt, bias=epsc[:, 0:1], scale=1.0 / D), [("psn", 0), ("epsc",)], [("rstdh",)])
            P.op("dve", lambda e: e.reciprocal(out=rstd[:, TOK:NT], in_=rstd[:, TOK:NT]), [("rstdh",)], [("rstdh",)])
            for kt in range(KT):
                P.op("dve", lambda e, kt=kt: e.scalar_tensor_tensor(out=hT[:, kt, TOK:NT], in0=xh[:, kt, :], scalar=vec["g_mix"][:, kt:kt + 1], in1=rstd[:, TOK:NT],
                                                                   op0=ALU.mult, op1=ALU.mult), [("xh",), ("rstdh",), ("gcol",)], [("h", kt)])

            def consume(mt, ci, s0, s1, pa, pg, ta, tg):
                vs = mt % 2
                n = s1 - s0
                d0 = 0 if s0 == TOK else HALO + s0
                P.op("act", lambda e, n=n, mt=mt: e.activation(out=tmp[:, 0, :n], in_=pg[:, :n], func=AF.Sigmoid, bias=vec["b_in_g"][:, mt:mt + 1], scale=1.0),
                     [tg, ("gcol",)], [("tmp", 0)])
                P.op("dve", lambda e, n=n, mt=mt, vs=vs, d0=d0: e.scalar_tensor_tensor(out=vext[:, vs, d0:d0 + n], in0=pa[:, :n], scalar=vec["b_in_a"][:, mt:mt + 1], in1=tmp[:, 0, :n],
                                                                                    op0=ALU.add, op1=ALU.mult), [ta, ("tmp", 0), ("gcol",)], [("vext", vs)])
                if s0 == TOK:
                    P.op("dve", lambda e, vs=vs: e.tensor_scalar(out=vext[:, vs, 0:HALO], in0=vext[:, vs, 0:HALO], scalar1=flag[:, 0:1], scalar2=None, op0=ALU.mult),
                         [("vext", vs), ("gcol",)], [("vext", vs)])
                    P.op("dve", lambda e, mt=mt, vs=vs: e.tensor_scalar(out=bigf[:, mt, :], in0=vext[:, vs, 2:2 + TOK], scalar1=dwc[:, mt, 0:1], scalar2=vec["dw_b"][:, mt:mt + 1],
                                                                      op0=ALU.mult, op1=ALU.add), [("vext", vs), ("gcol",), ("bigbar",)], [("cv", mt)])
                    for j in range(1, CW):
                        P.op("dve", lambda e, mt=mt, vs=vs, j=j: e.scalar_tensor_tensor(out=bigf[:, mt, :], in0=vext[:, vs, 2 + j:2 + j + TOK], scalar=dwc[:, mt, j:j + 1], in1=bigf[:, mt, :],
                                                                                      op0=ALU.mult, op1=ALU.add), [("vext", vs), ("cv", mt), ("gcol",)], [("cv", mt)])
            pair_glu(w1_d, vec["b_in_a"], vec["b_in_g"], NT, consume)
            for h in range(2):
                tsl = slice(h * 512, (h + 1) * 512)
                for kt in range(KT):
                    s = kt % 2
                    P.op("pe", lambda e, kt=kt, h=h, tsl=tsl: e.matmul(banks[h][:, :], lhsT=ones[:, :], rhs=bigf[:, kt, tsl], start=(kt == 0), stop=(kt == KT - 1)),
                         [("cv", kt), ("ones",)], [("ps", h)])
                    P.op("act", lambda e, kt=kt, s=s, tsl=tsl: e.activation(out=tmp[:, s, :], in_=bigf[:, kt, tsl], func=AF.Square), [("cv", kt)], [("tmp", s)])
                    P.op("pe", lambda e, kt=kt, h=h, s=s: e.matmul(banks[2 + h][:, :], lhsT=ones[:, :], rhs=tmp[:, s, :], start=(kt == 0), stop=(kt == KT - 1)),
                         [("tmp", s), ("ones",)], [("ps", 2 + h)])
                P.op("act", lambda e, h=h, tsl=tsl: e.activation(out=mean[:, tsl], in_=banks[h][:, :], func=AF.Copy, scale=1.0 / D), [("ps", h)], [("mean",)])
                P.op("act", lambda e, h=h, tsl=tsl: e.activation(out=rstd[:, tsl], in_=banks[2 + h][:, :], func=AF.Copy, scale=1.0 / D), [("ps", 2 + h)], [("rstd", h)])
                P.op("dve", lambda e, tsl=tsl: e.tensor_tensor(out=tmp[:, 0, :], in0=mean[:, tsl], in1=mean[:, tsl], op=ALU.mult), [("mean",)], [("tmp", 0)])
                P.op("dve", lambda e, tsl=tsl: e.tensor_tensor(out=rstd[:, tsl], in0=rstd[:, tsl], in1=tmp[:, 0, :], op=ALU.subtract), [("rstd", h), ("tmp", 0)], [("rstd", h)])
                P.op("act", lambda e, tsl=tsl: e.activation(out=rstd[:, tsl], in_=rstd[:, tsl], func=AF.Sqrt, bias=epsc[:, 0:1], scale=1.0), [("rstd", h), ("epsc",)], [("rstd", h)])
                P.op("dve", lambda e, tsl=tsl: e.reciprocal(out=rstd[:, tsl], in_=rstd[:, tsl]), [("rstd", h)], [("rstd", h)])
            for kt in range(KT):
                eng = "dve" if kt % 2 == 0 else "pool"
                P.op(eng, lambda e, kt=kt: e.tensor_tensor(out=bigf[:, kt, :], in0=bigf[:, kt, :], in1=mean[:, :], op=ALU.subtract), [("cv", kt), ("mean",)], [("cv", kt)])
                P.op(eng, lambda e, kt=kt: e.tensor_tensor(out=bigf[:, kt, :], in0=bigf[:, kt, :], in1=rstd[:, :TOK], op=ALU.mult), [("cv", kt), ("rstd", 0), ("rstd", 1)], [("cv", kt)])
                P.op("act", lambda e, kt=kt: e.activation(out=hT[:, kt, :TOK], in_=bigf[:, kt, :], func=AF.Silu, bias=vec["ln_b"][:, kt:kt + 1], scale=vec["ln_g"][:, kt:kt + 1]),
                     [("cv", kt), ("gcol",)], [("h", kt)])
            wo = w2_d.rearrange("(kt p) m -> p kt m", p=128)
            for dt_ in range(KT):
                wv, wt = ws.load(wo[:, :, dt_ * 128:(dt_ + 1) * 128], (KT, 128))
                for h in range(2):
                    tsl = slice(h * 512, (h + 1) * 512)
                    b = c.ps_rr % 6
                    c.ps_rr += 1
                    for kt in range(KT):
                        P.op("pe", lambda e, kt=kt, wv=wv, b=b, tsl=tsl: e.matmul(banks[b][:, :], lhsT=wv[:, kt, :], rhs=hT[:, kt, tsl], start=(kt == 0), stop=(kt == KT - 1)),
                             [wt, ("h", kt)], [("ps", b)])
                    P.op("dve", lambda e, dt_=dt_, b=b, tsl=tsl: e.scalar_tensor_tensor(out=xT[:, dt_, tsl], in0=banks[b][:, :], scalar=vec["b_out"][:, dt_:dt_ + 1], in1=xT[:, dt_, tsl],
                                                                                      op0=ALU.add, op1=ALU.add), [("ps", b), ("x", dt_), ("gcol",)], [("x", dt_)])
            big_readers = [("cv", kt) for kt in range(KT)]
        else:
            for kt in range(KT):
                P.dma("sp", lambda e, kt=kt: e.dma_start(out=bigf[:, kt, :], in_=y_d[kt * 128:(kt + 1) * 128, :]), [], [("cv", kt)], sem="yin")
                P.op("act", lambda e, kt=kt: e.activation(out=hT[:, kt, :], in_=bigf[:, kt, :], func=AF.Copy), [("cv", kt)], [("h", kt)])

            def consume(mt, ci, s0, s1, pa, pg, ta, tg):
                n = s1 - s0
                P.op("act", lambda e, n=n: e.activation(out=tmp[:, 0, :n], in_=pg[:, :n], func=AF.Sigmoid), [tg], [("tmp", 0)])
                P.op("dve", lambda e, n=n: e.tensor_tensor(out=tmp[:, 0, :n], in0=tmp[:, 0, :n], in1=pa[:, :n], op=ALU.mult), [ta, ("tmp", 0)], [("tmp", 0)])
                P.op("dve", lambda e, n=n, mt=mt, s0=s0, s1=s1: e.tensor_tensor(out=xT[:, mt, s0:s1], in0=xT[:, mt, s0:s1], in1=tmp[:, 0, :n], op=ALU.add),
                     [("tmp", 0), ("x", mt)], [("x", mt)])
            pair_glu(w1_d, None, None, TOK, consume)
            big_readers = [("cv", kt) for kt in range(KT)]

        P.op("dve", lambda e: e.memset(epsc[:, :], EPS), big_readers + [("epsc",)], [("epsc",)] + [("hid", m) for m in range(DFF // 128)])
        emit_rmsnorm(c, xT, hT, vec["g_mlp"], epsc, ones, tmp, rstd, psn)
        emit_mlp(c, xT, hT, wu_d, wd_d, ws, hid, tmp, banks)
        outs = []
        for kt in range(KT):
            outs.append(P.dma("sp", lambda e, kt=kt: e.dma_start(out=xo_d[kt * 128:(kt + 1) * 128, :], in_=xT[:, kt, :]), [("x", kt)], [], sem="xout"))
        P.op("dve", lambda e: e.memset(epsc[:, :], EPS), [("hid", m) for m in range(DFF // 128)] + [("epsc",)], [("epsc",)] + [("hn", kt) for kt in range(KT)])
        emit_rmsnorm(c, xT, bigf, vec["g_next"], epsc, ones, tmp, rstd, psn, h_tag="hn")
        for kt in range(KT):
            outs.append(P.dma("sp", lambda e, kt=kt: e.dma_start(out=ho_d[kt * 128:(kt + 1) * 128, :], in_=bigf[:, kt, :]), [("hn", kt)], [], sem="hout"))
        P.emit(final_wait_ops=outs)
    return nc


U32 = mybir.dt.uint32
NK = SEQ // TCH
KPC = TOK // TCH


def dense_phase(c, mode, xT, ones, epsc, io):
    from contextlib import ExitStack
    nc = c.nc
    P = c.P
    vec_names = ["g_mlp", "g_next"] + (["g_mix", "b_in_a", "b_in_g", "dw_b", "ln_g", "ln_b", "b_out"] if mode == "conv" else [])
    NT = TOK + HALO if mode == "conv" else TOK
    with ExitStack() as es:
        sb = lambda n, shp, dt=F32: es.enter_context(nc.sbuf_tensor(f"{n}_{io['uid']}", shp, dt))
        hT = sb("hT_sb", [128, KT, NT], BF16)
        big = sb("big", [128, KT * TOK], F32)
        bigf = big[:, :].rearrange("p (k t) -> p k t", k=KT)
        hid = big[:, :].bitcast(BF16)[:, :DFF // 128 * 512].rearrange("p (k t) -> p k t", k=DFF // 128)
        tmp = sb("tmp", [128, 2, 512], F32)
        rstd = sb("rstd", [128, NT], F32)
        vec = {n: sb("v_" + n, [128, KT], F32) for n in vec_names}
        ws = WStream.__new__(WStream)
        ws.c, ws.n, ws.i, ws.name = c, 2, 0, "ws"
        ws.buf = sb("ws", [128, 2, KT * 256], BF16)
        banks = [es.enter_context(nc.psum_tensor(f"b{i}_{io['uid']}", [128, 512], F32)) for i in range(6)]
        psn = [es.enter_context(nc.psum_tensor(f"n{i}_{io['uid']}", [128, 512], F32)) for i in range(2)]
        for n in vec_names:
            P.dma("sp", lambda e, n=n: e.dma_start(out=vec[n][:, :], in_=io[n].rearrange("(kt p) -> p kt", p=128), allow_slow_non_contiguous=True),
                  [], [("gcol",)], sem="vin")

        def pair_glu(w_d, n_cols, consume):
            wv_ = w_d.rearrange("(kt p) m -> p kt m", p=128)
            chunks = [(s0, min(s0 + 512, n_cols)) for s0 in range(0, n_cols, 512)]
            for mt in range(KT):
                wa, ta = ws.load(wv_[:, :, mt * 128:(mt + 1) * 128], (KT, 128))
                wg_, tg = ws.load(wv_[:, :, D + mt * 128:D + (mt + 1) * 128], (KT, 128))
                for ci, (s0, s1) in enumerate(chunks):
                    ba_, bb_ = ci, 3 + ci
                    for (wv, wt, b) in ((wa, ta, ba_), (wg_, tg, bb_)):
                        for kt in range(KT):
                            P.op("pe", lambda e, kt=kt, wv=wv, b=b, s0=s0, s1=s1: e.matmul(banks[b][:, :s1 - s0], lhsT=wv[:, kt, :], rhs=hT[:, kt, s0:s1],
                                                                                      start=(kt == 0), stop=(kt == KT - 1)),
                                 reads=[wt, ("h", kt)], writes=[("ps", b)])
                    consume(mt, ci, s0, s1, banks[ba_], banks[bb_], ("ps", ba_), ("ps", bb_))

        if mode == "conv":
            xh = sb("xh", [128, KT, HALO], F32)
            flag = sb("flag_sb", [128, 1], F32)
            dwc = sb("dwc", [128, KT, CW], F32)
            vext = sb("vext", [128, 2, TOK + HALO], F32)
            mean = sb("mean", [128, TOK], F32)
            if io["halo"][0] == "dram":
                P.dma("sp", lambda e: e.dma_start(out=xh[:, :, :], in_=io["halo"][1].rearrange("(kt p) t -> p kt t", p=128)), [], [("xh",)], sem="xh")
            else:
                xtg, idx2 = io["halo"][1], io["halo"][2]
                for kt in range(KT):
                    P.dma("pool", lambda e, kt=kt: e.indirect_dma_start(out=xh[:, kt, :], out_offset=None, in_=xtg[:, :],
                                                                        in_offset=bass.IndirectOffsetOnAxis(ap=idx2[:, 0:1], axis=0), element_offset=kt * 128 * HALO),
                          [("xtg",), ("idx",)], [("xh",)], sem="xh")
            P.dma("sp", lambda e: e.dma_start(out=flag[:, :], in_=io["flag"][:, :]), [], [("gcol",)], sem="vin")
            for kt in range(KT):
                P.dma("sp", lambda e, kt=kt: e.dma_start(out=dwc[:, kt, :], in_=io["dw"][:, kt * 128:(kt + 1) * 128].rearrange("j p -> p j"), allow_slow_non_contiguous=True),
                      [], [("gcol",)], sem="vin")
            emit_rmsnorm(c, xT, hT, vec["g_mix"], epsc, ones, tmp, rstd, psn)
            for kt in range(KT):
                s = kt % 2
                P.op("act", lambda e, kt=kt, s=s: e.activation(out=tmp[:, s, :HALO], in_=xh[:, kt, :], func=AF.Square), [("xh",)], [("tmp", s)])
                P.op("pe", lambda e, kt=kt, s=s: e.matmul(psn[0][:, :HALO], lhsT=ones[:, :], rhs=tmp[:, s, :HALO], start=(kt == 0), stop=(kt == KT - 1)),
                     [("tmp", s), ("ones",)], [("psn", 0)])
            P.op("act", lambda e: e.activation(out=rstd[:, TOK:NT], in_=psn[0][:, :HALO], func=AF.Sqrt, bias=epsc[:, 0:1], scale=1.0 / D), [("psn", 0), ("epsc",)], [("rstdh",)])
            P.op("dve", lambda e: e.reciprocal(out=rstd[:, TOK:NT], in_=rstd[:, TOK:NT]), [("rstdh",)], [("rstdh",)])
            for kt in range(KT):
                P.op("dve", lambda e, kt=kt: e.scalar_tensor_tensor(out=hT[:, kt, TOK:NT], in0=xh[:, kt, :], scalar=vec["g_mix"][:, kt:kt + 1], in1=rstd[:, TOK:NT],
                                                                   op0=ALU.mult, op1=ALU.mult), [("xh",), ("rstdh",), ("gcol",)], [("h", kt)])

            def consume(mt, ci, s0, s1, pa, pg, ta, tg):
                vs = mt % 2
                n = s1 - s0
                d0 = 0 if s0 == TOK else HALO + s0
                P.op("act", lambda e, n=n, mt=mt: e.activation(out=tmp[:, 0, :n], in_=pg[:, :n], func=AF.Sigmoid, bias=vec["b_in_g"][:, mt:mt + 1], scale=1.0),
                     [tg, ("gcol",)], [("tmp", 0)])
                P.op("dve", lambda e, n=n, mt=mt, vs=vs, d0=d0: e.scalar_tensor_tensor(out=vext[:, vs, d0:d0 + n], in0=pa[:, :n], scalar=vec["b_in_a"][:, mt:mt + 1], in1=tmp[:, 0, :n],
                                                                                    op0=ALU.add, op1=ALU.mult), [ta, ("tmp", 0), ("gcol",)], [("vext", vs)])
                if s0 == TOK:
                    P.op("dve", lambda e, vs=vs: e.tensor_scalar(out=vext[:, vs, 0:HALO], in0=vext[:, vs, 0:HALO], scalar1=flag[:, 0:1], scalar2=None, op0=ALU.mult),
                         [("vext", vs), ("gcol",)], [("vext", vs)])
                    P.op("dve", lambda e, mt=mt, vs=vs: e.tensor_scalar(out=bigf[:, mt, :], in0=vext[:, vs, 2:2 + TOK], scalar1=dwc[:, mt, 0:1], scalar2=vec["dw_b"][:, mt:mt + 1],
                                                                      op0=ALU.mult, op1=ALU.add), [("vext", vs), ("gcol",)], [("cv", mt)])
                    for j in range(1, CW):
                        P.op("dve", lambda e, mt=mt, vs=vs, j=j: e.scalar_tensor_tensor(out=bigf[:, mt, :], in0=vext[:, vs, 2 + j:2 + j + TOK], scalar=dwc[:, mt, j:j + 1], in1=bigf[:, mt, :],
                                                                                      op0=ALU.mult, op1=ALU.add), [("vext", vs), ("cv", mt), ("gcol",)], [("cv", mt)])
            pair_glu(io["w_in"], NT, consume)
            for h in range(2):
                tsl = slice(h * 512, (h + 1) * 512)
                for kt in range(KT):
                    s = kt % 2
                    P.op("pe", lambda e, kt=kt, h=h, tsl=tsl: e.matmul(banks[h][:, :], lhsT=ones[:, :], rhs=bigf[:, kt, tsl], start=(kt == 0), stop=(kt == KT - 1)),
                         [("cv", kt), ("ones",)], [("ps", h)])
                    P.op("act", lambda e, kt=kt, s=s, tsl=tsl: e.activation(out=tmp[:, s, :], in_=bigf[:, kt, tsl], func=AF.Square), [("cv", kt)], [("tmp", s)])
                    P.op("pe", lambda e, kt=kt, h=h, s=s: e.matmul(banks[2 + h][:, :], lhsT=ones[:, :], rhs=tmp[:, s, :], start=(kt == 0), stop=(kt == KT - 1)),
                         [("tmp", s), ("ones",)], [("ps", 2 + h)])
                P.op("act", lambda e, h=h, tsl=tsl: e.activation(out=mean[:, tsl], in_=banks[h][:, :], func=AF.Copy, scale=1.0 / D), [("ps", h)], [("mean",)])
                P.op("act", lambda e, h=h, tsl=tsl: e.activation(out=rstd[:, tsl], in_=banks[2 + h][:, :], func=AF.Copy, scale=1.0 / D), [("ps", 2 + h)], [("rstd", h)])
                P.op("dve", lambda e, tsl=tsl: e.tensor_tensor(out=tmp[:, 0, :], in0=mean[:, tsl], in1=mean[:, tsl], op=ALU.mult), [("mean",)], [("tmp", 0)])
                P.op("dve", lambda e, tsl=tsl: e.tensor_tensor(out=rstd[:, tsl], in0=rstd[:, tsl], in1=tmp[:, 0, :], op=ALU.subtract), [("rstd", h), ("tmp", 0)], [("rstd", h)])
                P.op("act", lambda e, tsl=tsl: e.activation(out=rstd[:, tsl], in_=rstd[:, tsl], func=AF.Sqrt, bias=epsc[:, 0:1], scale=1.0), [("rstd", h), ("epsc",)], [("rstd", h)])
                P.op("dve", lambda e, tsl=tsl: e.reciprocal(out=rstd[:, tsl], in_=rstd[:, tsl]), [("rstd", h)], [("rstd", h)])
            for kt in range(KT):
                eng = "dve" if kt % 2 == 0 else "pool"
                P.op(eng, lambda e, kt=kt: e.tensor_tensor(out=bigf[:, kt, :], in0=bigf[:, kt, :], in1=mean[:, :], op=ALU.subtract), [("cv", kt), ("mean",)], [("cv", kt)])
                P.op(eng, lambda e, kt=kt: e.tensor_tensor(out=bigf[:, kt, :], in0=bigf[:, kt, :], in1=rstd[:, :TOK], op=ALU.mult), [("cv", kt), ("rstd", 0), ("rstd", 1)], [("cv", kt)])
                P.op("act", lambda e, kt=kt: e.activation(out=hT[:, kt, :TOK], in_=bigf[:, kt, :], func=AF.Silu, bias=vec["ln_b"][:, kt:kt + 1], scale=vec["ln_g"][:, kt:kt + 1]),
                     [("cv", kt), ("gcol",)], [("h", kt)])
            wo = io["w_out"].rearrange("(kt p) m -> p kt m", p=128)
            for dt_ in range(KT):
                wv, wt = ws.load(wo[:, :, dt_ * 128:(dt_ + 1) * 128], (KT, 128))
                for h in range(2):
                    tsl = slice(h * 512, (h + 1) * 512)
                    b = c.ps_rr % 6
                    c.ps_rr += 1
                    for kt in range(KT):
                        P.op("pe", lambda e, kt=kt, wv=wv, b=b, tsl=tsl: e.matmul(banks[b][:, :], lhsT=wv[:, kt, :], rhs=hT[:, kt, tsl], start=(kt == 0), stop=(kt == KT - 1)),
                             [wt, ("h", kt)], [("ps", b)])
                    P.op("dve", lambda e, dt_=dt_, b=b, tsl=tsl: e.scalar_tensor_tensor(out=xT[:, dt_, tsl], in0=banks[b][:, :], scalar=vec["b_out"][:, dt_:dt_ + 1], in1=xT[:, dt_, tsl],
                                                                                      op0=ALU.add, op1=ALU.add), [("ps", b), ("x", dt_), ("gcol",)], [("x", dt_)])
        else:
            ygath, idx1 = io["ygath"], io["idx1"]
            for r in range(NCORES):
                for kk in range(KPC):
                    for tile in range(2):
                        kt = 2 * r + tile
                        eo = ((r * NK + kk) * 256 + tile * 128) * TCH
                        P.dma("pool", lambda e, kt=kt, kk=kk, eo=eo: e.indirect_dma_start(out=hT[:, kt, kk * TCH:(kk + 1) * TCH], out_offset=None, in_=ygath[:, :],
                                                                                          in_offset=bass.IndirectOffsetOnAxis(ap=idx1[:, 0:1], axis=0), element_offset=eo),
                              [("ygath",), ("idx",)], [("h", kt)], sem=("yg", kt % 4))

            def consume(mt, ci, s0, s1, pa, pg, ta, tg):
                n = s1 - s0
                P.op("act", lambda e, n=n: e.activation(out=tmp[:, 0, :n], in_=pg[:, :n], func=AF.Sigmoid), [tg], [("tmp", 0)])
                P.op("dve", lambda e, n=n: e.tensor_tensor(out=tmp[:, 0, :n], in0=tmp[:, 0, :n], in1=pa[:, :n], op=ALU.mult), [ta, ("tmp", 0)], [("tmp", 0)])
                P.op("dve", lambda e, n=n, mt=mt, s0=s0, s1=s1: e.tensor_tensor(out=xT[:, mt, s0:s1], in0=xT[:, mt, s0:s1], in1=tmp[:, 0, :n], op=ALU.add),
                     [("tmp", 0), ("x", mt)], [("x", mt)])
            pair_glu(io["w_glu"], TOK, consume)

        big_readers = [("cv", kt) for kt in range(KT)]
        P.op("dve", lambda e: e.memset(epsc[:, :], EPS), big_readers + [("epsc",)], [("epsc",)] + [("hid", m) for m in range(DFF // 128)])
        emit_rmsnorm(c, xT, hT, vec["g_mlp"], epsc, ones, tmp, rstd, psn)
        emit_mlp(c, xT, hT, io["w_up"], io["w_down"], ws, hid, tmp, banks)
        outs = []
        nxt = io["next"]
        if nxt[0] == "ssm":
            hb_d = nxt[1]
            emit_rmsnorm(c, xT, hT, vec["g_next"], epsc, ones, tmp, rstd, psn)
            hbv = hb_d.rearrange("kk (kt p) t -> p kt kk t", p=128)
            for kt in range(KT):
                P.dma("sp", lambda e, kt=kt: e.dma_start(out=hbv[:, kt, :, :], in_=hT[:, kt, :TOK].rearrange("p (kk t) -> p kk t", kk=KPC)), [("h", kt)], [("hb",)], sem="hbout")
        elif nxt[0] == "conv":
            xtb = nxt[1]
            for kt in range(KT):
                P.dma("sp", lambda e, kt=kt: e.dma_start(out=xtb[kt * 128:(kt + 1) * 128, :], in_=xT[:, kt, TOK - HALO:TOK]), [("x", kt)], [("xtb",)], sem="xtbout")
        else:
            out_d = nxt[1]
            P.op("dve", lambda e: e.memset(epsc[:, :], EPS), [("hid", m) for m in range(DFF // 128)] + [("epsc",)], [("epsc",)] + [("hn", kt) for kt in range(KT)])
            emit_rmsnorm(c, xT, bigf, vec["g_next"], epsc, ones, tmp, rstd, psn, h_tag="hn")
            for kt in range(KT):
                outs.append(P.dma("sp", lambda e, kt=kt: e.dma_start(out=out_d[kt * 128:(kt + 1) * 128, :], in_=bigf[:, kt, :]), [("hn", kt)], [], sem="hout"))
        P.barrier()
    return outs


def ssm_phase(c, io, seq=SEQ):
    from contextlib import ExitStack
    nc = c.nc
    P = c.P
    uid = io["uid"]
    LR1_d, LI1_d, DT1_d, LR2_d, LI2_d, DT2_d = io["LR1"], io["LI1"], io["DT1"], io["LR2"], io["LI2"], io["DT2"]
    BRT_d, BIT_d, C1_d, SGN_d, MASK_d, DCOL_d = io["BRT"], io["BIT"], io["C1"], io["SGN"], io["MASK"], io["DCOL"]
    hgath, idx0, ybd = io["hgath"], io["idx0"], io["yb"]
    nchunk = seq // TCH
    with ExitStack() as es:
        sbt = lambda n, s, dt=F32: es.enter_context(nc.sbuf_tensor(f"{n}_{uid}", s, dt))
        LR1, LI1, DT1 = sbt("LR1s", [128, GPC]), sbt("LI1s", [128, GPC]), sbt("DT1s", [128, GPC])
        LR2, LI2, DT2 = sbt("LR2s", [128, 128]), sbt("LI2s", [128, 128]), sbt("DT2s", [128, 128])
        BRT, BIT = sbt("BRTs", [128, 128]), sbt("BITs", [128, 128])
        C1 = sbt("C1s", [128, GPC * 16])
        SGN, MASK, DCOL = sbt("SGNs", [128, 1]), sbt("MASKs", [128, 8]), sbt("DCOLs", [128, 2])
        npi = sbt("npi", [128, 1])
        for i, (s_, d_) in enumerate([(LR1, LR1_d), (LI1, LI1_d), (DT1, DT1_d), (LR2, LR2_d), (LI2, LI2_d), (DT2, DT2_d),
                                      (BRT, BRT_d), (BIT, BIT_d), (C1, C1_d), (SGN, SGN_d), (MASK, MASK_d), (DCOL, DCOL_d)]):
            P.dma("sp", lambda e, s_=s_, d_=d_: e.dma_start(out=s_[:, :], in_=d_[:, :]), [], [("par",)], sem="par")
        P.op("dve", lambda e: e.memset(npi[:, :], -PI), [], [("par",)])

        def abar(LR, LI, DT, n, nm):
            dt = sbt(nm + "dt", [128, n]); t1 = sbt(nm + "t1", [128, n]); t2 = sbt(nm + "t2", [128, n])
            ar = sbt(nm + "ar", [128, n]); ai = sbt(nm + "ai", [128, n]); mg = sbt(nm + "mg", [128, n])
            T = [(nm,)]
            R = [("par",), (nm,)]
            P.op("act", lambda e: e.activation(out=dt[:, :], in_=DT[:, :], func=AF.Exp), R, T)
            P.op("dve", lambda e: e.tensor_tensor(out=t1[:, :], in0=LR[:, :], in1=dt[:, :], op=ALU.mult), R, T)
            P.op("act", lambda e: e.activation(out=mg[:, :], in_=t1[:, :], func=AF.Exp), R, T)
            P.op("dve", lambda e: e.tensor_tensor(out=t1[:, :], in0=LI[:, :], in1=dt[:, :], op=ALU.mult), R, T)
            ki = sbt(nm + "ki", [128, n], mybir.dt.int32); kf = sbt(nm + "kf", [128, n])

            def sin_of(dst, shift):
                P.op("dve", lambda e: e.tensor_scalar(out=t2[:, :], in0=t1[:, :], scalar1=shift, scalar2=None, op0=ALU.add), R, T)
                P.op("dve", lambda e: e.tensor_scalar(out=kf[:, :], in0=t2[:, :], scalar1=1.0 / (2 * PI), scalar2=0.5, op0=ALU.mult, op1=ALU.add), R, T)
                P.op("dve", lambda e: e.tensor_copy(out=ki[:, :], in_=kf[:, :]), R, T)
                P.op("dve", lambda e: e.tensor_copy(out=kf[:, :], in_=ki[:, :]), R, T)
                P.op("dve", lambda e: e.scalar_tensor_tensor(out=t2[:, :], in0=kf[:, :], scalar=-2 * PI, in1=t2[:, :], op0=ALU.mult, op1=ALU.add), R, T)
                P.op("dve", lambda e: e.tensor_scalar(out=kf[:, :], in0=t2[:, :], scalar1=-PI, scalar2=2 * PI, op0=ALU.is_lt, op1=ALU.mult), R, T)
                P.op("dve", lambda e: e.tensor_tensor(out=t2[:, :], in0=t2[:, :], in1=kf[:, :], op=ALU.add), R, T)
                P.op("dve", lambda e: e.tensor_scalar(out=t2[:, :], in0=t2[:, :], scalar1=-PI, scalar2=PI, op0=ALU.max, op1=ALU.min), R, T)
                P.op("act", lambda e: e.activation(out=t2[:, :], in_=t2[:, :], func=AF.Sin), R, T)
                P.op("dve", lambda e: e.tensor_tensor(out=dst[:, :], in0=mg[:, :], in1=t2[:, :], op=ALU.mult), R, T)
            sin_of(ai, 0.0)
            sin_of(ar, 0.5 * PI)
            return ar, ai

        ar1, ai1 = abar(LR1, LI1, DT1, GPC, "a1")
        ARR = sbt("ARR", [128, 2, GPC]); AIP = sbt("AIP", [128, 2, GPC])
        T1t = [("tab1",)]
        R1 = [("a1",), ("tab1",)]
        P.op("dve", lambda e: e.tensor_copy(out=ARR[:, 0, :], in_=ar1[:, :]), R1, T1t)
        P.op("dve", lambda e: e.tensor_copy(out=ARR[:, 1, :], in_=ar1[:, :]), R1, T1t)
        P.op("dve", lambda e: e.tensor_copy(out=AIP[:, 0, :], in_=ai1[:, :]), R1, T1t)
        P.op("dve", lambda e: e.tensor_scalar(out=AIP[:, 1, :], in0=ai1[:, :], scalar1=-1.0, scalar2=None, op0=ALU.mult), R1, T1t)

        ar2, ai2 = abar(LR2, LI2, DT2, 128, "a2")
        w = [sbt(f"w{i}", [128, 128]) for i in range(6)]
        R2 = [("a2",), ("par",), ("tab2",)]
        T2t = [("tab2",)]
        tt = lambda o, a, b, op: P.op("dve", lambda e: e.tensor_tensor(out=o, in0=a, in1=b, op=op), R2, T2t)
        nr, den, kre, kim, bbr, bbi = w
        P.op("dve", lambda e: e.tensor_scalar(out=nr[:, :], in0=ar2[:, :], scalar1=-1.0, scalar2=None, op0=ALU.add), R2, T2t)
        tt(den[:, :], LR2[:, :], LR2[:, :], ALU.mult)
        tt(kre[:, :], LI2[:, :], LI2[:, :], ALU.mult)
        tt(den[:, :], den[:, :], kre[:, :], ALU.add)
        P.op("dve", lambda e: e.reciprocal(out=den[:, :], in_=den[:, :]), R2, T2t)
        tt(kre[:, :], nr[:, :], LR2[:, :], ALU.mult)
        tt(kim[:, :], ai2[:, :], LI2[:, :], ALU.mult)
        tt(kre[:, :], kre[:, :], kim[:, :], ALU.add)
        tt(kre[:, :], kre[:, :], den[:, :], ALU.mult)
        tt(kim[:, :], ai2[:, :], LR2[:, :], ALU.mult)
        tt(bbr[:, :], nr[:, :], LI2[:, :], ALU.mult)
        tt(kim[:, :], kim[:, :], bbr[:, :], ALU.subtract)
        tt(kim[:, :], kim[:, :], den[:, :], ALU.mult)
        tt(bbr[:, :], kre[:, :], BRT[:, :], ALU.mult)
        tt(bbi[:, :], kim[:, :], BIT[:, :], ALU.mult)
        tt(bbr[:, :], bbr[:, :], bbi[:, :], ALU.subtract)
        tt(bbi[:, :], kre[:, :], BIT[:, :], ALU.mult)
        tt(nr[:, :], kim[:, :], BRT[:, :], ALU.mult)
        tt(bbi[:, :], bbi[:, :], nr[:, :], ALU.add)
        nbbi = den
        P.op("dve", lambda e: e.tensor_scalar(out=nbbi[:, :], in0=bbi[:, :], scalar1=-1.0, scalar2=None, op0=ALU.mult), R2, T2t)
        BzPad = sbt("BzPad", [128, GPC, 2, 128], BF16)
        for g in range(GPC):
            tile, j = g // 8, g % 8
            cs = slice(tile * 64, tile * 64 + 64)
            for zz, (lo, hi) in enumerate([(bbr, bbi), (nbbi, bbr)]):
                P.op("dve", lambda e, g=g, zz=zz, lo=lo, cs=cs, j=j: e.tensor_scalar(out=BzPad[:, g, zz, 0:64], in0=lo[:, cs], scalar1=MASK[:, j:j + 1], scalar2=None, op0=ALU.mult), R2, T2t)
                P.op("dve", lambda e, g=g, zz=zz, hi=hi, cs=cs, j=j: e.tensor_scalar(out=BzPad[:, g, zz, 64:128], in0=hi[:, cs], scalar1=MASK[:, j:j + 1], scalar2=None, op0=ALU.mult), R2, T2t)
        CzPad = sbt("CzPad", [128, GPC, 128], BF16)
        Cz = sbt("Cz", [128, GPC * 16])
        P.op("dve", lambda e: e.memset(CzPad[:, :, :], 0.0), R2, T2t)
        P.op("dve", lambda e: e.tensor_scalar(out=Cz[:, :], in0=C1[:, :], scalar1=SGN[:, 0:1], scalar2=None, op0=ALU.mult), R2, T2t)
        for g in range(GPC):
            j = g % 8
            P.op("dve", lambda e, g=g, j=j: e.tensor_copy(out=CzPad[:, g, 16 * j:16 * j + 16], in_=Cz[:, 16 * g:16 * g + 16]), R2, T2t)

        V = [sbt(f"V{i}", [128, 2, GPC, TCH]) for i in range(2)]
        Sbf = [sbt(f"Sbf{i}", [128, GPC, TCH], BF16) for i in range(2)]
        hb = [sbt(f"hb{i}", [128, 2, TCH], BF16) for i in range(2)]
        T1 = sbt("T1", [128, 2, GPC]); T2 = sbt("T2", [128, 2, GPC])
        Z0 = sbt("Z0", [128, 2, GPC])
        P.op("dve", lambda e: e.memset(Z0[:, :, :], 0.0), [], [("Z0",)])
        ysb = [sbt(f"ysb{i}", [128, TCH]) for i in range(2)]
        xg = [sbt(f"xg{i}", [128, TCH]) for i in range(2)]
        wg = [sbt(f"wg{i}", [128, TCH]) for i in range(2)]
        og = [sbt(f"og{i}", [128, TCH], BF16) for i in range(4)]
        pse = [es.enter_context(nc.psum_tensor(f"pse{i}_{uid}", [128, TCH], F32)) for i in range(4)]
        psy = [es.enter_context(nc.psum_tensor(f"psy{i}_{uid}", [128, TCH], F32)) for i in range(2)]
        outs = []
        er = 0
        yr = 0
        orr = 0
        for k in range(nchunk):
            b = k % 2
            tsl = slice(k * TCH, (k + 1) * TCH)
            for tile in range(2):
                eo = (k * D + tile * 128) * TCH
                P.dma("pool", lambda e, b=b, tile=tile, eo=eo: e.indirect_dma_start(out=hb[b][:, tile, :], out_offset=None, in_=hgath[:, :],
                                                                                   in_offset=bass.IndirectOffsetOnAxis(ap=idx0[:, 0:1], axis=0), element_offset=eo),
                      [("hgath",), ("idx",)], [("hb", b)], sem=("hbl", b, tile))
            for g in range(GPC):
                for zz in range(2):
                    pb = er % 4
                    er += 1
                    P.op("pe", lambda e, g=g, zz=zz, pb=pb, b=b: e.matmul(pse[pb][:, :], lhsT=BzPad[:, g, zz, :], rhs=hb[b][:, g // 8, :], start=True, stop=True),
                         [("hb", b), ("tab2",)], [("pse", pb)])
                    P.op("act", lambda e, g=g, zz=zz, pb=pb, b=b: e.activation(out=V[b][:, zz, g, :], in_=pse[pb][:, :], func=AF.Copy),
                         [("pse", pb)], [("V", b)])
            for t in range(TCH):
                if t == 0:
                    zp = Z0[:, :, :] if k == 0 else V[1 - b][:, :, :, TCH - 1]
                    zr = [("Z0",)] if k == 0 else [("V", 1 - b)]
                else:
                    zp = V[b][:, :, :, t - 1]
                    zr = [("V", b)]
                zp0 = Z0[:, 0, :] if (k == 0 and t == 0) else (V[1 - b][:, 0, :, TCH - 1] if t == 0 else V[b][:, 0, :, t - 1])
                zp1 = Z0[:, 1, :] if (k == 0 and t == 0) else (V[1 - b][:, 1, :, TCH - 1] if t == 0 else V[b][:, 1, :, t - 1])
                P.op("dve", lambda e, zp=zp: e.tensor_tensor(out=T1[:, :, :], in0=zp, in1=ARR[:, :, :], op=ALU.mult), zr + [("tab1",), ("T1",)], [("T1",)])
                P.op("dve", lambda e, zp1=zp1: e.tensor_tensor(out=T2[:, 0, :], in0=zp1, in1=AIP[:, 0, :], op=ALU.mult), zr + [("tab1",), ("T2",)], [("T2",)])
                P.op("dve", lambda e, zp0=zp0: e.tensor_tensor(out=T2[:, 1, :], in0=zp0, in1=AIP[:, 1, :], op=ALU.mult), zr + [("tab1",), ("T2",)], [("T2",)])
                P.op("dve", lambda e: e.tensor_tensor(out=T1[:, :, :], in0=T1[:, :, :], in1=T2[:, :, :], op=ALU.add), [("T1",), ("T2",)], [("T1",)])
                P.op("dve", lambda e, b=b, t=t: e.tensor_tensor(out=V[b][:, :, :, t], in0=V[b][:, :, :, t], in1=T1[:, :, :], op=ALU.add), [("T1",), ("V", b)], [("V", b)])
            P.op("act", lambda e, b=b: e.activation(out=Sbf[b][:, :, :], in_=V[b][:, 0, :, :], func=AF.Copy), [("V", b)], [("Sbf", b)])
            for tile in range(2):
                yb = yr % 2
                yr += 1
                for j in range(8):
                    g = tile * 8 + j
                    P.op("pe", lambda e, g=g, j=j, yb=yb, b=b: e.matmul(psy[yb][:, :], lhsT=CzPad[:, g, :], rhs=Sbf[b][:, g, :], start=(j == 0), stop=(j == 7)),
                         [("Sbf", b), ("tab2",)], [("psy", yb)])
                ob = orr % 4
                orr += 1
                P.op("act", lambda e, yb=yb: e.activation(out=ysb[yb][:, :], in_=psy[yb][:, :], func=AF.Copy), [("psy", yb)], [("ysb", yb)])
                P.op("pool", lambda e, yb=yb, b=b, tile=tile: e.tensor_scalar(out=xg[yb][:, :], in0=hb[b][:, tile, :], scalar1=DCOL[:, tile:tile + 1], scalar2=None, op0=ALU.mult),
                     [("hb", b), ("par",)], [("xg", yb)])
                P.op("pool", lambda e, yb=yb: e.tensor_tensor(out=xg[yb][:, :], in0=xg[yb][:, :], in1=ysb[yb][:, :], op=ALU.add), [("xg", yb), ("ysb", yb)], [("xg", yb)])
                P.op("pool", lambda e, yb=yb: e.tensor_tensor(out=wg[yb][:, :], in0=xg[yb][:, :], in1=xg[yb][:, :], op=ALU.mult), [("xg", yb)], [("wg", yb)])
                P.op("pool", lambda e, yb=yb: e.tensor_scalar(out=wg[yb][:, :], in0=wg[yb][:, :], scalar1=0.044715, scalar2=1.0, op0=ALU.mult, op1=ALU.add), [("wg", yb)], [("wg", yb)])
                P.op("pool", lambda e, yb=yb: e.tensor_tensor(out=wg[yb][:, :], in0=wg[yb][:, :], in1=xg[yb][:, :], op=ALU.mult), [("wg", yb), ("xg", yb)], [("wg", yb)])
                P.op("act", lambda e, yb=yb: e.activation(out=wg[yb][:, :], in_=wg[yb][:, :], func=AF.Sigmoid, scale=1.5957691216), [("wg", yb)], [("wg", yb)])
                P.op("pool", lambda e, yb=yb, ob=ob: e.tensor_tensor(out=og[ob][:, :], in0=wg[yb][:, :], in1=xg[yb][:, :], op=ALU.mult), [("wg", yb), ("xg", yb)], [("og", ob)])
                P.dma("sp", lambda e, ob=ob, tile=tile, k=k: e.dma_start(out=ybd[k * 256 + tile * 128:k * 256 + (tile + 1) * 128, :], in_=og[ob][:, :]), [("og", ob)], [("yb",)], sem=("og", ob))
        P.barrier()


SSM_KEYS = ["LR1", "LI1", "DT1", "LR2", "LI2", "DT2", "BRT", "BIT", "C1", "DCOL", "C2", "BQ1", "BQ2"]
SSM_SHAPES = {"LR1": [128, GPC], "LI1": [128, GPC], "DT1": [128, GPC], "LR2": [128, 128], "LI2": [128, 128], "DT2": [128, 128],
              "BRT": [128, 128], "BIT": [128, 128], "C1": [128, GPC * 16], "DCOL": [128, 2],
              "C2": [128, GPC * 16], "BQ1": [128, GPC * 16], "BQ2": [128, GPC * 16]}


def build_fused():
    from contextlib import ExitStack
    nc = bass.Bass("TRN2", target_bir_lowering=False)
    din = lambda n, s, dt=F32: nc.dram_tensor(n, s, dt, kind="ExternalInput").ap()
    x_d = din("xT", [D, TOK])
    xh0_d = din("xh0", [D, HALO])
    flag_d = din("flag", [128, 1])
    idx_d = din("idx", [128, 3], U32)
    mixn = din("mix_norm", [4, D]); mlpn = din("mlp_norm", [4, D]); finn = din("final_norm", [D])
    cwin = din("conv_w_in", [2, D, 2 * D]); cbin = din("conv_b_in", [2, 2 * D]); cdw = din("conv_dw", [2, CW, D])
    cdwb = din("conv_dw_b", [2, D]); clg = din("conv_ln_g", [2, D]); clb = din("conv_ln_b", [2, D])
    cwo = din("conv_w_out", [2, D, D]); cbo = din("conv_b_out", [2, D])
    wglu = din("ssm_w_glu", [2, D, 2 * D]); wup = din("mlp_w_up", [4, D, DFF]); wdn = din("mlp_w_down", [4, DFF, D])
    ssm_d = {k: din("S_" + k, [2] + SSM_SHAPES[k]) for k in SSM_KEYS}
    sgn_d = din("SGN", [128, 1]); mask_d = din("MASK", [128, 8]); gmask_d = din("GMASK", [128, GPC * 8])
    out_d = nc.dram_tensor("out", [D, TOK], F32, kind="ExternalOutput").ap()
    hb = [nc.dram_tensor(f"hb{j}", [KPC * D, TCH], BF16) for j in range(2)]
    hgath = [nc.dram_tensor(f"hgath{j}", [NCORES * KPC * D, TCH], BF16) for j in range(2)]
    yb = [nc.dram_tensor(f"yb{j}", [NK * 256, TCH], BF16) for j in range(2)]
    ygath = [nc.dram_tensor(f"ygath{j}", [NCORES * NK * 256, TCH], BF16) for j in range(2)]
    xtb = nc.dram_tensor("xtb", [D, HALO], F32)
    xtg = nc.dram_tensor("xtg", [NCORES * D, HALO], F32)
    rg = [list(range(NCORES))]
    with ExitStack() as es:
        c = Ctx(nc, es)
        P = c.P
        xT = c.sb("xT_sb", [128, KT, TOK], F32)
        ones = c.sb("ones", [128, 128], F32)
        epsc = c.sb("epsc", [128, 1], F32)
        idx = c.sb("idx_sb", [128, 3], U32)
        P.barrier_scratch = c.sb("bscr", [128, 1], F32)
        for kt in range(KT):
            P.dma("sp", lambda e, kt=kt: e.dma_start(out=xT[:, kt, :], in_=x_d[kt * 128:(kt + 1) * 128, :]), [], [("x", kt)], sem="xin")
        P.dma("sp", lambda e: e.dma_start(out=idx[:, :], in_=idx_d[:, :]), [], [("idx",)], sem="idxin")
        P.op("dve", lambda e: e.memset(ones[:, :], 1.0), [], [("ones",)])
        P.op("dve", lambda e: e.memset(epsc[:, :], EPS), [], [("epsc",)])
        outs = []
        for layer in range(4):
            j = layer // 2
            g_next = mixn[layer + 1] if layer < 3 else finn
            if layer % 2 == 0:
                io = {"uid": layer, "g_mlp": mlpn[layer], "g_next": g_next, "g_mix": mixn[layer], "b_in_a": cbin[j, 0:D], "b_in_g": cbin[j, D:2 * D],
                      "dw_b": cdwb[j], "ln_g": clg[j], "ln_b": clb[j], "b_out": cbo[j], "flag": flag_d, "dw": cdw[j], "w_in": cwin[j], "w_out": cwo[j],
                      "w_up": wup[layer], "w_down": wdn[layer],
                      "halo": ("dram", xh0_d) if layer == 0 else ("gather", xtg.ap(), idx[:, 2:3]),
                      "next": ("ssm", hb[j].ap().rearrange("(kk c) t -> kk c t", kk=KPC))}
                dense_phase(c, "conv", xT, ones, epsc, io)
                P.dma("pool", lambda e, j=j: e.collective_compute("AllGather", ALU.bypass, replica_groups=rg, ins=[hb[j].ap().opt()], outs=[hgath[j].ap().opt()]),
                      [("hb",)], [("hgath",)], sem=("cch", j), inc=1)
            else:
                io = {"uid": layer, "hgath": hgath[j].ap(), "idx0": idx[:, 0:1], "yb": yb[j].ap(), "SGN": sgn_d, "MASK": mask_d, "GMASK": gmask_d}
                for k in SSM_KEYS:
                    io[k] = ssm_d[k][j]
                ssm_phase_blk(c, io)
                P.dma("pool", lambda e, j=j: e.collective_compute("AllGather", ALU.bypass, replica_groups=rg, ins=[yb[j].ap().opt()], outs=[ygath[j].ap().opt()]),
                      [("yb",)], [("ygath",)], sem=("ccy", j), inc=1)
                io = {"uid": 10 + layer, "g_mlp": mlpn[layer], "g_next": g_next, "w_glu": wglu[j], "w_up": wup[layer], "w_down": wdn[layer],
                      "ygath": ygath[j].ap(), "idx1": idx[:, 1:2],
                      "next": ("conv", xtb.ap()) if layer < 3 else ("final", out_d)}
                outs += dense_phase(c, "glu", xT, ones, epsc, io)
                if layer < 3:
                    P.dma("pool", lambda e: e.collective_compute("AllGather", ALU.bypass, replica_groups=rg, ins=[xtb.ap().opt()], outs=[xtg.ap().opt()]),
                          [("xtb",)], [("xtg",)], sem="ccx", inc=1)
        P.emit(final_wait_ops=outs)
    return nc


_NC_CACHE = {}


def kernel(x, mix_norm, conv_w_in, conv_b_in, conv_dw, conv_dw_b, conv_ln_g, conv_ln_b, conv_w_out, conv_b_out,
           ssm_lambda_re, ssm_lambda_im, ssm_log_dt, ssm_b_re, ssm_b_im, ssm_c_re, ssm_c_im, ssm_d, ssm_w_glu,
           mlp_norm, mlp_w_up, mlp_w_down, final_norm):
    f = lambda a: np.ascontiguousarray(np.asarray(a, dtype=np.float32))
    x = f(x)
    cores = list(range(NCORES))
    if "nc" not in _NC_CACHE:
        _NC_CACHE["nc"] = build_fused()
    nc = _NC_CACHE["nc"]
    shared = {"mix_norm": f(mix_norm), "mlp_norm": f(mlp_norm), "final_norm": f(final_norm), "conv_w_in": f(conv_w_in), "conv_b_in": f(conv_b_in),
              "conv_dw": f(conv_dw), "conv_dw_b": f(conv_dw_b), "conv_ln_g": f(conv_ln_g), "conv_ln_b": f(conv_ln_b), "conv_w_out": f(conv_w_out),
              "conv_b_out": f(conv_b_out), "ssm_w_glu": f(ssm_w_glu), "mlp_w_up": f(mlp_w_up), "mlp_w_down": f(mlp_w_down)}
    lre, lim, ldt = f(ssm_lambda_re), f(ssm_lambda_im), f(ssm_log_dt)
    bre, bim, cre, cim, dsk = f(ssm_b_re), f(ssm_b_im), f(ssm_c_re), f(ssm_c_im), f(ssm_d)
    maps = []
    dummy_h = np.zeros((256, 1), np.float32)
    for c in cores:
        m = dict(shared)
        m["xT"] = f(x[0, c * TOK:(c + 1) * TOK].T)
        m["xh0"] = f(x[0, c * TOK - HALO:c * TOK].T) if c > 0 else np.zeros((D, HALO), np.float32)
        m["flag"] = np.full((128, 1), 0.0 if c == 0 else 1.0, np.float32)
        p = np.arange(128, dtype=np.uint32)
        m["idx"] = np.ascontiguousarray(np.stack([256 * c + p, 1024 * c + p, max(c - 1, 0) * D + p], axis=1).astype(np.uint32))
        per = [ssm_host_inputs_blk(lre[j], lim[j], ldt[j], bre[j], bim[j], cre[j], cim[j], dsk[j], c) for j in range(2)]
        for k in SSM_KEYS:
            m["S_" + k] = np.ascontiguousarray(np.stack([per[0][k], per[1][k]], axis=0))
        m["SGN"] = per[0]["SGN"]
        m["MASK"] = per[0]["MASK"]
        m["GMASK"] = per[0]["GMASK"]
        maps.append(m)
    res = run_bass_kernel_spmd(nc, maps, core_ids=cores)
    out = np.concatenate([r["out"].T for r in res.results], axis=0)[None]
    return np.ascontiguousarray(out.astype(np.float32))


NB = TCH // 8


def ssm_phase_blk(c, io, seq=SEQ):
    from contextlib import ExitStack
    nc = c.nc
    P = c.P
    uid = io["uid"]
    hgath, idx0, ybd = io["hgath"], io["idx0"], io["yb"]
    nchunk = seq // TCH
    with ExitStack() as es:
        sbt = lambda n, s, dt=F32: es.enter_context(nc.sbuf_tensor(f"{n}_{uid}", s, dt))
        par_names = ["LR1", "LI1", "DT1", "LR2", "LI2", "DT2", "BRT", "BIT", "C1", "C2", "BQ1", "BQ2", "SGN", "MASK", "DCOL", "GMASK"]
        par = {}
        for n in par_names:
            shp = list(io[n].shape)
            par[n] = sbt("p" + n, shp)
            P.dma("sp", lambda e, n=n: e.dma_start(out=par[n][:, :], in_=io[n][:, :]), [], [("par",)], sem="par")
        LR1, LI1, DT1, LR2, LI2, DT2 = (par[n] for n in ["LR1", "LI1", "DT1", "LR2", "LI2", "DT2"])
        BRT, BIT, C1, C2, BQ1, BQ2, SGN, MASK, DCOL, GMASK = (par[n] for n in ["BRT", "BIT", "C1", "C2", "BQ1", "BQ2", "SGN", "MASK", "DCOL", "GMASK"])
        R = [("par",), ("tab",)]
        T = [("tab",)]
        tt = lambda o, a, b, op: P.op("dve", lambda e: e.tensor_tensor(out=o, in0=a, in1=b, op=op), R, T)
        ts1 = lambda o, a, s1, op: P.op("dve", lambda e: e.tensor_scalar(out=o, in0=a, scalar1=s1, scalar2=None, op0=op), R, T)

        def abar(LR, LI, DT, n, nm):
            dt = sbt(nm + "dt", [128, n]); t1 = sbt(nm + "t1", [128, n]); t2 = sbt(nm + "t2", [128, n])
            ar = sbt(nm + "ar", [128, n]); ai = sbt(nm + "ai", [128, n]); mg = sbt(nm + "mg", [128, n])
            ki = sbt(nm + "ki", [128, n], mybir.dt.int32); kf = sbt(nm + "kf", [128, n])
            P.op("act", lambda e: e.activation(out=dt[:, :], in_=DT[:, :], func=AF.Exp), R, T)
            tt(t1[:, :], LR[:, :], dt[:, :], ALU.mult)
            P.op("act", lambda e: e.activation(out=mg[:, :], in_=t1[:, :], func=AF.Exp), R, T)
            tt(t1[:, :], LI[:, :], dt[:, :], ALU.mult)

            def sin_of(dst, shift):
                ts1(t2[:, :], t1[:, :], shift, ALU.add)
                P.op("dve", lambda e: e.tensor_scalar(out=kf[:, :], in0=t2[:, :], scalar1=1.0 / (2 * PI), scalar2=0.5, op0=ALU.mult, op1=ALU.add), R, T)
                P.op("dve", lambda e: e.tensor_copy(out=ki[:, :], in_=kf[:, :]), R, T)
                P.op("dve", lambda e: e.tensor_copy(out=kf[:, :], in_=ki[:, :]), R, T)
                P.op("dve", lambda e: e.scalar_tensor_tensor(out=t2[:, :], in0=kf[:, :], scalar=-2 * PI, in1=t2[:, :], op0=ALU.mult, op1=ALU.add), R, T)
                P.op("dve", lambda e: e.tensor_scalar(out=kf[:, :], in0=t2[:, :], scalar1=-PI, scalar2=2 * PI, op0=ALU.is_lt, op1=ALU.mult), R, T)
                tt(t2[:, :], t2[:, :], kf[:, :], ALU.add)
                P.op("dve", lambda e: e.tensor_scalar(out=t2[:, :], in0=t2[:, :], scalar1=-PI, scalar2=PI, op0=ALU.max, op1=ALU.min), R, T)
                P.op("act", lambda e: e.activation(out=t2[:, :], in_=t2[:, :], func=AF.Sin), R, T)
                tt(dst[:, :], mg[:, :], t2[:, :], ALU.mult)
            sin_of(ai, 0.0)
            sin_of(ar, 0.5 * PI)
            return ar, ai

        def kfac(LR, LI, ar, ai, n, nm):
            nr = sbt(nm + "nr", [128, n]); den = sbt(nm + "den", [128, n]); kre = sbt(nm + "kre", [128, n]); kim = sbt(nm + "kim", [128, n]); t = sbt(nm + "kt", [128, n])
            ts1(nr[:, :], ar[:, :], -1.0, ALU.add)
            tt(den[:, :], LR[:, :], LR[:, :], ALU.mult)
            tt(t[:, :], LI[:, :], LI[:, :], ALU.mult)
            tt(den[:, :], den[:, :], t[:, :], ALU.add)
            P.op("dve", lambda e: e.reciprocal(out=den[:, :], in_=den[:, :]), R, T)
            tt(kre[:, :], nr[:, :], LR[:, :], ALU.mult)
            tt(t[:, :], ai[:, :], LI[:, :], ALU.mult)
            tt(kre[:, :], kre[:, :], t[:, :], ALU.add)
            tt(kre[:, :], kre[:, :], den[:, :], ALU.mult)
            tt(kim[:, :], ai[:, :], LR[:, :], ALU.mult)
            tt(t[:, :], nr[:, :], LI[:, :], ALU.mult)
            tt(kim[:, :], kim[:, :], t[:, :], ALU.subtract)
            tt(kim[:, :], kim[:, :], den[:, :], ALU.mult)
            return kre, kim

        def cmul(o_r, o_i, a_r, a_i, b_r, b_i, t):
            tt(o_r, a_r, b_r, ALU.mult)
            tt(t, a_i, b_i, ALU.mult)
            tt(o_r, o_r, t, ALU.subtract)
            tt(o_i, a_r, b_i, ALU.mult)
            tt(t, a_i, b_r, ALU.mult)
            tt(o_i, o_i, t, ALU.add)

        ar1, ai1 = abar(LR1, LI1, DT1, GPC, "a1")
        kre1, kim1 = kfac(LR1, LI1, ar1, ai1, GPC, "k1")
        PW1 = sbt("PW1", [128, 9, 2, GPC])
        tmp1 = sbt("tmp1", [128, GPC])
        P.op("dve", lambda e: e.memset(PW1[:, 0, 0, :], 1.0), R, T)
        P.op("dve", lambda e: e.memset(PW1[:, 0, 1, :], 0.0), R, T)
        P.op("dve", lambda e: e.tensor_copy(out=PW1[:, 1, 0, :], in_=ar1[:, :]), R, T)
        P.op("dve", lambda e: e.tensor_copy(out=PW1[:, 1, 1, :], in_=ai1[:, :]), R, T)
        for m in range(2, 9):
            cmul(PW1[:, m, 0, :], PW1[:, m, 1, :], PW1[:, m - 1, 0, :], PW1[:, m - 1, 1, :], ar1[:, :], ai1[:, :], tmp1[:, :])
        ARR = sbt("ARR", [128, 2, GPC]); AIP = sbt("AIP", [128, 2, GPC])
        P.op("dve", lambda e: e.tensor_copy(out=ARR[:, 0, :], in_=PW1[:, 8, 0, :]), R, T)
        P.op("dve", lambda e: e.tensor_copy(out=ARR[:, 1, :], in_=PW1[:, 8, 0, :]), R, T)
        P.op("dve", lambda e: e.tensor_copy(out=AIP[:, 0, :], in_=PW1[:, 8, 1, :]), R, T)
        ts1(AIP[:, 1, :], PW1[:, 8, 1, :], -1.0, ALU.mult)

        C1s = sbt("C1s", [128, GPC, 16]); C2v = C2[:, :].rearrange("p (g c) -> p g c", g=GPC)
        ts1(C1s[:, :, :], C1[:, :].rearrange("p (g c) -> p g c", g=GPC), SGN[:, 0:1], ALU.mult)
        TCc = sbt("TCc", [128, GPC, 16]); TCt = sbt("TCt", [128, GPC, 16])
        CzPadL = sbt("CzPadL", [128, 9, GPC, 128], BF16)
        GM4 = GMASK[:, :].rearrange("p (g j) -> p g j", g=GPC).unsqueeze(3).to_broadcast([128, GPC, 8, 16])
        for l in range(9):
            m = l + 1 if l < 8 else 0
            prb = PW1[:, m, 0, :].unsqueeze(2).to_broadcast([128, GPC, 16])
            pib = PW1[:, m, 1, :].unsqueeze(2).to_broadcast([128, GPC, 16])
            tt(TCc[:, :, :], C1s[:, :, :], prb, ALU.mult)
            tt(TCt[:, :, :], C2v, pib, ALU.mult)
            tt(TCc[:, :, :], TCc[:, :, :], TCt[:, :, :], ALU.subtract)
            tt(CzPadL[:, l, :, :].rearrange("p g (j c) -> p g j c", j=8), TCc[:, :, :].unsqueeze(2).to_broadcast([128, GPC, 8, 16]), GM4, ALU.mult)

        Kbd = sbt("Kbd", [128, 2, 8, 128], BF16)
        with ExitStack() as es2:
            sb2 = lambda n, s, dt=F32: es2.enter_context(nc.sbuf_tensor(f"{n}_{uid}", s, dt))
            XPad = sb2("XPad", [128, 8, GPC, 128], BF16)
            Fr = sb2("Fr", [128, GPC]); Fi = sb2("Fi", [128, GPC])
            Xc = sb2("Xc", [128, GPC, 16]); Xt = sb2("Xt", [128, GPC, 16])
            BQ1v = BQ1[:, :].rearrange("p (g c) -> p g c", g=GPC)
            BQ2v = BQ2[:, :].rearrange("p (g c) -> p g c", g=GPC)
            pk = [es2.enter_context(nc.psum_tensor(f"pk{i}_{uid}", [128, 128], F32)) for i in range(2)]
            for lag in range(8):
                cmul(Fr[:, :], Fi[:, :], PW1[:, lag, 0, :], PW1[:, lag, 1, :], kre1[:, :], kim1[:, :], tmp1[:, :])
                ts1(Fi[:, :], Fi[:, :], SGN[:, 0:1], ALU.mult)
                tt(Xc[:, :, :], BQ1v, Fr[:, :].unsqueeze(2).to_broadcast([128, GPC, 16]), ALU.mult)
                tt(Xt[:, :, :], BQ2v, Fi[:, :].unsqueeze(2).to_broadcast([128, GPC, 16]), ALU.mult)
                tt(Xc[:, :, :], Xc[:, :, :], Xt[:, :, :], ALU.subtract)
                tt(XPad[:, lag, :, :].rearrange("p g (j c) -> p g j c", j=8), Xc[:, :, :].unsqueeze(2).to_broadcast([128, GPC, 8, 16]), GM4, ALU.mult)
            i = 0
            for tile in range(2):
                for lag in range(8):
                    pb = i % 2
                    i += 1
                    for j in range(8):
                        g = tile * 8 + j
                        P.op("pe", lambda e, g=g, j=j, lag=lag, pb=pb: e.matmul(pk[pb][:, :], lhsT=XPad[:, lag, g, :], rhs=CzPadL[:, 8, g, :], start=(j == 0), stop=(j == 7)),
                             [("tab",)], [("pk", pb)])
                    P.op("act", lambda e, tile=tile, lag=lag, pb=pb: e.activation(out=Kbd[:, tile, lag, :], in_=pk[pb][:, :], func=AF.Copy), [("pk", pb)], [("tabk",)])
            P.op("dve", lambda e: e.memset(tmp1[:, :], 0.0), [("tabk",), ("pk", 0), ("pk", 1)] + R, T)

        BzPadK = sbt("BzPadK", [128, GPC, 8, 2, 128], BF16)
        with ExitStack() as es3:
            sb3 = lambda n, s, dt=F32: es3.enter_context(nc.sbuf_tensor(f"{n}_{uid}", s, dt))
            sbt_save = sbt
            sbt = sb3
            ar2, ai2 = abar(LR2, LI2, DT2, 128, "a2")
            kre2, kim2 = kfac(LR2, LI2, ar2, ai2, 128, "k2")
            sbt = sbt_save
            bbr = sb3("bbr", [128, 128]); bbi = sb3("bbi", [128, 128]); t2_ = sb3("t2_", [128, 128])
            cmul(bbr[:, :], bbi[:, :], kre2[:, :], kim2[:, :], BRT[:, :], BIT[:, :], t2_[:, :])
            PW2 = sb3("PW2", [128, 8, 2, 128])
            P.op("dve", lambda e: e.memset(PW2[:, 0, 0, :], 1.0), R, T)
            P.op("dve", lambda e: e.memset(PW2[:, 0, 1, :], 0.0), R, T)
            for m in range(1, 8):
                cmul(PW2[:, m, 0, :], PW2[:, m, 1, :], PW2[:, m - 1, 0, :], PW2[:, m - 1, 1, :], ar2[:, :], ai2[:, :], t2_[:, :])
            Tz = sb3("Tz", [128, 2, 2, 128])
            xr = sb3("xr", [128, 128]); xi = sb3("xi", [128, 128])
            MK3 = MASK[:, :].unsqueeze(2).to_broadcast([128, 8, 128])
            for k in range(8):
                m = 7 - k
                cmul(xr[:, :], xi[:, :], PW2[:, m, 0, :], PW2[:, m, 1, :], bbr[:, :], bbi[:, :], t2_[:, :])
                for tile in range(2):
                    cs = slice(tile * 64, tile * 64 + 64)
                    P.op("dve", lambda e, tile=tile, cs=cs: e.tensor_copy(out=Tz[:, 0, tile, 0:64], in_=xr[:, cs]), R, T)
                    P.op("dve", lambda e, tile=tile, cs=cs: e.tensor_copy(out=Tz[:, 0, tile, 64:128], in_=xi[:, cs]), R, T)
                    ts1(Tz[:, 1, tile, 0:64], xi[:, cs], -1.0, ALU.mult)
                    P.op("dve", lambda e, tile=tile, cs=cs: e.tensor_copy(out=Tz[:, 1, tile, 64:128], in_=xr[:, cs]), R, T)
                    for zz in range(2):
                        tt(BzPadK[:, tile * 8:(tile + 1) * 8, k, zz, :], Tz[:, zz, tile, :].unsqueeze(1).to_broadcast([128, 8, 128]), MK3, ALU.mult)
            P.op("dve", lambda e: e.memset(tmp1[:, :], 0.0), R, T)

        V = [sbt(f"V{i}", [128, 2, GPC, NB]) for i in range(2)]
        Sbf = [sbt(f"Sbf{i}", [128, GPC, NB], BF16) for i in range(2)]
        hb = [sbt(f"hb{i}", [128, 2, TCH], BF16) for i in range(2)]
        T1 = sbt("T1", [128, 2, GPC]); T2 = sbt("T2", [128, 2, GPC])
        Z0 = sbt("Z0", [128, 2, GPC])
        P.op("dve", lambda e: e.memset(Z0[:, :, :], 0.0), [], [("Z0",)])
        ysb = [sbt(f"ysb{i}", [128, TCH]) for i in range(2)]
        xg = [sbt(f"xg{i}", [128, TCH]) for i in range(2)]
        wg = [sbt(f"wg{i}", [128, TCH]) for i in range(2)]
        og = [sbt(f"og{i}", [128, TCH], BF16) for i in range(4)]
        pse = [es.enter_context(nc.psum_tensor(f"pse{i}_{uid}", [128, TCH], F32)) for i in range(4)]
        psy = [es.enter_context(nc.psum_tensor(f"psy{i}_{uid}", [128, TCH], F32)) for i in range(2)]
        er = 0
        yr = 0
        orr = 0
        for k in range(nchunk):
            b = k % 2
            for tile in range(2):
                eo = (k * D + tile * 128) * TCH
                P.dma("pool", lambda e, b=b, tile=tile, eo=eo: e.indirect_dma_start(out=hb[b][:, tile, :], out_offset=None, in_=hgath[:, :],
                                                                                   in_offset=bass.IndirectOffsetOnAxis(ap=idx0[:, 0:1], axis=0), element_offset=eo),
                      [("hgath",), ("idx",), ("tab",)], [("hb", b)], sem=("hbl", b, tile))
            hbv = [hb[b][:, tile, :].rearrange("p (n k) -> p k n", k=8) for tile in range(2)]
            for zz in range(2):
                for tile in range(2):
                    pb = er % 4
                    er += 1
                    for j in range(8):
                        g = tile * 8 + j
                        for kk in range(8):
                            P.op("pe", lambda e, g=g, j=j, kk=kk, zz=zz, pb=pb, tile=tile, hbv=hbv: e.matmul(pse[pb][:, j * NB:(j + 1) * NB], lhsT=BzPadK[:, g, kk, zz, :], rhs=hbv[tile][:, kk, :],
                                                                                                     start=(kk == 0), stop=(kk == 7)),
                                 [("hb", b), ("tab",)], [("pse", pb)])
                    P.op("act", lambda e, zz=zz, tile=tile, pb=pb, b=b: e.activation(out=V[b][:, zz, tile * 8:(tile + 1) * 8, :], in_=pse[pb][:, :].rearrange("p (j n) -> p j n", j=8), func=AF.Copy),
                         [("pse", pb)], [("V", b)])
            if k == 0:
                P.op("act", lambda e, b=b: e.activation(out=Sbf[b][:, :, 0], in_=Z0[:, 0, :], func=AF.Copy), [("Z0",)], [("Sbf", b)])
            else:
                P.op("act", lambda e, b=b: e.activation(out=Sbf[b][:, :, 0], in_=V[1 - b][:, 0, :, NB - 1], func=AF.Copy), [("V", 1 - b)], [("Sbf", b)])
            for t in range(NB):
                if t == 0:
                    src, zr = (Z0, [("Z0",)]) if k == 0 else (None, [("V", 1 - b)])
                    zp = Z0[:, :, :] if k == 0 else V[1 - b][:, :, :, NB - 1]
                    zp0 = Z0[:, 0, :] if k == 0 else V[1 - b][:, 0, :, NB - 1]
                    zp1 = Z0[:, 1, :] if k == 0 else V[1 - b][:, 1, :, NB - 1]
                else:
                    zr = [("V", b)]
                    zp, zp0, zp1 = V[b][:, :, :, t - 1], V[b][:, 0, :, t - 1], V[b][:, 1, :, t - 1]
                P.op("dve", lambda e, zp=zp: e.tensor_tensor(out=T1[:, :, :], in0=zp, in1=ARR[:, :, :], op=ALU.mult), zr + [("tab",), ("T1",)], [("T1",)])
                P.op("dve", lambda e, zp1=zp1: e.tensor_tensor(out=T2[:, 0, :], in0=zp1, in1=AIP[:, 0, :], op=ALU.mult), zr + [("tab",), ("T2",)], [("T2",)])
                P.op("dve", lambda e, zp0=zp0: e.tensor_tensor(out=T2[:, 1, :], in0=zp0, in1=AIP[:, 1, :], op=ALU.mult), zr + [("tab",), ("T2",)], [("T2",)])
                P.op("dve", lambda e: e.tensor_tensor(out=T1[:, :, :], in0=T1[:, :, :], in1=T2[:, :, :], op=ALU.add), [("T1",), ("T2",)], [("T1",)])
                P.op("dve", lambda e, b=b, t=t: e.tensor_tensor(out=V[b][:, :, :, t], in0=V[b][:, :, :, t], in1=T1[:, :, :], op=ALU.add), [("T1",), ("V", b)], [("V", b)])
            P.op("act", lambda e, b=b: e.activation(out=Sbf[b][:, :, 1:NB], in_=V[b][:, 0, :, 0:NB - 1], func=AF.Copy), [("V", b)], [("Sbf", b)])
            for tile in range(2):
                yb_ = yr % 2
                yr += 1
                pyv = psy[yb_][:, :].rearrange("p (n l) -> p l n", l=8)
                for l in range(8):
                    nmm = 8 + l + 1
                    i = 0
                    for j in range(8):
                        g = tile * 8 + j
                        P.op("pe", lambda e, g=g, l=l, i=i, nmm=nmm, pyv=pyv, b=b: e.matmul(pyv[:, l, :], lhsT=CzPadL[:, l, g, :], rhs=Sbf[b][:, g, :], start=(i == 0), stop=(i == nmm - 1)),
                             [("Sbf", b), ("tab",)], [("psy", yb_)])
                        i += 1
                    for kk in range(l + 1):
                        P.op("pe", lambda e, kk=kk, l=l, i=i, nmm=nmm, pyv=pyv, tile=tile, hbv=hbv: e.matmul(pyv[:, l, :], lhsT=Kbd[:, tile, l - kk, :], rhs=hbv[tile][:, kk, :], start=(i == 0), stop=(i == nmm - 1)),
                             [("hb", b), ("tabk",)], [("psy", yb_)])
                        i += 1
                ob = orr % 4
                orr += 1
                yb = yb_
                P.op("act", lambda e, yb=yb: e.activation(out=ysb[yb][:, :], in_=psy[yb][:, :], func=AF.Copy), [("psy", yb)], [("ysb", yb)])
                P.op("pool", lambda e, yb=yb, b=b, tile=tile: e.tensor_scalar(out=xg[yb][:, :], in0=hb[b][:, tile, :], scalar1=DCOL[:, tile:tile + 1], scalar2=None, op0=ALU.mult),
                     [("hb", b), ("par",)], [("xg", yb)])
                P.op("pool", lambda e, yb=yb: e.tensor_tensor(out=xg[yb][:, :], in0=xg[yb][:, :], in1=ysb[yb][:, :], op=ALU.add), [("xg", yb), ("ysb", yb)], [("xg", yb)])
                P.op("pool", lambda e, yb=yb: e.tensor_tensor(out=wg[yb][:, :], in0=xg[yb][:, :], in1=xg[yb][:, :], op=ALU.mult), [("xg", yb)], [("wg", yb)])
                P.op("pool", lambda e, yb=yb: e.tensor_scalar(out=wg[yb][:, :], in0=wg[yb][:, :], scalar1=0.044715, scalar2=1.0, op0=ALU.mult, op1=ALU.add), [("wg", yb)], [("wg", yb)])
                P.op("pool", lambda e, yb=yb: e.tensor_tensor(out=wg[yb][:, :], in0=wg[yb][:, :], in1=xg[yb][:, :], op=ALU.mult), [("wg", yb), ("xg", yb)], [("wg", yb)])
                P.op("act", lambda e, yb=yb: e.activation(out=wg[yb][:, :], in_=wg[yb][:, :], func=AF.Sigmoid, scale=1.5957691216), [("wg", yb)], [("wg", yb)])
                P.op("pool", lambda e, yb=yb, ob=ob: e.tensor_tensor(out=og[ob][:, :], in0=wg[yb][:, :], in1=xg[yb][:, :], op=ALU.mult), [("wg", yb), ("xg", yb)], [("og", ob)])
                P.dma("sp", lambda e, ob=ob, tile=tile, k=k: e.dma_start(out=ybd[k * 256 + tile * 128:k * 256 + (tile + 1) * 128, :], in_=og[ob][:, :]), [("og", ob)], [("yb",)], sem=("og", ob))
        P.barrier()


def ssm_host_inputs_blk(lam_re, lam_im, log_dt, b_re, b_im, c_re, c_im, d, core):
    m = ssm_host_inputs(np.zeros((256, 1), np.float32), lam_re, lam_im, log_dt, b_re, b_im, c_re, c_im, d, core)
    gs = slice(core * GPC, (core + 1) * GPC)
    cr = c_re[gs].transpose(2, 0, 1)
    ci = c_im[gs].transpose(2, 0, 1)
    m["C2"] = np.ascontiguousarray(np.concatenate([ci, cr], 0).reshape(128, GPC * 16))
    br = b_re[gs].transpose(1, 0, 2)
    bi = b_im[gs].transpose(1, 0, 2)
    m["BQ1"] = np.ascontiguousarray(np.concatenate([br, bi], 0).reshape(128, GPC * 16))
    m["BQ2"] = np.ascontiguousarray(np.concatenate([bi, br], 0).reshape(128, GPC * 16))
    gm = np.zeros((128, GPC, 8), np.float32)
    for g in range(GPC):
        gm[:, g, g % 8] = 1.0
    m["GMASK"] = gm.reshape(128, GPC * 8)
    del m["hT"]
    return m
```

```python
import numpy as np
import concourse.bass as bass
import concourse.mybir as mybir
from concourse.bass_utils import run_bass_kernel_spmd

F32 = mybir.dt.float32
BF16 = mybir.dt.bfloat16
AF = mybir.ActivationFunctionType
ALU = mybir.AluOpType

NCORES = 8
D = 2048
SEQ = 8192
TOK = SEQ // NCORES
KT = D // 128
DFF = 4 * D
CW = 31
HALO = 32
EPS = 1e-6
SAME_ENGINE_SYNC = True
NOSYNC_ENGINES = ("pe",)


class Prog:
    ENG = ("pe", "act", "dve", "pool", "sp")

    def __init__(self, nc):
        self.nc = nc
        self.ops = []
        self.last_w = {}
        self.readers = {}
        self.dma_sems = {}
        self.epoch = 0

    def _deps(self, reads, writes):
        deps = set()
        for t in reads:
            w = self.last_w.get(t)
            if w is not None:
                deps.add(w)
        for t in writes:
            w = self.last_w.get(t)
            if w is not None:
                deps.add(w)
            for r in self.readers.get(t, ()):
                deps.add(r)
        return deps

    def _commit(self, idx, reads, writes):
        o = self.ops[idx]
        for t in reads:
            lst = self.readers.setdefault(t, [])
            if o["dma"] is None:
                lst[:] = [r for r in lst if not (self.ops[r]["dma"] is None and self.ops[r]["eng"] == o["eng"])]
            lst.append(idx)
        for t in writes:
            self.last_w[t] = idx
            self.readers[t] = []

    def barrier(self):
        nc = self.nc
        scr = self.barrier_scratch
        idx = len(self.ops)
        deps = self._deps([], [("phase",)])
        self.epoch += 1
        self.ops.append(dict(eng="dve", fn=lambda e: e.memset(scr[:, :], 0.0), deps=deps, dma=None, ms=None, inc=16, ep=self.epoch))
        self.last_w[("phase",)] = idx
        self.readers[("phase",)] = []
        return idx

    def op(self, eng, fn, reads=(), writes=()):
        idx = len(self.ops)
        reads = list(reads) + [("phase",)]
        deps = self._deps(reads, writes)
        deps.discard(idx)
        self.ops.append(dict(eng=eng, fn=fn, deps=deps, dma=None, ms=None, inc=16, ep=self.epoch))
        self._commit(idx, reads, writes)
        return idx

    def dma(self, eng, fn, reads, writes, sem, inc=16):
        idx = len(self.ops)
        reads = list(reads) + [("phase",)]
        deps = self._deps(reads, writes)
        cnt = self.dma_sems.setdefault(sem, [0])
        cnt[0] += inc
        self.ops.append(dict(eng=eng, fn=fn, deps=deps, dma=(sem, cnt[0]), ms=None, inc=inc, ep=self.epoch))
        self._commit(idx, reads, writes)
        return idx

    def emit(self, final_wait_ops=()):
        nc = self.nc
        ops = self.ops
        needed = set()
        for o in ops:
            for d in o["deps"]:
                po = ops[d]
                if po["dma"] is None and o["dma"] is None and po["eng"] == o["eng"] and ((not SAME_ENGINE_SYNC) or o["eng"] in NOSYNC_ENGINES):
                    continue
                needed.add(d)
        counters = {}
        for i, o in enumerate(ops):
            if o["dma"] is None and i in needed:
                k_ = (o["eng"], o["ep"])
                counters[k_] = counters.get(k_, 0) + 1
                o["ms"] = counters[k_]
        from contextlib import ExitStack
        with ExitStack() as es:
            esem = {k_: es.enter_context(nc.semaphore("e_%s_%d" % k_)) for k_ in counters}
            dsem = {k: es.enter_context(nc.semaphore("d_" + str(k))) for k in self.dma_sems}
            block = es.enter_context(nc.Block())

            dma_hist = {}
            for oi, o_ in enumerate(ops):
                if o_["dma"] is not None:
                    dma_hist.setdefault(o_["dma"][0], []).append((oi, o_["dma"][1]))

            def run_engine(eng_name, eng):
                waited = {}
                for i, o in enumerate(ops):
                    if o["eng"] != eng_name:
                        continue
                    w = {}
                    for d in o["deps"]:
                        po = ops[d]
                        if po["dma"] is not None:
                            key = ("d", po["dma"][0])
                            val = po["dma"][1]
                            for (oi, cv_) in dma_hist[po["dma"][0]]:
                                if oi < i and cv_ > val:
                                    val = cv_
                        else:
                            if po["eng"] == eng_name and o["dma"] is None and ((not SAME_ENGINE_SYNC) or eng_name in NOSYNC_ENGINES):
                                continue
                            key = ("e", (po["eng"], po["ep"]))
                            val = po["ms"]
                        if val > w.get(key, 0):
                            w[key] = val
                    for key, val in w.items():
                        if waited.get(key, 0) >= val:
                            continue
                        waited[key] = val
                        s = dsem[key[1]] if key[0] == "d" else esem[key[1]]
                        eng.wait_ge(s, val)
                    ins = o["fn"](eng)
                    if o["dma"] is not None:
                        if o["inc"] == 1:
                            ins.then_inc(dsem[o["dma"][0]])
                        else:
                            ins.then_inc(dsem[o["dma"][0]], 16)
                    elif o["ms"] is not None:
                        ins.then_inc(esem[(eng_name, o["ep"])], 1)
                if eng_name == "sp":
                    for i in final_wait_ops:
                        po = ops[i]
                        eng.wait_ge(dsem[po["dma"][0]], po["dma"][1])

            @block.tensor
            def _(e):
                run_engine("pe", e)

            @block.scalar
            def _(e):
                run_engine("act", e)

            @block.vector
            def _(e):
                run_engine("dve", e)

            @block.gpsimd
            def _(e):
                run_engine("pool", e)

            @block.sync
            def _(e):
                run_engine("sp", e)


class Ctx:
    def __init__(self, nc, es):
        self.nc = nc
        self.es = es
        self.P = Prog(nc)
        self.ps_rr = 0

    def sb(self, name, shape, dt):
        return self.es.enter_context(self.nc.sbuf_tensor(name, shape, dt))

    def ps(self, name, shape, dt=F32):
        return self.es.enter_context(self.nc.psum_tensor(name, shape, dt))


def emit_rmsnorm(c, xT, hT, gcol, epsc, ones, sqb, rstd, psb, n_tok=TOK, x_tag="x", h_tag="h"):
    P = c.P
    nh = n_tok // 512
    i = 0
    for kt in range(KT):
        for h in range(nh):
            s = i % 2
            i += 1
            P.op("act", lambda e, kt=kt, s=s, h=h: e.activation(out=sqb[:, s, :], in_=xT[:, kt, h * 512:(h + 1) * 512], func=AF.Square),
                 reads=[(x_tag, kt)], writes=[("tmp", s)])
            P.op("pe", lambda e, kt=kt, s=s, h=h: e.matmul(psb[h][:, :], lhsT=ones[:, :], rhs=sqb[:, s, :],
                                                          start=(kt == 0), stop=(kt == KT - 1)),
                 reads=[("tmp", s), ("ones",)], writes=[("psn", h)])
    for h in range(nh):
        P.op("act", lambda e, h=h: e.activation(out=rstd[:, h * 512:(h + 1) * 512], in_=psb[h][:, :], func=AF.Sqrt, bias=epsc[:, 0:1], scale=1.0 / D),
             reads=[("psn", h), ("epsc",)], writes=[("rstd", h)])
        P.op("dve", lambda e, h=h: e.reciprocal(out=rstd[:, h * 512:(h + 1) * 512], in_=rstd[:, h * 512:(h + 1) * 512]),
             reads=[("rstd", h)], writes=[("rstd", h)])
    for kt in range(KT):
        eng = "dve"
        P.op(eng, lambda e, kt=kt: e.scalar_tensor_tensor(out=hT[:, kt, :n_tok], in0=xT[:, kt, :n_tok], scalar=gcol[:, kt:kt + 1],
                                                          in1=rstd[:, :n_tok], op0=ALU.mult, op1=ALU.mult),
             reads=[(x_tag, kt), ("gcol",)] + [("rstd", h) for h in range(nh)], writes=[(h_tag, kt)])


class WStream:
    def __init__(self, c, nslots, slot_elems, name):
        self.c = c
        self.n = nslots
        self.buf = c.sb(name, [128, nslots, slot_elems], BF16)
        self.i = 0
        self.name = name

    def load(self, src_ap, shape):
        s = self.i % self.n
        a, b = shape
        view = self.buf[:, s, :a * b].rearrange("p (a b) -> p a b", a=a)
        tag = (self.name, s)
        stage = getattr(self, "stage", None)
        if stage is None:
            self.i += 1
            self.c.P.dma("pool", lambda e: e.dma_start(out=view, in_=src_ap), reads=[], writes=[tag], sem=(self.name, s))
            return view, tag
        sview = stage[:, s, :a * b].rearrange("p (a b) -> p a b", a=a)
        stag = (self.name + "_st", s)
        self.c.P.dma("sp", lambda e: e.dma_start(out=sview, in_=src_ap), reads=[], writes=[stag], sem=(self.name + "_st", s))
        ceng = ("act", "dve")[self.i % 2]
        self.i += 1
        if ceng == "act":
            self.c.P.op("act", lambda e: e.activation(out=view, in_=sview, func=AF.Copy), reads=[stag], writes=[tag])
        else:
            self.c.P.op("dve", lambda e: e.tensor_copy(out=view, in_=sview), reads=[stag], writes=[tag])
        return view, tag


def emit_mlp(c, xT, hT, w_up, w_down, ws, hid, tmp, psbanks):
    P = c.P
    wu = w_up.rearrange("(kt p) m -> p kt m", p=128)
    wd = w_down.rearrange("(kt p) m -> p kt m", p=128)
    nb = len(psbanks)
    NH = TOK // 512
    HT = DFF // 128 // 2
    hidv = hid.rearrange("p k t -> p (k t)").rearrange("p (k t) -> p k t", k=HT)
    for hh in range(2):
        for ml in range(HT):
            mt = hh * HT + ml
            wv, wtag = ws.load(wu[:, :, mt * 128:(mt + 1) * 128], (KT, 128))
            for half in range(NH):
                tsl = slice(half * 512, (half + 1) * 512)
                b = c.ps_rr % nb
                c.ps_rr += 1
                for kt in range(KT):
                    P.op("pe", lambda e, kt=kt, b=b, wv=wv, tsl=tsl: e.matmul(psbanks[b][:, :], lhsT=wv[:, kt, :], rhs=hT[:, kt, tsl], start=(kt == 0), stop=(kt == KT - 1)),
                         reads=[wtag, ("h", kt)], writes=[("ps", b)])
                ts_ = c.ps_rr % 2
                P.op("act", lambda e, b=b, ts_=ts_: e.activation(out=tmp[:, ts_, :], in_=psbanks[b][:, :], func=AF.Relu),
                     reads=[("ps", b)], writes=[("tmp", ts_)])
                eng = "dve" if (ml + half) % 2 == 0 else "pool"
                P.op(eng, lambda e, ml=ml, ts_=ts_, tsl=tsl: e.tensor_tensor(out=hidv[:, ml, tsl], in0=tmp[:, ts_, :], in1=tmp[:, ts_, :], op=ALU.mult),
                     reads=[("tmp", ts_)], writes=[("hid", ml)])
        for dt_ in range(KT):
            bs = []
            for half in range(NH):
                bs.append(c.ps_rr % nb)
                c.ps_rr += 1
            for q in range(HT // 16):
                k0 = hh * HT + q * 16
                wv, wtag = ws.load(wd[:, k0:k0 + 16, dt_ * 128:(dt_ + 1) * 128], (16, 128))
                for k2 in range(16):
                    kl = q * 16 + k2
                    for half in range(NH):
                        tsl = slice(half * 512, (half + 1) * 512)
                        P.op("pe", lambda e, k2=k2, kl=kl, b=bs[half], wv=wv, tsl=tsl: e.matmul(psbanks[b][:, :], lhsT=wv[:, k2, :], rhs=hidv[:, kl, tsl],
                                                                                          start=(kl == 0), stop=(kl == HT - 1)),
                             reads=[wtag, ("hid", kl)], writes=[("ps", bs[half])])
            for half in range(NH):
                tsl = slice(half * 512, (half + 1) * 512)
                P.op("dve", lambda e, dt_=dt_, b=bs[half], tsl=tsl: e.tensor_tensor(out=xT[:, dt_, tsl], in0=xT[:, dt_, tsl], in1=psbanks[b][:, :], op=ALU.add),
                     reads=[("ps", bs[half]), ("x", dt_)], writes=[("x", dt_)])


GPC = 16
TCH = 256
PI = float(np.pi)


def build_ssm(seq=SEQ):
    from contextlib import ExitStack
    nc = bass.Bass("TRN2", target_bir_lowering=False)
    din = lambda n, s: nc.dram_tensor(n, s, F32, kind="ExternalInput").ap()
    hT_d = din("hT", [256, seq])
    LR1_d, LI1_d, DT1_d = din("LR1", [128, GPC]), din("LI1", [128, GPC]), din("DT1", [128, GPC])
    LR2_d, LI2_d, DT2_d = din("LR2", [128, 128]), din("LI2", [128, 128]), din("DT2", [128, 128])
    BRT_d, BIT_d = din("BRT", [128, 128]), din("BIT", [128, 128])
    C1_d = din("C1", [128, GPC * 16])
    SGN_d, MASK_d, DCOL_d = din("SGN", [128, 1]), din("MASK", [128, 8]), din("DCOL", [128, 2])
    yT_d = nc.dram_tensor("yT", [256, seq], F32, kind="ExternalOutput").ap()
    nchunk = seq // TCH
    with ExitStack() as es:
        c = Ctx(nc, es)
        P = c.P
        sbt = lambda n, s, dt=F32: c.sb(n, s, dt)
        LR1, LI1, DT1 = sbt("LR1s", [128, GPC]), sbt("LI1s", [128, GPC]), sbt("DT1s", [128, GPC])
        LR2, LI2, DT2 = sbt("LR2s", [128, 128]), sbt("LI2s", [128, 128]), sbt("DT2s", [128, 128])
        BRT, BIT = sbt("BRTs", [128, 128]), sbt("BITs", [128, 128])
        C1 = sbt("C1s", [128, GPC * 16])
        SGN, MASK, DCOL = sbt("SGNs", [128, 1]), sbt("MASKs", [128, 8]), sbt("DCOLs", [128, 2])
        npi = sbt("npi", [128, 1])
        for i, (s_, d_) in enumerate([(LR1, LR1_d), (LI1, LI1_d), (DT1, DT1_d), (LR2, LR2_d), (LI2, LI2_d), (DT2, DT2_d),
                                      (BRT, BRT_d), (BIT, BIT_d), (C1, C1_d), (SGN, SGN_d), (MASK, MASK_d), (DCOL, DCOL_d)]):
            P.dma("sp", lambda e, s_=s_, d_=d_: e.dma_start(out=s_[:, :], in_=d_[:, :]), [], [("par",)], sem="par")
        P.op("dve", lambda e: e.memset(npi[:, :], -PI), [], [("par",)])

        def abar(LR, LI, DT, n, nm):
            dt = sbt(nm + "dt", [128, n]); t1 = sbt(nm + "t1", [128, n]); t2 = sbt(nm + "t2", [128, n])
            ar = sbt(nm + "ar", [128, n]); ai = sbt(nm + "ai", [128, n]); mg = sbt(nm + "mg", [128, n])
            T = [(nm,)]
            R = [("par",), (nm,)]
            P.op("act", lambda e: e.activation(out=dt[:, :], in_=DT[:, :], func=AF.Exp), R, T)
            P.op("dve", lambda e: e.tensor_tensor(out=t1[:, :], in0=LR[:, :], in1=dt[:, :], op=ALU.mult), R, T)
            P.op("act", lambda e: e.activation(out=mg[:, :], in_=t1[:, :], func=AF.Exp), R, T)
            P.op("dve", lambda e: e.tensor_tensor(out=t1[:, :], in0=LI[:, :], in1=dt[:, :], op=ALU.mult), R, T)
            ki = sbt(nm + "ki", [128, n], mybir.dt.int32); kf = sbt(nm + "kf", [128, n])

            def sin_of(dst, shift):
                P.op("dve", lambda e: e.tensor_scalar(out=t2[:, :], in0=t1[:, :], scalar1=shift, scalar2=None, op0=ALU.add), R, T)
                P.op("dve", lambda e: e.tensor_scalar(out=kf[:, :], in0=t2[:, :], scalar1=1.0 / (2 * PI), scalar2=0.5, op0=ALU.mult, op1=ALU.add), R, T)
                P.op("dve", lambda e: e.tensor_copy(out=ki[:, :], in_=kf[:, :]), R, T)
                P.op("dve", lambda e: e.tensor_copy(out=kf[:, :], in_=ki[:, :]), R, T)
                P.op("dve", lambda e: e.scalar_tensor_tensor(out=t2[:, :], in0=kf[:, :], scalar=-2 * PI, in1=t2[:, :], op0=ALU.mult, op1=ALU.add), R, T)
                P.op("dve", lambda e: e.tensor_scalar(out=kf[:, :], in0=t2[:, :], scalar1=-PI, scalar2=2 * PI, op0=ALU.is_lt, op1=ALU.mult), R, T)
                P.op("dve", lambda e: e.tensor_tensor(out=t2[:, :], in0=t2[:, :], in1=kf[:, :], op=ALU.add), R, T)
                P.op("dve", lambda e: e.tensor_scalar(out=t2[:, :], in0=t2[:, :], scalar1=-PI, scalar2=PI, op0=ALU.max, op1=ALU.min), R, T)
                P.op("act", lambda e: e.activation(out=t2[:, :], in_=t2[:, :], func=AF.Sin), R, T)
                P.op("dve", lambda e: e.tensor_tensor(out=dst[:, :], in0=mg[:, :], in1=t2[:, :], op=ALU.mult), R, T)
            sin_of(ai, 0.0)
            sin_of(ar, 0.5 * PI)
            return ar, ai

        ar1, ai1 = abar(LR1, LI1, DT1, GPC, "a1")
        ARR = sbt("ARR", [128, 2, GPC]); AIP = sbt("AIP", [128, 2, GPC])
        T1t = [("tab1",)]
        R1 = [("a1",), ("tab1",)]
        P.op("dve", lambda e: e.tensor_copy(out=ARR[:, 0, :], in_=ar1[:, :]), R1, T1t)
        P.op("dve", lambda e: e.tensor_copy(out=ARR[:, 1, :], in_=ar1[:, :]), R1, T1t)
        P.op("dve", lambda e: e.tensor_copy(out=AIP[:, 0, :], in_=ai1[:, :]), R1, T1t)
        P.op("dve", lambda e: e.tensor_scalar(out=AIP[:, 1, :], in0=ai1[:, :], scalar1=-1.0, scalar2=None, op0=ALU.mult), R1, T1t)

        ar2, ai2 = abar(LR2, LI2, DT2, 128, "a2")
        w = [sbt(f"w{i}", [128, 128]) for i in range(6)]
        R2 = [("a2",), ("par",), ("tab2",)]
        T2t = [("tab2",)]
        tt = lambda o, a, b, op: P.op("dve", lambda e: e.tensor_tensor(out=o, in0=a, in1=b, op=op), R2, T2t)
        nr, den, kre, kim, bbr, bbi = w
        P.op("dve", lambda e: e.tensor_scalar(out=nr[:, :], in0=ar2[:, :], scalar1=-1.0, scalar2=None, op0=ALU.add), R2, T2t)
        tt(den[:, :], LR2[:, :], LR2[:, :], ALU.mult)
        tt(kre[:, :], LI2[:, :], LI2[:, :], ALU.mult)
        tt(den[:, :], den[:, :], kre[:, :], ALU.add)
        P.op("dve", lambda e: e.reciprocal(out=den[:, :], in_=den[:, :]), R2, T2t)
        tt(kre[:, :], nr[:, :], LR2[:, :], ALU.mult)
        tt(kim[:, :], ai2[:, :], LI2[:, :], ALU.mult)
        tt(kre[:, :], kre[:, :], kim[:, :], ALU.add)
        tt(kre[:, :], kre[:, :], den[:, :], ALU.mult)
        tt(kim[:, :], ai2[:, :], LR2[:, :], ALU.mult)
        tt(bbr[:, :], nr[:, :], LI2[:, :], ALU.mult)
        tt(kim[:, :], kim[:, :], bbr[:, :], ALU.subtract)
        tt(kim[:, :], kim[:, :], den[:, :], ALU.mult)
        tt(bbr[:, :], kre[:, :], BRT[:, :], ALU.mult)
        tt(bbi[:, :], kim[:, :], BIT[:, :], ALU.mult)
        tt(bbr[:, :], bbr[:, :], bbi[:, :], ALU.subtract)
        tt(bbi[:, :], kre[:, :], BIT[:, :], ALU.mult)
        tt(nr[:, :], kim[:, :], BRT[:, :], ALU.mult)
        tt(bbi[:, :], bbi[:, :], nr[:, :], ALU.add)
        nbbi = den
        P.op("dve", lambda e: e.tensor_scalar(out=nbbi[:, :], in0=bbi[:, :], scalar1=-1.0, scalar2=None, op0=ALU.mult), R2, T2t)
        BzPad = sbt("BzPad", [128, GPC, 2, 128], BF16)
        for g in range(GPC):
            tile, j = g // 8, g % 8
            cs = slice(tile * 64, tile * 64 + 64)
            for zz, (lo, hi) in enumerate([(bbr, bbi), (nbbi, bbr)]):
                P.op("dve", lambda e, g=g, zz=zz, lo=lo, cs=cs, j=j: e.tensor_scalar(out=BzPad[:, g, zz, 0:64], in0=lo[:, cs], scalar1=MASK[:, j:j + 1], scalar2=None, op0=ALU.mult), R2, T2t)
                P.op("dve", lambda e, g=g, zz=zz, hi=hi, cs=cs, j=j: e.tensor_scalar(out=BzPad[:, g, zz, 64:128], in0=hi[:, cs], scalar1=MASK[:, j:j + 1], scalar2=None, op0=ALU.mult), R2, T2t)
        CzPad = sbt("CzPad", [128, GPC, 128], BF16)
        Cz = sbt("Cz", [128, GPC * 16])
        P.op("dve", lambda e: e.memset(CzPad[:, :, :], 0.0), R2, T2t)
        P.op("dve", lambda e: e.tensor_scalar(out=Cz[:, :], in0=C1[:, :], scalar1=SGN[:, 0:1], scalar2=None, op0=ALU.mult), R2, T2t)
        for g in range(GPC):
            j = g % 8
            P.op("dve", lambda e, g=g, j=j: e.tensor_copy(out=CzPad[:, g, 16 * j:16 * j + 16], in_=Cz[:, 16 * g:16 * g + 16]), R2, T2t)

        V = [sbt(f"V{i}", [128, 2, GPC, TCH]) for i in range(2)]
        Sbf = [sbt(f"Sbf{i}", [128, GPC, TCH], BF16) for i in range(2)]
        hc = [sbt(f"hc{i}", [128, 2, TCH]) for i in range(2)]
        hb = [sbt(f"hb{i}", [128, 2, TCH], BF16) for i in range(2)]
        T1 = sbt("T1", [128, 2, GPC]); T2 = sbt("T2", [128, 2, GPC])
        Z0 = sbt("Z0", [128, 2, GPC])
        P.op("dve", lambda e: e.memset(Z0[:, :, :], 0.0), [], [("Z0",)])
        ysb = [sbt(f"ysb{i}", [128, TCH]) for i in range(2)]
        xg = [sbt(f"xg{i}", [128, TCH]) for i in range(2)]
        wg = [sbt(f"wg{i}", [128, TCH]) for i in range(2)]
        og = [sbt(f"og{i}", [128, TCH]) for i in range(4)]
        pse = [c.ps(f"pse{i}", [128, TCH]) for i in range(4)]
        psy = [c.ps(f"psy{i}", [128, TCH]) for i in range(2)]
        hTv = hT_d.rearrange("(tile p) t -> p tile t", p=128)
        yTv = yT_d.rearrange("(tile p) t -> p tile t", p=128)
        outs = []
        er = 0
        yr = 0
        orr = 0
        for k in range(nchunk):
            b = k % 2
            tsl = slice(k * TCH, (k + 1) * TCH)
            P.dma("sp", lambda e, b=b, tsl=tsl: e.dma_start(out=hc[b][:, :, :], in_=hTv[:, :, tsl]), [], [("hc", b)], sem=("hc", b))
            P.op("act", lambda e, b=b: e.activation(out=hb[b][:, :, :], in_=hc[b][:, :, :], func=AF.Copy), [("hc", b)], [("hb", b)])
            for g in range(GPC):
                for zz in range(2):
                    pb = er % 4
                    er += 1
                    P.op("pe", lambda e, g=g, zz=zz, pb=pb, b=b: e.matmul(pse[pb][:, :], lhsT=BzPad[:, g, zz, :], rhs=hb[b][:, g // 8, :], start=True, stop=True),
                         [("hb", b), ("tab2",)], [("pse", pb)])
                    P.op("act", lambda e, g=g, zz=zz, pb=pb, b=b: e.activation(out=V[b][:, zz, g, :], in_=pse[pb][:, :], func=AF.Copy),
                         [("pse", pb)], [("V", b)])
            for t in range(TCH):
                if t == 0:
                    zp = Z0[:, :, :] if k == 0 else V[1 - b][:, :, :, TCH - 1]
                    zr = [("Z0",)] if k == 0 else [("V", 1 - b)]
                else:
                    zp = V[b][:, :, :, t - 1]
                    zr = [("V", b)]
                zp0 = Z0[:, 0, :] if (k == 0 and t == 0) else (V[1 - b][:, 0, :, TCH - 1] if t == 0 else V[b][:, 0, :, t - 1])
                zp1 = Z0[:, 1, :] if (k == 0 and t == 0) else (V[1 - b][:, 1, :, TCH - 1] if t == 0 else V[b][:, 1, :, t - 1])
                P.op("dve", lambda e, zp=zp: e.tensor_tensor(out=T1[:, :, :], in0=zp, in1=ARR[:, :, :], op=ALU.mult), zr + [("tab1",), ("T1",)], [("T1",)])
                P.op("dve", lambda e, zp1=zp1: e.tensor_tensor(out=T2[:, 0, :], in0=zp1, in1=AIP[:, 0, :], op=ALU.mult), zr + [("tab1",), ("T2",)], [("T2",)])
                P.op("dve", lambda e, zp0=zp0: e.tensor_tensor(out=T2[:, 1, :], in0=zp0, in1=AIP[:, 1, :], op=ALU.mult), zr + [("tab1",), ("T2",)], [("T2",)])
                P.op("dve", lambda e: e.tensor_tensor(out=T1[:, :, :], in0=T1[:, :, :], in1=T2[:, :, :], op=ALU.add), [("T1",), ("T2",)], [("T1",)])
                P.op("dve", lambda e, b=b, t=t: e.tensor_tensor(out=V[b][:, :, :, t], in0=V[b][:, :, :, t], in1=T1[:, :, :], op=ALU.add), [("T1",), ("V", b)], [("V", b)])
            P.op("act", lambda e, b=b: e.activation(out=Sbf[b][:, :, :], in_=V[b][:, 0, :, :], func=AF.Copy), [("V", b)], [("Sbf", b)])
            for tile in range(2):
                yb = yr % 2
                yr += 1
                for j in range(8):
                    g = tile * 8 + j
                    P.op("pe", lambda e, g=g, j=j, yb=yb, b=b: e.matmul(psy[yb][:, :], lhsT=CzPad[:, g, :], rhs=Sbf[b][:, g, :], start=(j == 0), stop=(j == 7)),
                         [("Sbf", b), ("tab2",)], [("psy", yb)])
                ob = orr % 4
                orr += 1
                P.op("act", lambda e, yb=yb: e.activation(out=ysb[yb][:, :], in_=psy[yb][:, :], func=AF.Copy), [("psy", yb)], [("ysb", yb)])
                P.op("pool", lambda e, yb=yb, b=b, tile=tile: e.tensor_scalar(out=xg[yb][:, :], in0=hc[b][:, tile, :], scalar1=DCOL[:, tile:tile + 1], scalar2=None, op0=ALU.mult),
                     [("hc", b), ("par",)], [("xg", yb)])
                P.op("pool", lambda e, yb=yb: e.tensor_tensor(out=xg[yb][:, :], in0=xg[yb][:, :], in1=ysb[yb][:, :], op=ALU.add), [("xg", yb), ("ysb", yb)], [("xg", yb)])
                P.op("pool", lambda e, yb=yb: e.tensor_tensor(out=wg[yb][:, :], in0=xg[yb][:, :], in1=xg[yb][:, :], op=ALU.mult), [("xg", yb)], [("wg", yb)])
                P.op("pool", lambda e, yb=yb: e.tensor_scalar(out=wg[yb][:, :], in0=wg[yb][:, :], scalar1=0.044715, scalar2=1.0, op0=ALU.mult, op1=ALU.add), [("wg", yb)], [("wg", yb)])
                P.op("pool", lambda e, yb=yb: e.tensor_tensor(out=wg[yb][:, :], in0=wg[yb][:, :], in1=xg[yb][:, :], op=ALU.mult), [("wg", yb), ("xg", yb)], [("wg", yb)])
                P.op("act", lambda e, yb=yb: e.activation(out=wg[yb][:, :], in_=wg[yb][:, :], func=AF.Sigmoid, scale=1.5957691216), [("wg", yb)], [("wg", yb)])
                P.op("pool", lambda e, yb=yb, ob=ob: e.tensor_tensor(out=og[ob][:, :], in0=wg[yb][:, :], in1=xg[yb][:, :], op=ALU.mult), [("wg", yb), ("xg", yb)], [("og", ob)])
                outs.append(P.dma("sp", lambda e, ob=ob, tile=tile, tsl=tsl: e.dma_start(out=yTv[:, tile, tsl], in_=og[ob][:, :]), [("og", ob)], [], sem=("og", ob)))
        P.emit(final_wait_ops=outs[-4:])
    return nc


def ssm_host_inputs(hT_core, lam_re, lam_im, log_dt, b_re, b_im, c_re, c_im, d, core):
    G0 = core * GPC
    gs = slice(G0, G0 + GPC)
    lr, li, ld = lam_re[gs], lam_im[gs], log_dt[gs]
    LR1 = np.ascontiguousarray(np.concatenate([lr.T, lr.T], 0))
    LI1 = np.ascontiguousarray(np.concatenate([li.T, li.T], 0))
    DT1 = np.ascontiguousarray(np.broadcast_to(ld[None, :], (128, GPC)))

    def l2(a):
        a = a.reshape(2, 8, 1, 64)
        a = np.broadcast_to(a, (2, 8, 16, 64))
        return np.ascontiguousarray(a.transpose(1, 2, 0, 3).reshape(128, 128))
    LR2, LI2 = l2(lr), l2(li)
    DT2 = l2(np.broadcast_to(ld[:, None], (GPC, 64)))

    def bt(bb):
        a = bb.reshape(2, 8, 64, 16).transpose(1, 3, 0, 2)
        return np.ascontiguousarray(a.reshape(128, 128))
    BRT, BIT = bt(b_re[gs]), bt(b_im[gs])
    cr = c_re[gs].transpose(2, 0, 1)
    ci = c_im[gs].transpose(2, 0, 1)
    C1 = np.ascontiguousarray(np.concatenate([cr, ci], 0).reshape(128, GPC * 16))
    SGN = np.concatenate([np.ones((64, 1), np.float32), -np.ones((64, 1), np.float32)], 0)
    MASK = np.zeros((128, 8), np.float32)
    for j in range(8):
        MASK[16 * j:16 * j + 16, j] = 1.0
    DCOL = np.ascontiguousarray(d[core * 256:(core + 1) * 256].reshape(2, 128).T)
    return {"hT": np.ascontiguousarray(hT_core), "LR1": LR1, "LI1": LI1, "DT1": DT1, "LR2": LR2, "LI2": LI2, "DT2": DT2,
            "BRT": BRT, "BIT": BIT, "C1": C1, "SGN": SGN, "MASK": MASK, "DCOL": DCOL}


def build_dense(mode):
    from contextlib import ExitStack
    nc = bass.Bass("TRN2", target_bir_lowering=False)
    din = lambda n, s: nc.dram_tensor(n, s, F32, kind="ExternalInput").ap()
    x_d = din("xT", [D, TOK])
    vec_names = ["g_mlp", "g_next"]
    if mode == "conv":
        xh_d = din("xhT", [D, HALO])
        flag_d = din("flag", [128, 1])
        w1_d = din("w_in", [D, 2 * D])
        w2_d = din("w_out", [D, D])
        dw_d = din("dw", [CW, D])
        vec_names += ["g_mix", "b_in_a", "b_in_g", "dw_b", "ln_g", "ln_b", "b_out"]
    else:
        y_d = din("yT", [D, TOK])
        w1_d = din("w_glu", [D, 2 * D])
    vec_d = {n: din(n, [D]) for n in vec_names}
    wu_d = din("w_up", [D, DFF])
    wd_d = din("w_down", [DFF, D])
    xo_d = nc.dram_tensor("xo", [D, TOK], F32, kind="ExternalOutput").ap()
    ho_d = nc.dram_tensor("ho", [D, TOK], F32, kind="ExternalOutput").ap()
    NT = TOK + HALO if mode == "conv" else TOK
    with ExitStack() as es:
        c = Ctx(nc, es)
        P = c.P
        xT = c.sb("xT_sb", [128, KT, TOK], F32)
        hT = c.sb("hT_sb", [128, KT, NT], BF16)
        big = c.sb("big", [128, KT * TOK], F32)
        bigf = big[:, :].rearrange("p (k t) -> p k t", k=KT)
        hid = big[:, :].bitcast(BF16)[:, :DFF // 128 * 512].rearrange("p (k t) -> p k t", k=DFF // 128) if hasattr(big[:, :], "bitcast") else None
        tmp = c.sb("tmp", [128, 2, 512], F32)
        rstd = c.sb("rstd", [128, NT], F32)
        ones = c.sb("ones", [128, 128], F32)
        epsc = c.sb("epsc", [128, 1], F32)
        vec = {n: c.sb("v_" + n, [128, KT], F32) for n in vec_names}
        ws = WStream(c, 2, KT * 256, "ws")
        banks = [c.ps(f"b{i}", [128, 512]) for i in range(6)]
        psn = [c.ps(f"n{i}", [128, 512]) for i in range(2)]
        for kt in range(KT):
            P.dma("sp", lambda e, kt=kt: e.dma_start(out=xT[:, kt, :], in_=x_d[kt * 128:(kt + 1) * 128, :]), [], [("x", kt)], sem="xin")
        for n in vec_names:
            P.dma("sp", lambda e, n=n: e.dma_start(out=vec[n][:, :], in_=vec_d[n].rearrange("(kt p) -> p kt", p=128), allow_slow_non_contiguous=True),
                  [], [("gcol",)], sem="vin")
        P.op("dve", lambda e: e.memset(ones[:, :], 1.0), [], [("ones",)])
        P.op("dve", lambda e: e.memset(epsc[:, :], EPS), [], [("epsc",)])

        def pair_glu(w_d, ba, bg, n_cols, consume):
            wv_ = w_d.rearrange("(kt p) m -> p kt m", p=128)
            chunks = [(s0, min(s0 + 512, n_cols)) for s0 in range(0, n_cols, 512)]
            for mt in range(KT):
                wa, ta = ws.load(wv_[:, :, mt * 128:(mt + 1) * 128], (KT, 128))
                wg_, tg = ws.load(wv_[:, :, D + mt * 128:D + (mt + 1) * 128], (KT, 128))
                for ci, (s0, s1) in enumerate(chunks):
                    ba_, bb_ = ci, 3 + ci
                    for (wv, wt, b) in ((wa, ta, ba_), (wg_, tg, bb_)):
                        for kt in range(KT):
                            P.op("pe", lambda e, kt=kt, wv=wv, b=b, s0=s0, s1=s1: e.matmul(banks[b][:, :s1 - s0], lhsT=wv[:, kt, :], rhs=hT[:, kt, s0:s1],
                                                                                      start=(kt == 0), stop=(kt == KT - 1)),
                                 reads=[wt, ("h", kt)], writes=[("ps", b)])
                    consume(mt, ci, s0, s1, banks[ba_], banks[bb_], ("ps", ba_), ("ps", bb_))

        if mode == "conv":
            xh = c.sb("xh", [128, KT, HALO], F32)
            flag = c.sb("flag_sb", [128, 1], F32)
            dwc = c.sb("dwc", [128, KT, CW], F32)
            vext = c.sb("vext", [128, 2, TOK + HALO], F32)
            mean = c.sb("mean", [128, TOK], F32)
            P.dma("sp", lambda e: e.dma_start(out=xh[:, :, :], in_=xh_d.rearrange("(kt p) t -> p kt t", p=128)), [], [("xh",)], sem="xh")
            P.dma("sp", lambda e: e.dma_start(out=flag[:, :], in_=flag_d[:, :]), [], [("gcol",)], sem="vin")
            for kt in range(KT):
                P.dma("sp", lambda e, kt=kt: e.dma_start(out=dwc[:, kt, :], in_=dw_d[:, kt * 128:(kt + 1) * 128].rearrange("j p -> p j"), allow_slow_non_contiguous=True),
                      [], [("gcol",)], sem="vin")
            emit_rmsnorm(c, xT, hT, vec["g_mix"], epsc, ones, tmp, rstd, psn)
            for kt in range(KT):
                s = kt % 2
                P.op("act", lambda e, kt=kt, s=s: e.activation(out=tmp[:, s, :HALO], in_=xh[:, kt, :], func=AF.Square), [("xh",)], [("tmp", s)])
                P.op("pe", lambda e, kt=kt, s=s: e.matmul(psn[0][:, :HALO], lhsT=ones[:, :], rhs=tmp[:, s, :HALO], start=(kt == 0), stop=(kt == KT - 1)),
                     [("tmp", s), ("ones",)], [("psn", 0)])
            P.op("act", lambda e: e.activation(out=rstd[:, TOK:NT], in_=psn[0][:, :HALO], func=AF.Sqrt, bias=epsc[:, 0:1], scale=1.0 / D), [("psn", 0), ("epsc",)], [("rstdh",)])
            P.op("dve", lambda e: e.reciprocal(out=rstd[:, TOK:NT], in_=rstd[:, TOK:NT]), [("rstdh",)], [("rstdh",)])
            for kt in range(KT):
                P.op("dve", lambda e, kt=kt: e.scalar_tensor_tensor(out=hT[:, kt, TOK:NT], in0=xh[:, kt, :], scalar=vec["g_mix"][:, kt:kt + 1], in1=rstd[:, TOK:NT],
                                                                   op0=ALU.mult, op1=ALU.mult), [("xh",), ("rstdh",), ("gcol",)], [("h", kt)])

            def consume(mt, ci, s0, s1, pa, pg, ta, tg):
                vs = mt % 2
                n = s1 - s0
                d0 = 0 if s0 == TOK else HALO + s0
                P.op("act", lambda e, n=n, mt=mt: e.activation(out=tmp[:, 0, :n], in_=pg[:, :n], func=AF.Sigmoid, bias=vec["b_in_g"][:, mt:mt + 1], scale=1.0),
                     [tg, ("gcol",)], [("tmp", 0)])
                P.op("dve", lambda e, n=n, mt=mt, vs=vs, d0=d0: e.scalar_tensor_tensor(out=vext[:, vs, d0:d0 + n], in0=pa[:, :n], scalar=vec["b_in_a"][:, mt:mt + 1], in1=tmp[:, 0, :n],
                                                                                    op0=ALU.add, op1=ALU.mult), [ta, ("tmp", 0), ("gcol",)], [("vext", vs)])
                if s0 == TOK:
                    P.op("dve", lambda e, vs=vs: e.tensor_scalar(out=vext[:, vs, 0:HALO], in0=vext[:, vs, 0:HALO], scalar1=flag[:, 0:1], scalar2=None, op0=ALU.mult),
                         [("vext", vs), ("gcol",)], [("vext", vs)])
                    P.op("dve", lambda e, mt=mt, vs=vs: e.tensor_scalar(out=bigf[:, mt, :], in0=vext[:, vs, 2:2 + TOK], scalar1=dwc[:, mt, 0:1], scalar2=vec["dw_b"][:, mt:mt + 1],
                                                                      op0=ALU.mult, op1=ALU.add), [("vext", vs), ("gcol",), ("bigbar",)], [("cv", mt)])
                    for j in range(1, CW):
                        P.op("dve", lambda e, mt=mt, vs=vs, j=j: e.scalar_tensor_tensor(out=bigf[:, mt, :], in0=vext[:, vs, 2 + j:2 + j + TOK], scalar=dwc[:, mt, j:j + 1], in1=bigf[:, mt, :],
                                                                                      op0=ALU.mult, op1=ALU.add), [("vext", vs), ("cv", mt), ("gcol",)], [("cv", mt)])
            pair_glu(w1_d, vec["b_in_a"], vec["b_in_g"], NT, consume)
            for h in range(2):
                tsl = slice(h * 512, (h + 1) * 512)
                for kt in range(KT):
                    s = kt % 2
                    P.op("pe", lambda e, kt=kt, h=h, tsl=tsl: e.matmul(banks[h][:, :], lhsT=ones[:, :], rhs=bigf[:, kt, tsl], start=(kt == 0), stop=(kt == KT - 1)),
                         [("cv", kt), ("ones",)], [("ps", h)])
                    P.op("act", lambda e, kt=kt, s=s, tsl=tsl: e.activation(out=tmp[:, s, :], in_=bigf[:, kt, tsl], func=AF.Square), [("cv", kt)], [("tmp", s)])
                    P.op("pe", lambda e, kt=kt, h=h, s=s: e.matmul(banks[2 + h][:, :], lhsT=ones[:, :], rhs=tmp[:, s, :], start=(kt == 0), stop=(kt == KT - 1)),
                         [("tmp", s), ("ones",)], [("ps", 2 + h)])
                P.op("act", lambda e, h=h, tsl=tsl: e.activation(out=mean[:, tsl], in_=banks[h][:, :], func=AF.Copy, scale=1.0 / D), [("ps", h)], [("mean",)])
                P.op("act", lambda e, h=h, tsl=tsl: e.activation(out=rstd[:, tsl], in_=banks[2 + h][:, :], func=AF.Copy, scale=1.0 / D), [("ps", 2 + h)], [("rstd", h)])
                P.op("dve", lambda e, tsl=tsl: e.tensor_tensor(out=tmp[:, 0, :], in0=mean[:, tsl], in1=mean[:, tsl], op=ALU.mult), [("mean",)], [("tmp", 0)])
                P.op("dve", lambda e, tsl=tsl: e.tensor_tensor(out=rstd[:, tsl], in0=rstd[:, tsl], in1=tmp[:, 0, :], op=ALU.subtract), [("rstd", h), ("tmp", 0)], [("rstd", h)])
                P.op("act", lambda e, tsl=tsl: e.activation(out=rstd[:, tsl], in_=rstd[:, tsl], func=AF.Sqrt, bias=epsc[:, 0:1], scale=1.0), [("rstd", h), ("epsc",)], [("rstd", h)])
                P.op("dve", lambda e, tsl=tsl: e.reciprocal(out=rstd[:, tsl], in_=rstd[:, tsl]), [("rstd", h)], [("rstd", h)])
            for kt in range(KT):
                eng = "dve" if kt % 2 == 0 else "pool"
                P.op(eng, lambda e, kt=kt: e.tensor_tensor(out=bigf[:, kt, :], in0=bigf[:, kt, :], in1=mean[:, :], op=ALU.subtract), [("cv", kt), ("mean",)], [("cv", kt)])
                P.op(eng, lambda e, kt=kt: e.tensor_tensor(out=bigf[:, kt, :], in0=bigf[:, kt, :], in1=rstd[:, :TOK], op=ALU.mult), [("cv", kt), ("rstd", 0), ("rstd", 1)], [("cv", kt)])
                P.op("act", lambda e, kt=kt: e.activation(out=hT[:, kt, :TOK], in_=bigf[:, kt, :], func=AF.Silu, bias=vec["ln_b"][:, kt:kt + 1], scale=vec["ln_g"][:, kt:kt + 1]),
                     [("cv", kt), ("gcol",)], [("h", kt)])
            wo = w2_d.rearrange("(kt p) m -> p kt m", p=128)
            for dt_ in range(KT):
                wv, wt = ws.load(wo[:, :, dt_ * 128:(dt_ + 1) * 128], (KT, 128))
                for h in range(2):
                    tsl = slice(h * 512, (h + 1) * 512)
                    b = c.ps_rr % 6
                    c.ps_rr += 1
                    for kt in range(KT):
                        P.op("pe", lambda e, kt=kt, wv=wv, b=b, tsl=tsl: e.matmul(banks[b][:, :], lhsT=wv[:, kt, :], rhs=hT[:, kt, tsl], start=(kt == 0), stop=(kt == KT - 1)),
                             [wt, ("h", kt)], [("ps", b)])
                    P.op("dve", lambda e, dt_=dt_, b=b, tsl=tsl: e.scalar_tensor_tensor(out=xT[:, dt_, tsl], in0=banks[b][:, :], scalar=vec["b_out"][:, dt_:dt_ + 1], in1=xT[:, dt_, tsl],
                                                                                      op0=ALU.add, op1=ALU.add), [("ps", b), ("x", dt_), ("gcol",)], [("x", dt_)])
            big_readers = [("cv", kt) for kt in range(KT)]
        else:
            for kt in range(KT):
                P.dma("sp", lambda e, kt=kt: e.dma_start(out=bigf[:, kt, :], in_=y_d[kt * 128:(kt + 1) * 128, :]), [], [("cv", kt)], sem="yin")
                P.op("act", lambda e, kt=kt: e.activation(out=hT[:, kt, :], in_=bigf[:, kt, :], func=AF.Copy), [("cv", kt)], [("h", kt)])

            def consume(mt, ci, s0, s1, pa, pg, ta, tg):
                n = s1 - s0
                P.op("act", lambda e, n=n: e.activation(out=tmp[:, 0, :n], in_=pg[:, :n], func=AF.Sigmoid), [tg], [("tmp", 0)])
                P.op("dve", lambda e, n=n: e.tensor_tensor(out=tmp[:, 0, :n], in0=tmp[:, 0, :n], in1=pa[:, :n], op=ALU.mult), [ta, ("tmp", 0)], [("tmp", 0)])
                P.op("dve", lambda e, n=n, mt=mt, s0=s0, s1=s1: e.tensor_tensor(out=xT[:, mt, s0:s1], in0=xT[:, mt, s0:s1], in1=tmp[:, 0, :n], op=ALU.add),
                     [("tmp", 0), ("x", mt)], [("x", mt)])
            pair_glu(w1_d, None, None, TOK, consume)
            big_readers = [("cv", kt) for kt in range(KT)]

        P.op("dve", lambda e: e.memset(epsc[:, :], EPS), big_readers + [("epsc",)], [("epsc",)] + [("hid", m) for m in range(DFF // 128)])
        emit_rmsnorm(c, xT, hT, vec["g_mlp"], epsc, ones, tmp, rstd, psn)
        emit_mlp(c, xT, hT, wu_d, wd_d, ws, hid, tmp, banks)
        outs = []
        for kt in range(KT):
            outs.append(P.dma("sp", lambda e, kt=kt: e.dma_start(out=xo_d[kt * 128:(kt + 1) * 128, :], in_=xT[:, kt, :]), [("x", kt)], [], sem="xout"))
        P.op("dve", lambda e: e.memset(epsc[:, :], EPS), [("hid", m) for m in range(DFF // 128)] + [("epsc",)], [("epsc",)] + [("hn", kt) for kt in range(KT)])
        emit_rmsnorm(c, xT, bigf, vec["g_next"], epsc, ones, tmp, rstd, psn, h_tag="hn")
        for kt in range(KT):
            outs.append(P.dma("sp", lambda e, kt=kt: e.dma_start(out=ho_d[kt * 128:(kt + 1) * 128, :], in_=bigf[:, kt, :]), [("hn", kt)], [], sem="hout"))
        P.emit(final_wait_ops=outs)
    return nc


U32 = mybir.dt.uint32
NK = SEQ // TCH
KPC = TOK // TCH


def dense_phase(c, mode, xT, ones, epsc, io):
    from contextlib import ExitStack
    nc = c.nc
    P = c.P
    vec_names = ["g_mlp", "g_next"] + (["g_mix", "b_in_a", "b_in_g", "dw_b", "ln_g", "ln_b", "b_out"] if mode == "conv" else [])
    NT = TOK + HALO if mode == "conv" else TOK
    with ExitStack() as es:
        sb = lambda n, shp, dt=F32: es.enter_context(nc.sbuf_tensor(f"{n}_{io['uid']}", shp, dt))
        hT = sb("hT_sb", [128, KT, NT], BF16)
        big = sb("big", [128, KT * TOK], F32)
        bigf = big[:, :].rearrange("p (k t) -> p k t", k=KT)
        hid = big[:, :].bitcast(BF16)[:, :DFF // 128 * 512].rearrange("p (k t) -> p k t", k=DFF // 128)
        tmp = sb("tmp", [128, 2, 512], F32)
        rstd = sb("rstd", [128, NT], F32)
        vec = {n: sb("v_" + n, [128, KT], F32) for n in vec_names}
        ws = WStream.__new__(WStream)
        ws.c, ws.n, ws.i, ws.name = c, 2, 0, "ws"
        ws.buf = sb("ws", [128, 2, KT * 128], BF16)
        ws.stage = sb("wstage", [128, 2, KT * 128], F32)
        banks = [es.enter_context(nc.psum_tensor(f"b{i}_{io['uid']}", [128, 512], F32)) for i in range(6)]
        psn = [es.enter_context(nc.psum_tensor(f"n{i}_{io['uid']}", [128, 512], F32)) for i in range(2)]
        for n in vec_names:
            P.dma("sp", lambda e, n=n: e.dma_start(out=vec[n][:, :], in_=io[n].rearrange("(kt p) -> p kt", p=128), allow_slow_non_contiguous=True),
                  [], [("gcol",)], sem="vin")

        def pair_glu(w_d, n_cols, consume):
            wv_ = w_d.rearrange("(kt p) m -> p kt m", p=128)
            chunks = [(s0, min(s0 + 512, n_cols)) for s0 in range(0, n_cols, 512)]
            for mt in range(KT):
                wa, ta = ws.load(wv_[:, :, mt * 128:(mt + 1) * 128], (KT, 128))
                wg_, tg = ws.load(wv_[:, :, D + mt * 128:D + (mt + 1) * 128], (KT, 128))
                for ci, (s0, s1) in enumerate(chunks):
                    ba_, bb_ = ci, 3 + ci
                    for (wv, wt, b) in ((wa, ta, ba_), (wg_, tg, bb_)):
                        for kt in range(KT):
                            P.op("pe", lambda e, kt=kt, wv=wv, b=b, s0=s0, s1=s1: e.matmul(banks[b][:, :s1 - s0], lhsT=wv[:, kt, :], rhs=hT[:, kt, s0:s1],
                                                                                      start=(kt == 0), stop=(kt == KT - 1)),
                                 reads=[wt, ("h", kt)], writes=[("ps", b)])
                    consume(mt, ci, s0, s1, banks[ba_], banks[bb_], ("ps", ba_), ("ps", bb_))

        if mode == "conv":
            xh = sb("xh", [128, KT, HALO], F32)
            flag = sb("flag_sb", [128, 1], F32)
            dwc = sb("dwc", [128, KT, CW], F32)
            vext = sb("vext", [128, 1, TOK + HALO], F32)
            mean = sb("mean", [128, TOK], F32)
            if io["halo"][0] == "dram":
                P.dma("sp", lambda e: e.dma_start(out=xh[:, :, :], in_=io["halo"][1].rearrange("(kt p) t -> p kt t", p=128)), [], [("xh",)], sem="xh")
            else:
                xtg, idx2 = io["halo"][1], io["halo"][2]
                for kt in range(KT):
                    P.dma("pool", lambda e, kt=kt: e.indirect_dma_start(out=xh[:, kt, :], out_offset=None, in_=xtg[:, :],
                                                                        in_offset=bass.IndirectOffsetOnAxis(ap=idx2[:, 0:1], axis=0), element_offset=kt * 128 * HALO),
                          [("xtg",), ("idx",)], [("xh",)], sem="xh")
            P.dma("sp", lambda e: e.dma_start(out=flag[:, :], in_=io["flag"][:, :]), [], [("gcol",)], sem="vin")
            for kt in range(KT):
                P.dma("sp", lambda e, kt=kt: e.dma_start(out=dwc[:, kt, :], in_=io["dw"][:, kt * 128:(kt + 1) * 128].rearrange("j p -> p j"), allow_slow_non_contiguous=True),
                      [], [("gcol",)], sem="vin")
            emit_rmsnorm(c, xT, hT, vec["g_mix"], epsc, ones, tmp, rstd, psn)
            for kt in range(KT):
                s = kt % 2
                P.op("act", lambda e, kt=kt, s=s: e.activation(out=tmp[:, s, :HALO], in_=xh[:, kt, :], func=AF.Square), [("xh",)], [("tmp", s)])
                P.op("pe", lambda e, kt=kt, s=s: e.matmul(psn[0][:, :HALO], lhsT=ones[:, :], rhs=tmp[:, s, :HALO], start=(kt == 0), stop=(kt == KT - 1)),
                     [("tmp", s), ("ones",)], [("psn", 0)])
            P.op("act", lambda e: e.activation(out=rstd[:, TOK:NT], in_=psn[0][:, :HALO], func=AF.Sqrt, bias=epsc[:, 0:1], scale=1.0 / D), [("psn", 0), ("epsc",)], [("rstdh",)])
            P.op("dve", lambda e: e.reciprocal(out=rstd[:, TOK:NT], in_=rstd[:, TOK:NT]), [("rstdh",)], [("rstdh",)])
            for kt in range(KT):
                P.op("dve", lambda e, kt=kt: e.scalar_tensor_tensor(out=hT[:, kt, TOK:NT], in0=xh[:, kt, :], scalar=vec["g_mix"][:, kt:kt + 1], in1=rstd[:, TOK:NT],
                                                                   op0=ALU.mult, op1=ALU.mult), [("xh",), ("rstdh",), ("gcol",)], [("h", kt)])

            def consume(mt, ci, s0, s1, pa, pg, ta, tg):
                vs = 0
                n = s1 - s0
                d0 = 0 if s0 == TOK else HALO + s0
                P.op("act", lambda e, n=n, mt=mt: e.activation(out=tmp[:, 0, :n], in_=pg[:, :n], func=AF.Sigmoid, bias=vec["b_in_g"][:, mt:mt + 1], scale=1.0),
                     [tg, ("gcol",)], [("tmp", 0)])
                P.op("dve", lambda e, n=n, mt=mt, vs=vs, d0=d0: e.scalar_tensor_tensor(out=vext[:, vs, d0:d0 + n], in0=pa[:, :n], scalar=vec["b_in_a"][:, mt:mt + 1], in1=tmp[:, 0, :n],
                                                                                    op0=ALU.add, op1=ALU.mult), [ta, ("tmp", 0), ("gcol",)], [("vext", vs)])
                if s0 == TOK:
                    P.op("dve", lambda e, vs=vs: e.tensor_scalar(out=vext[:, vs, 0:HALO], in0=vext[:, vs, 0:HALO], scalar1=flag[:, 0:1], scalar2=None, op0=ALU.mult),
                         [("vext", vs), ("gcol",)], [("vext", vs)])
                    P.op("dve", lambda e, mt=mt, vs=vs: e.tensor_scalar(out=bigf[:, mt, :], in0=vext[:, vs, 2:2 + TOK], scalar1=dwc[:, mt, 0:1], scalar2=vec["dw_b"][:, mt:mt + 1],
                                                                      op0=ALU.mult, op1=ALU.add), [("vext", vs), ("gcol",)], [("cv", mt)])
                    for j in range(1, CW):
                        P.op("dve", lambda e, mt=mt, vs=vs, j=j: e.scalar_tensor_tensor(out=bigf[:, mt, :], in0=vext[:, vs, 2 + j:2 + j + TOK], scalar=dwc[:, mt, j:j + 1], in1=bigf[:, mt, :],
                                                                                      op0=ALU.mult, op1=ALU.add), [("vext", vs), ("cv", mt), ("gcol",)], [("cv", mt)])
            pair_glu(io["w_in"], NT, consume)
            for h in range(2):
                tsl = slice(h * 512, (h + 1) * 512)
                for kt in range(KT):
                    s = kt % 2
                    P.op("pe", lambda e, kt=kt, h=h, tsl=tsl: e.matmul(banks[h][:, :], lhsT=ones[:, :], rhs=bigf[:, kt, tsl], start=(kt == 0), stop=(kt == KT - 1)),
                         [("cv", kt), ("ones",)], [("ps", h)])
                    P.op("act", lambda e, kt=kt, s=s, tsl=tsl: e.activation(out=tmp[:, s, :], in_=bigf[:, kt, tsl], func=AF.Square), [("cv", kt)], [("tmp", s)])
                    P.op("pe", lambda e, kt=kt, h=h, s=s: e.matmul(banks[2 + h][:, :], lhsT=ones[:, :], rhs=tmp[:, s, :], start=(kt == 0), stop=(kt == KT - 1)),
                         [("tmp", s), ("ones",)], [("ps", 2 + h)])
                P.op("act", lambda e, h=h, tsl=tsl: e.activation(out=mean[:, tsl], in_=banks[h][:, :], func=AF.Copy, scale=1.0 / D), [("ps", h)], [("mean",)])
                P.op("act", lambda e, h=h, tsl=tsl: e.activation(out=rstd[:, tsl], in_=banks[2 + h][:, :], func=AF.Copy, scale=1.0 / D), [("ps", 2 + h)], [("rstd", h)])
                P.op("dve", lambda e, tsl=tsl: e.tensor_tensor(out=tmp[:, 0, :], in0=mean[:, tsl], in1=mean[:, tsl], op=ALU.mult), [("mean",)], [("tmp", 0)])
                P.op("dve", lambda e, tsl=tsl: e.tensor_tensor(out=rstd[:, tsl], in0=rstd[:, tsl], in1=tmp[:, 0, :], op=ALU.subtract), [("rstd", h), ("tmp", 0)], [("rstd", h)])
                P.op("act", lambda e, tsl=tsl: e.activation(out=rstd[:, tsl], in_=rstd[:, tsl], func=AF.Sqrt, bias=epsc[:, 0:1], scale=1.0), [("rstd", h), ("epsc",)], [("rstd", h)])
                P.op("dve", lambda e, tsl=tsl: e.reciprocal(out=rstd[:, tsl], in_=rstd[:, tsl]), [("rstd", h)], [("rstd", h)])
            for kt in range(KT):
                eng = "dve" if kt % 2 == 0 else "pool"
                P.op(eng, lambda e, kt=kt: e.tensor_tensor(out=bigf[:, kt, :], in0=bigf[:, kt, :], in1=mean[:, :], op=ALU.subtract), [("cv", kt), ("mean",)], [("cv", kt)])
                P.op(eng, lambda e, kt=kt: e.tensor_tensor(out=bigf[:, kt, :], in0=bigf[:, kt, :], in1=rstd[:, :TOK], op=ALU.mult), [("cv", kt), ("rstd", 0), ("rstd", 1)], [("cv", kt)])
                P.op("act", lambda e, kt=kt: e.activation(out=hT[:, kt, :TOK], in_=bigf[:, kt, :], func=AF.Silu, bias=vec["ln_b"][:, kt:kt + 1], scale=vec["ln_g"][:, kt:kt + 1]),
                     [("cv", kt), ("gcol",)], [("h", kt)])
            wo = io["w_out"].rearrange("(kt p) m -> p kt m", p=128)
            for dt_ in range(KT):
                wv, wt = ws.load(wo[:, :, dt_ * 128:(dt_ + 1) * 128], (KT, 128))
                for h in range(2):
                    tsl = slice(h * 512, (h + 1) * 512)
                    b = c.ps_rr % 6
                    c.ps_rr += 1
                    for kt in range(KT):
                        P.op("pe", lambda e, kt=kt, wv=wv, b=b, tsl=tsl: e.matmul(banks[b][:, :], lhsT=wv[:, kt, :], rhs=hT[:, kt, tsl], start=(kt == 0), stop=(kt == KT - 1)),
                             [wt, ("h", kt)], [("ps", b)])
                    P.op("dve", lambda e, dt_=dt_, b=b, tsl=tsl: e.scalar_tensor_tensor(out=xT[:, dt_, tsl], in0=banks[b][:, :], scalar=vec["b_out"][:, dt_:dt_ + 1], in1=xT[:, dt_, tsl],
                                                                                      op0=ALU.add, op1=ALU.add), [("ps", b), ("x", dt_), ("gcol",)], [("x", dt_)])
        else:
            ygath, idx1 = io["ygath"], io["idx1"]
            for r in range(NCORES):
                for kk in range(KPC):
                    for tile in range(2):
                        kt = 2 * r + tile
                        eo = ((r * NK + kk) * 256 + tile * 128) * TCH
                        P.dma("pool", lambda e, kt=kt, kk=kk, eo=eo: e.indirect_dma_start(out=hT[:, kt, kk * TCH:(kk + 1) * TCH], out_offset=None, in_=ygath[:, :],
                                                                                          in_offset=bass.IndirectOffsetOnAxis(ap=idx1[:, 0:1], axis=0), element_offset=eo),
                              [("ygath",), ("idx",)], [("h", kt)], sem=("yg", kt % 4))

            def consume(mt, ci, s0, s1, pa, pg, ta, tg):
                n = s1 - s0
                P.op("act", lambda e, n=n: e.activation(out=tmp[:, 0, :n], in_=pg[:, :n], func=AF.Sigmoid), [tg], [("tmp", 0)])
                P.op("dve", lambda e, n=n: e.tensor_tensor(out=tmp[:, 0, :n], in0=tmp[:, 0, :n], in1=pa[:, :n], op=ALU.mult), [ta, ("tmp", 0)], [("tmp", 0)])
                P.op("dve", lambda e, n=n, mt=mt, s0=s0, s1=s1: e.tensor_tensor(out=xT[:, mt, s0:s1], in0=xT[:, mt, s0:s1], in1=tmp[:, 0, :n], op=ALU.add),
                     [("tmp", 0), ("x", mt)], [("x", mt)])
            pair_glu(io["w_glu"], TOK, consume)

        big_readers = [("cv", kt) for kt in range(KT)]
        P.op("dve", lambda e: e.memset(epsc[:, :], EPS), big_readers + [("epsc",)], [("epsc",)] + [("hid", m) for m in range(DFF // 128)])
        emit_rmsnorm(c, xT, hT, vec["g_mlp"], epsc, ones, tmp, rstd, psn)
        emit_mlp(c, xT, hT, io["w_up"], io["w_down"], ws, hid, tmp, banks)
        outs = []
        nxt = io["next"]
        if nxt[0] == "ssm":
            hb_d = nxt[1]
            emit_rmsnorm(c, xT, hT, vec["g_next"], epsc, ones, tmp, rstd, psn)
            hbv = hb_d.rearrange("kk (kt p) t -> p kt kk t", p=128)
            for kt in range(KT):
                P.dma("sp", lambda e, kt=kt: e.dma_start(out=hbv[:, kt, :, :], in_=hT[:, kt, :TOK].rearrange("p (kk t) -> p kk t", kk=KPC)), [("h", kt)], [("hb",)], sem="hbout")
        elif nxt[0] == "conv":
            xtb = nxt[1]
            for kt in range(KT):
                P.dma("sp", lambda e, kt=kt: e.dma_start(out=xtb[kt * 128:(kt + 1) * 128, :], in_=xT[:, kt, TOK - HALO:TOK]), [("x", kt)], [("xtb",)], sem="xtbout")
        else:
            out_d = nxt[1]
            P.op("dve", lambda e: e.memset(epsc[:, :], EPS), [("hid", m) for m in range(DFF // 128)] + [("epsc",)], [("epsc",)] + [("hn", kt) for kt in range(KT)])
            emit_rmsnorm(c, xT, bigf, vec["g_next"], epsc, ones, tmp, rstd, psn, h_tag="hn")
            for kt in range(KT):
                outs.append(P.dma("sp", lambda e, kt=kt: e.dma_start(out=out_d[kt * 128:(kt + 1) * 128, :], in_=bigf[:, kt, :]), [("hn", kt)], [], sem="hout"))
        P.barrier()
    return outs


def ssm_phase(c, io, seq=SEQ):
    from contextlib import ExitStack
    nc = c.nc
    P = c.P
    uid = io["uid"]
    LR1_d, LI1_d, DT1_d, LR2_d, LI2_d, DT2_d = io["LR1"], io["LI1"], io["DT1"], io["LR2"], io["LI2"], io["DT2"]
    BRT_d, BIT_d, C1_d, SGN_d, MASK_d, DCOL_d = io["BRT"], io["BIT"], io["C1"], io["SGN"], io["MASK"], io["DCOL"]
    hgath, idx0, ybd = io["hgath"], io["idx0"], io["yb"]
    nchunk = seq // TCH
    with ExitStack() as es:
        sbt = lambda n, s, dt=F32: es.enter_context(nc.sbuf_tensor(f"{n}_{uid}", s, dt))
        LR1, LI1, DT1 = sbt("LR1s", [128, GPC]), sbt("LI1s", [128, GPC]), sbt("DT1s", [128, GPC])
        LR2, LI2, DT2 = sbt("LR2s", [128, 128]), sbt("LI2s", [128, 128]), sbt("DT2s", [128, 128])
        BRT, BIT = sbt("BRTs", [128, 128]), sbt("BITs", [128, 128])
        C1 = sbt("C1s", [128, GPC * 16])
        SGN, MASK, DCOL = sbt("SGNs", [128, 1]), sbt("MASKs", [128, 8]), sbt("DCOLs", [128, 2])
        npi = sbt("npi", [128, 1])
        for i, (s_, d_) in enumerate([(LR1, LR1_d), (LI1, LI1_d), (DT1, DT1_d), (LR2, LR2_d), (LI2, LI2_d), (DT2, DT2_d),
                                      (BRT, BRT_d), (BIT, BIT_d), (C1, C1_d), (SGN, SGN_d), (MASK, MASK_d), (DCOL, DCOL_d)]):
            P.dma("sp", lambda e, s_=s_, d_=d_: e.dma_start(out=s_[:, :], in_=d_[:, :]), [], [("par",)], sem="par")
        P.op("dve", lambda e: e.memset(npi[:, :], -PI), [], [("par",)])

        def abar(LR, LI, DT, n, nm):
            dt = sbt(nm + "dt", [128, n]); t1 = sbt(nm + "t1", [128, n]); t2 = sbt(nm + "t2", [128, n])
            ar = sbt(nm + "ar", [128, n]); ai = sbt(nm + "ai", [128, n]); mg = sbt(nm + "mg", [128, n])
            T = [(nm,)]
            R = [("par",), (nm,)]
            P.op("act", lambda e: e.activation(out=dt[:, :], in_=DT[:, :], func=AF.Exp), R, T)
            P.op("dve", lambda e: e.tensor_tensor(out=t1[:, :], in0=LR[:, :], in1=dt[:, :], op=ALU.mult), R, T)
            P.op("act", lambda e: e.activation(out=mg[:, :], in_=t1[:, :], func=AF.Exp), R, T)
            P.op("dve", lambda e: e.tensor_tensor(out=t1[:, :], in0=LI[:, :], in1=dt[:, :], op=ALU.mult), R, T)
            ki = sbt(nm + "ki", [128, n], mybir.dt.int32); kf = sbt(nm + "kf", [128, n])

            def sin_of(dst, shift):
                P.op("dve", lambda e: e.tensor_scalar(out=t2[:, :], in0=t1[:, :], scalar1=shift, scalar2=None, op0=ALU.add), R, T)
                P.op("dve", lambda e: e.tensor_scalar(out=kf[:, :], in0=t2[:, :], scalar1=1.0 / (2 * PI), scalar2=0.5, op0=ALU.mult, op1=ALU.add), R, T)
                P.op("dve", lambda e: e.tensor_copy(out=ki[:, :], in_=kf[:, :]), R, T)
                P.op("dve", lambda e: e.tensor_copy(out=kf[:, :], in_=ki[:, :]), R, T)
                P.op("dve", lambda e: e.scalar_tensor_tensor(out=t2[:, :], in0=kf[:, :], scalar=-2 * PI, in1=t2[:, :], op0=ALU.mult, op1=ALU.add), R, T)
                P.op("dve", lambda e: e.tensor_scalar(out=kf[:, :], in0=t2[:, :], scalar1=-PI, scalar2=2 * PI, op0=ALU.is_lt, op1=ALU.mult), R, T)
                P.op("dve", lambda e: e.tensor_tensor(out=t2[:, :], in0=t2[:, :], in1=kf[:, :], op=ALU.add), R, T)
                P.op("dve", lambda e: e.tensor_scalar(out=t2[:, :], in0=t2[:, :], scalar1=-PI, scalar2=PI, op0=ALU.max, op1=ALU.min), R, T)
                P.op("act", lambda e: e.activation(out=t2[:, :], in_=t2[:, :], func=AF.Sin), R, T)
                P.op("dve", lambda e: e.tensor_tensor(out=dst[:, :], in0=mg[:, :], in1=t2[:, :], op=ALU.mult), R, T)
            sin_of(ai, 0.0)
            sin_of(ar, 0.5 * PI)
            return ar, ai

        ar1, ai1 = abar(LR1, LI1, DT1, GPC, "a1")
        ARR = sbt("ARR", [128, 2, GPC]); AIP = sbt("AIP", [128, 2, GPC])
        T1t = [("tab1",)]
        R1 = [("a1",), ("tab1",)]
        P.op("dve", lambda e: e.tensor_copy(out=ARR[:, 0, :], in_=ar1[:, :]), R1, T1t)
        P.op("dve", lambda e: e.tensor_copy(out=ARR[:, 1, :], in_=ar1[:, :]), R1, T1t)
        P.op("dve", lambda e: e.tensor_copy(out=AIP[:, 0, :], in_=ai1[:, :]), R1, T1t)
        P.op("dve", lambda e: e.tensor_scalar(out=AIP[:, 1, :], in0=ai1[:, :], scalar1=-1.0, scalar2=None, op0=ALU.mult), R1, T1t)

        ar2, ai2 = abar(LR2, LI2, DT2, 128, "a2")
        w = [sbt(f"w{i}", [128, 128]) for i in range(6)]
        R2 = [("a2",), ("par",), ("tab2",)]
        T2t = [("tab2",)]
        tt = lambda o, a, b, op: P.op("dve", lambda e: e.tensor_tensor(out=o, in0=a, in1=b, op=op), R2, T2t)
        nr, den, kre, kim, bbr, bbi = w
        P.op("dve", lambda e: e.tensor_scalar(out=nr[:, :], in0=ar2[:, :], scalar1=-1.0, scalar2=None, op0=ALU.add), R2, T2t)
        tt(den[:, :], LR2[:, :], LR2[:, :], ALU.mult)
        tt(kre[:, :], LI2[:, :], LI2[:, :], ALU.mult)
        tt(den[:, :], den[:, :], kre[:, :], ALU.add)
        P.op("dve", lambda e: e.reciprocal(out=den[:, :], in_=den[:, :]), R2, T2t)
        tt(kre[:, :], nr[:, :], LR2[:, :], ALU.mult)
        tt(kim[:, :], ai2[:, :], LI2[:, :], ALU.mult)
        tt(kre[:, :], kre[:, :], kim[:, :], ALU.add)
        tt(kre[:, :], kre[:, :], den[:, :], ALU.mult)
        tt(kim[:, :], ai2[:, :], LR2[:, :], ALU.mult)
        tt(bbr[:, :], nr[:, :], LI2[:, :], ALU.mult)
        tt(kim[:, :], kim[:, :], bbr[:, :], ALU.subtract)
        tt(kim[:, :], kim[:, :], den[:, :], ALU.mult)
        tt(bbr[:, :], kre[:, :], BRT[:, :], ALU.mult)
        tt(bbi[:, :], kim[:, :], BIT[:, :], ALU.mult)
        tt(bbr[:, :], bbr[:, :], bbi[:, :], ALU.subtract)
        tt(bbi[:, :], kre[:, :], BIT[:, :], ALU.mult)
        tt(nr[:, :], kim[:, :], BRT[:, :], ALU.mult)
        tt(bbi[:, :], bbi[:, :], nr[:, :], ALU.add)
        nbbi = den
        P.op("dve", lambda e: e.tensor_scalar(out=nbbi[:, :], in0=bbi[:, :], scalar1=-1.0, scalar2=None, op0=ALU.mult), R2, T2t)
        BzPad = sbt("BzPad", [128, GPC, 2, 128], BF16)
        for g in range(GPC):
            tile, j = g // 8, g % 8
            cs = slice(tile * 64, tile * 64 + 64)
            for zz, (lo, hi) in enumerate([(bbr, bbi), (nbbi, bbr)]):
                P.op("dve", lambda e, g=g, zz=zz, lo=lo, cs=cs, j=j: e.tensor_scalar(out=BzPad[:, g, zz, 0:64], in0=lo[:, cs], scalar1=MASK[:, j:j + 1], scalar2=None, op0=ALU.mult), R2, T2t)
                P.op("dve", lambda e, g=g, zz=zz, hi=hi, cs=cs, j=j: e.tensor_scalar(out=BzPad[:, g, zz, 64:128], in0=hi[:, cs], scalar1=MASK[:, j:j + 1], scalar2=None, op0=ALU.mult), R2, T2t)
        CzPad = sbt("CzPad", [128, GPC, 128], BF16)
        Cz = sbt("Cz", [128, GPC * 16])
        P.op("dve", lambda e: e.memset(CzPad[:, :, :], 0.0), R2, T2t)
        P.op("dve", lambda e: e.tensor_scalar(out=Cz[:, :], in0=C1[:, :], scalar1=SGN[:, 0:1], scalar2=None, op0=ALU.mult), R2, T2t)
        for g in range(GPC):
            j = g % 8
            P.op("dve", lambda e, g=g, j=j: e.tensor_copy(out=CzPad[:, g, 16 * j:16 * j + 16], in_=Cz[:, 16 * g:16 * g + 16]), R2, T2t)

        V = [sbt(f"V{i}", [128, 2, GPC, TCH]) for i in range(2)]
        Sbf = [sbt(f"Sbf{i}", [128, GPC, TCH], BF16) for i in range(2)]
        hb = [sbt(f"hb{i}", [128, 2, TCH], BF16) for i in range(2)]
        T1 = sbt("T1", [128, 2, GPC]); T2 = sbt("T2", [128, 2, GPC])
        Z0 = sbt("Z0", [128, 2, GPC])
        P.op("dve", lambda e: e.memset(Z0[:, :, :], 0.0), [], [("Z0",)])
        ysb = [sbt(f"ysb{i}", [128, TCH]) for i in range(2)]
        xg = [sbt(f"xg{i}", [128, TCH]) for i in range(2)]
        wg = [sbt(f"wg{i}", [128, TCH]) for i in range(2)]
        og = [sbt(f"og{i}", [128, TCH], BF16) for i in range(4)]
        pse = [es.enter_context(nc.psum_tensor(f"pse{i}_{uid}", [128, TCH], F32)) for i in range(4)]
        psy = [es.enter_context(nc.psum_tensor(f"psy{i}_{uid}", [128, TCH], F32)) for i in range(2)]
        outs = []
        er = 0
        yr = 0
        orr = 0
        for k in range(nchunk):
            b = k % 2
            tsl = slice(k * TCH, (k + 1) * TCH)
            for tile in range(2):
                eo = (k * D + tile * 128) * TCH
                P.dma("pool", lambda e, b=b, tile=tile, eo=eo: e.indirect_dma_start(out=hb[b][:, tile, :], out_offset=None, in_=hgath[:, :],
                                                                                   in_offset=bass.IndirectOffsetOnAxis(ap=idx0[:, 0:1], axis=0), element_offset=eo),
                      [("hgath",), ("idx",)], [("hb", b)], sem=("hbl", b, tile))
            for g in range(GPC):
                for zz in range(2):
                    pb = er % 4
                    er += 1
                    P.op("pe", lambda e, g=g, zz=zz, pb=pb, b=b: e.matmul(pse[pb][:, :], lhsT=BzPad[:, g, zz, :], rhs=hb[b][:, g // 8, :], start=True, stop=True),
                         [("hb", b), ("tab2",)], [("pse", pb)])
                    P.op("act", lambda e, g=g, zz=zz, pb=pb, b=b: e.activation(out=V[b][:, zz, g, :], in_=pse[pb][:, :], func=AF.Copy),
                         [("pse", pb)], [("V", b)])
            for t in range(TCH):
                if t == 0:
                    zp = Z0[:, :, :] if k == 0 else V[1 - b][:, :, :, TCH - 1]
                    zr = [("Z0",)] if k == 0 else [("V", 1 - b)]
                else:
                    zp = V[b][:, :, :, t - 1]
                    zr = [("V", b)]
                zp0 = Z0[:, 0, :] if (k == 0 and t == 0) else (V[1 - b][:, 0, :, TCH - 1] if t == 0 else V[b][:, 0, :, t - 1])
                zp1 = Z0[:, 1, :] if (k == 0 and t == 0) else (V[1 - b][:, 1, :, TCH - 1] if t == 0 else V[b][:, 1, :, t - 1])
                P.op("dve", lambda e, zp=zp: e.tensor_tensor(out=T1[:, :, :], in0=zp, in1=ARR[:, :, :], op=ALU.mult), zr + [("tab1",), ("T1",)], [("T1",)])
                P.op("dve", lambda e, zp1=zp1: e.tensor_tensor(out=T2[:, 0, :], in0=zp1, in1=AIP[:, 0, :], op=ALU.mult), zr + [("tab1",), ("T2",)], [("T2",)])
                P.op("dve", lambda e, zp0=zp0: e.tensor_tensor(out=T2[:, 1, :], in0=zp0, in1=AIP[:, 1, :], op=ALU.mult), zr + [("tab1",), ("T2",)], [("T2",)])
                P.op("dve", lambda e: e.tensor_tensor(out=T1[:, :, :], in0=T1[:, :, :], in1=T2[:, :, :], op=ALU.add), [("T1",), ("T2",)], [("T1",)])
                P.op("dve", lambda e, b=b, t=t: e.tensor_tensor(out=V[b][:, :, :, t], in0=V[b][:, :, :, t], in1=T1[:, :, :], op=ALU.add), [("T1",), ("V", b)], [("V", b)])
            P.op("act", lambda e, b=b: e.activation(out=Sbf[b][:, :, :], in_=V[b][:, 0, :, :], func=AF.Copy), [("V", b)], [("Sbf", b)])
            for tile in range(2):
                yb = yr % 2
                yr += 1
                for j in range(8):
                    g = tile * 8 + j
                    P.op("pe", lambda e, g=g, j=j, yb=yb, b=b: e.matmul(psy[yb][:, :], lhsT=CzPad[:, g, :], rhs=Sbf[b][:, g, :], start=(j == 0), stop=(j == 7)),
                         [("Sbf", b), ("tab2",)], [("psy", yb)])
                ob = orr % 4
                orr += 1
                P.op("act", lambda e, yb=yb: e.activation(out=ysb[yb][:, :], in_=psy[yb][:, :], func=AF.Copy), [("psy", yb)], [("ysb", yb)])
                P.op("pool", lambda e, yb=yb, b=b, tile=tile: e.tensor_scalar(out=xg[yb][:, :], in0=hb[b][:, tile, :], scalar1=DCOL[:, tile:tile + 1], scalar2=None, op0=ALU.mult),
                     [("hb", b), ("par",)], [("xg", yb)])
                P.op("pool", lambda e, yb=yb: e.tensor_tensor(out=xg[yb][:, :], in0=xg[yb][:, :], in1=ysb[yb][:, :], op=ALU.add), [("xg", yb), ("ysb", yb)], [("xg", yb)])
                P.op("pool", lambda e, yb=yb: e.tensor_tensor(out=wg[yb][:, :], in0=xg[yb][:, :], in1=xg[yb][:, :], op=ALU.mult), [("xg", yb)], [("wg", yb)])
                P.op("pool", lambda e, yb=yb: e.tensor_scalar(out=wg[yb][:, :], in0=wg[yb][:, :], scalar1=0.044715, scalar2=1.0, op0=ALU.mult, op1=ALU.add), [("wg", yb)], [("wg", yb)])
                P.op("pool", lambda e, yb=yb: e.tensor_tensor(out=wg[yb][:, :], in0=wg[yb][:, :], in1=xg[yb][:, :], op=ALU.mult), [("wg", yb), ("xg", yb)], [("wg", yb)])
                P.op("act", lambda e, yb=yb: e.activation(out=wg[yb][:, :], in_=wg[yb][:, :], func=AF.Sigmoid, scale=1.5957691216), [("wg", yb)], [("wg", yb)])
                P.op("pool", lambda e, yb=yb, ob=ob: e.tensor_tensor(out=og[ob][:, :], in0=wg[yb][:, :], in1=xg[yb][:, :], op=ALU.mult), [("wg", yb), ("xg", yb)], [("og", ob)])
                P.dma("sp", lambda e, ob=ob, tile=tile, k=k: e.dma_start(out=ybd[k * 256 + tile * 128:k * 256 + (tile + 1) * 128, :], in_=og[ob][:, :]), [("og", ob)], [("yb",)], sem=("og", ob))
        P.barrier()


SSM_KEYS = ["LR1", "LI1", "DT1", "LR2", "LI2", "DT2", "BRT", "BIT", "C1", "DCOL", "C2", "BQ1", "BQ2"]
SSM_SHAPES = {"LR1": [128, GPC], "LI1": [128, GPC], "DT1": [128, GPC], "LR2": [128, 128], "LI2": [128, 128], "DT2": [128, 128],
              "BRT": [128, 128], "BIT": [128, 128], "C1": [128, GPC * 16], "DCOL": [128, 2],
              "C2": [128, GPC * 16], "BQ1": [128, GPC * 16], "BQ2": [128, GPC * 16]}


def build_fused():
    from contextlib import ExitStack
    nc = bass.Bass("TRN2", target_bir_lowering=False)
    din = lambda n, s, dt=F32: nc.dram_tensor(n, s, dt, kind="ExternalInput").ap()
    x_d = din("xT", [D, TOK])
    xh0_d = din("xh0", [D, HALO])
    flag_d = din("flag", [128, 1])
    idx_d = din("idx", [128, 3], U32)
    mixn = din("mix_norm", [4, D]); mlpn = din("mlp_norm", [4, D]); finn = din("final_norm", [D])
    cwin = din("conv_w_in", [2, D, 2 * D]); cbin = din("conv_b_in", [2, 2 * D]); cdw = din("conv_dw", [2, CW, D])
    cdwb = din("conv_dw_b", [2, D]); clg = din("conv_ln_g", [2, D]); clb = din("conv_ln_b", [2, D])
    cwo = din("conv_w_out", [2, D, D]); cbo = din("conv_b_out", [2, D])
    wglu = din("ssm_w_glu", [2, D, 2 * D]); wup = din("mlp_w_up", [4, D, DFF]); wdn = din("mlp_w_down", [4, DFF, D])
    ssm_d = {k: din("S_" + k, [2] + SSM_SHAPES[k]) for k in SSM_KEYS}
    sgn_d = din("SGN", [128, 1]); mask_d = din("MASK", [128, 8]); gmask_d = din("GMASK", [128, GPC * 8])
    out_d = nc.dram_tensor("out", [D, TOK], F32, kind="ExternalOutput").ap()
    hb = [nc.dram_tensor(f"hb{j}", [KPC * D, TCH], BF16) for j in range(2)]
    hgath = [nc.dram_tensor(f"hgath{j}", [NCORES * KPC * D, TCH], BF16) for j in range(2)]
    yb = [nc.dram_tensor(f"yb{j}", [NK * 256, TCH], BF16) for j in range(2)]
    ygath = [nc.dram_tensor(f"ygath{j}", [NCORES * NK * 256, TCH], BF16) for j in range(2)]
    xtb = nc.dram_tensor("xtb", [D, HALO], F32)
    xtg = nc.dram_tensor("xtg", [NCORES * D, HALO], F32)
    rg = [list(range(NCORES))]
    with ExitStack() as es:
        c = Ctx(nc, es)
        P = c.P
        xT = c.sb("xT_sb", [128, KT, TOK], F32)
        ones = c.sb("ones", [128, 128], F32)
        epsc = c.sb("epsc", [128, 1], F32)
        idx = c.sb("idx_sb", [128, 3], U32)
        P.barrier_scratch = c.sb("bscr", [128, 1], F32)
        for kt in range(KT):
            P.dma("sp", lambda e, kt=kt: e.dma_start(out=xT[:, kt, :], in_=x_d[kt * 128:(kt + 1) * 128, :]), [], [("x", kt)], sem="xin")
        P.dma("sp", lambda e: e.dma_start(out=idx[:, :], in_=idx_d[:, :]), [], [("idx",)], sem="idxin")
        P.op("dve", lambda e: e.memset(ones[:, :], 1.0), [], [("ones",)])
        P.op("dve", lambda e: e.memset(epsc[:, :], EPS), [], [("epsc",)])
        outs = []
        for layer in range(4):
            j = layer // 2
            g_next = mixn[layer + 1] if layer < 3 else finn
            if layer % 2 == 0:
                io = {"uid": layer, "g_mlp": mlpn[layer], "g_next": g_next, "g_mix": mixn[layer], "b_in_a": cbin[j, 0:D], "b_in_g": cbin[j, D:2 * D],
                      "dw_b": cdwb[j], "ln_g": clg[j], "ln_b": clb[j], "b_out": cbo[j], "flag": flag_d, "dw": cdw[j], "w_in": cwin[j], "w_out": cwo[j],
                      "w_up": wup[layer], "w_down": wdn[layer],
                      "halo": ("dram", xh0_d) if layer == 0 else ("gather", xtg.ap(), idx[:, 2:3]),
                      "next": ("ssm", hb[j].ap().rearrange("(kk c) t -> kk c t", kk=KPC))}
                dense_phase(c, "conv", xT, ones, epsc, io)
                P.dma("pool", lambda e, j=j: e.collective_compute("AllGather", ALU.bypass, replica_groups=rg, ins=[hb[j].ap().opt()], outs=[hgath[j].ap().opt()]),
                      [("hb",)], [("hgath",)], sem=("cch", j), inc=1)
            else:
                io = {"uid": layer, "hgath": hgath[j].ap(), "idx0": idx[:, 0:1], "yb": yb[j].ap(), "SGN": sgn_d, "MASK": mask_d, "GMASK": gmask_d}
                for k in SSM_KEYS:
                    io[k] = ssm_d[k][j]
                ssm_phase_blk(c, io)
                P.dma("pool", lambda e, j=j: e.collective_compute("AllGather", ALU.bypass, replica_groups=rg, ins=[yb[j].ap().opt()], outs=[ygath[j].ap().opt()]),
                      [("yb",)], [("ygath",)], sem=("ccy", j), inc=1)
                io = {"uid": 10 + layer, "g_mlp": mlpn[layer], "g_next": g_next, "w_glu": wglu[j], "w_up": wup[layer], "w_down": wdn[layer],
                      "ygath": ygath[j].ap(), "idx1": idx[:, 1:2],
                      "next": ("conv", xtb.ap()) if layer < 3 else ("final", out_d)}
                outs += dense_phase(c, "glu", xT, ones, epsc, io)
                if layer < 3:
                    P.dma("pool", lambda e: e.collective_compute("AllGather", ALU.bypass, replica_groups=rg, ins=[xtb.ap().opt()], outs=[xtg.ap().opt()]),
                          [("xtb",)], [("xtg",)], sem="ccx", inc=1)
        P.emit(final_wait_ops=outs)
    return nc


_NC_CACHE = {}


def kernel(x, mix_norm, conv_w_in, conv_b_in, conv_dw, conv_dw_b, conv_ln_g, conv_ln_b, conv_w_out, conv_b_out,
           ssm_lambda_re, ssm_lambda_im, ssm_log_dt, ssm_b_re, ssm_b_im, ssm_c_re, ssm_c_im, ssm_d, ssm_w_glu,
           mlp_norm, mlp_w_up, mlp_w_down, final_norm):
    f = lambda a: np.ascontiguousarray(np.asarray(a, dtype=np.float32))
    x = f(x)
    cores = list(range(NCORES))
    if "nc" not in _NC_CACHE:
        _NC_CACHE["nc"] = build_fused()
    nc = _NC_CACHE["nc"]
    shared = {"mix_norm": f(mix_norm), "mlp_norm": f(mlp_norm), "final_norm": f(final_norm), "conv_w_in": f(conv_w_in), "conv_b_in": f(conv_b_in),
              "conv_dw": f(conv_dw), "conv_dw_b": f(conv_dw_b), "conv_ln_g": f(conv_ln_g), "conv_ln_b": f(conv_ln_b), "conv_w_out": f(conv_w_out),
              "conv_b_out": f(conv_b_out), "ssm_w_glu": f(ssm_w_glu), "mlp_w_up": f(mlp_w_up), "mlp_w_down": f(mlp_w_down)}
    lre, lim, ldt = f(ssm_lambda_re), f(ssm_lambda_im), f(ssm_log_dt)
    bre, bim, cre, cim, dsk = f(ssm_b_re), f(ssm_b_im), f(ssm_c_re), f(ssm_c_im), f(ssm_d)
    maps = []
    dummy_h = np.zeros((256, 1), np.float32)
    for c in cores:
        m = dict(shared)
        m["xT"] = f(x[0, c * TOK:(c + 1) * TOK].T)
        m["xh0"] = f(x[0, c * TOK - HALO:c * TOK].T) if c > 0 else np.zeros((D, HALO), np.float32)
        m["flag"] = np.full((128, 1), 0.0 if c == 0 else 1.0, np.float32)
        p = np.arange(128, dtype=np.uint32)
        m["idx"] = np.ascontiguousarray(np.stack([256 * c + p, 1024 * c + p, max(c - 1, 0) * D + p], axis=1).astype(np.uint32))
        per = [ssm_host_inputs_blk(lre[j], lim[j], ldt[j], bre[j], bim[j], cre[j], cim[j], dsk[j], c) for j in range(2)]
        for k in SSM_KEYS:
            m["S_" + k] = np.ascontiguousarray(np.stack([per[0][k], per[1][k]], axis=0))
        m["SGN"] = per[0]["SGN"]
        m["MASK"] = per[0]["MASK"]
        m["GMASK"] = per[0]["GMASK"]
        maps.append(m)
    res = run_bass_kernel_spmd(nc, maps, core_ids=cores)
    out = np.concatenate([r["out"].T for r in res.results], axis=0)[None]
    return np.ascontiguousarray(out.astype(np.float32))


NB = TCH // 8


def ssm_phase_blk(c, io, seq=SEQ):
    from contextlib import ExitStack
    nc = c.nc
    P = c.P
    uid = io["uid"]
    hgath, idx0, ybd = io["hgath"], io["idx0"], io["yb"]
    nchunk = seq // TCH
    with ExitStack() as es:
        sbt = lambda n, s, dt=F32: es.enter_context(nc.sbuf_tensor(f"{n}_{uid}", s, dt))
        par_names = ["LR1", "LI1", "DT1", "LR2", "LI2", "DT2", "BRT", "BIT", "C1", "C2", "BQ1", "BQ2", "SGN", "MASK", "DCOL", "GMASK"]
        par = {}
        for n in par_names:
            shp = list(io[n].shape)
            par[n] = sbt("p" + n, shp)
            P.dma("sp", lambda e, n=n: e.dma_start(out=par[n][:, :], in_=io[n][:, :]), [], [("par",)], sem="par")
        LR1, LI1, DT1, LR2, LI2, DT2 = (par[n] for n in ["LR1", "LI1", "DT1", "LR2", "LI2", "DT2"])
        BRT, BIT, C1, C2, BQ1, BQ2, SGN, MASK, DCOL, GMASK = (par[n] for n in ["BRT", "BIT", "C1", "C2", "BQ1", "BQ2", "SGN", "MASK", "DCOL", "GMASK"])
        R = [("par",), ("tab",)]
        T = [("tab",)]
        tt = lambda o, a, b, op: P.op("dve", lambda e: e.tensor_tensor(out=o, in0=a, in1=b, op=op), R, T)
        ts1 = lambda o, a, s1, op: P.op("dve", lambda e: e.tensor_scalar(out=o, in0=a, scalar1=s1, scalar2=None, op0=op), R, T)

        def abar(LR, LI, DT, n, nm):
            dt = sbt(nm + "dt", [128, n]); t1 = sbt(nm + "t1", [128, n]); t2 = sbt(nm + "t2", [128, n])
            ar = sbt(nm + "ar", [128, n]); ai = sbt(nm + "ai", [128, n]); mg = sbt(nm + "mg", [128, n])
            ki = sbt(nm + "ki", [128, n], mybir.dt.int32); kf = sbt(nm + "kf", [128, n])
            P.op("act", lambda e: e.activation(out=dt[:, :], in_=DT[:, :], func=AF.Exp), R, T)
            tt(t1[:, :], LR[:, :], dt[:, :], ALU.mult)
            P.op("act", lambda e: e.activation(out=mg[:, :], in_=t1[:, :], func=AF.Exp), R, T)
            tt(t1[:, :], LI[:, :], dt[:, :], ALU.mult)

            def sin_of(dst, shift):
                ts1(t2[:, :], t1[:, :], shift, ALU.add)
                P.op("dve", lambda e: e.tensor_scalar(out=kf[:, :], in0=t2[:, :], scalar1=1.0 / (2 * PI), scalar2=0.5, op0=ALU.mult, op1=ALU.add), R, T)
                P.op("dve", lambda e: e.tensor_copy(out=ki[:, :], in_=kf[:, :]), R, T)
                P.op("dve", lambda e: e.tensor_copy(out=kf[:, :], in_=ki[:, :]), R, T)
                P.op("dve", lambda e: e.scalar_tensor_tensor(out=t2[:, :], in0=kf[:, :], scalar=-2 * PI, in1=t2[:, :], op0=ALU.mult, op1=ALU.add), R, T)
                P.op("dve", lambda e: e.tensor_scalar(out=kf[:, :], in0=t2[:, :], scalar1=-PI, scalar2=2 * PI, op0=ALU.is_lt, op1=ALU.mult), R, T)
                tt(t2[:, :], t2[:, :], kf[:, :], ALU.add)
                P.op("dve", lambda e: e.tensor_scalar(out=t2[:, :], in0=t2[:, :], scalar1=-PI, scalar2=PI, op0=ALU.max, op1=ALU.min), R, T)
                P.op("act", lambda e: e.activation(out=t2[:, :], in_=t2[:, :], func=AF.Sin), R, T)
                tt(dst[:, :], mg[:, :], t2[:, :], ALU.mult)
            sin_of(ai, 0.0)
            sin_of(ar, 0.5 * PI)
            return ar, ai

        def kfac(LR, LI, ar, ai, n, nm):
            nr = sbt(nm + "nr", [128, n]); den = sbt(nm + "den", [128, n]); kre = sbt(nm + "kre", [128, n]); kim = sbt(nm + "kim", [128, n]); t = sbt(nm + "kt", [128, n])
            ts1(nr[:, :], ar[:, :], -1.0, ALU.add)
            tt(den[:, :], LR[:, :], LR[:, :], ALU.mult)
            tt(t[:, :], LI[:, :], LI[:, :], ALU.mult)
            tt(den[:, :], den[:, :], t[:, :], ALU.add)
            P.op("dve", lambda e: e.reciprocal(out=den[:, :], in_=den[:, :]), R, T)
            tt(kre[:, :], nr[:, :], LR[:, :], ALU.mult)
            tt(t[:, :], ai[:, :], LI[:, :], ALU.mult)
            tt(kre[:, :], kre[:, :], t[:, :], ALU.add)
            tt(kre[:, :], kre[:, :], den[:, :], ALU.mult)
            tt(kim[:, :], ai[:, :], LR[:, :], ALU.mult)
            tt(t[:, :], nr[:, :], LI[:, :], ALU.mult)
            tt(kim[:, :], kim[:, :], t[:, :], ALU.subtract)
            tt(kim[:, :], kim[:, :], den[:, :], ALU.mult)
            return kre, kim

        def cmul(o_r, o_i, a_r, a_i, b_r, b_i, t):
            tt(o_r, a_r, b_r, ALU.mult)
            tt(t, a_i, b_i, ALU.mult)
            tt(o_r, o_r, t, ALU.subtract)
            tt(o_i, a_r, b_i, ALU.mult)
            tt(t, a_i, b_r, ALU.mult)
            tt(o_i, o_i, t, ALU.add)

        ar1, ai1 = abar(LR1, LI1, DT1, GPC, "a1")
        kre1, kim1 = kfac(LR1, LI1, ar1, ai1, GPC, "k1")
        PW1 = sbt("PW1", [128, 9, 2, GPC])
        tmp1 = sbt("tmp1", [128, GPC])
        P.op("dve", lambda e: e.memset(PW1[:, 0, 0, :], 1.0), R, T)
        P.op("dve", lambda e: e.memset(PW1[:, 0, 1, :], 0.0), R, T)
        P.op("dve", lambda e: e.tensor_copy(out=PW1[:, 1, 0, :], in_=ar1[:, :]), R, T)
        P.op("dve", lambda e: e.tensor_copy(out=PW1[:, 1, 1, :], in_=ai1[:, :]), R, T)
        for m in range(2, 9):
            cmul(PW1[:, m, 0, :], PW1[:, m, 1, :], PW1[:, m - 1, 0, :], PW1[:, m - 1, 1, :], ar1[:, :], ai1[:, :], tmp1[:, :])
        ARR = sbt("ARR", [128, 2, GPC]); AIP = sbt("AIP", [128, 2, GPC])
        P.op("dve", lambda e: e.tensor_copy(out=ARR[:, 0, :], in_=PW1[:, 8, 0, :]), R, T)
        P.op("dve", lambda e: e.tensor_copy(out=ARR[:, 1, :], in_=PW1[:, 8, 0, :]), R, T)
        P.op("dve", lambda e: e.tensor_copy(out=AIP[:, 0, :], in_=PW1[:, 8, 1, :]), R, T)
        ts1(AIP[:, 1, :], PW1[:, 8, 1, :], -1.0, ALU.mult)

        C1s = sbt("C1s", [128, GPC, 16]); C2v = C2[:, :].rearrange("p (g c) -> p g c", g=GPC)
        ts1(C1s[:, :, :], C1[:, :].rearrange("p (g c) -> p g c", g=GPC), SGN[:, 0:1], ALU.mult)
        TCc = sbt("TCc", [128, GPC, 16]); TCt = sbt("TCt", [128, GPC, 16])
        CzPadL = sbt("CzPadL", [128, 9, GPC, 128], BF16)
        GM4 = GMASK[:, :].rearrange("p (g j) -> p g j", g=GPC).unsqueeze(3).to_broadcast([128, GPC, 8, 16])
        for l in range(9):
            m = l + 1 if l < 8 else 0
            prb = PW1[:, m, 0, :].unsqueeze(2).to_broadcast([128, GPC, 16])
            pib = PW1[:, m, 1, :].unsqueeze(2).to_broadcast([128, GPC, 16])
            tt(TCc[:, :, :], C1s[:, :, :], prb, ALU.mult)
            tt(TCt[:, :, :], C2v, pib, ALU.mult)
            tt(TCc[:, :, :], TCc[:, :, :], TCt[:, :, :], ALU.subtract)
            tt(CzPadL[:, l, :, :].rearrange("p g (j c) -> p g j c", j=8), TCc[:, :, :].unsqueeze(2).to_broadcast([128, GPC, 8, 16]), GM4, ALU.mult)

        Kbd = sbt("Kbd", [128, 2, 8, 128], BF16)
        with ExitStack() as es2:
            sb2 = lambda n, s, dt=F32: es2.enter_context(nc.sbuf_tensor(f"{n}_{uid}", s, dt))
            XPad = sb2("XPad", [128, 8, GPC, 128], BF16)
            Fr = sb2("Fr", [128, GPC]); Fi = sb2("Fi", [128, GPC])
            Xc = sb2("Xc", [128, GPC, 16]); Xt = sb2("Xt", [128, GPC, 16])
            BQ1v = BQ1[:, :].rearrange("p (g c) -> p g c", g=GPC)
            BQ2v = BQ2[:, :].rearrange("p (g c) -> p g c", g=GPC)
            pk = [es2.enter_context(nc.psum_tensor(f"pk{i}_{uid}", [128, 128], F32)) for i in range(2)]
            for lag in range(8):
                cmul(Fr[:, :], Fi[:, :], PW1[:, lag, 0, :], PW1[:, lag, 1, :], kre1[:, :], kim1[:, :], tmp1[:, :])
                ts1(Fi[:, :], Fi[:, :], SGN[:, 0:1], ALU.mult)
                tt(Xc[:, :, :], BQ1v, Fr[:, :].unsqueeze(2).to_broadcast([128, GPC, 16]), ALU.mult)
                tt(Xt[:, :, :], BQ2v, Fi[:, :].unsqueeze(2).to_broadcast([128, GPC, 16]), ALU.mult)
                tt(Xc[:, :, :], Xc[:, :, :], Xt[:, :, :], ALU.subtract)
                tt(XPad[:, lag, :, :].rearrange("p g (j c) -> p g j c", j=8), Xc[:, :, :].unsqueeze(2).to_broadcast([128, GPC, 8, 16]), GM4, ALU.mult)
            i = 0
            for tile in range(2):
                for lag in range(8):
                    pb = i % 2
                    i += 1
                    for j in range(8):
                        g = tile * 8 + j
                        P.op("pe", lambda e, g=g, j=j, lag=lag, pb=pb: e.matmul(pk[pb][:, :], lhsT=XPad[:, lag, g, :], rhs=CzPadL[:, 8, g, :], start=(j == 0), stop=(j == 7)),
                             [("tab",)], [("pk", pb)])
                    P.op("act", lambda e, tile=tile, lag=lag, pb=pb: e.activation(out=Kbd[:, tile, lag, :], in_=pk[pb][:, :], func=AF.Copy), [("pk", pb)], [("tabk",)])
            P.op("dve", lambda e: e.memset(tmp1[:, :], 0.0), [("tabk",), ("pk", 0), ("pk", 1)] + R, T)

        BzPadK = sbt("BzPadK", [128, GPC, 8, 2, 128], BF16)
        with ExitStack() as es3:
            sb3 = lambda n, s, dt=F32: es3.enter_context(nc.sbuf_tensor(f"{n}_{uid}", s, dt))
            sbt_save = sbt
            sbt = sb3
            ar2, ai2 = abar(LR2, LI2, DT2, 128, "a2")
            kre2, kim2 = kfac(LR2, LI2, ar2, ai2, 128, "k2")
            sbt = sbt_save
            bbr = sb3("bbr", [128, 128]); bbi = sb3("bbi", [128, 128]); t2_ = sb3("t2_", [128, 128])
            cmul(bbr[:, :], bbi[:, :], kre2[:, :], kim2[:, :], BRT[:, :], BIT[:, :], t2_[:, :])
            PW2 = sb3("PW2", [128, 8, 2, 128])
            P.op("dve", lambda e: e.memset(PW2[:, 0, 0, :], 1.0), R, T)
            P.op("dve", lambda e: e.memset(PW2[:, 0, 1, :], 0.0), R, T)
            for m in range(1, 8):
                cmul(PW2[:, m, 0, :], PW2[:, m, 1, :], PW2[:, m - 1, 0, :], PW2[:, m - 1, 1, :], ar2[:, :], ai2[:, :], t2_[:, :])
            Tz = sb3("Tz", [128, 2, 2, 128])
            xr = sb3("xr", [128, 128]); xi = sb3("xi", [128, 128])
            MK3 = MASK[:, :].unsqueeze(2).to_broadcast([128, 8, 128])
            for k in range(8):
                m = 7 - k
                cmul(xr[:, :], xi[:, :], PW2[:, m, 0, :], PW2[:, m, 1, :], bbr[:, :], bbi[:, :], t2_[:, :])
                for tile in range(2):
                    cs = slice(tile * 64, tile * 64 + 64)
                    P.op("dve", lambda e, tile=tile, cs=cs: e.tensor_copy(out=Tz[:, 0, tile, 0:64], in_=xr[:, cs]), R, T)
                    P.op("dve", lambda e, tile=tile, cs=cs: e.tensor_copy(out=Tz[:, 0, tile, 64:128], in_=xi[:, cs]), R, T)
                    ts1(Tz[:, 1, tile, 0:64], xi[:, cs], -1.0, ALU.mult)
                    P.op("dve", lambda e, tile=tile, cs=cs: e.tensor_copy(out=Tz[:, 1, tile, 64:128], in_=xr[:, cs]), R, T)
                    for zz in range(2):
                        tt(BzPadK[:, tile * 8:(tile + 1) * 8, k, zz, :], Tz[:, zz, tile, :].unsqueeze(1).to_broadcast([128, 8, 128]), MK3, ALU.mult)
            P.op("dve", lambda e: e.memset(tmp1[:, :], 0.0), R, T)

        V = [sbt(f"V{i}", [128, 2, GPC, NB]) for i in range(2)]
        Sbf = [sbt(f"Sbf{i}", [128, GPC, NB], BF16) for i in range(2)]
        hb = [sbt(f"hb{i}", [128, 2, TCH], BF16) for i in range(2)]
        T1 = sbt("T1", [128, 2, GPC]); T2 = sbt("T2", [128, 2, GPC])
        Z0 = sbt("Z0", [128, 2, GPC])
        P.op("dve", lambda e: e.memset(Z0[:, :, :], 0.0), [], [("Z0",)])
        ysb = [sbt(f"ysb{i}", [128, TCH]) for i in range(2)]
        xg = [sbt(f"xg{i}", [128, TCH]) for i in range(2)]
        wg = [sbt(f"wg{i}", [128, TCH]) for i in range(2)]
        og = [sbt(f"og{i}", [128, TCH], BF16) for i in range(4)]
        pse = [es.enter_context(nc.psum_tensor(f"pse{i}_{uid}", [128, TCH], F32)) for i in range(4)]
        psy = [es.enter_context(nc.psum_tensor(f"psy{i}_{uid}", [128, TCH], F32)) for i in range(2)]
        er = 0
        yr = 0
        orr = 0
        for k in range(nchunk):
            b = k % 2
            for tile in range(2):
                eo = (k * D + tile * 128) * TCH
                P.dma("pool", lambda e, b=b, tile=tile, eo=eo: e.indirect_dma_start(out=hb[b][:, tile, :], out_offset=None, in_=hgath[:, :],
                                                                                   in_offset=bass.IndirectOffsetOnAxis(ap=idx0[:, 0:1], axis=0), element_offset=eo),
                      [("hgath",), ("idx",), ("tab",)], [("hb", b)], sem=("hbl", b, tile))
            hbv = [hb[b][:, tile, :].rearrange("p (n k) -> p k n", k=8) for tile in range(2)]
            for zz in range(2):
                for tile in range(2):
                    pb = er % 4
                    er += 1
                    for j in range(8):
                        g = tile * 8 + j
                        for kk in range(8):
                            P.op("pe", lambda e, g=g, j=j, kk=kk, zz=zz, pb=pb, tile=tile, hbv=hbv: e.matmul(pse[pb][:, j * NB:(j + 1) * NB], lhsT=BzPadK[:, g, kk, zz, :], rhs=hbv[tile][:, kk, :],
                                                                                                     start=(kk == 0), stop=(kk == 7)),
                                 [("hb", b), ("tab",)], [("pse", pb)])
                    P.op("act", lambda e, zz=zz, tile=tile, pb=pb, b=b: e.activation(out=V[b][:, zz, tile * 8:(tile + 1) * 8, :], in_=pse[pb][:, :].rearrange("p (j n) -> p j n", j=8), func=AF.Copy),
                         [("pse", pb)], [("V", b)])
            if k == 0:
                P.op("act", lambda e, b=b: e.activation(out=Sbf[b][:, :, 0], in_=Z0[:, 0, :], func=AF.Copy), [("Z0",)], [("Sbf", b)])
            else:
                P.op("act", lambda e, b=b: e.activation(out=Sbf[b][:, :, 0], in_=V[1 - b][:, 0, :, NB - 1], func=AF.Copy), [("V", 1 - b)], [("Sbf", b)])
            for t in range(NB):
                if t == 0:
                    src, zr = (Z0, [("Z0",)]) if k == 0 else (None, [("V", 1 - b)])
                    zp = Z0[:, :, :] if k == 0 else V[1 - b][:, :, :, NB - 1]
                    zp0 = Z0[:, 0, :] if k == 0 else V[1 - b][:, 0, :, NB - 1]
                    zp1 = Z0[:, 1, :] if k == 0 else V[1 - b][:, 1, :, NB - 1]
                else:
                    zr = [("V", b)]
                    zp, zp0, zp1 = V[b][:, :, :, t - 1], V[b][:, 0, :, t - 1], V[b][:, 1, :, t - 1]
                P.op("dve", lambda e, zp=zp: e.tensor_tensor(out=T1[:, :, :], in0=zp, in1=ARR[:, :, :], op=ALU.mult), zr + [("tab",), ("T1",)], [("T1",)])
                P.op("dve", lambda e, zp1=zp1: e.tensor_tensor(out=T2[:, 0, :], in0=zp1, in1=AIP[:, 0, :], op=ALU.mult), zr + [("tab",), ("T2",)], [("T2",)])
                P.op("dve", lambda e, zp0=zp0: e.tensor_tensor(out=T2[:, 1, :], in0=zp0, in1=AIP[:, 1, :], op=ALU.mult), zr + [("tab",), ("T2",)], [("T2",)])
                P.op("dve", lambda e: e.tensor_tensor(out=T1[:, :, :], in0=T1[:, :, :], in1=T2[:, :, :], op=ALU.add), [("T1",), ("T2",)], [("T1",)])
                P.op("dve", lambda e, b=b, t=t: e.tensor_tensor(out=V[b][:, :, :, t], in0=V[b][:, :, :, t], in1=T1[:, :, :], op=ALU.add), [("T1",), ("V", b)], [("V", b)])
            P.op("act", lambda e, b=b: e.activation(out=Sbf[b][:, :, 1:NB], in_=V[b][:, 0, :, 0:NB - 1], func=AF.Copy), [("V", b)], [("Sbf", b)])
            for tile in range(2):
                yb_ = yr % 2
                yr += 1
                pyv = psy[yb_][:, :].rearrange("p (n l) -> p l n", l=8)
                for l in range(8):
                    nmm = 8 + l + 1
                    i = 0
                    for j in range(8):
                        g = tile * 8 + j
                        P.op("pe", lambda e, g=g, l=l, i=i, nmm=nmm, pyv=pyv, b=b: e.matmul(pyv[:, l, :], lhsT=CzPadL[:, l, g, :], rhs=Sbf[b][:, g, :], start=(i == 0), stop=(i == nmm - 1)),
                             [("Sbf", b), ("tab",)], [("psy", yb_)])
                        i += 1
                    for kk in range(l + 1):
                        P.op("pe", lambda e, kk=kk, l=l, i=i, nmm=nmm, pyv=pyv, tile=tile, hbv=hbv: e.matmul(pyv[:, l, :], lhsT=Kbd[:, tile, l - kk, :], rhs=hbv[tile][:, kk, :], start=(i == 0), stop=(i == nmm - 1)),
                             [("hb", b), ("tabk",)], [("psy", yb_)])
                        i += 1
                ob = orr % 4
                orr += 1
                yb = yb_
                P.op("act", lambda e, yb=yb: e.activation(out=ysb[yb][:, :], in_=psy[yb][:, :], func=AF.Copy), [("psy", yb)], [("ysb", yb)])
                P.op("pool", lambda e, yb=yb, b=b, tile=tile: e.tensor_scalar(out=xg[yb][:, :], in0=hb[b][:, tile, :], scalar1=DCOL[:, tile:tile + 1], scalar2=None, op0=ALU.mult),
                     [("hb", b), ("par",)], [("xg", yb)])
                P.op("pool", lambda e, yb=yb: e.tensor_tensor(out=xg[yb][:, :], in0=xg[yb][:, :], in1=ysb[yb][:, :], op=ALU.add), [("xg", yb), ("ysb", yb)], [("xg", yb)])
                P.op("pool", lambda e, yb=yb: e.tensor_tensor(out=wg[yb][:, :], in0=xg[yb][:, :], in1=xg[yb][:, :], op=ALU.mult), [("xg", yb)], [("wg", yb)])
                P.op("pool", lambda e, yb=yb: e.tensor_scalar(out=wg[yb][:, :], in0=wg[yb][:, :], scalar1=0.044715, scalar2=1.0, op0=ALU.mult, op1=ALU.add), [("wg", yb)], [("wg", yb)])
                P.op("pool", lambda e, yb=yb: e.tensor_tensor(out=wg[yb][:, :], in0=wg[yb][:, :], in1=xg[yb][:, :], op=ALU.mult), [("wg", yb), ("xg", yb)], [("wg", yb)])
                P.op("act", lambda e, yb=yb: e.activation(out=wg[yb][:, :], in_=wg[yb][:, :], func=AF.Sigmoid, scale=1.5957691216), [("wg", yb)], [("wg", yb)])
                P.op("pool", lambda e, yb=yb, ob=ob: e.tensor_tensor(out=og[ob][:, :], in0=wg[yb][:, :], in1=xg[yb][:, :], op=ALU.mult), [("wg", yb), ("xg", yb)], [("og", ob)])
                P.dma("sp", lambda e, ob=ob, tile=tile, k=k: e.dma_start(out=ybd[k * 256 + tile * 128:k * 256 + (tile + 1) * 128, :], in_=og[ob][:, :]), [("og", ob)], [("yb",)], sem=("og", ob))
        P.barrier()


def ssm_host_inputs_blk(lam_re, lam_im, log_dt, b_re, b_im, c_re, c_im, d, core):
    m = ssm_host_inputs(np.zeros((256, 1), np.float32), lam_re, lam_im, log_dt, b_re, b_im, c_re, c_im, d, core)
    gs = slice(core * GPC, (core + 1) * GPC)
    cr = c_re[gs].transpose(2, 0, 1)
    ci = c_im[gs].transpose(2, 0, 1)
    m["C2"] = np.ascontiguousarray(np.concatenate([ci, cr], 0).reshape(128, GPC * 16))
    br = b_re[gs].transpose(1, 0, 2)
    bi = b_im[gs].transpose(1, 0, 2)
    m["BQ1"] = np.ascontiguousarray(np.concatenate([br, bi], 0).reshape(128, GPC * 16))
    m["BQ2"] = np.ascontiguousarray(np.concatenate([bi, br], 0).reshape(128, GPC * 16))
    gm = np.zeros((128, GPC, 8), np.float32)
    for g in range(GPC):
        gm[:, g, g % 8] = 1.0
    m["GMASK"] = gm.reshape(128, GPC * 8)
    del m["hT"]
    return m
```

```python
import numpy as np
import concourse.bass as bass
import concourse.mybir as mybir
from concourse.bass_utils import run_bass_kernel_spmd

F32 = mybir.dt.float32
BF16 = mybir.dt.bfloat16
AF = mybir.ActivationFunctionType
ALU = mybir.AluOpType

NCORES = 8
D = 2048
SEQ = 8192
TOK = SEQ // NCORES
KT = D // 128
DFF = 4 * D
CW = 31
HALO = 32
EPS = 1e-6
SAME_ENGINE_SYNC = True
NOSYNC_ENGINES = ("pe",)


class Prog:
    ENG = ("pe", "act", "dve", "pool", "sp")

    def __init__(self, nc):
        self.nc = nc
        self.ops = []
        self.last_w = {}
        self.readers = {}
        self.dma_sems = {}
        self.epoch = 0

    def _deps(self, reads, writes):
        deps = set()
        for t in reads:
            w = self.last_w.get(t)
            if w is not None:
                deps.add(w)
        for t in writes:
            w = self.last_w.get(t)
            if w is not None:
                deps.add(w)
            for r in self.readers.get(t, ()):
                deps.add(r)
        return deps

    def _commit(self, idx, reads, writes):
        o = self.ops[idx]
        for t in reads:
            lst = self.readers.setdefault(t, [])
            if o["dma"] is None:
                lst[:] = [r for r in lst if not (self.ops[r]["dma"] is None and self.ops[r]["eng"] == o["eng"])]
            lst.append(idx)
        for t in writes:
            self.last_w[t] = idx
            self.readers[t] = []

    def barrier(self):
        nc = self.nc
        scr = self.barrier_scratch
        idx = len(self.ops)
        deps = self._deps([], [("phase",)])
        self.epoch += 1
        self.ops.append(dict(eng="dve", fn=lambda e: e.memset(scr[:, :], 0.0), deps=deps, dma=None, ms=None, inc=16, ep=self.epoch))
        self.last_w[("phase",)] = idx
        self.readers[("phase",)] = []
        return idx

    def op(self, eng, fn, reads=(), writes=(), nosame=False):
        idx = len(self.ops)
        reads = list(reads) + [("phase",)]
        deps = self._deps(reads, writes)
        deps.discard(idx)
        if nosame:
            deps = {d for d in deps if not (self.ops[d]["dma"] is None and self.ops[d]["eng"] == eng)}
        self.ops.append(dict(eng=eng, fn=fn, deps=deps, dma=None, ms=None, inc=16, ep=self.epoch))
        self._commit(idx, reads, writes)
        return idx

    def dma(self, eng, fn, reads, writes, sem, inc=16):
        idx = len(self.ops)
        reads = list(reads) + [("phase",)]
        deps = self._deps(reads, writes)
        cnt = self.dma_sems.setdefault(sem, [0])
        cnt[0] += inc
        self.ops.append(dict(eng=eng, fn=fn, deps=deps, dma=(sem, cnt[0]), ms=None, inc=inc, ep=self.epoch))
        self._commit(idx, reads, writes)
        return idx

    def emit(self, final_wait_ops=()):
        nc = self.nc
        ops = self.ops
        needed = set()
        for o in ops:
            for d in o["deps"]:
                po = ops[d]
                if po["dma"] is None and o["dma"] is None and po["eng"] == o["eng"] and ((not SAME_ENGINE_SYNC) or o["eng"] in NOSYNC_ENGINES):
                    continue
                needed.add(d)
        counters = {}
        for i, o in enumerate(ops):
            if o["dma"] is None and i in needed:
                k_ = (o["eng"], o["ep"])
                counters[k_] = counters.get(k_, 0) + 1
                o["ms"] = counters[k_]
        from contextlib import ExitStack
        with ExitStack() as es:
            esem = {k_: es.enter_context(nc.semaphore("e_%s_%d" % k_)) for k_ in counters}
            dsem = {k: es.enter_context(nc.semaphore("d_" + str(k))) for k in self.dma_sems}
            block = es.enter_context(nc.Block())

            dma_hist = {}
            for oi, o_ in enumerate(ops):
                if o_["dma"] is not None:
                    dma_hist.setdefault(o_["dma"][0], []).append((oi, o_["dma"][1]))

            def run_engine(eng_name, eng):
                waited = {}
                for i, o in enumerate(ops):
                    if o["eng"] != eng_name:
                        continue
                    w = {}
                    for d in o["deps"]:
                        po = ops[d]
                        if po["dma"] is not None:
                            key = ("d", po["dma"][0])
                            val = po["dma"][1]
                            for (oi, cv_) in dma_hist[po["dma"][0]]:
                                if oi < i and cv_ > val:
                                    val = cv_
                        else:
                            if po["eng"] == eng_name and o["dma"] is None and ((not SAME_ENGINE_SYNC) or eng_name in NOSYNC_ENGINES):
                                continue
                            key = ("e", (po["eng"], po["ep"]))
                            val = po["ms"]
                        if val > w.get(key, 0):
                            w[key] = val
                    for key, val in w.items():
                        if waited.get(key, 0) >= val:
                            continue
                        waited[key] = val
                        s = dsem[key[1]] if key[0] == "d" else esem[key[1]]
                        eng.wait_ge(s, val)
                    ins = o["fn"](eng)
                    if o["dma"] is not None:
                        if o["inc"] == 1:
                            ins.then_inc(dsem[o["dma"][0]])
                        else:
                            ins.then_inc(dsem[o["dma"][0]], 16)
                    elif o["ms"] is not None:
                        ins.then_inc(esem[(eng_name, o["ep"])], 1)
                if eng_name == "sp":
                    for i in final_wait_ops:
                        po = ops[i]
                        eng.wait_ge(dsem[po["dma"][0]], po["dma"][1])

            @block.tensor
            def _(e):
                run_engine("pe", e)

            @block.scalar
            def _(e):
                run_engine("act", e)

            @block.vector
            def _(e):
                run_engine("dve", e)

            @block.gpsimd
            def _(e):
                run_engine("pool", e)

            @block.sync
            def _(e):
                run_engine("sp", e)


class Ctx:
    def __init__(self, nc, es):
        self.nc = nc
        self.es = es
        self.P = Prog(nc)
        self.ps_rr = 0

    def sb(self, name, shape, dt):
        return self.es.enter_context(self.nc.sbuf_tensor(name, shape, dt))

    def ps(self, name, shape, dt=F32):
        return self.es.enter_context(self.nc.psum_tensor(name, shape, dt))


def emit_rmsnorm(c, xT, hT, gcol, epsc, ones, sqb, rstd, psb, n_tok=TOK, x_tag="x", h_tag="h"):
    P = c.P
    nh = n_tok // 512
    i = 0
    for kt in range(KT):
        for h in range(nh):
            s = i % 2
            i += 1
            P.op("act", lambda e, kt=kt, s=s, h=h: e.activation(out=sqb[:, s, :], in_=xT[:, kt, h * 512:(h + 1) * 512], func=AF.Square),
                 reads=[(x_tag, kt)], writes=[("tmp", s)])
            P.op("pe", lambda e, kt=kt, s=s, h=h: e.matmul(psb[h][:, :], lhsT=ones[:, :], rhs=sqb[:, s, :],
                                                          start=(kt == 0), stop=(kt == KT - 1)),
                 reads=[("tmp", s), ("ones",)], writes=[("psn", h)])
    for h in range(nh):
        P.op("act", lambda e, h=h: e.activation(out=rstd[:, h * 512:(h + 1) * 512], in_=psb[h][:, :], func=AF.Sqrt, bias=epsc[:, 0:1], scale=1.0 / D),
             reads=[("psn", h), ("epsc",)], writes=[("rstd", h)])
        P.op("dve", lambda e, h=h: e.reciprocal(out=rstd[:, h * 512:(h + 1) * 512], in_=rstd[:, h * 512:(h + 1) * 512]),
             reads=[("rstd", h)], writes=[("rstd", h)])
    for kt in range(KT):
        eng = "dve"
        P.op(eng, lambda e, kt=kt: e.scalar_tensor_tensor(out=hT[:, kt, :n_tok], in0=xT[:, kt, :n_tok], scalar=gcol[:, kt:kt + 1],
                                                          in1=rstd[:, :n_tok], op0=ALU.mult, op1=ALU.mult),
             reads=[(x_tag, kt), ("gcol",)] + [("rstd", h) for h in range(nh)], writes=[(h_tag, kt)])


class WStream:
    def __init__(self, c, nslots, slot_elems, name):
        self.c = c
        self.n = nslots
        self.buf = c.sb(name, [128, nslots, slot_elems], BF16)
        self.i = 0
        self.name = name

    def load(self, src_ap, shape):
        s = self.i % self.n
        a, b = shape
        view = self.buf[:, s, :a * b].rearrange("p (a b) -> p a b", a=a)
        tag = (self.name, s)
        stage = getattr(self, "stage", None)
        if stage is None:
            self.i += 1
            self.c.P.dma("pool", lambda e: e.dma_start(out=view, in_=src_ap), reads=[], writes=[tag], sem=(self.name, s))
            return view, tag
        sview = stage[:, s, :a * b].rearrange("p (a b) -> p a b", a=a)
        stag = (self.name + "_st", s)
        self.c.P.dma("sp", lambda e: e.dma_start(out=sview, in_=src_ap), reads=[], writes=[stag], sem=(self.name + "_st", s))
        ceng = ("act", "dve")[self.i % 2]
        self.i += 1
        if ceng == "act":
            self.c.P.op("act", lambda e: e.activation(out=view, in_=sview, func=AF.Copy), reads=[stag], writes=[tag])
        else:
            self.c.P.op("dve", lambda e: e.tensor_copy(out=view, in_=sview), reads=[stag], writes=[tag])
        return view, tag


def emit_mlp(c, xT, hT, w_up, w_down, ws, hid, tmp, psbanks):
    P = c.P
    wu = w_up.rearrange("(kt p) m -> p kt m", p=128)
    wd = w_down.rearrange("(kt p) m -> p kt m", p=128)
    nb = len(psbanks)
    NH = TOK // 512
    HT = DFF // 128 // 2
    hidv = hid.rearrange("p k t -> p (k t)").rearrange("p (k t) -> p k t", k=HT)
    for hh in range(2):
        for ml in range(HT):
            mt = hh * HT + ml
            wv, wtag = ws.load(wu[:, :, mt * 128:(mt + 1) * 128], (KT, 128))
            for half in range(NH):
                tsl = slice(half * 512, (half + 1) * 512)
                b = c.ps_rr % nb
                c.ps_rr += 1
                for kt in range(KT):
                    P.op("pe", lambda e, kt=kt, b=b, wv=wv, tsl=tsl: e.matmul(psbanks[b][:, :], lhsT=wv[:, kt, :], rhs=hT[:, kt, tsl], start=(kt == 0), stop=(kt == KT - 1)),
                         reads=[wtag, ("h", kt)], writes=[("ps", b)])
                ts_ = c.ps_rr % 2
                P.op("act", lambda e, b=b, ts_=ts_: e.activation(out=tmp[:, ts_, :], in_=psbanks[b][:, :], func=AF.Relu),
                     reads=[("ps", b)], writes=[("tmp", ts_)])
                eng = "dve" if (ml + half) % 2 == 0 else "pool"
                P.op(eng, lambda e, ml=ml, ts_=ts_, tsl=tsl: e.tensor_tensor(out=hidv[:, ml, tsl], in0=tmp[:, ts_, :], in1=tmp[:, ts_, :], op=ALU.mult),
                     reads=[("tmp", ts_)], writes=[("hid", ml)])
        for dt_ in range(KT):
            bs = []
            for half in range(NH):
                bs.append(c.ps_rr % nb)
                c.ps_rr += 1
            for q in range(HT // 16):
                k0 = hh * HT + q * 16
                wv, wtag = ws.load(wd[:, k0:k0 + 16, dt_ * 128:(dt_ + 1) * 128], (16, 128))
                for k2 in range(16):
                    kl = q * 16 + k2
                    for half in range(NH):
                        tsl = slice(half * 512, (half + 1) * 512)
                        P.op("pe", lambda e, k2=k2, kl=kl, b=bs[half], wv=wv, tsl=tsl: e.matmul(psbanks[b][:, :], lhsT=wv[:, k2, :], rhs=hidv[:, kl, tsl],
                                                                                          start=(kl == 0), stop=(kl == HT - 1)),
                             reads=[wtag, ("hid", kl)], writes=[("ps", bs[half])])
            for half in range(NH):
                tsl = slice(half * 512, (half + 1) * 512)
                P.op("dve", lambda e, dt_=dt_, b=bs[half], tsl=tsl: e.tensor_tensor(out=xT[:, dt_, tsl], in0=xT[:, dt_, tsl], in1=psbanks[b][:, :], op=ALU.add),
                     reads=[("ps", bs[half]), ("x", dt_)], writes=[("x", dt_)])


GPC = 16
TCH = 256
PI = float(np.pi)


def build_ssm(seq=SEQ):
    from contextlib import ExitStack
    nc = bass.Bass("TRN2", target_bir_lowering=False)
    din = lambda n, s: nc.dram_tensor(n, s, F32, kind="ExternalInput").ap()
    hT_d = din("hT", [256, seq])
    LR1_d, LI1_d, DT1_d = din("LR1", [128, GPC]), din("LI1", [128, GPC]), din("DT1", [128, GPC])
    LR2_d, LI2_d, DT2_d = din("LR2", [128, 128]), din("LI2", [128, 128]), din("DT2", [128, 128])
    BRT_d, BIT_d = din("BRT", [128, 128]), din("BIT", [128, 128])
    C1_d = din("C1", [128, GPC * 16])
    SGN_d, MASK_d, DCOL_d = din("SGN", [128, 1]), din("MASK", [128, 8]), din("DCOL", [128, 2])
    yT_d = nc.dram_tensor("yT", [256, seq], F32, kind="ExternalOutput").ap()
    nchunk = seq // TCH
    with ExitStack() as es:
        c = Ctx(nc, es)
        P = c.P
        sbt = lambda n, s, dt=F32: c.sb(n, s, dt)
        LR1, LI1, DT1 = sbt("LR1s", [128, GPC]), sbt("LI1s", [128, GPC]), sbt("DT1s", [128, GPC])
        LR2, LI2, DT2 = sbt("LR2s", [128, 128]), sbt("LI2s", [128, 128]), sbt("DT2s", [128, 128])
        BRT, BIT = sbt("BRTs", [128, 128]), sbt("BITs", [128, 128])
        C1 = sbt("C1s", [128, GPC * 16])
        SGN, MASK, DCOL = sbt("SGNs", [128, 1]), sbt("MASKs", [128, 8]), sbt("DCOLs", [128, 2])
        npi = sbt("npi", [128, 1])
        for i, (s_, d_) in enumerate([(LR1, LR1_d), (LI1, LI1_d), (DT1, DT1_d), (LR2, LR2_d), (LI2, LI2_d), (DT2, DT2_d),
                                      (BRT, BRT_d), (BIT, BIT_d), (C1, C1_d), (SGN, SGN_d), (MASK, MASK_d), (DCOL, DCOL_d)]):
            P.dma("sp", lambda e, s_=s_, d_=d_: e.dma_start(out=s_[:, :], in_=d_[:, :]), [], [("par",)], sem="par")
        P.op("dve", lambda e: e.memset(npi[:, :], -PI), [], [("par",)])

        def abar(LR, LI, DT, n, nm):
            dt = sbt(nm + "dt", [128, n]); t1 = sbt(nm + "t1", [128, n]); t2 = sbt(nm + "t2", [128, n])
            ar = sbt(nm + "ar", [128, n]); ai = sbt(nm + "ai", [128, n]); mg = sbt(nm + "mg", [128, n])
            T = [(nm,)]
            R = [("par",), (nm,)]
            P.op("act", lambda e: e.activation(out=dt[:, :], in_=DT[:, :], func=AF.Exp), R, T)
            P.op("dve", lambda e: e.tensor_tensor(out=t1[:, :], in0=LR[:, :], in1=dt[:, :], op=ALU.mult), R, T)
            P.op("act", lambda e: e.activation(out=mg[:, :], in_=t1[:, :], func=AF.Exp), R, T)
            P.op("dve", lambda e: e.tensor_tensor(out=t1[:, :], in0=LI[:, :], in1=dt[:, :], op=ALU.mult), R, T)
            ki = sbt(nm + "ki", [128, n], mybir.dt.int32); kf = sbt(nm + "kf", [128, n])

            def sin_of(dst, shift):
                P.op("dve", lambda e: e.tensor_scalar(out=t2[:, :], in0=t1[:, :], scalar1=shift, scalar2=None, op0=ALU.add), R, T)
                P.op("dve", lambda e: e.tensor_scalar(out=kf[:, :], in0=t2[:, :], scalar1=1.0 / (2 * PI), scalar2=0.5, op0=ALU.mult, op1=ALU.add), R, T)
                P.op("dve", lambda e: e.tensor_copy(out=ki[:, :], in_=kf[:, :]), R, T)
                P.op("dve", lambda e: e.tensor_copy(out=kf[:, :], in_=ki[:, :]), R, T)
                P.op("dve", lambda e: e.scalar_tensor_tensor(out=t2[:, :], in0=kf[:, :], scalar=-2 * PI, in1=t2[:, :], op0=ALU.mult, op1=ALU.add), R, T)
                P.op("dve", lambda e: e.tensor_scalar(out=kf[:, :], in0=t2[:, :], scalar1=-PI, scalar2=2 * PI, op0=ALU.is_lt, op1=ALU.mult), R, T)
                P.op("dve", lambda e: e.tensor_tensor(out=t2[:, :], in0=t2[:, :], in1=kf[:, :], op=ALU.add), R, T)
                P.op("dve", lambda e: e.tensor_scalar(out=t2[:, :], in0=t2[:, :], scalar1=-PI, scalar2=PI, op0=ALU.max, op1=ALU.min), R, T)
                P.op("act", lambda e: e.activation(out=t2[:, :], in_=t2[:, :], func=AF.Sin), R, T)
                P.op("dve", lambda e: e.tensor_tensor(out=dst[:, :], in0=mg[:, :], in1=t2[:, :], op=ALU.mult), R, T)
            sin_of(ai, 0.0)
            sin_of(ar, 0.5 * PI)
            return ar, ai

        ar1, ai1 = abar(LR1, LI1, DT1, GPC, "a1")
        ARR = sbt("ARR", [128, 2, GPC]); AIP = sbt("AIP", [128, 2, GPC])
        T1t = [("tab1",)]
        R1 = [("a1",), ("tab1",)]
        P.op("dve", lambda e: e.tensor_copy(out=ARR[:, 0, :], in_=ar1[:, :]), R1, T1t)
        P.op("dve", lambda e: e.tensor_copy(out=ARR[:, 1, :], in_=ar1[:, :]), R1, T1t)
        P.op("dve", lambda e: e.tensor_copy(out=AIP[:, 0, :], in_=ai1[:, :]), R1, T1t)
        P.op("dve", lambda e: e.tensor_scalar(out=AIP[:, 1, :], in0=ai1[:, :], scalar1=-1.0, scalar2=None, op0=ALU.mult), R1, T1t)

        ar2, ai2 = abar(LR2, LI2, DT2, 128, "a2")
        w = [sbt(f"w{i}", [128, 128]) for i in range(6)]
        R2 = [("a2",), ("par",), ("tab2",)]
        T2t = [("tab2",)]
        tt = lambda o, a, b, op: P.op("dve", lambda e: e.tensor_tensor(out=o, in0=a, in1=b, op=op), R2, T2t)
        nr, den, kre, kim, bbr, bbi = w
        P.op("dve", lambda e: e.tensor_scalar(out=nr[:, :], in0=ar2[:, :], scalar1=-1.0, scalar2=None, op0=ALU.add), R2, T2t)
        tt(den[:, :], LR2[:, :], LR2[:, :], ALU.mult)
        tt(kre[:, :], LI2[:, :], LI2[:, :], ALU.mult)
        tt(den[:, :], den[:, :], kre[:, :], ALU.add)
        P.op("dve", lambda e: e.reciprocal(out=den[:, :], in_=den[:, :]), R2, T2t)
        tt(kre[:, :], nr[:, :], LR2[:, :], ALU.mult)
        tt(kim[:, :], ai2[:, :], LI2[:, :], ALU.mult)
        tt(kre[:, :], kre[:, :], kim[:, :], ALU.add)
        tt(kre[:, :], kre[:, :], den[:, :], ALU.mult)
        tt(kim[:, :], ai2[:, :], LR2[:, :], ALU.mult)
        tt(bbr[:, :], nr[:, :], LI2[:, :], ALU.mult)
        tt(kim[:, :], kim[:, :], bbr[:, :], ALU.subtract)
        tt(kim[:, :], kim[:, :], den[:, :], ALU.mult)
        tt(bbr[:, :], kre[:, :], BRT[:, :], ALU.mult)
        tt(bbi[:, :], kim[:, :], BIT[:, :], ALU.mult)
        tt(bbr[:, :], bbr[:, :], bbi[:, :], ALU.subtract)
        tt(bbi[:, :], kre[:, :], BIT[:, :], ALU.mult)
        tt(nr[:, :], kim[:, :], BRT[:, :], ALU.mult)
        tt(bbi[:, :], bbi[:, :], nr[:, :], ALU.add)
        nbbi = den
        P.op("dve", lambda e: e.tensor_scalar(out=nbbi[:, :], in0=bbi[:, :], scalar1=-1.0, scalar2=None, op0=ALU.mult), R2, T2t)
        BzPad = sbt("BzPad", [128, GPC, 2, 128], BF16)
        for g in range(GPC):
            tile, j = g // 8, g % 8
            cs = slice(tile * 64, tile * 64 + 64)
            for zz, (lo, hi) in enumerate([(bbr, bbi), (nbbi, bbr)]):
                P.op("dve", lambda e, g=g, zz=zz, lo=lo, cs=cs, j=j: e.tensor_scalar(out=BzPad[:, g, zz, 0:64], in0=lo[:, cs], scalar1=MASK[:, j:j + 1], scalar2=None, op0=ALU.mult), R2, T2t)
                P.op("dve", lambda e, g=g, zz=zz, hi=hi, cs=cs, j=j: e.tensor_scalar(out=BzPad[:, g, zz, 64:128], in0=hi[:, cs], scalar1=MASK[:, j:j + 1], scalar2=None, op0=ALU.mult), R2, T2t)
        CzPad = sbt("CzPad", [128, GPC, 128], BF16)
        Cz = sbt("Cz", [128, GPC * 16])
        P.op("dve", lambda e: e.memset(CzPad[:, :, :], 0.0), R2, T2t)
        P.op("dve", lambda e: e.tensor_scalar(out=Cz[:, :], in0=C1[:, :], scalar1=SGN[:, 0:1], scalar2=None, op0=ALU.mult), R2, T2t)
        for g in range(GPC):
            j = g % 8
            P.op("dve", lambda e, g=g, j=j: e.tensor_copy(out=CzPad[:, g, 16 * j:16 * j + 16], in_=Cz[:, 16 * g:16 * g + 16]), R2, T2t)

        V = [sbt(f"V{i}", [128, 2, GPC, TCH]) for i in range(2)]
        Sbf = [sbt(f"Sbf{i}", [128, GPC, TCH], BF16) for i in range(2)]
        hc = [sbt(f"hc{i}", [128, 2, TCH]) for i in range(2)]
        hb = [sbt(f"hb{i}", [128, 2, TCH], BF16) for i in range(2)]
        T1 = sbt("T1", [128, 2, GPC]); T2 = sbt("T2", [128, 2, GPC])
        Z0 = sbt("Z0", [128, 2, GPC])
        P.op("dve", lambda e: e.memset(Z0[:, :, :], 0.0), [], [("Z0",)])
        ysb = [sbt(f"ysb{i}", [128, TCH]) for i in range(2)]
        xg = [sbt(f"xg{i}", [128, TCH]) for i in range(2)]
        wg = [sbt(f"wg{i}", [128, TCH]) for i in range(2)]
        og = [sbt(f"og{i}", [128, TCH]) for i in range(4)]
        pse = [c.ps(f"pse{i}", [128, TCH]) for i in range(4)]
        psy = [c.ps(f"psy{i}", [128, TCH]) for i in range(2)]
        hTv = hT_d.rearrange("(tile p) t -> p tile t", p=128)
        yTv = yT_d.rearrange("(tile p) t -> p tile t", p=128)
        outs = []
        er = 0
        yr = 0
        orr = 0
        for k in range(nchunk):
            b = k % 2
            tsl = slice(k * TCH, (k + 1) * TCH)
            P.dma("sp", lambda e, b=b, tsl=tsl: e.dma_start(out=hc[b][:, :, :], in_=hTv[:, :, tsl]), [], [("hc", b)], sem=("hc", b))
            P.op("act", lambda e, b=b: e.activation(out=hb[b][:, :, :], in_=hc[b][:, :, :], func=AF.Copy), [("hc", b)], [("hb", b)])
            for g in range(GPC):
                for zz in range(2):
                    pb = er % 4
                    er += 1
                    P.op("pe", lambda e, g=g, zz=zz, pb=pb, b=b: e.matmul(pse[pb][:, :], lhsT=BzPad[:, g, zz, :], rhs=hb[b][:, g // 8, :], start=True, stop=True),
                         [("hb", b), ("tab2",)], [("pse", pb)])
                    P.op("act", lambda e, g=g, zz=zz, pb=pb, b=b: e.activation(out=V[b][:, zz, g, :], in_=pse[pb][:, :], func=AF.Copy),
                         [("pse", pb)], [("V", b)])
            for t in range(TCH):
                if t == 0:
                    zp = Z0[:, :, :] if k == 0 else V[1 - b][:, :, :, TCH - 1]
                    zr = [("Z0",)] if k == 0 else [("V", 1 - b)]
                else:
                    zp = V[b][:, :, :, t - 1]
                    zr = [("V", b)]
                zp0 = Z0[:, 0, :] if (k == 0 and t == 0) else (V[1 - b][:, 0, :, TCH - 1] if t == 0 else V[b][:, 0, :, t - 1])
                zp1 = Z0[:, 1, :] if (k == 0 and t == 0) else (V[1 - b][:, 1, :, TCH - 1] if t == 0 else V[b][:, 1, :, t - 1])
                P.op("dve", lambda e, zp=zp: e.tensor_tensor(out=T1[:, :, :], in0=zp, in1=ARR[:, :, :], op=ALU.mult), zr + [("tab1",), ("T1",)], [("T1",)])
                P.op("dve", lambda e, zp1=zp1: e.tensor_tensor(out=T2[:, 0, :], in0=zp1, in1=AIP[:, 0, :], op=ALU.mult), zr + [("tab1",), ("T2",)], [("T2",)])
                P.op("dve", lambda e, zp0=zp0: e.tensor_tensor(out=T2[:, 1, :], in0=zp0, in1=AIP[:, 1, :], op=ALU.mult), zr + [("tab1",), ("T2",)], [("T2",)])
                P.op("dve", lambda e: e.tensor_tensor(out=T1[:, :, :], in0=T1[:, :, :], in1=T2[:, :, :], op=ALU.add), [("T1",), ("T2",)], [("T1",)])
                P.op("dve", lambda e, b=b, t=t: e.tensor_tensor(out=V[b][:, :, :, t], in0=V[b][:, :, :, t], in1=T1[:, :, :], op=ALU.add), [("T1",), ("V", b)], [("V", b)])
            P.op("act", lambda e, b=b: e.activation(out=Sbf[b][:, :, :], in_=V[b][:, 0, :, :], func=AF.Copy), [("V", b)], [("Sbf", b)])
            for tile in range(2):
                yb = yr % 2
                yr += 1
                for j in range(8):
                    g = tile * 8 + j
                    P.op("pe", lambda e, g=g, j=j, yb=yb, b=b: e.matmul(psy[yb][:, :], lhsT=CzPad[:, g, :], rhs=Sbf[b][:, g, :], start=(j == 0), stop=(j == 7)),
                         [("Sbf", b), ("tab2",)], [("psy", yb)])
                ob = orr % 4
                orr += 1
                P.op("act", lambda e, yb=yb: e.activation(out=ysb[yb][:, :], in_=psy[yb][:, :], func=AF.Copy), [("psy", yb)], [("ysb", yb)])
                P.op("pool", lambda e, yb=yb, b=b, tile=tile: e.tensor_scalar(out=xg[yb][:, :], in0=hc[b][:, tile, :], scalar1=DCOL[:, tile:tile + 1], scalar2=None, op0=ALU.mult),
                     [("hc", b), ("par",)], [("xg", yb)])
                P.op("pool", lambda e, yb=yb: e.tensor_tensor(out=xg[yb][:, :], in0=xg[yb][:, :], in1=ysb[yb][:, :], op=ALU.add), [("xg", yb), ("ysb", yb)], [("xg", yb)])
                P.op("pool", lambda e, yb=yb: e.tensor_tensor(out=wg[yb][:, :], in0=xg[yb][:, :], in1=xg[yb][:, :], op=ALU.mult), [("xg", yb)], [("wg", yb)])
                P.op("pool", lambda e, yb=yb: e.tensor_scalar(out=wg[yb][:, :], in0=wg[yb][:, :], scalar1=0.044715, scalar2=1.0, op0=ALU.mult, op1=ALU.add), [("wg", yb)], [("wg", yb)])
                P.op("pool", lambda e, yb=yb: e.tensor_tensor(out=wg[yb][:, :], in0=wg[yb][:, :], in1=xg[yb][:, :], op=ALU.mult), [("wg", yb), ("xg", yb)], [("wg", yb)])
                P.op("act", lambda e, yb=yb: e.activation(out=wg[yb][:, :], in_=wg[yb][:, :], func=AF.Sigmoid, scale=1.5957691216), [("wg", yb)], [("wg", yb)])
                P.op("pool", lambda e, yb=yb, ob=ob: e.tensor_tensor(out=og[ob][:, :], in0=wg[yb][:, :], in1=xg[yb][:, :], op=ALU.mult), [("wg", yb), ("xg", yb)], [("og", ob)])
                outs.append(P.dma("sp", lambda e, ob=ob, tile=tile, tsl=tsl: e.dma_start(out=yTv[:, tile, tsl], in_=og[ob][:, :]), [("og", ob)], [], sem=("og", ob)))
        P.emit(final_wait_ops=outs[-4:])
    return nc


def ssm_host_inputs(hT_core, lam_re, lam_im, log_dt, b_re, b_im, c_re, c_im, d, core):
    G0 = core * GPC
    gs = slice(G0, G0 + GPC)
    lr, li, ld = lam_re[gs], lam_im[gs], log_dt[gs]
    LR1 = np.ascontiguousarray(np.concatenate([lr.T, lr.T], 0))
    LI1 = np.ascontiguousarray(np.concatenate([li.T, li.T], 0))
    DT1 = np.ascontiguousarray(np.broadcast_to(ld[None, :], (128, GPC)))

    def l2(a):
        a = a.reshape(2, 8, 1, 64)
        a = np.broadcast_to(a, (2, 8, 16, 64))
        return np.ascontiguousarray(a.transpose(1, 2, 0, 3).reshape(128, 128))
    LR2, LI2 = l2(lr), l2(li)
    DT2 = l2(np.broadcast_to(ld[:, None], (GPC, 64)))

    def bt(bb):
        a = bb.reshape(2, 8, 64, 16).transpose(1, 3, 0, 2)
        return np.ascontiguousarray(a.reshape(128, 128))
    BRT, BIT = bt(b_re[gs]), bt(b_im[gs])
    cr = c_re[gs].transpose(2, 0, 1)
    ci = c_im[gs].transpose(2, 0, 1)
    C1 = np.ascontiguousarray(np.concatenate([cr, ci], 0).reshape(128, GPC * 16))
    SGN = np.concatenate([np.ones((64, 1), np.float32), -np.ones((64, 1), np.float32)], 0)
    MASK = np.zeros((128, 8), np.float32)
    for j in range(8):
        MASK[16 * j:16 * j + 16, j] = 1.0
    DCOL = np.ascontiguousarray(d[core * 256:(core + 1) * 256].reshape(2, 128).T)
    return {"hT": np.ascontiguousarray(hT_core), "LR1": LR1, "LI1": LI1, "DT1": DT1, "LR2": LR2, "LI2": LI2, "DT2": DT2,
            "BRT": BRT, "BIT": BIT, "C1": C1, "SGN": SGN, "MASK": MASK, "DCOL": DCOL}


def build_dense(mode):
    from contextlib import ExitStack
    nc = bass.Bass("TRN2", target_bir_lowering=False)
    din = lambda n, s: nc.dram_tensor(n, s, F32, kind="ExternalInput").ap()
    x_d = din("xT", [D, TOK])
    vec_names = ["g_mlp", "g_next"]
    if mode == "conv":
        xh_d = din("xhT", [D, HALO])
        flag_d = din("flag", [128, 1])
        w1_d = din("w_in", [D, 2 * D])
        w2_d = din("w_out", [D, D])
        dw_d = din("dw", [CW, D])
        vec_names += ["g_mix", "b_in_a", "b_in_g", "dw_b", "ln_g", "ln_b", "b_out"]
    else:
        y_d = din("yT", [D, TOK])
        w1_d = din("w_glu", [D, 2 * D])
    vec_d = {n: din(n, [D]) for n in vec_names}
    wu_d = din("w_up", [D, DFF])
    wd_d = din("w_down", [DFF, D])
    xo_d = nc.dram_tensor("xo", [D, TOK], F32, kind="ExternalOutput").ap()
    ho_d = nc.dram_tensor("ho", [D, TOK], F32, kind="ExternalOutput").ap()
    NT = TOK + HALO if mode == "conv" else TOK
    with ExitStack() as es:
        c = Ctx(nc, es)
        P = c.P
        xT = c.sb("xT_sb", [128, KT, TOK], F32)
        hT = c.sb("hT_sb", [128, KT, NT], BF16)
        big = c.sb("big", [128, KT * TOK], F32)
        bigf = big[:, :].rearrange("p (k t) -> p k t", k=KT)
        hid = big[:, :].bitcast(BF16)[:, :DFF // 128 * 512].rearrange("p (k t) -> p k t", k=DFF // 128) if hasattr(big[:, :], "bitcast") else None
        tmp = c.sb("tmp", [128, 2, 512], F32)
        rstd = c.sb("rstd", [128, NT], F32)
        ones = c.sb("ones", [128, 128], F32)
        epsc = c.sb("epsc", [128, 1], F32)
        vec = {n: c.sb("v_" + n, [128, KT], F32) for n in vec_names}
        ws = WStream(c, 2, KT * 128, "ws")
        if mode == "glu":
            ws.stage = c.sb("wstage", [128, 2, KT * 128], F32)
        banks = [c.ps(f"b{i}", [128, 512]) for i in range(6)]
        psn = [c.ps(f"n{i}", [128, 512]) for i in range(2)]
        for kt in range(KT):
            P.dma("sp", lambda e, kt=kt: e.dma_start(out=xT[:, kt, :], in_=x_d[kt * 128:(kt + 1) * 128, :]), [], [("x", kt)], sem="xin")
        for n in vec_names:
            P.dma("sp", lambda e, n=n: e.dma_start(out=vec[n][:, :], in_=vec_d[n].rearrange("(kt p) -> p kt", p=128), allow_slow_non_contiguous=True),
                  [], [("gcol",)], sem="vin")
        P.op("dve", lambda e: e.memset(ones[:, :], 1.0), [], [("ones",)])
        P.op("dve", lambda e: e.memset(epsc[:, :], EPS), [], [("epsc",)])

        def pair_glu(w_d, ba, bg, n_cols, consume):
            wv_ = w_d.rearrange("(kt p) m -> p kt m", p=128)
            chunks = [(s0, min(s0 + 512, n_cols)) for s0 in range(0, n_cols, 512)]
            for mt in range(KT):
                wa, ta = ws.load(wv_[:, :, mt * 128:(mt + 1) * 128], (KT, 128))
                wg_, tg = ws.load(wv_[:, :, D + mt * 128:D + (mt + 1) * 128], (KT, 128))
                for ci, (s0, s1) in enumerate(chunks):
                    ba_, bb_ = ci, 3 + ci
                    for (wv, wt, b) in ((wa, ta, ba_), (wg_, tg, bb_)):
                        for kt in range(KT):
                            P.op("pe", lambda e, kt=kt, wv=wv, b=b, s0=s0, s1=s1: e.matmul(banks[b][:, :s1 - s0], lhsT=wv[:, kt, :], rhs=hT[:, kt, s0:s1],
                                                                                      start=(kt == 0), stop=(kt == KT - 1)),
                                 reads=[wt, ("h", kt)], writes=[("ps", b)])
                    consume(mt, ci, s0, s1, banks[ba_], banks[bb_], ("ps", ba_), ("ps", bb_))

        if mode == "conv":
            xh = c.sb("xh", [128, KT, HALO], F32)
            flag = c.sb("flag_sb", [128, 1], F32)
            dwc = c.sb("dwc", [128, KT, CW], F32)
            vext = c.sb("vext", [128, 2, TOK + HALO], F32)
            mean = c.sb("mean", [128, TOK], F32)
            P.dma("sp", lambda e: e.dma_start(out=xh[:, :, :], in_=xh_d.rearrange("(kt p) t -> p kt t", p=128)), [], [("xh",)], sem="xh")
            P.dma("sp", lambda e: e.dma_start(out=flag[:, :], in_=flag_d[:, :]), [], [("gcol",)], sem="vin")
            for kt in range(KT):
                P.dma("sp", lambda e, kt=kt: e.dma_start(out=dwc[:, kt, :], in_=dw_d[:, kt * 128:(kt + 1) * 128].rearrange("j p -> p j"), allow_slow_non_contiguous=True),
                      [], [("gcol",)], sem="vin")
            emit_rmsnorm(c, xT, hT, vec["g_mix"], epsc, ones, tmp, rstd, psn)
            for kt in range(KT):
                s = kt % 2
                P.op("act", lambda e, kt=kt, s=s: e.activation(out=tmp[:, s, :HALO], in_=xh[:, kt, :], func=AF.Square), [("xh",)], [("tmp", s)])
                P.op("pe", lambda e, kt=kt, s=s: e.matmul(psn[0][:, :HALO], lhsT=ones[:, :], rhs=tmp[:, s, :HALO], start=(kt == 0), stop=(kt == KT - 1)),
                     [("tmp", s), ("ones",)], [("psn", 0)])
            P.op("act", lambda e: e.activation(out=rstd[:, TOK:NT], in_=psn[0][:, :HALO], func=AF.Sqrt, bias=epsc[:, 0:1], scale=1.0 / D), [("psn", 0), ("epsc",)], [("rstdh",)])
            P.op("dve", lambda e: e.reciprocal(out=rstd[:, TOK:NT], in_=rstd[:, TOK:NT]), [("rstdh",)], [("rstdh",)])
            for kt in range(KT):
                P.op("dve", lambda e, kt=kt: e.scalar_tensor_tensor(out=hT[:, kt, TOK:NT], in0=xh[:, kt, :], scalar=vec["g_mix"][:, kt:kt + 1], in1=rstd[:, TOK:NT],
                                                                   op0=ALU.mult, op1=ALU.mult), [("xh",), ("rstdh",), ("gcol",)], [("h", kt)])

            def consume(mt, ci, s0, s1, pa, pg, ta, tg):
                vs = mt % 2
                n = s1 - s0
                d0 = 0 if s0 == TOK else HALO + s0
                P.op("act", lambda e, n=n, mt=mt: e.activation(out=tmp[:, 0, :n], in_=pg[:, :n], func=AF.Sigmoid, bias=vec["b_in_g"][:, mt:mt + 1], scale=1.0),
                     [tg, ("gcol",)], [("tmp", 0)])
                P.op("dve", lambda e, n=n, mt=mt, vs=vs, d0=d0: e.scalar_tensor_tensor(out=vext[:, vs, d0:d0 + n], in0=pa[:, :n], scalar=vec["b_in_a"][:, mt:mt + 1], in1=tmp[:, 0, :n],
                                                                                    op0=ALU.add, op1=ALU.mult), [ta, ("tmp", 0), ("gcol",)], [("vext", vs)])
                if s0 == TOK:
                    P.op("dve", lambda e, vs=vs: e.tensor_scalar(out=vext[:, vs, 0:HALO], in0=vext[:, vs, 0:HALO], scalar1=flag[:, 0:1], scalar2=None, op0=ALU.mult),
                         [("vext", vs), ("gcol",)], [("vext", vs)])
                    P.op("dve", lambda e, mt=mt, vs=vs: e.tensor_scalar(out=bigf[:, mt, :], in0=vext[:, vs, 2:2 + TOK], scalar1=dwc[:, mt, 0:1], scalar2=vec["dw_b"][:, mt:mt + 1],
                                                                      op0=ALU.mult, op1=ALU.add), [("vext", vs), ("gcol",), ("bigbar",)], [("cv", mt)])
                    for j in range(1, CW):
                        P.op("dve", lambda e, mt=mt, vs=vs, j=j: e.scalar_tensor_tensor(out=bigf[:, mt, :], in0=vext[:, vs, 2 + j:2 + j + TOK], scalar=dwc[:, mt, j:j + 1], in1=bigf[:, mt, :],
                                                                                      op0=ALU.mult, op1=ALU.add), [("vext", vs), ("cv", mt), ("gcol",)], [("cv", mt)])
            pair_glu(w1_d, vec["b_in_a"], vec["b_in_g"], NT, consume)
            for h in range(2):
                tsl = slice(h * 512, (h + 1) * 512)
                for kt in range(KT):
                    s = kt % 2
                    P.op("pe", lambda e, kt=kt, h=h, tsl=tsl: e.matmul(banks[h][:, :], lhsT=ones[:, :], rhs=bigf[:, kt, tsl], start=(kt == 0), stop=(kt == KT - 1)),
                         [("cv", kt), ("ones",)], [("ps", h)])
                    P.op("act", lambda e, kt=kt, s=s, tsl=tsl: e.activation(out=tmp[:, s, :], in_=bigf[:, kt, tsl], func=AF.Square), [("cv", kt)], [("tmp", s)])
                    P.op("pe", lambda e, kt=kt, h=h, s=s: e.matmul(banks[2 + h][:, :], lhsT=ones[:, :], rhs=tmp[:, s, :], start=(kt == 0), stop=(kt == KT - 1)),
                         [("tmp", s), ("ones",)], [("ps", 2 + h)])
                P.op("act", lambda e, h=h, tsl=tsl: e.activation(out=mean[:, tsl], in_=banks[h][:, :], func=AF.Copy, scale=1.0 / D), [("ps", h)], [("mean",)])
                P.op("act", lambda e, h=h, tsl=tsl: e.activation(out=rstd[:, tsl], in_=banks[2 + h][:, :], func=AF.Copy, scale=1.0 / D), [("ps", 2 + h)], [("rstd", h)])
                P.op("dve", lambda e, tsl=tsl: e.tensor_tensor(out=tmp[:, 0, :], in0=mean[:, tsl], in1=mean[:, tsl], op=ALU.mult), [("mean",)], [("tmp", 0)])
                P.op("dve", lambda e, tsl=tsl: e.tensor_tensor(out=rstd[:, tsl], in0=rstd[:, tsl], in1=tmp[:, 0, :], op=ALU.subtract), [("rstd", h), ("tmp", 0)], [("rstd", h)])
                P.op("act", lambda e, tsl=tsl: e.activation(out=rstd[:, tsl], in_=rstd[:, tsl], func=AF.Sqrt, bias=epsc[:, 0:1], scale=1.0), [("rstd", h), ("epsc",)], [("rstd", h)])
                P.op("dve", lambda e, tsl=tsl: e.reciprocal(out=rstd[:, tsl], in_=rstd[:, tsl]), [("rstd", h)], [("rstd", h)])
            for kt in range(KT):
                eng = "dve" if kt % 2 == 0 else "pool"
                P.op(eng, lambda e, kt=kt: e.tensor_tensor(out=bigf[:, kt, :], in0=bigf[:, kt, :], in1=mean[:, :], op=ALU.subtract), [("cv", kt), ("mean",)], [("cv", kt)])
                P.op(eng, lambda e, kt=kt: e.tensor_tensor(out=bigf[:, kt, :], in0=bigf[:, kt, :], in1=rstd[:, :TOK], op=ALU.mult), [("cv", kt), ("rstd", 0), ("rstd", 1)], [("cv", kt)])
                P.op("act", lambda e, kt=kt: e.activation(out=hT[:, kt, :TOK], in_=bigf[:, kt, :], func=AF.Silu, bias=vec["ln_b"][:, kt:kt + 1], scale=vec["ln_g"][:, kt:kt + 1]),
                     [("cv", kt), ("gcol",)], [("h", kt)])
            wo = w2_d.rearrange("(kt p) m -> p kt m", p=128)
            for dt_ in range(KT):
                wv, wt = ws.load(wo[:, :, dt_ * 128:(dt_ + 1) * 128], (KT, 128))
                for h in range(2):
                    tsl = slice(h * 512, (h + 1) * 512)
                    b = c.ps_rr % 6
                    c.ps_rr += 1
                    for kt in range(KT):
                        P.op("pe", lambda e, kt=kt, wv=wv, b=b, tsl=tsl: e.matmul(banks[b][:, :], lhsT=wv[:, kt, :], rhs=hT[:, kt, tsl], start=(kt == 0), stop=(kt == KT - 1)),
                             [wt, ("h", kt)], [("ps", b)])
                    P.op("dve", lambda e, dt_=dt_, b=b, tsl=tsl: e.scalar_tensor_tensor(out=xT[:, dt_, tsl], in0=banks[b][:, :], scalar=vec["b_out"][:, dt_:dt_ + 1], in1=xT[:, dt_, tsl],
                                                                                      op0=ALU.add, op1=ALU.add), [("ps", b), ("x", dt_), ("gcol",)], [("x", dt_)])
            big_readers = [("cv", kt) for kt in range(KT)]
        else:
            for kt in range(KT):
                P.dma("sp", lambda e, kt=kt: e.dma_start(out=bigf[:, kt, :], in_=y_d[kt * 128:(kt + 1) * 128, :]), [], [("cv", kt)], sem="yin")
                P.op("act", lambda e, kt=kt: e.activation(out=hT[:, kt, :], in_=bigf[:, kt, :], func=AF.Copy), [("cv", kt)], [("h", kt)])

            def consume(mt, ci, s0, s1, pa, pg, ta, tg):
                n = s1 - s0
                P.op("act", lambda e, n=n: e.activation(out=tmp[:, 0, :n], in_=pg[:, :n], func=AF.Sigmoid), [tg], [("tmp", 0)])
                P.op("dve", lambda e, n=n: e.tensor_tensor(out=tmp[:, 0, :n], in0=tmp[:, 0, :n], in1=pa[:, :n], op=ALU.mult), [ta, ("tmp", 0)], [("tmp", 0)])
                P.op("dve", lambda e, n=n, mt=mt, s0=s0, s1=s1: e.tensor_tensor(out=xT[:, mt, s0:s1], in0=xT[:, mt, s0:s1], in1=tmp[:, 0, :n], op=ALU.add),
                     [("tmp", 0), ("x", mt)], [("x", mt)])
            pair_glu(w1_d, None, None, TOK, consume)
            big_readers = [("cv", kt) for kt in range(KT)]

        P.op("dve", lambda e: e.memset(epsc[:, :], EPS), big_readers + [("epsc",)], [("epsc",)] + [("hid", m) for m in range(DFF // 128)])
        emit_rmsnorm(c, xT, hT, vec["g_mlp"], epsc, ones, tmp, rstd, psn)
        emit_mlp(c, xT, hT, wu_d, wd_d, ws, hid, tmp, banks)
        outs = []
        for kt in range(KT):
            outs.append(P.dma("sp", lambda e, kt=kt: e.dma_start(out=xo_d[kt * 128:(kt + 1) * 128, :], in_=xT[:, kt, :]), [("x", kt)], [], sem="xout"))
        P.op("dve", lambda e: e.memset(epsc[:, :], EPS), [("hid", m) for m in range(DFF // 128)] + [("epsc",)], [("epsc",)] + [("hn", kt) for kt in range(KT)])
        emit_rmsnorm(c, xT, bigf, vec["g_next"], epsc, ones, tmp, rstd, psn, h_tag="hn")
        for kt in range(KT):
            outs.append(P.dma("sp", lambda e, kt=kt: e.dma_start(out=ho_d[kt * 128:(kt + 1) * 128, :], in_=bigf[:, kt, :]), [("hn", kt)], [], sem="hout"))
        P.emit(final_wait_ops=outs)
    return nc


U32 = mybir.dt.uint32
NK = SEQ // TCH
KPC = TOK // TCH


def dense_phase(c, mode, xT, ones, epsc, io):
    from contextlib import ExitStack
    nc = c.nc
    P = c.P
    vec_names = ["g_mlp", "g_next"] + (["g_mix", "b_in_a", "b_in_g", "dw_b", "ln_g", "ln_b", "b_out"] if mode == "conv" else [])
    NT = TOK + HALO if mode == "conv" else TOK
    with ExitStack() as es:
        sb = lambda n, shp, dt=F32: es.enter_context(nc.sbuf_tensor(f"{n}_{io['uid']}", shp, dt))
        hT = sb("hT_sb", [128, KT, NT], BF16)
        big = sb("big", [128, KT * TOK], F32)
        bigf = big[:, :].rearrange("p (k t) -> p k t", k=KT)
        hid = big[:, :].bitcast(BF16)[:, :DFF // 128 * 512].rearrange("p (k t) -> p k t", k=DFF // 128)
        tmp = sb("tmp", [128, 2, 512], F32)
        rstd = sb("rstd", [128, NT], F32)
        vec = {n: sb("v_" + n, [128, KT], F32) for n in vec_names}
        ws = WStream.__new__(WStream)
        ws.c, ws.n, ws.i, ws.name = c, 2, 0, "ws"
        ws.buf = sb("ws", [128, 2, KT * 128], BF16)
        ws.stage = sb("wstage", [128, 2, KT * 128], F32)
        banks = [es.enter_context(nc.psum_tensor(f"b{i}_{io['uid']}", [128, 512], F32)) for i in range(6)]
        psn = [es.enter_context(nc.psum_tensor(f"n{i}_{io['uid']}", [128, 512], F32)) for i in range(2)]
        for n in vec_names:
            P.dma("sp", lambda e, n=n: e.dma_start(out=vec[n][:, :], in_=io[n].rearrange("(kt p) -> p kt", p=128), allow_slow_non_contiguous=True),
                  [], [("gcol",)], sem="vin")

        def pair_glu(w_d, n_cols, consume):
            wv_ = w_d.rearrange("(kt p) m -> p kt m", p=128)
            chunks = [(s0, min(s0 + 512, n_cols)) for s0 in range(0, n_cols, 512)]
            for mt in range(KT):
                wa, ta = ws.load(wv_[:, :, mt * 128:(mt + 1) * 128], (KT, 128))
                wg_, tg = ws.load(wv_[:, :, D + mt * 128:D + (mt + 1) * 128], (KT, 128))
                for ci, (s0, s1) in enumerate(chunks):
                    ba_, bb_ = ci, 3 + ci
                    for (wv, wt, b) in ((wa, ta, ba_), (wg_, tg, bb_)):
                        for kt in range(KT):
                            P.op("pe", lambda e, kt=kt, wv=wv, b=b, s0=s0, s1=s1: e.matmul(banks[b][:, :s1 - s0], lhsT=wv[:, kt, :], rhs=hT[:, kt, s0:s1],
                                                                                      start=(kt == 0), stop=(kt == KT - 1)),
                                 reads=[wt, ("h", kt)], writes=[("ps", b)])
                    consume(mt, ci, s0, s1, banks[ba_], banks[bb_], ("ps", ba_), ("ps", bb_))

        if mode == "conv":
            xh = sb("xh", [128, KT, HALO], F32)
            flag = sb("flag_sb", [128, 1], F32)
            dwc = sb("dwc", [128, KT, CW], F32)
            vext = sb("vext", [128, 1, TOK + HALO], F32)
            mean = sb("mean", [128, TOK], F32)
            if io["halo"][0] == "dram":
                P.dma("sp", lambda e: e.dma_start(out=xh[:, :, :], in_=io["halo"][1].rearrange("(kt p) t -> p kt t", p=128)), [], [("xh",)], sem="xh")
            else:
                xtg, idx2 = io["halo"][1], io["halo"][2]
                for kt in range(KT):
                    P.dma("pool", lambda e, kt=kt: e.indirect_dma_start(out=xh[:, kt, :], out_offset=None, in_=xtg[:, :],
                                                                        in_offset=bass.IndirectOffsetOnAxis(ap=idx2[:, 0:1], axis=0), element_offset=kt * 128 * HALO),
                          [("xtg",), ("idx",)], [("xh",)], sem="xh")
            P.dma("sp", lambda e: e.dma_start(out=flag[:, :], in_=io["flag"][:, :]), [], [("gcol",)], sem="vin")
            for kt in range(KT):
                P.dma("sp", lambda e, kt=kt: e.dma_start(out=dwc[:, kt, :], in_=io["dw"][:, kt * 128:(kt + 1) * 128].rearrange("j p -> p j"), allow_slow_non_contiguous=True),
                      [], [("gcol",)], sem="vin")
            emit_rmsnorm(c, xT, hT, vec["g_mix"], epsc, ones, tmp, rstd, psn)
            for kt in range(KT):
                s = kt % 2
                P.op("act", lambda e, kt=kt, s=s: e.activation(out=tmp[:, s, :HALO], in_=xh[:, kt, :], func=AF.Square), [("xh",)], [("tmp", s)])
                P.op("pe", lambda e, kt=kt, s=s: e.matmul(psn[0][:, :HALO], lhsT=ones[:, :], rhs=tmp[:, s, :HALO], start=(kt == 0), stop=(kt == KT - 1)),
                     [("tmp", s), ("ones",)], [("psn", 0)])
            P.op("act", lambda e: e.activation(out=rstd[:, TOK:NT], in_=psn[0][:, :HALO], func=AF.Sqrt, bias=epsc[:, 0:1], scale=1.0 / D), [("psn", 0), ("epsc",)], [("rstdh",)])
            P.op("dve", lambda e: e.reciprocal(out=rstd[:, TOK:NT], in_=rstd[:, TOK:NT]), [("rstdh",)], [("rstdh",)])
            for kt in range(KT):
                P.op("dve", lambda e, kt=kt: e.scalar_tensor_tensor(out=hT[:, kt, TOK:NT], in0=xh[:, kt, :], scalar=vec["g_mix"][:, kt:kt + 1], in1=rstd[:, TOK:NT],
                                                                   op0=ALU.mult, op1=ALU.mult), [("xh",), ("rstdh",), ("gcol",)], [("h", kt)])

            def consume(mt, ci, s0, s1, pa, pg, ta, tg):
                vs = 0
                n = s1 - s0
                d0 = 0 if s0 == TOK else HALO + s0
                P.op("act", lambda e, n=n, mt=mt: e.activation(out=tmp[:, 0, :n], in_=pg[:, :n], func=AF.Sigmoid, bias=vec["b_in_g"][:, mt:mt + 1], scale=1.0),
                     [tg, ("gcol",)], [("tmp", 0)])
                P.op("dve", lambda e, n=n, mt=mt, vs=vs, d0=d0: e.scalar_tensor_tensor(out=vext[:, vs, d0:d0 + n], in0=pa[:, :n], scalar=vec["b_in_a"][:, mt:mt + 1], in1=tmp[:, 0, :n],
                                                                                    op0=ALU.add, op1=ALU.mult), [ta, ("tmp", 0), ("gcol",)], [("vext", vs)])
                if s0 == TOK:
                    P.op("dve", lambda e, vs=vs: e.tensor_scalar(out=vext[:, vs, 0:HALO], in0=vext[:, vs, 0:HALO], scalar1=flag[:, 0:1], scalar2=None, op0=ALU.mult),
                         [("vext", vs), ("gcol",)], [("vext", vs)])
                    P.op("dve", lambda e, mt=mt, vs=vs: e.tensor_scalar(out=bigf[:, mt, :], in0=vext[:, vs, 2:2 + TOK], scalar1=dwc[:, mt, 0:1], scalar2=vec["dw_b"][:, mt:mt + 1],
                                                                      op0=ALU.mult, op1=ALU.add), [("vext", vs), ("gcol",)], [("cv", mt)])
                    for j in range(1, CW):
                        P.op("dve", lambda e, mt=mt, vs=vs, j=j: e.scalar_tensor_tensor(out=bigf[:, mt, :], in0=vext[:, vs, 2 + j:2 + j + TOK], scalar=dwc[:, mt, j:j + 1], in1=bigf[:, mt, :],
                                                                                      op0=ALU.mult, op1=ALU.add), [("vext", vs), ("cv", mt), ("gcol",)], [("cv", mt)])
            pair_glu(io["w_in"], NT, consume)
            for h in range(2):
                tsl = slice(h * 512, (h + 1) * 512)
                for kt in range(KT):
                    s = kt % 2
                    P.op("pe", lambda e, kt=kt, h=h, tsl=tsl: e.matmul(banks[h][:, :], lhsT=ones[:, :], rhs=bigf[:, kt, tsl], start=(kt == 0), stop=(kt == KT - 1)),
                         [("cv", kt), ("ones",)], [("ps", h)])
                    P.op("act", lambda e, kt=kt, s=s, tsl=tsl: e.activation(out=tmp[:, s, :], in_=bigf[:, kt, tsl], func=AF.Square), [("cv", kt)], [("tmp", s)])
                    P.op("pe", lambda e, kt=kt, h=h, s=s: e.matmul(banks[2 + h][:, :], lhsT=ones[:, :], rhs=tmp[:, s, :], start=(kt == 0), stop=(kt == KT - 1)),
                         [("tmp", s), ("ones",)], [("ps", 2 + h)])
                P.op("act", lambda e, h=h, tsl=tsl: e.activation(out=mean[:, tsl], in_=banks[h][:, :], func=AF.Copy, scale=1.0 / D), [("ps", h)], [("mean",)])
                P.op("act", lambda e, h=h, tsl=tsl: e.activation(out=rstd[:, tsl], in_=banks[2 + h][:, :], func=AF.Copy, scale=1.0 / D), [("ps", 2 + h)], [("rstd", h)])
                P.op("dve", lambda e, tsl=tsl: e.tensor_tensor(out=tmp[:, 0, :], in0=mean[:, tsl], in1=mean[:, tsl], op=ALU.mult), [("mean",)], [("tmp", 0)])
                P.op("dve", lambda e, tsl=tsl: e.tensor_tensor(out=rstd[:, tsl], in0=rstd[:, tsl], in1=tmp[:, 0, :], op=ALU.subtract), [("rstd", h), ("tmp", 0)], [("rstd", h)])
                P.op("act", lambda e, tsl=tsl: e.activation(out=rstd[:, tsl], in_=rstd[:, tsl], func=AF.Sqrt, bias=epsc[:, 0:1], scale=1.0), [("rstd", h), ("epsc",)], [("rstd", h)])
                P.op("dve", lambda e, tsl=tsl: e.reciprocal(out=rstd[:, tsl], in_=rstd[:, tsl]), [("rstd", h)], [("rstd", h)])
            for kt in range(KT):
                eng = "dve" if kt % 2 == 0 else "pool"
                P.op(eng, lambda e, kt=kt: e.tensor_tensor(out=bigf[:, kt, :], in0=bigf[:, kt, :], in1=mean[:, :], op=ALU.subtract), [("cv", kt), ("mean",)], [("cv", kt)])
                P.op(eng, lambda e, kt=kt: e.tensor_tensor(out=bigf[:, kt, :], in0=bigf[:, kt, :], in1=rstd[:, :TOK], op=ALU.mult), [("cv", kt), ("rstd", 0), ("rstd", 1)], [("cv", kt)])
                P.op("act", lambda e, kt=kt: e.activation(out=hT[:, kt, :TOK], in_=bigf[:, kt, :], func=AF.Silu, bias=vec["ln_b"][:, kt:kt + 1], scale=vec["ln_g"][:, kt:kt + 1]),
                     [("cv", kt), ("gcol",)], [("h", kt)])
            wo = io["w_out"].rearrange("(kt p) m -> p kt m", p=128)
            for dt_ in range(KT):
                wv, wt = ws.load(wo[:, :, dt_ * 128:(dt_ + 1) * 128], (KT, 128))
                for h in range(2):
                    tsl = slice(h * 512, (h + 1) * 512)
                    b = c.ps_rr % 6
                    c.ps_rr += 1
                    for kt in range(KT):
                        P.op("pe", lambda e, kt=kt, wv=wv, b=b, tsl=tsl: e.matmul(banks[b][:, :], lhsT=wv[:, kt, :], rhs=hT[:, kt, tsl], start=(kt == 0), stop=(kt == KT - 1)),
                             [wt, ("h", kt)], [("ps", b)])
                    P.op("dve", lambda e, dt_=dt_, b=b, tsl=tsl: e.scalar_tensor_tensor(out=xT[:, dt_, tsl], in0=banks[b][:, :], scalar=vec["b_out"][:, dt_:dt_ + 1], in1=xT[:, dt_, tsl],
                                                                                      op0=ALU.add, op1=ALU.add), [("ps", b), ("x", dt_), ("gcol",)], [("x", dt_)])
        else:
            ygath, idx1 = io["ygath"], io["idx1"]
            for r in range(NCORES):
                for kk in range(KPC):
                    for tile in range(2):
                        kt = 2 * r + tile
                        eo = ((r * NK + kk) * 256 + tile * 128) * TCH
                        P.dma("pool", lambda e, kt=kt, kk=kk, eo=eo: e.indirect_dma_start(out=hT[:, kt, kk * TCH:(kk + 1) * TCH], out_offset=None, in_=ygath[:, :],
                                                                                          in_offset=bass.IndirectOffsetOnAxis(ap=idx1[:, 0:1], axis=0), element_offset=eo),
                              [("ygath",), ("idx",)], [("h", kt)], sem=("yg", kt % 4))

            def consume(mt, ci, s0, s1, pa, pg, ta, tg):
                n = s1 - s0
                P.op("act", lambda e, n=n: e.activation(out=tmp[:, 0, :n], in_=pg[:, :n], func=AF.Sigmoid), [tg], [("tmp", 0)])
                P.op("dve", lambda e, n=n: e.tensor_tensor(out=tmp[:, 0, :n], in0=tmp[:, 0, :n], in1=pa[:, :n], op=ALU.mult), [ta, ("tmp", 0)], [("tmp", 0)])
                P.op("dve", lambda e, n=n, mt=mt, s0=s0, s1=s1: e.tensor_tensor(out=xT[:, mt, s0:s1], in0=xT[:, mt, s0:s1], in1=tmp[:, 0, :n], op=ALU.add),
                     [("tmp", 0), ("x", mt)], [("x", mt)])
            pair_glu(io["w_glu"], TOK, consume)

        big_readers = [("cv", kt) for kt in range(KT)]
        P.op("dve", lambda e: e.memset(epsc[:, :], EPS), big_readers + [("epsc",)], [("epsc",)] + [("hid", m) for m in range(DFF // 128)])
        emit_rmsnorm(c, xT, hT, vec["g_mlp"], epsc, ones, tmp, rstd, psn)
        emit_mlp(c, xT, hT, io["w_up"], io["w_down"], ws, hid, tmp, banks)
        outs = []
        nxt = io["next"]
        if nxt[0] == "ssm":
            hb_d = nxt[1]
            emit_rmsnorm(c, xT, hT, vec["g_next"], epsc, ones, tmp, rstd, psn)
            hbv = hb_d.rearrange("kk (kt p) t -> p kt kk t", p=128)
            for kt in range(KT):
                P.dma("sp", lambda e, kt=kt: e.dma_start(out=hbv[:, kt, :, :], in_=hT[:, kt, :TOK].rearrange("p (kk t) -> p kk t", kk=KPC)), [("h", kt)], [("hb",)], sem="hbout")
        elif nxt[0] == "conv":
            xtb = nxt[1]
            for kt in range(KT):
                P.dma("sp", lambda e, kt=kt: e.dma_start(out=xtb[kt * 128:(kt + 1) * 128, :], in_=xT[:, kt, TOK - HALO:TOK]), [("x", kt)], [("xtb",)], sem="xtbout")
        else:
            out_d = nxt[1]
            P.op("dve", lambda e: e.memset(epsc[:, :], EPS), [("hid", m) for m in range(DFF // 128)] + [("epsc",)], [("epsc",)] + [("hn", kt) for kt in range(KT)])
            emit_rmsnorm(c, xT, bigf, vec["g_next"], epsc, ones, tmp, rstd, psn, h_tag="hn")
            for kt in range(KT):
                outs.append(P.dma("sp", lambda e, kt=kt: e.dma_start(out=out_d[kt * 128:(kt + 1) * 128, :], in_=bigf[:, kt, :]), [("hn", kt)], [], sem="hout"))
        P.barrier()
    return outs


def ssm_phase(c, io, seq=SEQ):
    from contextlib import ExitStack
    nc = c.nc
    P = c.P
    uid = io["uid"]
    LR1_d, LI1_d, DT1_d, LR2_d, LI2_d, DT2_d = io["LR1"], io["LI1"], io["DT1"], io["LR2"], io["LI2"], io["DT2"]
    BRT_d, BIT_d, C1_d, SGN_d, MASK_d, DCOL_d = io["BRT"], io["BIT"], io["C1"], io["SGN"], io["MASK"], io["DCOL"]
    hgath, idx0, ybd = io["hgath"], io["idx0"], io["yb"]
    nchunk = seq // TCH
    with ExitStack() as es:
        sbt = lambda n, s, dt=F32: es.enter_context(nc.sbuf_tensor(f"{n}_{uid}", s, dt))
        LR1, LI1, DT1 = sbt("LR1s", [128, GPC]), sbt("LI1s", [128, GPC]), sbt("DT1s", [128, GPC])
        LR2, LI2, DT2 = sbt("LR2s", [128, 128]), sbt("LI2s", [128, 128]), sbt("DT2s", [128, 128])
        BRT, BIT = sbt("BRTs", [128, 128]), sbt("BITs", [128, 128])
        C1 = sbt("C1s", [128, GPC * 16])
        SGN, MASK, DCOL = sbt("SGNs", [128, 1]), sbt("MASKs", [128, 8]), sbt("DCOLs", [128, 2])
        npi = sbt("npi", [128, 1])
        for i, (s_, d_) in enumerate([(LR1, LR1_d), (LI1, LI1_d), (DT1, DT1_d), (LR2, LR2_d), (LI2, LI2_d), (DT2, DT2_d),
                                      (BRT, BRT_d), (BIT, BIT_d), (C1, C1_d), (SGN, SGN_d), (MASK, MASK_d), (DCOL, DCOL_d)]):
            P.dma("sp", lambda e, s_=s_, d_=d_: e.dma_start(out=s_[:, :], in_=d_[:, :]), [], [("par",)], sem="par")
        P.op("dve", lambda e: e.memset(npi[:, :], -PI), [], [("par",)])

        def abar(LR, LI, DT, n, nm):
            dt = sbt(nm + "dt", [128, n]); t1 = sbt(nm + "t1", [128, n]); t2 = sbt(nm + "t2", [128, n])
            ar = sbt(nm + "ar", [128, n]); ai = sbt(nm + "ai", [128, n]); mg = sbt(nm + "mg", [128, n])
            T = [(nm,)]
            R = [("par",), (nm,)]
            P.op("act", lambda e: e.activation(out=dt[:, :], in_=DT[:, :], func=AF.Exp), R, T)
            P.op("dve", lambda e: e.tensor_tensor(out=t1[:, :], in0=LR[:, :], in1=dt[:, :], op=ALU.mult), R, T)
            P.op("act", lambda e: e.activation(out=mg[:, :], in_=t1[:, :], func=AF.Exp), R, T)
            P.op("dve", lambda e: e.tensor_tensor(out=t1[:, :], in0=LI[:, :], in1=dt[:, :], op=ALU.mult), R, T)
            ki = sbt(nm + "ki", [128, n], mybir.dt.int32); kf = sbt(nm + "kf", [128, n])

            def sin_of(dst, shift):
                P.op("dve", lambda e: e.tensor_scalar(out=t2[:, :], in0=t1[:, :], scalar1=shift, scalar2=None, op0=ALU.add), R, T)
                P.op("dve", lambda e: e.tensor_scalar(out=kf[:, :], in0=t2[:, :], scalar1=1.0 / (2 * PI), scalar2=0.5, op0=ALU.mult, op1=ALU.add), R, T)
                P.op("dve", lambda e: e.tensor_copy(out=ki[:, :], in_=kf[:, :]), R, T)
                P.op("dve", lambda e: e.tensor_copy(out=kf[:, :], in_=ki[:, :]), R, T)
                P.op("dve", lambda e: e.scalar_tensor_tensor(out=t2[:, :], in0=kf[:, :], scalar=-2 * PI, in1=t2[:, :], op0=ALU.mult, op1=ALU.add), R, T)
                P.op("dve", lambda e: e.tensor_scalar(out=kf[:, :], in0=t2[:, :], scalar1=-PI, scalar2=2 * PI, op0=ALU.is_lt, op1=ALU.mult), R, T)
                P.op("dve", lambda e: e.tensor_tensor(out=t2[:, :], in0=t2[:, :], in1=kf[:, :], op=ALU.add), R, T)
                P.op("dve", lambda e: e.tensor_scalar(out=t2[:, :], in0=t2[:, :], scalar1=-PI, scalar2=PI, op0=ALU.max, op1=ALU.min), R, T)
                P.op("act", lambda e: e.activation(out=t2[:, :], in_=t2[:, :], func=AF.Sin), R, T)
                P.op("dve", lambda e: e.tensor_tensor(out=dst[:, :], in0=mg[:, :], in1=t2[:, :], op=ALU.mult), R, T)
            sin_of(ai, 0.0)
            sin_of(ar, 0.5 * PI)
            return ar, ai

        ar1, ai1 = abar(LR1, LI1, DT1, GPC, "a1")
        ARR = sbt("ARR", [128, 2, GPC]); AIP = sbt("AIP", [128, 2, GPC])
        T1t = [("tab1",)]
        R1 = [("a1",), ("tab1",)]
        P.op("dve", lambda e: e.tensor_copy(out=ARR[:, 0, :], in_=ar1[:, :]), R1, T1t)
        P.op("dve", lambda e: e.tensor_copy(out=ARR[:, 1, :], in_=ar1[:, :]), R1, T1t)
        P.op("dve", lambda e: e.tensor_copy(out=AIP[:, 0, :], in_=ai1[:, :]), R1, T1t)
        P.op("dve", lambda e: e.tensor_scalar(out=AIP[:, 1, :], in0=ai1[:, :], scalar1=-1.0, scalar2=None, op0=ALU.mult), R1, T1t)

        ar2, ai2 = abar(LR2, LI2, DT2, 128, "a2")
        w = [sbt(f"w{i}", [128, 128]) for i in range(6)]
        R2 = [("a2",), ("par",), ("tab2",)]
        T2t = [("tab2",)]
        tt = lambda o, a, b, op: P.op("dve", lambda e: e.tensor_tensor(out=o, in0=a, in1=b, op=op), R2, T2t)
        nr, den, kre, kim, bbr, bbi = w
        P.op("dve", lambda e: e.tensor_scalar(out=nr[:, :], in0=ar2[:, :], scalar1=-1.0, scalar2=None, op0=ALU.add), R2, T2t)
        tt(den[:, :], LR2[:, :], LR2[:, :], ALU.mult)
        tt(kre[:, :], LI2[:, :], LI2[:, :], ALU.mult)
        tt(den[:, :], den[:, :], kre[:, :], ALU.add)
        P.op("dve", lambda e: e.reciprocal(out=den[:, :], in_=den[:, :]), R2, T2t)
        tt(kre[:, :], nr[:, :], LR2[:, :], ALU.mult)
        tt(kim[:, :], ai2[:, :], LI2[:, :], ALU.mult)
        tt(kre[:, :], kre[:, :], kim[:, :], ALU.add)
        tt(kre[:, :], kre[:, :], den[:, :], ALU.mult)
        tt(kim[:, :], ai2[:, :], LR2[:, :], ALU.mult)
        tt(bbr[:, :], nr[:, :], LI2[:, :], ALU.mult)
        tt(kim[:, :], kim[:, :], bbr[:, :], ALU.subtract)
        tt(kim[:, :], kim[:, :], den[:, :], ALU.mult)
        tt(bbr[:, :], kre[:, :], BRT[:, :], ALU.mult)
        tt(bbi[:, :], kim[:, :], BIT[:, :], ALU.mult)
        tt(bbr[:, :], bbr[:, :], bbi[:, :], ALU.subtract)
        tt(bbi[:, :], kre[:, :], BIT[:, :], ALU.mult)
        tt(nr[:, :], kim[:, :], BRT[:, :], ALU.mult)
        tt(bbi[:, :], bbi[:, :], nr[:, :], ALU.add)
        nbbi = den
        P.op("dve", lambda e: e.tensor_scalar(out=nbbi[:, :], in0=bbi[:, :], scalar1=-1.0, scalar2=None, op0=ALU.mult), R2, T2t)
        BzPad = sbt("BzPad", [128, GPC, 2, 128], BF16)
        for g in range(GPC):
            tile, j = g // 8, g % 8
            cs = slice(tile * 64, tile * 64 + 64)
            for zz, (lo, hi) in enumerate([(bbr, bbi), (nbbi, bbr)]):
                P.op("dve", lambda e, g=g, zz=zz, lo=lo, cs=cs, j=j: e.tensor_scalar(out=BzPad[:, g, zz, 0:64], in0=lo[:, cs], scalar1=MASK[:, j:j + 1], scalar2=None, op0=ALU.mult), R2, T2t)
                P.op("dve", lambda e, g=g, zz=zz, hi=hi, cs=cs, j=j: e.tensor_scalar(out=BzPad[:, g, zz, 64:128], in0=hi[:, cs], scalar1=MASK[:, j:j + 1], scalar2=None, op0=ALU.mult), R2, T2t)
        CzPad = sbt("CzPad", [128, GPC, 128], BF16)
        Cz = sbt("Cz", [128, GPC * 16])
        P.op("dve", lambda e: e.memset(CzPad[:, :, :], 0.0), R2, T2t)
        P.op("dve", lambda e: e.tensor_scalar(out=Cz[:, :], in0=C1[:, :], scalar1=SGN[:, 0:1], scalar2=None, op0=ALU.mult), R2, T2t)
        for g in range(GPC):
            j = g % 8
            P.op("dve", lambda e, g=g, j=j: e.tensor_copy(out=CzPad[:, g, 16 * j:16 * j + 16], in_=Cz[:, 16 * g:16 * g + 16]), R2, T2t)

        V = [sbt(f"V{i}", [128, 2, GPC, TCH]) for i in range(2)]
        Sbf = [sbt(f"Sbf{i}", [128, GPC, TCH], BF16) for i in range(2)]
        hb = [sbt(f"hb{i}", [128, 2, TCH], BF16) for i in range(2)]
        T1 = sbt("T1", [128, 2, GPC]); T2 = sbt("T2", [128, 2, GPC])
        Z0 = sbt("Z0", [128, 2, GPC])
        P.op("dve", lambda e: e.memset(Z0[:, :, :], 0.0), [], [("Z0",)])
        ysb = [sbt(f"ysb{i}", [128, TCH]) for i in range(2)]
        xg = [sbt(f"xg{i}", [128, TCH]) for i in range(2)]
        wg = [sbt(f"wg{i}", [128, TCH]) for i in range(2)]
        og = [sbt(f"og{i}", [128, TCH], BF16) for i in range(4)]
        pse = [es.enter_context(nc.psum_tensor(f"pse{i}_{uid}", [128, TCH], F32)) for i in range(4)]
        psy = [es.enter_context(nc.psum_tensor(f"psy{i}_{uid}", [128, TCH], F32)) for i in range(2)]
        outs = []
        er = 0
        yr = 0
        orr = 0
        for k in range(nchunk):
            b = k % 2
            tsl = slice(k * TCH, (k + 1) * TCH)
            for tile in range(2):
                eo = (k * D + tile * 128) * TCH
                P.dma("pool", lambda e, b=b, tile=tile, eo=eo: e.indirect_dma_start(out=hb[b][:, tile, :], out_offset=None, in_=hgath[:, :],
                                                                                   in_offset=bass.IndirectOffsetOnAxis(ap=idx0[:, 0:1], axis=0), element_offset=eo),
                      [("hgath",), ("idx",)], [("hb", b)], sem=("hbl", b, tile))
            for g in range(GPC):
                for zz in range(2):
                    pb = er % 4
                    er += 1
                    P.op("pe", lambda e, g=g, zz=zz, pb=pb, b=b: e.matmul(pse[pb][:, :], lhsT=BzPad[:, g, zz, :], rhs=hb[b][:, g // 8, :], start=True, stop=True),
                         [("hb", b), ("tab2",)], [("pse", pb)])
                    P.op("act", lambda e, g=g, zz=zz, pb=pb, b=b: e.activation(out=V[b][:, zz, g, :], in_=pse[pb][:, :], func=AF.Copy),
                         [("pse", pb)], [("V", b)])
            for t in range(TCH):
                if t == 0:
                    zp = Z0[:, :, :] if k == 0 else V[1 - b][:, :, :, TCH - 1]
                    zr = [("Z0",)] if k == 0 else [("V", 1 - b)]
                else:
                    zp = V[b][:, :, :, t - 1]
                    zr = [("V", b)]
                zp0 = Z0[:, 0, :] if (k == 0 and t == 0) else (V[1 - b][:, 0, :, TCH - 1] if t == 0 else V[b][:, 0, :, t - 1])
                zp1 = Z0[:, 1, :] if (k == 0 and t == 0) else (V[1 - b][:, 1, :, TCH - 1] if t == 0 else V[b][:, 1, :, t - 1])
                P.op("dve", lambda e, zp=zp: e.tensor_tensor(out=T1[:, :, :], in0=zp, in1=ARR[:, :, :], op=ALU.mult), zr + [("tab1",), ("T1",)], [("T1",)])
                P.op("dve", lambda e, zp1=zp1: e.tensor_tensor(out=T2[:, 0, :], in0=zp1, in1=AIP[:, 0, :], op=ALU.mult), zr + [("tab1",), ("T2",)], [("T2",)])
                P.op("dve", lambda e, zp0=zp0: e.tensor_tensor(out=T2[:, 1, :], in0=zp0, in1=AIP[:, 1, :], op=ALU.mult), zr + [("tab1",), ("T2",)], [("T2",)])
                P.op("dve", lambda e: e.tensor_tensor(out=T1[:, :, :], in0=T1[:, :, :], in1=T2[:, :, :], op=ALU.add), [("T1",), ("T2",)], [("T1",)])
                P.op("dve", lambda e, b=b, t=t: e.tensor_tensor(out=V[b][:, :, :, t], in0=V[b][:, :, :, t], in1=T1[:, :, :], op=ALU.add), [("T1",), ("V", b)], [("V", b)])
            P.op("act", lambda e, b=b: e.activation(out=Sbf[b][:, :, :], in_=V[b][:, 0, :, :], func=AF.Copy), [("V", b)], [("Sbf", b)])
            for tile in range(2):
                yb = yr % 2
                yr += 1
                for j in range(8):
                    g = tile * 8 + j
                    P.op("pe", lambda e, g=g, j=j, yb=yb, b=b: e.matmul(psy[yb][:, :], lhsT=CzPad[:, g, :], rhs=Sbf[b][:, g, :], start=(j == 0), stop=(j == 7)),
                         [("Sbf", b), ("tab2",)], [("psy", yb)])
                ob = orr % 4
                orr += 1
                P.op("act", lambda e, yb=yb: e.activation(out=ysb[yb][:, :], in_=psy[yb][:, :], func=AF.Copy), [("psy", yb)], [("ysb", yb)])
                P.op("pool", lambda e, yb=yb, b=b, tile=tile: e.tensor_scalar(out=xg[yb][:, :], in0=hb[b][:, tile, :], scalar1=DCOL[:, tile:tile + 1], scalar2=None, op0=ALU.mult),
                     [("hb", b), ("par",)], [("xg", yb)])
                P.op("pool", lambda e, yb=yb: e.tensor_tensor(out=xg[yb][:, :], in0=xg[yb][:, :], in1=ysb[yb][:, :], op=ALU.add), [("xg", yb), ("ysb", yb)], [("xg", yb)])
                P.op("pool", lambda e, yb=yb: e.tensor_tensor(out=wg[yb][:, :], in0=xg[yb][:, :], in1=xg[yb][:, :], op=ALU.mult), [("xg", yb)], [("wg", yb)])
                P.op("pool", lambda e, yb=yb: e.tensor_scalar(out=wg[yb][:, :], in0=wg[yb][:, :], scalar1=0.044715, scalar2=1.0, op0=ALU.mult, op1=ALU.add), [("wg", yb)], [("wg", yb)])
                P.op("pool", lambda e, yb=yb: e.tensor_tensor(out=wg[yb][:, :], in0=wg[yb][:, :], in1=xg[yb][:, :], op=ALU.mult), [("wg", yb), ("xg", yb)], [("wg", yb)])
                P.op("act", lambda e, yb=yb: e.activation(out=wg[yb][:, :], in_=wg[yb][:, :], func=AF.Sigmoid, scale=1.5957691216), [("wg", yb)], [("wg", yb)])
                P.op("pool", lambda e, yb=yb, ob=ob: e.tensor_tensor(out=og[ob][:, :], in0=wg[yb][:, :], in1=xg[yb][:, :], op=ALU.mult), [("wg", yb), ("xg", yb)], [("og", ob)])
                P.dma("sp", lambda e, ob=ob, tile=tile, k=k: e.dma_start(out=ybd[k * 256 + tile * 128:k * 256 + (tile + 1) * 128, :], in_=og[ob][:, :]), [("og", ob)], [("yb",)], sem=("og", ob))
        P.barrier()


SSM_KEYS = ["LR1", "LI1", "DT1", "LR2", "LI2", "DT2", "BRT", "BIT", "C1", "DCOL", "C2", "BQ1", "BQ2"]
SSM_SHAPES = {"LR1": [128, GPC], "LI1": [128, GPC], "DT1": [128, GPC], "LR2": [128, 128], "LI2": [128, 128], "DT2": [128, 128],
              "BRT": [128, 128], "BIT": [128, 128], "C1": [128, GPC * 16], "DCOL": [128, 2],
              "C2": [128, GPC * 16], "BQ1": [128, GPC * 16], "BQ2": [128, GPC * 16]}


def build_fused():
    from contextlib import ExitStack
    nc = bass.Bass("TRN2", target_bir_lowering=False)
    din = lambda n, s, dt=F32: nc.dram_tensor(n, s, dt, kind="ExternalInput").ap()
    x_d = din("xT", [D, TOK])
    xh0_d = din("xh0", [D, HALO])
    flag_d = din("flag", [128, 1])
    idx_d = din("idx", [128, 3], U32)
    mixn = din("mix_norm", [4, D]); mlpn = din("mlp_norm", [4, D]); finn = din("final_norm", [D])
    cwin = din("conv_w_in", [2, D, 2 * D]); cbin = din("conv_b_in", [2, 2 * D]); cdw = din("conv_dw", [2, CW, D])
    cdwb = din("conv_dw_b", [2, D]); clg = din("conv_ln_g", [2, D]); clb = din("conv_ln_b", [2, D])
    cwo = din("conv_w_out", [2, D, D]); cbo = din("conv_b_out", [2, D])
    wglu = din("ssm_w_glu", [2, D, 2 * D]); wup = din("mlp_w_up", [4, D, DFF]); wdn = din("mlp_w_down", [4, DFF, D])
    ssm_d = {k: din("S_" + k, [2] + SSM_SHAPES[k]) for k in SSM_KEYS}
    sgn_d = din("SGN", [128, 1]); mask_d = din("MASK", [128, 8]); gmask_d = din("GMASK", [128, GPC * 8])
    out_d = nc.dram_tensor("out", [D, TOK], F32, kind="ExternalOutput").ap()
    hb = [nc.dram_tensor(f"hb{j}", [KPC * D, TCH], BF16) for j in range(2)]
    hgath = [nc.dram_tensor(f"hgath{j}", [NCORES * KPC * D, TCH], BF16) for j in range(2)]
    yb = [nc.dram_tensor(f"yb{j}", [NK * 256, TCH], BF16) for j in range(2)]
    ygath = [nc.dram_tensor(f"ygath{j}", [NCORES * NK * 256, TCH], BF16) for j in range(2)]
    xtb = nc.dram_tensor("xtb", [D, HALO], F32)
    xtg = nc.dram_tensor("xtg", [NCORES * D, HALO], F32)
    rg = [list(range(NCORES))]
    with ExitStack() as es:
        c = Ctx(nc, es)
        P = c.P
        xT = c.sb("xT_sb", [128, KT, TOK], F32)
        ones = c.sb("ones", [128, 128], F32)
        epsc = c.sb("epsc", [128, 1], F32)
        idx = c.sb("idx_sb", [128, 3], U32)
        P.barrier_scratch = c.sb("bscr", [128, 1], F32)
        for kt in range(KT):
            P.dma("sp", lambda e, kt=kt: e.dma_start(out=xT[:, kt, :], in_=x_d[kt * 128:(kt + 1) * 128, :]), [], [("x", kt)], sem="xin")
        P.dma("sp", lambda e: e.dma_start(out=idx[:, :], in_=idx_d[:, :]), [], [("idx",)], sem="idxin")
        P.op("dve", lambda e: e.memset(ones[:, :], 1.0), [], [("ones",)])
        P.op("dve", lambda e: e.memset(epsc[:, :], EPS), [], [("epsc",)])
        outs = []
        for layer in range(4):
            j = layer // 2
            g_next = mixn[layer + 1] if layer < 3 else finn
            if layer % 2 == 0:
                io = {"uid": layer, "g_mlp": mlpn[layer], "g_next": g_next, "g_mix": mixn[layer], "b_in_a": cbin[j, 0:D], "b_in_g": cbin[j, D:2 * D],
                      "dw_b": cdwb[j], "ln_g": clg[j], "ln_b": clb[j], "b_out": cbo[j], "flag": flag_d, "dw": cdw[j], "w_in": cwin[j], "w_out": cwo[j],
                      "w_up": wup[layer], "w_down": wdn[layer],
                      "halo": ("dram", xh0_d) if layer == 0 else ("gather", xtg.ap(), idx[:, 2:3]),
                      "next": ("ssm", hb[j].ap().rearrange("(kk c) t -> kk c t", kk=KPC))}
                dense_phase(c, "conv", xT, ones, epsc, io)
                P.dma("pool", lambda e, j=j: e.collective_compute("AllGather", ALU.bypass, replica_groups=rg, ins=[hb[j].ap().opt()], outs=[hgath[j].ap().opt()]),
                      [("hb",)], [("hgath",)], sem=("cch", j), inc=1)
            else:
                io = {"uid": layer, "hgath": hgath[j].ap(), "idx0": idx[:, 0:1], "yb": yb[j].ap(), "SGN": sgn_d, "MASK": mask_d, "GMASK": gmask_d}
                for k in SSM_KEYS:
                    io[k] = ssm_d[k][j]
                ssm_phase_blk(c, io)
                P.dma("pool", lambda e, j=j: e.collective_compute("AllGather", ALU.bypass, replica_groups=rg, ins=[yb[j].ap().opt()], outs=[ygath[j].ap().opt()]),
                      [("yb",)], [("ygath",)], sem=("ccy", j), inc=1)
                io = {"uid": 10 + layer, "g_mlp": mlpn[layer], "g_next": g_next, "w_glu": wglu[j], "w_up": wup[layer], "w_down": wdn[layer],
                      "ygath": ygath[j].ap(), "idx1": idx[:, 1:2],
                      "next": ("conv", xtb.ap()) if layer < 3 else ("final", out_d)}
                outs += dense_phase(c, "glu", xT, ones, epsc, io)
                if layer < 3:
                    P.dma("pool", lambda e: e.collective_compute("AllGather", ALU.bypass, replica_groups=rg, ins=[xtb.ap().opt()], outs=[xtg.ap().opt()]),
                          [("xtb",)], [("xtg",)], sem="ccx", inc=1)
        P.emit(final_wait_ops=outs)
    return nc


_NC_CACHE = {}


def kernel(x, mix_norm, conv_w_in, conv_b_in, conv_dw, conv_dw_b, conv_ln_g, conv_ln_b, conv_w_out, conv_b_out,
           ssm_lambda_re, ssm_lambda_im, ssm_log_dt, ssm_b_re, ssm_b_im, ssm_c_re, ssm_c_im, ssm_d, ssm_w_glu,
           mlp_norm, mlp_w_up, mlp_w_down, final_norm):
    f = lambda a: np.ascontiguousarray(np.asarray(a, dtype=np.float32))
    x = f(x)
    cores = list(range(NCORES))
    if "nc" not in _NC_CACHE:
        _NC_CACHE["nc"] = build_fused()
    nc = _NC_CACHE["nc"]
    shared = {"mix_norm": f(mix_norm), "mlp_norm": f(mlp_norm), "final_norm": f(final_norm), "conv_w_in": f(conv_w_in), "conv_b_in": f(conv_b_in),
              "conv_dw": f(conv_dw), "conv_dw_b": f(conv_dw_b), "conv_ln_g": f(conv_ln_g), "conv_ln_b": f(conv_ln_b), "conv_w_out": f(conv_w_out),
              "conv_b_out": f(conv_b_out), "ssm_w_glu": f(ssm_w_glu), "mlp_w_up": f(mlp_w_up), "mlp_w_down": f(mlp_w_down)}
    lre, lim, ldt = f(ssm_lambda_re), f(ssm_lambda_im), f(ssm_log_dt)
    bre, bim, cre, cim, dsk = f(ssm_b_re), f(ssm_b_im), f(ssm_c_re), f(ssm_c_im), f(ssm_d)
    maps = []
    dummy_h = np.zeros((256, 1), np.float32)
    for c in cores:
        m = dict(shared)
        m["xT"] = f(x[0, c * TOK:(c + 1) * TOK].T)
        m["xh0"] = f(x[0, c * TOK - HALO:c * TOK].T) if c > 0 else np.zeros((D, HALO), np.float32)
        m["flag"] = np.full((128, 1), 0.0 if c == 0 else 1.0, np.float32)
        p = np.arange(128, dtype=np.uint32)
        m["idx"] = np.ascontiguousarray(np.stack([256 * c + p, 1024 * c + p, max(c - 1, 0) * D + p], axis=1).astype(np.uint32))
        per = [ssm_host_inputs_blk(lre[j], lim[j], ldt[j], bre[j], bim[j], cre[j], cim[j], dsk[j], c) for j in range(2)]
        for k in SSM_KEYS:
            m["S_" + k] = np.ascontiguousarray(np.stack([per[0][k], per[1][k]], axis=0))
        m["SGN"] = per[0]["SGN"]
        m["MASK"] = per[0]["MASK"]
        m["GMASK"] = per[0]["GMASK"]
        maps.append(m)
    res = run_bass_kernel_spmd(nc, maps, core_ids=cores)
    out = np.concatenate([r["out"].T for r in res.results], axis=0)[None]
    return np.ascontiguousarray(out.astype(np.float32))


NB = TCH // 8


def ssm_phase_blk(c, io, seq=SEQ):
    from contextlib import ExitStack
    nc = c.nc
    P = c.P
    uid = io["uid"]
    hgath, idx0, ybd = io["hgath"], io["idx0"], io["yb"]
    nchunk = seq // TCH
    with ExitStack() as es:
        sbt = lambda n, s, dt=F32: es.enter_context(nc.sbuf_tensor(f"{n}_{uid}", s, dt))
        par_names = ["LR1", "LI1", "DT1", "LR2", "LI2", "DT2", "BRT", "BIT", "C1", "C2", "BQ1", "BQ2", "SGN", "MASK", "DCOL", "GMASK"]
        par = {}
        for n in par_names:
            shp = list(io[n].shape)
            par[n] = sbt("p" + n, shp)
            P.dma("sp", lambda e, n=n: e.dma_start(out=par[n][:, :], in_=io[n][:, :]), [], [("par",)], sem="par")
        LR1, LI1, DT1, LR2, LI2, DT2 = (par[n] for n in ["LR1", "LI1", "DT1", "LR2", "LI2", "DT2"])
        BRT, BIT, C1, C2, BQ1, BQ2, SGN, MASK, DCOL, GMASK = (par[n] for n in ["BRT", "BIT", "C1", "C2", "BQ1", "BQ2", "SGN", "MASK", "DCOL", "GMASK"])
        R = [("par",), ("tab",)]
        T = [("tab",)]
        tt = lambda o, a, b, op: P.op("dve", lambda e: e.tensor_tensor(out=o, in0=a, in1=b, op=op), R, T)
        ts1 = lambda o, a, s1, op: P.op("dve", lambda e: e.tensor_scalar(out=o, in0=a, scalar1=s1, scalar2=None, op0=op), R, T)

        def abar(LR, LI, DT, n, nm):
            dt = sbt(nm + "dt", [128, n]); t1 = sbt(nm + "t1", [128, n]); t2 = sbt(nm + "t2", [128, n])
            ar = sbt(nm + "ar", [128, n]); ai = sbt(nm + "ai", [128, n]); mg = sbt(nm + "mg", [128, n])
            ki = sbt(nm + "ki", [128, n], mybir.dt.int32); kf = sbt(nm + "kf", [128, n])
            P.op("act", lambda e: e.activation(out=dt[:, :], in_=DT[:, :], func=AF.Exp), R, T)
            tt(t1[:, :], LR[:, :], dt[:, :], ALU.mult)
            P.op("act", lambda e: e.activation(out=mg[:, :], in_=t1[:, :], func=AF.Exp), R, T)
            tt(t1[:, :], LI[:, :], dt[:, :], ALU.mult)

            def sin_of(dst, shift):
                ts1(t2[:, :], t1[:, :], shift, ALU.add)
                P.op("dve", lambda e: e.tensor_scalar(out=kf[:, :], in0=t2[:, :], scalar1=1.0 / (2 * PI), scalar2=0.5, op0=ALU.mult, op1=ALU.add), R, T)
                P.op("dve", lambda e: e.tensor_copy(out=ki[:, :], in_=kf[:, :]), R, T)
                P.op("dve", lambda e: e.tensor_copy(out=kf[:, :], in_=ki[:, :]), R, T)
                P.op("dve", lambda e: e.scalar_tensor_tensor(out=t2[:, :], in0=kf[:, :], scalar=-2 * PI, in1=t2[:, :], op0=ALU.mult, op1=ALU.add), R, T)
                P.op("dve", lambda e: e.tensor_scalar(out=kf[:, :], in0=t2[:, :], scalar1=-PI, scalar2=2 * PI, op0=ALU.is_lt, op1=ALU.mult), R, T)
                tt(t2[:, :], t2[:, :], kf[:, :], ALU.add)
                P.op("dve", lambda e: e.tensor_scalar(out=t2[:, :], in0=t2[:, :], scalar1=-PI, scalar2=PI, op0=ALU.max, op1=ALU.min), R, T)
                P.op("act", lambda e: e.activation(out=t2[:, :], in_=t2[:, :], func=AF.Sin), R, T)
                tt(dst[:, :], mg[:, :], t2[:, :], ALU.mult)
            sin_of(ai, 0.0)
            sin_of(ar, 0.5 * PI)
            return ar, ai

        def kfac(LR, LI, ar, ai, n, nm):
            nr = sbt(nm + "nr", [128, n]); den = sbt(nm + "den", [128, n]); kre = sbt(nm + "kre", [128, n]); kim = sbt(nm + "kim", [128, n]); t = sbt(nm + "kt", [128, n])
            ts1(nr[:, :], ar[:, :], -1.0, ALU.add)
            tt(den[:, :], LR[:, :], LR[:, :], ALU.mult)
            tt(t[:, :], LI[:, :], LI[:, :], ALU.mult)
            tt(den[:, :], den[:, :], t[:, :], ALU.add)
            P.op("dve", lambda e: e.reciprocal(out=den[:, :], in_=den[:, :]), R, T)
            tt(kre[:, :], nr[:, :], LR[:, :], ALU.mult)
            tt(t[:, :], ai[:, :], LI[:, :], ALU.mult)
            tt(kre[:, :], kre[:, :], t[:, :], ALU.add)
            tt(kre[:, :], kre[:, :], den[:, :], ALU.mult)
            tt(kim[:, :], ai[:, :], LR[:, :], ALU.mult)
            tt(t[:, :], nr[:, :], LI[:, :], ALU.mult)
            tt(kim[:, :], kim[:, :], t[:, :], ALU.subtract)
            tt(kim[:, :], kim[:, :], den[:, :], ALU.mult)
            return kre, kim

        def cmul(o_r, o_i, a_r, a_i, b_r, b_i, t):
            tt(o_r, a_r, b_r, ALU.mult)
            tt(t, a_i, b_i, ALU.mult)
            tt(o_r, o_r, t, ALU.subtract)
            tt(o_i, a_r, b_i, ALU.mult)
            tt(t, a_i, b_r, ALU.mult)
            tt(o_i, o_i, t, ALU.add)

        ar1, ai1 = abar(LR1, LI1, DT1, GPC, "a1")
        kre1, kim1 = kfac(LR1, LI1, ar1, ai1, GPC, "k1")
        PW1 = sbt("PW1", [128, 9, 2, GPC])
        tmp1 = sbt("tmp1", [128, GPC])
        P.op("dve", lambda e: e.memset(PW1[:, 0, 0, :], 1.0), R, T)
        P.op("dve", lambda e: e.memset(PW1[:, 0, 1, :], 0.0), R, T)
        P.op("dve", lambda e: e.tensor_copy(out=PW1[:, 1, 0, :], in_=ar1[:, :]), R, T)
        P.op("dve", lambda e: e.tensor_copy(out=PW1[:, 1, 1, :], in_=ai1[:, :]), R, T)
        for m in range(2, 9):
            cmul(PW1[:, m, 0, :], PW1[:, m, 1, :], PW1[:, m - 1, 0, :], PW1[:, m - 1, 1, :], ar1[:, :], ai1[:, :], tmp1[:, :])
        ARR = sbt("ARR", [128, 2, GPC]); AIP = sbt("AIP", [128, 2, GPC])
        P.op("dve", lambda e: e.tensor_copy(out=ARR[:, 0, :], in_=PW1[:, 8, 0, :]), R, T)
        P.op("dve", lambda e: e.tensor_copy(out=ARR[:, 1, :], in_=PW1[:, 8, 0, :]), R, T)
        P.op("dve", lambda e: e.tensor_copy(out=AIP[:, 0, :], in_=PW1[:, 8, 1, :]), R, T)
        ts1(AIP[:, 1, :], PW1[:, 8, 1, :], -1.0, ALU.mult)

        C1s = sbt("C1s", [128, GPC, 16]); C2v = C2[:, :].rearrange("p (g c) -> p g c", g=GPC)
        ts1(C1s[:, :, :], C1[:, :].rearrange("p (g c) -> p g c", g=GPC), SGN[:, 0:1], ALU.mult)
        TCc = sbt("TCc", [128, GPC, 16]); TCt = sbt("TCt", [128, GPC, 16])
        CzPadL = sbt("CzPadL", [128, 9, GPC, 128], BF16)
        GM4 = GMASK[:, :].rearrange("p (g j) -> p g j", g=GPC).unsqueeze(3).to_broadcast([128, GPC, 8, 16])
        for l in range(9):
            m = l + 1 if l < 8 else 0
            prb = PW1[:, m, 0, :].unsqueeze(2).to_broadcast([128, GPC, 16])
            pib = PW1[:, m, 1, :].unsqueeze(2).to_broadcast([128, GPC, 16])
            tt(TCc[:, :, :], C1s[:, :, :], prb, ALU.mult)
            tt(TCt[:, :, :], C2v, pib, ALU.mult)
            tt(TCc[:, :, :], TCc[:, :, :], TCt[:, :, :], ALU.subtract)
            tt(CzPadL[:, l, :, :].rearrange("p g (j c) -> p g j c", j=8), TCc[:, :, :].unsqueeze(2).to_broadcast([128, GPC, 8, 16]), GM4, ALU.mult)

        Kbd = sbt("Kbd", [128, 2, 8, 128], BF16)
        with ExitStack() as es2:
            sb2 = lambda n, s, dt=F32: es2.enter_context(nc.sbuf_tensor(f"{n}_{uid}", s, dt))
            XPad = sb2("XPad", [128, 8, GPC, 128], BF16)
            Fr = sb2("Fr", [128, GPC]); Fi = sb2("Fi", [128, GPC])
            Xc = sb2("Xc", [128, GPC, 16]); Xt = sb2("Xt", [128, GPC, 16])
            BQ1v = BQ1[:, :].rearrange("p (g c) -> p g c", g=GPC)
            BQ2v = BQ2[:, :].rearrange("p (g c) -> p g c", g=GPC)
            pk = [es2.enter_context(nc.psum_tensor(f"pk{i}_{uid}", [128, 128], F32)) for i in range(2)]
            for lag in range(8):
                cmul(Fr[:, :], Fi[:, :], PW1[:, lag, 0, :], PW1[:, lag, 1, :], kre1[:, :], kim1[:, :], tmp1[:, :])
                ts1(Fi[:, :], Fi[:, :], SGN[:, 0:1], ALU.mult)
                tt(Xc[:, :, :], BQ1v, Fr[:, :].unsqueeze(2).to_broadcast([128, GPC, 16]), ALU.mult)
                tt(Xt[:, :, :], BQ2v, Fi[:, :].unsqueeze(2).to_broadcast([128, GPC, 16]), ALU.mult)
                tt(Xc[:, :, :], Xc[:, :, :], Xt[:, :, :], ALU.subtract)
                tt(XPad[:, lag, :, :].rearrange("p g (j c) -> p g j c", j=8), Xc[:, :, :].unsqueeze(2).to_broadcast([128, GPC, 8, 16]), GM4, ALU.mult)
            i = 0
            for tile in range(2):
                for lag in range(8):
                    pb = i % 2
                    i += 1
                    for j in range(8):
                        g = tile * 8 + j
                        P.op("pe", lambda e, g=g, j=j, lag=lag, pb=pb: e.matmul(pk[pb][:, :], lhsT=XPad[:, lag, g, :], rhs=CzPadL[:, 8, g, :], start=(j == 0), stop=(j == 7)),
                             [("tab",)], [("pk", pb)])
                    P.op("act", lambda e, tile=tile, lag=lag, pb=pb: e.activation(out=Kbd[:, tile, lag, :], in_=pk[pb][:, :], func=AF.Copy), [("pk", pb)], [("tabk",)])
            P.op("dve", lambda e: e.memset(tmp1[:, :], 0.0), [("tabk",), ("pk", 0), ("pk", 1)] + R, T)

        BzPadK = sbt("BzPadK", [128, GPC, 8, 2, 128], BF16)
        with ExitStack() as es3:
            sb3 = lambda n, s, dt=F32: es3.enter_context(nc.sbuf_tensor(f"{n}_{uid}", s, dt))
            sbt_save = sbt
            sbt = sb3
            ar2, ai2 = abar(LR2, LI2, DT2, 128, "a2")
            kre2, kim2 = kfac(LR2, LI2, ar2, ai2, 128, "k2")
            sbt = sbt_save
            bbr = sb3("bbr", [128, 128]); bbi = sb3("bbi", [128, 128]); t2_ = sb3("t2_", [128, 128])
            cmul(bbr[:, :], bbi[:, :], kre2[:, :], kim2[:, :], BRT[:, :], BIT[:, :], t2_[:, :])
            PW2 = sb3("PW2", [128, 8, 2, 128])
            P.op("dve", lambda e: e.memset(PW2[:, 0, 0, :], 1.0), R, T)
            P.op("dve", lambda e: e.memset(PW2[:, 0, 1, :], 0.0), R, T)
            for m in range(1, 8):
                cmul(PW2[:, m, 0, :], PW2[:, m, 1, :], PW2[:, m - 1, 0, :], PW2[:, m - 1, 1, :], ar2[:, :], ai2[:, :], t2_[:, :])
            Tz = sb3("Tz", [128, 2, 2, 128])
            xr = sb3("xr", [128, 128]); xi = sb3("xi", [128, 128])
            MK3 = MASK[:, :].unsqueeze(2).to_broadcast([128, 8, 128])
            for k in range(8):
                m = 7 - k
                cmul(xr[:, :], xi[:, :], PW2[:, m, 0, :], PW2[:, m, 1, :], bbr[:, :], bbi[:, :], t2_[:, :])
                for tile in range(2):
                    cs = slice(tile * 64, tile * 64 + 64)
                    P.op("dve", lambda e, tile=tile, cs=cs: e.tensor_copy(out=Tz[:, 0, tile, 0:64], in_=xr[:, cs]), R, T)
                    P.op("dve", lambda e, tile=tile, cs=cs: e.tensor_copy(out=Tz[:, 0, tile, 64:128], in_=xi[:, cs]), R, T)
                    ts1(Tz[:, 1, tile, 0:64], xi[:, cs], -1.0, ALU.mult)
                    P.op("dve", lambda e, tile=tile, cs=cs: e.tensor_copy(out=Tz[:, 1, tile, 64:128], in_=xr[:, cs]), R, T)
                    for zz in range(2):
                        tt(BzPadK[:, tile * 8:(tile + 1) * 8, k, zz, :], Tz[:, zz, tile, :].unsqueeze(1).to_broadcast([128, 8, 128]), MK3, ALU.mult)
            P.op("dve", lambda e: e.memset(tmp1[:, :], 0.0), R, T)

        V = [sbt(f"V{i}", [128, 2, GPC, NB]) for i in range(2)]
        Sbf = [sbt(f"Sbf{i}", [128, GPC, NB], BF16) for i in range(2)]
        hb = [sbt(f"hb{i}", [128, 2, TCH], BF16) for i in range(2)]
        TX1 = [sbt(f"TX1{i}", [128, 2, 8]) for i in range(2)]; TX2 = [sbt(f"TX2{i}", [128, 2, 8]) for i in range(2)]
        Z0 = sbt("Z0", [128, 2, GPC])
        P.op("dve", lambda e: e.memset(Z0[:, :, :], 0.0), [], [("Z0",)])
        ysb = [sbt(f"ysb{i}", [128, TCH]) for i in range(2)]
        xg = [sbt(f"xg{i}", [128, TCH]) for i in range(2)]
        wg = [sbt(f"wg{i}", [128, TCH]) for i in range(2)]
        og = [sbt(f"og{i}", [128, TCH], BF16) for i in range(4)]
        pse = [es.enter_context(nc.psum_tensor(f"pse{i}_{uid}", [128, TCH], F32)) for i in range(4)]
        psy = [es.enter_context(nc.psum_tensor(f"psy{i}_{uid}", [128, TCH], F32)) for i in range(2)]
        er = 0
        yr = 0
        orr = 0
        for k in range(nchunk):
            b = k % 2
            for tile in range(2):
                eo = (k * D + tile * 128) * TCH
                P.dma("pool", lambda e, b=b, tile=tile, eo=eo: e.indirect_dma_start(out=hb[b][:, tile, :], out_offset=None, in_=hgath[:, :],
                                                                                   in_offset=bass.IndirectOffsetOnAxis(ap=idx0[:, 0:1], axis=0), element_offset=eo),
                      [("hgath",), ("idx",), ("tab",)], [("hb", b)], sem=("hbl", b, tile))
            hbv = [hb[b][:, tile, :].rearrange("p (n k) -> p k n", k=8) for tile in range(2)]
            for zz in range(2):
                for tile in range(2):
                    pb = er % 4
                    er += 1
                    for j in range(8):
                        g = tile * 8 + j
                        for kk in range(8):
                            P.op("pe", lambda e, g=g, j=j, kk=kk, zz=zz, pb=pb, tile=tile, hbv=hbv: e.matmul(pse[pb][:, j * NB:(j + 1) * NB], lhsT=BzPadK[:, g, kk, zz, :], rhs=hbv[tile][:, kk, :],
                                                                                                     start=(kk == 0), stop=(kk == 7)),
                                 [("hb", b), ("tab",)], [("pse", pb)])
                    P.op("act", lambda e, zz=zz, tile=tile, pb=pb, b=b: e.activation(out=V[b][:, zz, tile * 8:(tile + 1) * 8, :], in_=pse[pb][:, :].rearrange("p (j n) -> p j n", j=8), func=AF.Copy),
                         [("pse", pb)], [("V", b)])
            if k == 0:
                P.op("act", lambda e, b=b: e.activation(out=Sbf[b][:, :, 0], in_=Z0[:, 0, :], func=AF.Copy), [("Z0",)], [("Sbf", b)])
            else:
                P.op("act", lambda e, b=b: e.activation(out=Sbf[b][:, :, 0], in_=V[1 - b][:, 0, :, NB - 1], func=AF.Copy), [("V", 1 - b)], [("Sbf", b)])
            for t in range(NB):
                first = (k == 0 and t == 0)
                srcb = (1 - b) if t == 0 else b
                tp = NB - 1 if t == 0 else t - 1
                zr = [("Z0",)] if first else [("V", srcb)]
                seqs = []
                for X in range(2):
                    gs_ = slice(8 * X, 8 * X + 8)
                    if first:
                        zp, zp0, zp1 = Z0[:, :, gs_], Z0[:, 0, gs_], Z0[:, 1, gs_]
                    else:
                        zp, zp0, zp1 = V[srcb][:, :, gs_, tp], V[srcb][:, 0, gs_, tp], V[srcb][:, 1, gs_, tp]
                    t1, t2 = TX1[X], TX2[X]
                    seqs.append([
                        (lambda e, zp=zp, t1=t1, gs_=gs_: e.tensor_tensor(out=t1[:, :, :], in0=zp, in1=ARR[:, :, gs_], op=ALU.mult), zr + [("tab",), ("T1", X)], [("T1", X)]),
                        (lambda e, zp1=zp1, t2=t2, gs_=gs_: e.tensor_tensor(out=t2[:, 0, :], in0=zp1, in1=AIP[:, 0, gs_], op=ALU.mult), zr + [("tab",), ("T2", X)], [("T2", X)]),
                        (lambda e, zp0=zp0, t2=t2, gs_=gs_: e.tensor_tensor(out=t2[:, 1, :], in0=zp0, in1=AIP[:, 1, gs_], op=ALU.mult), zr + [("tab",), ("T2", X)], [("T2", X)]),
                        (lambda e, t1=t1, t2=t2: e.tensor_tensor(out=t1[:, :, :], in0=t1[:, :, :], in1=t2[:, :, :], op=ALU.add), [("T1", X), ("T2", X)], [("T1", X)]),
                        (lambda e, b=b, t=t, t1=t1, gs_=gs_: e.tensor_tensor(out=V[b][:, :, gs_, t], in0=V[b][:, :, gs_, t], in1=t1[:, :, :], op=ALU.add), [("T1", X), ("V", b)], [("V", b)]),
                    ])
                for i5 in range(5):
                    for X in range(2):
                        fn, rd, wr = seqs[X][i5]
                        P.op("dve", fn, rd, wr, nosame=True)
            P.op("act", lambda e, b=b: e.activation(out=Sbf[b][:, :, 1:NB], in_=V[b][:, 0, :, 0:NB - 1], func=AF.Copy), [("V", b)], [("Sbf", b)])
            for tile in range(2):
                yb_ = yr % 2
                yr += 1
                pyv = psy[yb_][:, :].rearrange("p (n l) -> p l n", l=8)
                for l in range(8):
                    nmm = 8 + l + 1
                    i = 0
                    for j in range(8):
                        g = tile * 8 + j
                        P.op("pe", lambda e, g=g, l=l, i=i, nmm=nmm, pyv=pyv, b=b: e.matmul(pyv[:, l, :], lhsT=CzPadL[:, l, g, :], rhs=Sbf[b][:, g, :], start=(i == 0), stop=(i == nmm - 1)),
                             [("Sbf", b), ("tab",)], [("psy", yb_)])
                        i += 1
                    for kk in range(l + 1):
                        P.op("pe", lambda e, kk=kk, l=l, i=i, nmm=nmm, pyv=pyv, tile=tile, hbv=hbv: e.matmul(pyv[:, l, :], lhsT=Kbd[:, tile, l - kk, :], rhs=hbv[tile][:, kk, :], start=(i == 0), stop=(i == nmm - 1)),
                             [("hb", b), ("tabk",)], [("psy", yb_)])
                        i += 1
                ob = orr % 4
                orr += 1
                yb = yb_
                P.op("act", lambda e, yb=yb: e.activation(out=ysb[yb][:, :], in_=psy[yb][:, :], func=AF.Copy), [("psy", yb)], [("ysb", yb)])
                P.op("pool", lambda e, yb=yb, b=b, tile=tile: e.tensor_scalar(out=xg[yb][:, :], in0=hb[b][:, tile, :], scalar1=DCOL[:, tile:tile + 1], scalar2=None, op0=ALU.mult),
                     [("hb", b), ("par",)], [("xg", yb)])
                P.op("pool", lambda e, yb=yb: e.tensor_tensor(out=xg[yb][:, :], in0=xg[yb][:, :], in1=ysb[yb][:, :], op=ALU.add), [("xg", yb), ("ysb", yb)], [("xg", yb)])
                P.op("pool", lambda e, yb=yb: e.tensor_tensor(out=wg[yb][:, :], in0=xg[yb][:, :], in1=xg[yb][:, :], op=ALU.mult), [("xg", yb)], [("wg", yb)])
                P.op("pool", lambda e, yb=yb: e.tensor_scalar(out=wg[yb][:, :], in0=wg[yb][:, :], scalar1=0.044715, scalar2=1.0, op0=ALU.mult, op1=ALU.add), [("wg", yb)], [("wg", yb)])
                P.op("pool", lambda e, yb=yb: e.tensor_tensor(out=wg[yb][:, :], in0=wg[yb][:, :], in1=xg[yb][:, :], op=ALU.mult), [("wg", yb), ("xg", yb)], [("wg", yb)])
                P.op("act", lambda e, yb=yb: e.activation(out=wg[yb][:, :], in_=wg[yb][:, :], func=AF.Sigmoid, scale=1.5957691216), [("wg", yb)], [("wg", yb)])
                P.op("pool", lambda e, yb=yb, ob=ob: e.tensor_tensor(out=og[ob][:, :], in0=wg[yb][:, :], in1=xg[yb][:, :], op=ALU.mult), [("wg", yb), ("xg", yb)], [("og", ob)])
                P.dma("sp", lambda e, ob=ob, tile=tile, k=k: e.dma_start(out=ybd[k * 256 + tile * 128:k * 256 + (tile + 1) * 128, :], in_=og[ob][:, :]), [("og", ob)], [("yb",)], sem=("og", ob))
        P.barrier()


def ssm_host_inputs_blk(lam_re, lam_im, log_dt, b_re, b_im, c_re, c_im, d, core):
    m = ssm_host_inputs(np.zeros((256, 1), np.float32), lam_re, lam_im, log_dt, b_re, b_im, c_re, c_im, d, core)
    gs = slice(core * GPC, (core + 1) * GPC)
    cr = c_re[gs].transpose(2, 0, 1)
    ci = c_im[gs].transpose(2, 0, 1)
    m["C2"] = np.ascontiguousarray(np.concatenate([ci, cr], 0).reshape(128, GPC * 16))
    br = b_re[gs].transpose(1, 0, 2)
    bi = b_im[gs].transpose(1, 0, 2)
    m["BQ1"] = np.ascontiguousarray(np.concatenate([br, bi], 0).reshape(128, GPC * 16))
    m["BQ2"] = np.ascontiguousarray(np.concatenate([bi, br], 0).reshape(128, GPC * 16))
    gm = np.zeros((128, GPC, 8), np.float32)
    for g in range(GPC):
        gm[:, g, g % 8] = 1.0
    m["GMASK"] = gm.reshape(128, GPC * 8)
    del m["hT"]
    return m
```
